# Optimizing a Trainium2 kernel written in Bass

```python
import math
import jax, jax.numpy as jnp
from jax import lax
import numpy as np

D_MODEL = 1024
BATCH = 1
SEQ = 16384
DEPTH = 1
DEC_BATCH = 32
DEC_SEQ = 2048
PAST_LEN = 128

N_META = 16
CHUNK = 128
NPAD = CHUNK - N_META
N_ATT_HEADS = 8
ATT_DH = 64
ATT_VDIM = N_ATT_HEADS * 2 * ATT_DH
SSM_HD = 64
D_SSM = D_MODEL
H_SSM = D_SSM // SSM_HD
SSM_G = 2
SSM_R = H_SSM // SSM_G
SSM_N = 128
D_CONV = 5
CONV_DIM = D_SSM + 2 * SSM_G * SSM_N
MIX_DIM = ATT_VDIM + D_SSM
SPLIT_SIZES = (ATT_VDIM, ATT_VDIM, ATT_VDIM, D_SSM, CONV_DIM, 2 * H_SSM)
IN_DIM = sum(SPLIT_SIZES)
D_FF = 4 * D_MODEL
EPS = 1e-5

kernel_name = "hymba_diffattn_bissd_encoder"


def rmsnorm(x, w):
    xf = x.astype(jnp.float32)
    y = xf * lax.rsqrt(jnp.mean(xf * xf, axis=-1, keepdims=True) + EPS)
    return (y * w.astype(jnp.float32)).astype(x.dtype)


def pad_front(t):
    return jnp.pad(t, [(0, 0), (NPAD, 0)] + [(0, 0)] * (t.ndim - 2))


def alibi_slopes():
    return 2.0 ** (-8.0 * (jnp.arange(N_ATT_HEADS, dtype=jnp.float32) + 1.0) / N_ATT_HEADS)


def dwconv_centred(x, w, b):
    kern = w[:, None, :].astype(x.dtype)
    y = lax.conv_general_dilated(x, kern, window_strides=(1,), padding=[(D_CONV // 2, D_CONV // 2)],
                                 dimension_numbers=('NWC', 'WIO', 'NWC'), feature_group_count=x.shape[-1])
    return y + b.astype(x.dtype)


def diff_attention(q, k, v, lam, norm_w, lam_init):
    b, L, _ = q.shape
    P = L + NPAD
    nb = P // CHUNK
    qh = pad_front(q).reshape(b, P, N_ATT_HEADS, 2, ATT_DH)
    kh = pad_front(k).reshape(b, P, N_ATT_HEADS, 2, ATT_DH)
    vh = pad_front(v).reshape(b, P, N_ATT_HEADS, 2 * ATT_DH)
    idx = jnp.arange(P)
    kpos = (idx - NPAD).astype(jnp.float32)
    kvalid = idx >= NPAD
    slopes = alibi_slopes()
    scale = ATT_DH ** -0.5
    qb = jnp.swapaxes(qh.reshape(b, nb, CHUNK, N_ATT_HEADS, 2, ATT_DH), 0, 1)
    qpos = kpos.reshape(nb, CHUNK)

    def block(args):
        qblk, qp = args
        s = jnp.einsum('bqhjd,bkhjd->bhjqk', qblk, kh, preferred_element_type=jnp.float32) * scale
        bias = -slopes[:, None, None] * jnp.abs(qp[:, None] - kpos[None, :])
        s = jnp.where(kvalid, s + bias[None, :, None], -1e30)
        a = jax.nn.softmax(s, axis=-1)
        pm = a[:, :, 0] - lam * a[:, :, 1]
        return jnp.einsum('bhqk,bkhe->bqhe', pm.astype(vh.dtype), vh)

    o = lax.map(block, (qb, qpos))
    o = jnp.swapaxes(o, 0, 1).reshape(b, P, N_ATT_HEADS, 2 * ATT_DH)[:, NPAD:]
    o = rmsnorm(o, norm_w) * (1.0 - lam_init)
    return o.reshape(b, L, ATT_VDIM)


def ssd_scan(x, dt, a, bm, cm):
    b, P = x.shape[:2]
    c = P // CHUNK
    xg = (x * dt[..., None]).reshape(b, c, CHUNK, SSM_G, SSM_R, SSM_HD)
    adt = (dt * a).reshape(b, c, CHUNK, SSM_G, SSM_R).transpose(0, 3, 4, 1, 2)
    bc = bm.reshape(b, c, CHUNK, SSM_G, SSM_N)
    cc = cm.reshape(b, c, CHUNK, SSM_G, SSM_N)
    acs = jnp.cumsum(adt, axis=-1)
    lower = jnp.tril(jnp.ones((CHUNK, CHUNK), dtype=bool))
    seg = acs[..., :, None] - acs[..., None, :]
    lmat = jnp.where(lower, jnp.exp(jnp.where(lower, seg, 0.0)), 0.0)
    y_diag = jnp.einsum('bclgn,bcsgn,bgrcls,bcsgrp->bclgrp', cc, bc, lmat, xg)
    decay_states = jnp.exp(acs[..., -1:] - acs)
    states = jnp.einsum('bcsgn,bgrcs,bcsgrp->bcgrpn', bc, decay_states, xg)
    chunk_decay = jnp.exp(acs[..., -1])

    def step(carry, inp):
        st, dec = inp
        return carry * dec[..., None, None] + st, carry

    init = jnp.zeros((b, SSM_G, SSM_R, SSM_HD, SSM_N), jnp.float32)
    _, prev = lax.scan(step, init, (jnp.moveaxis(states, 1, 0), jnp.moveaxis(chunk_decay, 3, 0)))
    prev = jnp.moveaxis(prev, 0, 1)
    y_off = jnp.einsum('bclgn,bcgrpn,bgrcl->bclgrp', cc, prev, jnp.exp(acs))
    return (y_diag + y_off).reshape(b, P, H_SSM, SSM_HD)


def bissd_mixer(xs, bm, cm, z, dt_raw, dt_bias_f, dt_bias_b, a_log_f, a_log_b, d_skip, norm_w):
    b, L, _ = xs.shape
    P = L + NPAD
    f32 = jnp.float32
    xh = pad_front(xs.astype(f32)).reshape(b, P, H_SSM, SSM_HD)
    bmp = pad_front(bm.astype(f32)).reshape(b, P, SSM_G, SSM_N)
    cmp_ = pad_front(cm.astype(f32)).reshape(b, P, SSM_G, SSM_N)
    dtr = dt_raw.astype(f32)
    dt_f = pad_front(jax.nn.softplus(dtr[..., :H_SSM] + dt_bias_f.astype(f32)))
    dt_b = pad_front(jax.nn.softplus(dtr[..., H_SSM:] + dt_bias_b.astype(f32)))
    a_f = -jnp.exp(a_log_f.astype(f32))
    a_b = -jnp.exp(a_log_b.astype(f32))
    flip = lambda t: jnp.flip(t, axis=1)
    y_f = ssd_scan(xh, dt_f, a_f, bmp, cmp_)
    y_b = flip(ssd_scan(flip(xh), flip(dt_b), a_b, flip(bmp), flip(cmp_)))
    y = y_f + y_b + xh * d_skip.astype(f32)[:, None]
    y = y[:, NPAD:].reshape(b, L, D_SSM)
    return rmsnorm(y * jax.nn.silu(z.astype(f32)), norm_w).astype(z.dtype)


def trunk(x, meta_tokens, norm1_w, w_in, conv_w, conv_b, lambda_q1, lambda_k1, lambda_q2, lambda_k2,
          attn_norm_w, dt_bias_f, dt_bias_b, a_log_f, a_log_b, d_skip, ssm_norm_w, w_out,
          norm2_w, w_up, w_down, final_norm_w):
    b = x.shape[0]
    meta = jnp.broadcast_to(meta_tokens.astype(x.dtype)[None], (b, N_META, D_MODEL))
    h = jnp.concatenate([meta, x], axis=1)
    cuts = list(np.cumsum(SPLIT_SIZES)[:-1])
    for li in range(DEPTH):
        lam_init = 0.8 - 0.6 * math.exp(-0.3 * li)
        u = rmsnorm(h, norm1_w[li])
        proj = u @ w_in[li]
        q, k, v, z, xbc, dt_raw = jnp.split(proj, cuts, axis=-1)
        xbc = jax.nn.silu(dwconv_centred(xbc, conv_w[li], conv_b[li]))
        xs, bm, cm = jnp.split(xbc, [D_SSM, D_SSM + SSM_G * SSM_N], axis=-1)
        lam = (jnp.exp(jnp.sum(lambda_q1[li].astype(jnp.float32) * lambda_k1[li].astype(jnp.float32)))
               - jnp.exp(jnp.sum(lambda_q2[li].astype(jnp.float32) * lambda_k2[li].astype(jnp.float32)))
               + lam_init)
        att = diff_attention(q, k, v, lam, attn_norm_w[li], lam_init)
        ssm = bissd_mixer(xs, bm, cm, z, dt_raw, dt_bias_f[li], dt_bias_b[li], a_log_f[li], a_log_b[li],
                          d_skip[li], ssm_norm_w[li])
        h = h + jnp.concatenate([att, ssm.astype(att.dtype)], axis=-1) @ w_out[li]
        u2 = rmsnorm(h, norm2_w[li])
        h = h + jnp.square(jax.nn.relu(u2 @ w_up[li])) @ w_down[li]
    h = rmsnorm(h, final_norm_w)
    return h[:, N_META:]


def setup_inputs(seed: int = 0) -> dict:
    key = jax.random.key(seed)
    ks = jax.random.split(key, 24)
    f32 = jnp.float32
    nrm = lambda k, shape, s: jax.random.normal(k, shape, f32) * s
    dt0 = jnp.exp(jax.random.uniform(ks[12], (DEPTH, H_SSM), f32, math.log(1e-3), math.log(1e-1)))
    dt1 = jnp.exp(jax.random.uniform(ks[13], (DEPTH, H_SSM), f32, math.log(1e-3), math.log(1e-1)))
    inv_sp = lambda d: d + jnp.log(-jnp.expm1(-d))
    return {
        "x_prompt": nrm(ks[0], (BATCH, SEQ, D_MODEL), 1.0),
        "x_sample": nrm(ks[1], (DEC_BATCH, DEC_SEQ, D_MODEL), 1.0),
        "meta_tokens": nrm(ks[2], (N_META, D_MODEL), 1.0),
        "norm1_w": 1.0 + nrm(ks[3], (DEPTH, D_MODEL), 0.02),
        "w_in": nrm(ks[4], (DEPTH, D_MODEL, IN_DIM), D_MODEL ** -0.5),
        "conv_w": nrm(ks[5], (DEPTH, D_CONV, CONV_DIM), D_CONV ** -0.5),
        "conv_b": nrm(ks[6], (DEPTH, CONV_DIM), 0.02),
        "lambda_q1": nrm(ks[7], (DEPTH, ATT_DH), 0.1),
        "lambda_k1": nrm(ks[8], (DEPTH, ATT_DH), 0.1),
        "lambda_q2": nrm(ks[9], (DEPTH, ATT_DH), 0.1),
        "lambda_k2": nrm(ks[10], (DEPTH, ATT_DH), 0.1),
        "attn_norm_w": 1.0 + nrm(ks[11], (DEPTH, 2 * ATT_DH), 0.02),
        "dt_bias_f": inv_sp(dt0),
        "dt_bias_b": inv_sp(dt1),
        "a_log_f": jnp.log(jax.random.uniform(ks[14], (DEPTH, H_SSM), f32, 1.0, 16.0)),
        "a_log_b": jnp.log(jax.random.uniform(ks[15], (DEPTH, H_SSM), f32, 1.0, 16.0)),
        "d_skip": 1.0 + nrm(ks[16], (DEPTH, H_SSM), 0.02),
        "ssm_norm_w": 1.0 + nrm(ks[17], (DEPTH, D_SSM), 0.02),
        "w_out": nrm(ks[18], (DEPTH, MIX_DIM, D_MODEL), MIX_DIM ** -0.5),
        "norm2_w": 1.0 + nrm(ks[19], (DEPTH, D_MODEL), 0.02),
        "w_up": nrm(ks[20], (DEPTH, D_MODEL, D_FF), D_MODEL ** -0.5),
        "w_down": nrm(ks[21], (DEPTH, D_FF, D_MODEL), D_FF ** -0.5),
        "final_norm_w": 1.0 + nrm(ks[22], (D_MODEL,), 0.02),
    }


def reference(x_prompt, x_sample, meta_tokens, norm1_w, w_in, conv_w, conv_b, lambda_q1, lambda_k1,
              lambda_q2, lambda_k2, attn_norm_w, dt_bias_f, dt_bias_b, a_log_f, a_log_b, d_skip,
              ssm_norm_w, w_out, norm2_w, w_up, w_down, final_norm_w):
    params = (meta_tokens, norm1_w, w_in, conv_w, conv_b, lambda_q1, lambda_k1, lambda_q2, lambda_k2,
              attn_norm_w, dt_bias_f, dt_bias_b, a_log_f, a_log_b, d_skip, ssm_norm_w, w_out,
              norm2_w, w_up, w_down, final_norm_w)
    y_prompt = trunk(x_prompt, *params)
    y_sample = trunk(x_sample, *params)
    return (y_prompt, y_sample)
```

```python
import math
from contextlib import ExitStack
import numpy as np
import ml_dtypes
import concourse.bass as bass
import concourse.mybir as mybir
from concourse.bass_utils import run_bass_kernel_spmd

F32 = mybir.dt.float32
BF16 = mybir.dt.bfloat16
AF = mybir.ActivationFunctionType
ALU = mybir.AluOpType
PE, ACT, DVE, POOL, SP = "tensor", "scalar", "vector", "gpsimd", "sync"
COMPUTE = (PE, ACT, DVE, POOL)
ENGS = (PE, ACT, DVE, POOL, SP)

D = 1024
NH = 8
EPS = 1e-5
TT = 140
N_META = 16
LAM_INIT = 0.8 - 0.6 * math.exp(-0.3 * 0)
CQ, CK, CV, CZ, CX, CDT = 0, 1024, 2048, 3072, 4096, 5632


class Buf:
    __slots__ = ("name", "w", "rd", "dsem", "dcnt", "keep")

    def __init__(self, name=""):
        self.name = name
        self.keep = bool(name)
        self.w = None
        self.rd = []
        self.dsem = None
        self.dcnt = 0


class Op:
    __slots__ = ("eng", "fn", "deps", "is_dma", "flag", "sem", "val", "dbuf", "win", "pos")

    def __init__(self, eng, fn, is_dma):
        self.eng, self.fn, self.is_dma = eng, fn, is_dma
        self.deps = []
        self.flag = False
        self.sem = None
        self.val = 0
        self.dbuf = None
        self.win = 0
        self.pos = 0


class _Rec:
    def __init__(self):
        self.call = None

    def __getattr__(self, name):
        def f(*a, **k):
            self.call = (name, a, k)
            return None
        return f


class Prog:
    def __init__(self, nc):
        self.nc = nc
        self.pending = []
        self.bufs = []
        self.win = 0
        self.engs = {PE: nc.tensor, ACT: nc.scalar, DVE: nc.vector, POOL: nc.gpsimd, SP: nc.sync}
        self.esem = {e: nc.alloc_semaphore(f"prog_{e}") for e in ENGS}
        self.ecnt = {e: 0 for e in ENGS}
        self.seen = {e: {} for e in ENGS}
        self.winlast = {}
        self.lastop = {e: None for e in ENGS}
        self.n_ops = 0
        self.n_waits = 0
        self.sempool = []
        self.sempool_sw = []
        self.semq = {}
        self.nsem = 0

    def buf(self, name=""):
        b = Buf(name)
        self.bufs.append(b)
        return b

    def mark(self):
        return len(self.bufs)

    def release(self, mk):
        del self.bufs[mk:]

    def _add(self, op, reads, writes):
        deps = {}
        raw = set()
        for b in reads:
            if b.w is not None:
                deps[id(b.w)] = b.w
                raw.add(id(b.w))
        for b in writes:
            if b.w is not None and not (op.is_dma and b.w.is_dma and b.w.dbuf is op.dbuf):
                deps[id(b.w)] = b.w
            for r in b.rd:
                deps[id(r)] = r
        dl = []
        for d in deps.values():
            if d is op:
                continue
            if (not d.is_dma) and (not op.is_dma) and d.eng == op.eng:
                if op.eng == PE or id(d) not in raw:
                    continue
            dl.append(d)
        op.deps = dl
        for b in reads:
            b.rd.append(op)
        for b in writes:
            b.w = op
            b.rd = []
        op.win = self.win
        self.pending.append(op)
        if not op.is_dma:
            self.lastop[op.eng] = op
        self.n_ops += 1
        return op

    @staticmethod
    def _bind(fn):
        r = _Rec()
        fn(r)
        c = r.call
        return lambda e: getattr(e, c[0])(*c[1], **c[2])

    def op(self, eng, fn, reads=(), writes=()):
        return self._add(Op(eng, self._bind(fn), False), list(reads), list(writes))

    def dma(self, eng, fn, reads=(), writes=(), sbuf=None):
        o = Op(eng, self._bind(fn), True)
        o.dbuf = sbuf
        return self._add(o, list(reads), list(writes))

    def barrier(self):
        deps = {}
        for e in COMPUTE:
            if self.lastop[e] is not None:
                deps[id(self.lastop[e])] = self.lastop[e]
        for b in self.bufs:
            if b.w is not None and b.w.is_dma:
                deps[id(b.w)] = b.w
            for r in b.rd:
                if r.is_dma:
                    deps[id(r)] = r
        b0 = Op(SP, lambda e: e.nop(), False)
        b0.deps = list(deps.values())
        b0.win = self.win
        self.pending.append(b0)
        self.lastop[SP] = b0
        for e in COMPUTE + (SP,):
            o = Op(e, lambda en: en.nop(), False)
            o.deps = [b0]
            o.win = self.win
            self.pending.append(o)
            self.lastop[e] = o
        for b in self.bufs:
            b.w = None
            b.rd = []
        self.flush()
        for b in self.bufs:
            if b.dsem is not None:
                (self.sempool_sw if self.semq.get(id(b.dsem)) == POOL else self.sempool).append((b.dsem, b.dcnt))
                b.dsem = None

    def flush(self):
        nc = self.nc
        ops = self.pending
        self.pending = []
        last = {}
        for o in ops:
            for d in o.deps:
                d.flag = True
            if not o.is_dma:
                last[o.eng] = o
        for e, o in last.items():
            o.flag = True
            self.winlast[(self.win, e)] = o
        for o in ops:
            if o.is_dma:
                b = o.dbuf
                if b.dsem is None:
                    pool = self.sempool_sw if o.eng == POOL else self.sempool
                    if pool:
                        b.dsem, b.dcnt = pool.pop()
                    else:
                        self.nsem += 1
                        b.dsem = nc.alloc_semaphore(f"dma_{self.nsem}")
                        b.dcnt = 0
                    self.semq[id(b.dsem)] = o.eng
                b.dcnt += 16
                o.sem, o.val = b.dsem, b.dcnt
            elif o.flag:
                self.ecnt[o.eng] += 1
                o.sem, o.val = self.esem[o.eng], self.ecnt[o.eng]
        for o in ops:
            e = self.engs[o.eng]
            need = {}
            for d in o.deps:
                if d.sem is None:
                    d = self.winlast[(d.win, d.eng)]
                k = id(d.sem)
                if k not in need or need[k][1] < d.val:
                    need[k] = (d.sem, d.val)
            sn = self.seen[o.eng]
            for k, (s, v) in need.items():
                if sn.get(k, 0) >= v:
                    continue
                e.wait_ge(s, v)
                sn[k] = v
                self.n_waits += 1
            ins = o.fn(e)
            if o.is_dma:
                ins.then_inc(o.sem, 16)
            elif o.flag:
                ins.then_inc(o.sem, 1)
        self.win += 1

    def finish(self):
        self.flush()
        fe = self.engs[SP]
        for b in self.bufs:
            if b.dsem is not None:
                fe.wait_ge(b.dsem, b.dcnt)
        for (sm, cnt) in self.sempool + self.sempool_sw:
            if cnt > 0:
                fe.wait_ge(sm, cnt)


class Rot:
    def __init__(self, items):
        self.items = items
        self.i = 0

    def next(self):
        it = self.items[self.i % len(self.items)]
        self.i += 1
        return it


class Cfg:
    def __init__(self, ncores=8, sp=16384, ss=2048, nseq=4, nown=4):
        self.NC, self.SP, self.SS, self.NSEQ, self.NOWN = ncores, sp, ss, nseq, nown
        assert sp == ncores * nown * 512 and ss == nown * 512
        self.NCTX = sp // 512 - nown
        self.jobs = [self.NCTX] + [0] * nseq
        self.OWNT = nown * 512

    def ntiles(self, j):
        return 1 + self.jobs[j] + self.NOWN

    def L(self, j):
        return 16 + 512 * (self.ntiles(j) - 1)

    def nch(self, j):
        return 1 + 4 * (self.ntiles(j) - 1)


def build_program(cfg):
    nc = bass.Bass("TRN2", target_bir_lowering=False)
    P = Prog(nc)
    NOWN, OWNT = cfg.NOWN, cfg.OWNT
    NJ = len(cfg.jobs)
    Lmax = max(cfg.L(j) for j in range(NJ))
    NCHmax = max(cfg.nch(j) for j in range(NJ))

    def din(name, shape, dt=F32):
        return nc.dram_tensor(name, list(shape), dt, kind="ExternalInput").ap()

    def dscr(name, shape, dt=BF16):
        return nc.dram_tensor(name, list(shape), dt, kind="Internal").ap()

    xw = [din(f"xw{j}", [cfg.ntiles(j), 516, D]) for j in range(NJ)]
    kaug = [din(f"kaug{j}", [NH, 8, cfg.L(j)], BF16) for j in range(NJ)]
    qaug = [din(f"qaug{j}", [3, NH, 8, OWNT], BF16) for j in range(NJ)]
    ttab = [din(f"ttab{j}", [cfg.ntiles(j), 128, TT]) for j in range(NJ)]
    yout = [nc.dram_tensor(f"y{j}", [OWNT, D], F32, kind="ExternalOutput").ap() for j in range(NJ)]
    w_in = din("w_in", [D, 5664])
    w_out = din("w_out", [2048, D])
    w_up = din("w_up", [D, 4096])
    w_dn = din("w_dn", [4096, D])
    cst = din("cst", [128, 6 * 128 + 4 * 512])
    gtab = din("gtab", [128, 8 + 8 + 16 + 1024 + 1024 + 128 + 256])
    ws_in = dscr("ws_in", [128, 8, 5664])
    ws_out = dscr("ws_out", [128, 16, D])
    ws_up = dscr("ws_up", [128, 8, 4096])
    ws_dn = dscr("ws_dn", [128, 32, D])
    KT = dscr("KT", [NH, 2, 72, Lmax])
    QT = dscr("QT", [NH, 2, 64, OWNT])
    VA = dscr("VA", [NH, 128, NCHmax, 129])
    PFB = dscr("PFB", [2, 4 * NOWN, 128, D])
    SBS = dscr("SBS", [4 * NOWN, 128, D])
    b_ws = {k: P.buf("ws_" + k) for k in ("in", "out", "up", "dn")}
    b_KT, b_QT, b_VA, b_PFB, b_SBS = P.buf("KT"), P.buf("QT"), P.buf("VA"), P.buf("PFB"), P.buf("SBS")

    ES = ExitStack()
    G = ES

    _nm = [0]

    def sb(st, name, shape, dt=F32):
        _nm[0] += 1
        return st.enter_context(nc.sbuf_tensor(f"{name}_{_nm[0]}", list(shape), dt))

    allbanks = nc.alloc_psum_tensor("allbanks", [128, 4096], F32)
    banks = [allbanks[:, 512 * i:512 * (i + 1)] for i in range(8)]
    bk = [P.buf(f"bank{i}") for i in range(8)]

    cst_t = sb(G, "cst_t", [128, 6 * 128 + 4 * 512])
    gtab_t = sb(G, "gtab_t", [128, 8 + 8 + 16 + 1024 + 1024 + 128 + 256])
    identb = sb(G, "identb", [128, 128], BF16)
    lam_t = sb(G, "lam_t", [128, 8])
    CF = sb(G, "CF", [128, D])
    CB = sb(G, "CB", [128, D])
    decs = sb(G, "decs", [128, 4 * NOWN, 16])
    b_cst, b_gt, b_idb, b_lam, b_CF, b_CB, b_decs = (P.buf(n) for n in ("cst", "gt", "idb", "lam", "CF", "CB", "decs"))
    identf = cst_t[:, 0:128]
    uincl = cst_t[:, 128:256]
    negus = cst_t[:, 256:384]
    onesf = cst_t[:, 384:512]
    maskf = cst_t[:, 512:640]
    maskb = cst_t[:, 640:768]
    absd = [cst_t[:, 768 + 512 * i: 768 + 512 * (i + 1)] for i in range(4)]
    w1T = gtab_t[:, 0:8]
    w2T = gtab_t[:, 8:16]
    Dbc = gtab_t[:, 16:32]
    ssmw = gtab_t[:, 32:32 + 1024]
    finw = gtab_t[:, 1056:1056 + 1024]
    attw = gtab_t[:, 2080:2080 + 128]
    lamv = gtab_t[:, 2208:2208 + 256]

    P.dma(SP, lambda e: e.dma_start(out=cst_t[:], in_=cst[:, :]), writes=[b_cst], sbuf=b_cst)
    P.dma(SP, lambda e: e.dma_start(out=gtab_t[:], in_=gtab[:, :]), writes=[b_gt], sbuf=b_gt)
    P.op(DVE, lambda e: e.tensor_copy(identb[:], identf), reads=[b_cst], writes=[b_idb])
    umb = sb(G, "umb", [128, 2, 128], BF16)
    b_cstb = P.buf("cstb")
    P.op(DVE, lambda e: e.tensor_copy(umb[:, 0, :], uincl), reads=[b_cst], writes=[b_cstb])
    P.op(DVE, lambda e: e.tensor_copy(umb[:, 1, :], negus), reads=[b_cst], writes=[b_cstb])
    P.op(DVE, lambda e: e.tensor_tensor(lamv[:, 0:64], lamv[:, 0:64], lamv[:, 64:128], ALU.mult),
         reads=[b_gt], writes=[b_gt])
    P.op(DVE, lambda e: e.tensor_tensor(lamv[:, 128:192], lamv[:, 128:192], lamv[:, 192:256], ALU.mult),
         reads=[b_gt], writes=[b_gt])
    P.op(DVE, lambda e: e.reduce_sum(lam_t[:, 0:1], lamv[:, 0:64], mybir.AxisListType.X), reads=[b_gt], writes=[b_lam])
    P.op(DVE, lambda e: e.reduce_sum(lam_t[:, 1:2], lamv[:, 128:192], mybir.AxisListType.X), reads=[b_gt], writes=[b_lam])
    P.op(ACT, lambda e: e.activation(lam_t[:, 2:4], lam_t[:, 0:2], AF.Exp), reads=[b_lam], writes=[b_lam])
    P.op(DVE, lambda e: e.tensor_tensor(lam_t[:, 4:5], lam_t[:, 3:4], lam_t[:, 2:3], ALU.subtract), reads=[b_lam], writes=[b_lam])
    P.op(DVE, lambda e: e.tensor_scalar(lam_t[:, 4:5], lam_t[:, 4:5], -LAM_INIT, None, ALU.add), reads=[b_lam], writes=[b_lam])
    P.op(DVE, lambda e: e.tensor_scalar(attw, attw, 1.0 - LAM_INIT, None, ALU.mult), reads=[b_gt], writes=[b_gt])
    neglam = lam_t[:, 4:5]

    with ExitStack() as st:
        stf = [sb(st, f"stf{i}", [128, 2048]) for i in range(4)]
        stb = [sb(st, f"stb{i}", [128, 2048], BF16) for i in range(4)]
        b_stf = [P.buf(f"stf{i}") for i in range(4)]
        b_stb = [P.buf(f"stb{i}") for i in range(4)]
        cnt = [0]

        def conv_w(W, scr, bscr, K, N):
            for kc in range(K // 128):
                for c0 in range(0, N, 2048):
                    w = min(2048, N - c0)
                    i = cnt[0] % 4
                    ce = (DVE, ACT)[cnt[0] % 2]
                    cnt[0] += 1
                    P.dma(SP, lambda e, i=i, kc=kc, c0=c0, w=w: e.dma_start(out=stf[i][:, 0:w], in_=W[kc * 128:(kc + 1) * 128, c0:c0 + w]),
                          writes=[b_stf[i]], sbuf=b_stf[i])
                    if ce == ACT:
                        P.op(ACT, lambda e, i=i, w=w: e.copy(stb[i][:, 0:w], stf[i][:, 0:w]), reads=[b_stf[i]], writes=[b_stb[i]])
                    else:
                        P.op(ce, lambda e, i=i, w=w: e.tensor_copy(stb[i][:, 0:w], stf[i][:, 0:w]), reads=[b_stf[i]], writes=[b_stb[i]])
                    P.dma(POOL, lambda e, i=i, kc=kc, c0=c0, w=w: e.dma_start(out=scr[:, kc, c0:c0 + w], in_=stb[i][:, 0:w]),
                          reads=[b_stb[i]], writes=[bscr], sbuf=b_stb[i])

        conv_w(w_in, ws_in, b_ws["in"], D, 5664)
        conv_w(w_out, ws_out, b_ws["out"], 2048, D)
        conv_w(w_up, ws_up, b_ws["up"], D, 4096)
        conv_w(w_dn, ws_dn, b_ws["dn"], 4096, D)
        P.barrier()

    def make_uT(st, pfx):
        xs = [sb(st, f"{pfx}xs{i}", [128, D]) for i in range(2)]
        xb = [sb(st, f"{pfx}xb{i}", [128, D], BF16) for i in range(2)]
        sq = sb(st, f"{pfx}sq", [128, D], BF16)
        ss = sb(st, f"{pfx}ss", [128, 4])
        uT = sb(st, f"{pfx}uT", [128, 8, 516], BF16)
        return dict(xs=xs, xb=xb, sq=sq, ss=ss, uT=uT,
                    b_xs=[P.buf() for _ in range(2)], b_xb=[P.buf() for _ in range(2)],
                    b_sq=P.buf(), b_ss=[P.buf(), P.buf()], b_uT=P.buf(), cnt=[0])

    def rms_rstd(eng_sq, src_ap, n, width, sqjunk, b_junk, ssap, b_ss, src_bufs):
        P.op(ACT, lambda e: e.activation(sqjunk, src_ap, AF.Square, accum_out=ssap), reads=src_bufs, writes=[b_junk, b_ss])
        P.op(ACT, lambda e: e.activation(ssap, ssap, AF.Ln, scale=1.0 / width, bias=EPS), reads=[b_ss], writes=[b_ss])
        P.op(ACT, lambda e: e.activation(ssap, ssap, AF.Exp, scale=-0.5), reads=[b_ss], writes=[b_ss])

    def uT_thunks(U, src, W, wT, tp_bank):
        uT = U["uT"]
        tpv = banks[tp_bank][:, :].bitcast(BF16).rearrange("p (k t) -> p k t", k=8)
        out = []
        r0 = 0
        while r0 < W:
            n = min(128, W - r0)

            def mk(r0=r0, n=n):
                st_ = {}

                def stepA():
                    i = U["cnt"][0] % 2
                    U["cnt"][0] += 1
                    st_["i"] = i
                    xs, xb, bxs, bxb = U["xs"][i], U["xb"][i], U["b_xs"][i], U["b_xb"][i]
                    P.dma(SP, lambda e: e.dma_start(out=xs[0:n, :], in_=src[r0:r0 + n, :]), writes=[bxs], sbuf=bxs)
                    ssap = U["ss"][0:n, i:i + 1]
                    rms_rstd(ACT, xs[0:n, :], n, D, U["sq"][0:n, :], U["b_sq"], ssap, U["b_ss"][i], [bxs])
                    P.op(ACT, lambda e: e.activation(xb[0:n, :], xs[0:n, :], AF.Copy, scale=ssap), reads=[bxs, U["b_ss"][i]], writes=[bxb])

                def stepB():
                    i = st_["i"]
                    xb, bxb = U["xb"][i], U["b_xb"][i]
                    for kc in range(8):
                        P.op(PE, lambda e, kc=kc: e.transpose(tpv[:, kc, 0:n], xb[0:n, kc * 128:(kc + 1) * 128], identb[0:n, 0:n]),
                             reads=[bxb, b_idb], writes=[bk[tp_bank]])
                    P.op(DVE, lambda e: e.tensor_tensor(uT[:, :, r0:r0 + n], tpv[:, :, 0:n], wT.unsqueeze(2).to_broadcast([128, 8, n]), ALU.mult),
                         reads=[bk[tp_bank], b_gt], writes=[U["b_uT"]])
                return stepA, stepB
            out.append(mk())
            r0 += n
        return out

    def build_uT(U, src, W, wT, tp_bank):
        for (sa, sb_) in uT_thunks(U, src, W, wT, tp_bank):
            sa()
            sb_()

    def make_ssd(st, pfx):
        S = dict()
        S["tab"] = sb(st, f"{pfx}tab", [128, TT])
        S["raw"] = [sb(st, f"{pfx}raw{i}", [128, 516]) for i in range(2)]
        S["acc"] = [sb(st, f"{pfx}acc{i}", [128, 512]) for i in range(2)]
        S["xcT"] = sb(st, f"{pfx}xcT", [128, 12, 512], BF16)
        S["xtm"] = sb(st, f"{pfx}xtm", [128, 4, D], BF16)
        S["btm"] = sb(st, f"{pfx}btm", [128, 4, 256], BF16)
        S["dt"] = sb(st, f"{pfx}dt", [128, 4, 32])
        S["adt"] = sb(st, f"{pfx}adt", [128, 4, 32])
        S["cs"] = sb(st, f"{pfx}cs", [128, 4, 32])
        S["tot"] = sb(st, f"{pfx}tot", [128, 4, 32])
        S["A2"] = sb(st, f"{pfx}A2", [128, 32])
        S["tmp32"] = sb(st, f"{pfx}tmp32", [128, 32])
        for k in ("tab", "xcT", "xtm", "btm", "dtq", "A2", "tmp32"):
            S["b_" + k] = P.buf(pfx + k)
        S["b_raw"] = [P.buf() for _ in range(2)]
        S["b_acc"] = [P.buf() for _ in range(2)]
        S["n"] = 0
        return S

    def ssd_par(st, S, pfx):
        S2 = dict(S)
        S2["tab"] = sb(st, f"{pfx}tabB", [128, TT])
        S2["dt"] = sb(st, f"{pfx}dtB", [128, 4, 32])
        S2["adt"] = sb(st, f"{pfx}adtB", [128, 4, 32])
        S2["cs"] = sb(st, f"{pfx}csB", [128, 4, 32])
        S2["tot"] = sb(st, f"{pfx}totB", [128, 4, 32])
        S2["A2"] = sb(st, f"{pfx}A2B", [128, 32])
        for k in ("tab", "dtq", "A2"):
            S2["b_" + k] = P.buf()
        return S2

    def ssd_thunks(S, U, j, t, T, W, wv, cx, cdt, pb_main, pb_small, pb_tp):
        uT = U["uT"]
        tab = S["tab"]
        nchk = max(1, T // 128)
        cs_ = min(T, 128)
        def pre():
            P.dma(SP, lambda e: e.dma_start(out=tab[:], in_=ttab[j][t, :, :]), writes=[S["b_tab"]], sbuf=S["b_tab"])
            P.op(ACT, lambda e: e.activation(S["A2"][:], tab[:, 104:136], AF.Exp), reads=[S["b_tab"]], writes=[S["b_A2"]])
            P.op(DVE, lambda e: e.tensor_scalar(S["A2"][:], S["A2"][:], -1.0, None, ALU.mult), reads=[S["b_A2"]], writes=[S["b_A2"]])
            for k in range(nchk):
                for kc in range(8):
                    P.op(PE, lambda e, kc=kc, k=k: e.matmul(banks[pb_small][0:cs_, 64 + 32 * k:96 + 32 * k], uT[:, kc, 2 + 128 * k:2 + 128 * k + cs_], wv[:, kc, cdt:cdt + 32],
                                                            start=(kc == 0), stop=(kc == 7)),
                         reads=[U["b_uT"], b_wres], writes=[bk[pb_small]])
            dtr = banks[pb_small][0:cs_, 64:64 + 32 * nchk].rearrange("p (k c) -> p k c", c=32)
            dt, adt = S["dt"], S["adt"]
            P.op(DVE, lambda e: e.tensor_copy(dt[0:cs_, 0:nchk, 16:32], dtr[:, :, 16:32]), reads=[bk[pb_small]], writes=[S["b_dtq"]])
            P.op(DVE, lambda e: e.tensor_scalar(dt[0:cs_, 0:nchk, 0:16], dtr[:, :, 0:16], tab[0:cs_, 136:137], None, ALU.mult),
                 reads=[bk[pb_small], S["b_tab"]], writes=[S["b_dtq"]])
            P.op(DVE, lambda e: e.scalar_tensor_tensor(dt[0:cs_, 0:nchk, 0:16], dt[0:cs_, 0:nchk, 16:32], tab[0:cs_, 137:138], dt[0:cs_, 0:nchk, 0:16], ALU.mult, ALU.add),
                 reads=[S["b_dtq"], S["b_tab"]], writes=[S["b_dtq"]])
            P.op(DVE, lambda e: e.tensor_tensor(dt[0:cs_, 0:nchk, :], dt[0:cs_, 0:nchk, :], tab[0:cs_, 72:104].unsqueeze(1).to_broadcast([cs_, nchk, 32]), ALU.add),
                 reads=[S["b_dtq"], S["b_tab"]], writes=[S["b_dtq"]])
            P.op(ACT, lambda e: e.activation(dt[0:cs_, 0:nchk, :], dt[0:cs_, 0:nchk, :], AF.Exp), reads=[S["b_dtq"]], writes=[S["b_dtq"]])
            P.op(ACT, lambda e: e.activation(dt[0:cs_, 0:nchk, :], dt[0:cs_, 0:nchk, :], AF.Ln, bias=1.0), reads=[S["b_dtq"]], writes=[S["b_dtq"]])
            P.op(DVE, lambda e: e.tensor_tensor(adt[0:cs_, 0:nchk, :], dt[0:cs_, 0:nchk, :], S["A2"][0:cs_, :].unsqueeze(1).to_broadcast([cs_, nchk, 32]), ALU.mult),
                 reads=[S["b_dtq"], S["b_A2"]], writes=[S["b_dtq"]])
            for k in range(nchk):
                P.op(PE, lambda e, k=k: e.matmul(banks[pb_small][0:cs_, 192 + 32 * k:224 + 32 * k], uincl[0:cs_, 0:cs_], adt[0:cs_, k, :], start=True, stop=True),
                     reads=[S["b_dtq"], b_cst], writes=[bk[pb_small]])
                P.op(PE, lambda e, k=k: e.matmul(banks[pb_small][:, 320 + 32 * k:352 + 32 * k], onesf[0:cs_, :], adt[0:cs_, k, :], start=True, stop=True),
                     reads=[S["b_dtq"], b_cst], writes=[bk[pb_small]])
            P.op(DVE, lambda e: e.tensor_copy(S["cs"][0:cs_, 0:nchk, :], banks[pb_small][0:cs_, 192:192 + 32 * nchk].rearrange("p (k c) -> p k c", c=32)),
                 reads=[bk[pb_small]], writes=[S["b_dtq"]])
            P.op(DVE, lambda e: e.tensor_copy(S["tot"][:, 0:nchk, :], banks[pb_small][:, 320:320 + 32 * nchk].rearrange("p (k c) -> p k c", c=32)),
                 reads=[bk[pb_small]], writes=[S["b_dtq"]])

        pend_silu = []

        def one_fc(fc):
            i = S["n"] % 2
            S["n"] += 1
            raw, acc, braw, bacc = S["raw"][i], S["acc"][i], S["b_raw"][i], S["b_acc"][i]
            pbm = pb_main[fc % len(pb_main)]
            wa = min(W, 512)
            for kc in range(8):
                P.op(PE, lambda e, kc=kc, fc=fc, pbm=pbm, wa=wa: e.matmul(banks[pbm][:, 0:wa], wv[:, kc, cx + fc * 128: cx + (fc + 1) * 128], uT[:, kc, 0:wa],
                                                                   start=(kc == 0), stop=(kc == 7)),
                     reads=[U["b_uT"], b_wres], writes=[bk[pbm]])
            P.op(ACT, lambda e, raw=raw, pbm=pbm, wa=wa: e.copy(raw[:, 0:wa], banks[pbm][:, 0:wa]), reads=[bk[pbm]], writes=[braw])
            if W > 512:
                for kc in range(8):
                    P.op(PE, lambda e, kc=kc, fc=fc: e.matmul(banks[pb_small][:, 0:W - 512], wv[:, kc, cx + fc * 128: cx + (fc + 1) * 128], uT[:, kc, 512:W],
                                                             start=(kc == 0), stop=(kc == 7)),
                         reads=[U["b_uT"], b_wres], writes=[bk[pb_small]])
                P.op(ACT, lambda e, raw=raw: e.copy(raw[:, 512:W], banks[pb_small][:, 0:W - 512]), reads=[bk[pb_small]], writes=[braw])
            flush_silu()
            P.op(DVE, lambda e, raw=raw, acc=acc, fc=fc: e.tensor_scalar(acc[:, 0:T], raw[:, 0:T], tab[:, fc * 5:fc * 5 + 1], tab[:, 60 + fc:61 + fc], ALU.mult, ALU.add),
                 reads=[braw, S["b_tab"]], writes=[bacc])
            for jj in range(1, 5):
                P.op(DVE, lambda e, raw=raw, acc=acc, fc=fc, jj=jj: e.scalar_tensor_tensor(acc[:, 0:T], raw[:, jj:jj + T], tab[:, fc * 5 + jj:fc * 5 + jj + 1], acc[:, 0:T], ALU.mult, ALU.add),
                     reads=[braw, S["b_tab"], bacc], writes=[bacc])
            pend_silu.append((acc, bacc, fc))

        def flush_silu():
            while pend_silu:
                acc, bacc, fc = pend_silu.pop(0)
                P.op(ACT, lambda e: e.activation(S["xcT"][:, fc, 0:T], acc[:, 0:T], AF.Silu), reads=[bacc], writes=[S["b_xcT"]])

        def post():
            flush_silu()
            tpv = banks[pb_tp][:, :].bitcast(BF16)
            for k in range(nchk):
                for fc in range(8):
                    P.op(PE, lambda e, k=k, fc=fc: e.transpose(tpv[0:cs_, fc * 128:(fc + 1) * 128], S["xcT"][:, fc, 128 * k:128 * k + cs_], identb[:, :]),
                         reads=[S["b_xcT"], b_idb], writes=[bk[pb_tp]])
                P.op(ACT, lambda e, k=k: e.copy(S["xtm"][0:cs_, k, :], tpv[0:cs_, :]), reads=[bk[pb_tp]], writes=[S["b_xtm"]])
                for g in range(2):
                    P.op(PE, lambda e, k=k, g=g: e.transpose(tpv[0:cs_, g * 128:(g + 1) * 128], S["xcT"][:, 8 + g, 128 * k:128 * k + cs_], identb[:, :]),
                         reads=[S["b_xcT"], b_idb], writes=[bk[pb_tp]])
                P.op(ACT, lambda e, k=k: e.copy(S["btm"][0:cs_, k, :], tpv[0:cs_, 0:256]), reads=[bk[pb_tp]], writes=[S["b_btm"]])
        return pre, [(lambda fc=fc: one_fc(fc)) for fc in range(12)], post

    def ssd_prep(S, U, j, t, T, W, wv, cx, cdt, pb_main, pb_small, pb_tp):
        pre, fcs, post = ssd_thunks(S, U, j, t, T, W, wv, cx, cdt, pb_main, pb_small, pb_tp)
        pre()
        for f in fcs:
            f()
        post()

    b_wres = P.buf("wres")

    for j in range(NJ):
        nt = cfg.ntiles(j)
        nctx = cfg.jobs[j]
        with ExitStack() as st:
            mkS = P.mark()
            wS = sb(st, "wS", [128, 8, 4640], BF16)
            P.dma(SP, lambda e: e.dma_start(out=wS[:, :, 0:3072], in_=ws_in[:, :, 0:3072]), reads=[b_ws["in"]], writes=[b_wres], sbuf=b_wres)
            P.dma(SP, lambda e: e.dma_start(out=wS[:, :, 3072:4640], in_=ws_in[:, :, 4096:5664]), reads=[b_ws["in"]], writes=[b_wres], sbuf=b_wres)
            for m in range(2):
                P.dma(POOL, lambda e, m=m: e.dma_start(out=KT[:, m, 64:72, 0:cfg.L(j)], in_=kaug[j][:, :, :]), writes=[b_KT], sbuf=b_KT)
            U0 = make_uT(st, "s")
            U1 = dict(U0)
            U1["uT"] = sb(st, "suT1", [128, 8, 516], BF16)
            U1["b_uT"] = P.buf()
            Us = [U0, U1]
            S_0 = make_ssd(st, "s")
            S_1 = ssd_par(st, S_0, "s")
            Ss = [S_0, S_1]
            kst = [sb(st, f"kst{i}", [128, 512], BF16) for i in range(3)]
            b_kst = [P.buf() for _ in range(3)]
            vst = sb(st, "vst", [128, 8, 4, 129], BF16)
            b_vst = P.buf("vst")
            wgt = sb(st, "wgt", [128, 32])
            xwp = sb(st, "xwp", [128, 16, 64], BF16)
            xws = sb(st, "xws", [128, 16, 64], BF16)
            decp = sb(st, "decp", [128, 32])
            snap = [sb(st, f"snap{i}", [128, D], BF16) for i in range(2)]
            b_wgt, b_xwp, b_xws, b_decp = P.buf(), P.buf(), P.buf(), P.buf()
            b_snap = [P.buf() for _ in range(2)]
            P.op(POOL, lambda e: e.memset(vst[:], 1.0), writes=[b_vst])
            P.op(POOL, lambda e: e.memset(CF[:], 0.0), writes=[b_CF])
            P.op(POOL, lambda e: e.memset(CB[:], 0.0), writes=[b_CB])
            kn = [0]
            sn = [0]

            def make_states(S, t, own, oi, cs_, nchk):
                def one_chunk(k):
                    jj = oi * 4 + k
                    dt, cs, tot = S["dt"], S["cs"], S["tot"]
                    nw = 32 if own else 16
                    xt3 = S["xtm"][0:cs_, k, :].rearrange("p (h d) -> p h d", h=16)
                    P.op(DVE, lambda e: e.tensor_tensor(wgt[0:cs_, 0:16], tot[0:cs_, k, 0:16], cs[0:cs_, k, 0:16], ALU.subtract), reads=[S["b_dtq"]], writes=[b_wgt])
                    if own:
                        P.op(DVE, lambda e: e.tensor_tensor(wgt[0:cs_, 16:32], cs[0:cs_, k, 16:32], S["adt"][0:cs_, k, 16:32], ALU.subtract), reads=[S["b_dtq"], b_wgt], writes=[b_wgt])
                    yield
                    P.op(ACT, lambda e: e.activation(wgt[0:cs_, 0:nw], wgt[0:cs_, 0:nw], AF.Exp), reads=[b_wgt], writes=[b_wgt])
                    P.op(ACT, lambda e: e.activation(decp[:, :], tot[:, k, :], AF.Exp), reads=[S["b_dtq"]], writes=[b_decp])
                    yield
                    P.op(DVE, lambda e: e.tensor_tensor(wgt[0:cs_, 0:nw], wgt[0:cs_, 0:nw], dt[0:cs_, k, 0:nw], ALU.mult), reads=[b_wgt, S["b_dtq"]], writes=[b_wgt])
                    P.op(DVE, lambda e: e.tensor_tensor(xwp[0:cs_, :, :], xt3, wgt[0:cs_, 0:16].unsqueeze(2).to_broadcast([cs_, 16, 64]), ALU.mult),
                         reads=[S["b_xtm"], b_wgt], writes=[b_xwp])
                    if own:
                        P.op(DVE, lambda e: e.tensor_tensor(xws[0:cs_, :, :], xt3, wgt[0:cs_, 16:32].unsqueeze(2).to_broadcast([cs_, 16, 64]), ALU.mult),
                             reads=[S["b_xtm"], b_wgt], writes=[b_xws])
                    yield
                    for g in range(2):
                        P.op(PE, lambda e, g=g: e.matmul(banks[6 + g][:, :], S["btm"][0:cs_, k, g * 128:(g + 1) * 128], xwp[0:cs_, g * 8:(g + 1) * 8, :].rearrange("p h d -> p (h d)"),
                                                         start=True, stop=True),
                             reads=[S["b_btm"], b_xwp], writes=[bk[6 + g]])
                    if not own:
                        carry, bcar = CB, b_CB
                    else:
                        carry, bcar = CF, b_CF
                        i = sn[0] % 2
                        sn[0] += 1
                        P.op(ACT, lambda e: e.copy(snap[i][:], CF[:]), reads=[b_CF], writes=[b_snap[i]])
                        P.dma(POOL, lambda e: e.dma_start(out=PFB[0, jj, :, :], in_=snap[i][:]), reads=[b_snap[i]], writes=[b_PFB], sbuf=b_snap[i])
                    yield
                    c3 = carry[:].rearrange("p (h d) -> p h d", h=16)
                    P.op(DVE, lambda e: e.tensor_tensor(c3, c3, decp[:, 0:16].unsqueeze(2).to_broadcast([128, 16, 64]), ALU.mult), reads=[bcar, b_decp], writes=[bcar])
                    for g in range(2):
                        P.op(DVE, lambda e, g=g: e.tensor_tensor(carry[:, g * 512:(g + 1) * 512], carry[:, g * 512:(g + 1) * 512], banks[6 + g][:, :], ALU.add),
                             reads=[bcar, bk[6 + g]], writes=[bcar])
                    if own:
                        P.op(POOL, lambda e: e.tensor_copy(decs[:, jj, :], decp[:, 16:32]), reads=[b_decp], writes=[b_decs])
                    yield
                    if own:
                        i2 = sn[0] % 2
                        sn[0] += 1
                        for g in range(2):
                            P.op(PE, lambda e, g=g: e.matmul(banks[6 + g][:, :], S["btm"][0:cs_, k, g * 128:(g + 1) * 128], xws[0:cs_, g * 8:(g + 1) * 8, :].rearrange("p h d -> p (h d)"),
                                                             start=True, stop=True),
                                 reads=[S["b_btm"], b_xws], writes=[bk[6 + g]])
                            P.op(ACT, lambda e, g=g: e.copy(snap[i2][:, g * 512:(g + 1) * 512], banks[6 + g][:, :]), reads=[bk[6 + g]], writes=[b_snap[i2]])
                        P.dma(POOL, lambda e: e.dma_start(out=SBS[jj, :, :], in_=snap[i2][:]), reads=[b_snap[i2]], writes=[b_SBS], sbuf=b_snap[i2])
                    yield

                def stages(k):
                    gen = one_chunk(k)
                    return [(lambda: next(gen, None)) for _ in range(6)]

                def tile_end():
                    if not own:
                        P.op(DVE, lambda e: e.scalar_tensor_tensor(CF[:], CB[:], S["tab"][:, 138:139], CF[:], ALU.mult, ALU.add), reads=[b_CB, b_CF, S["b_tab"]], writes=[b_CF])
                        P.op(DVE, lambda e: e.tensor_scalar(CB[:], CB[:], S["tab"][:, 139:140], None, ALU.mult), reads=[b_CB, S["b_tab"]], writes=[b_CB])
                    return None
                out = []
                for k in range(nchk):
                    out += stages(k)
                return out + [tile_end]

            pend_states = []
            for t in range(nt):
                T = 16 if t == 0 else 512
                W = T + 4
                own = t > nctx
                oi = t - nctx - 1
                soff = 0 if t == 0 else 16 + 512 * (t - 1)
                ch0 = 0 if t == 0 else 1 + 4 * (t - 1)
                nchk = max(1, T // 128)
                cs_ = min(T, 128)
                U = Us[t % 2]
                S = Ss[t % 2]
                if t == 0:
                    build_uT(U, xw[j][0], W, w1T, 0)
                uT = U["uT"]
                side = []
                if t + 1 < nt:
                    th = uT_thunks(Us[(t + 1) % 2], xw[j][t + 1], 516, w1T, 0)
                    side.append(th[0][0])
                    for q_ in range(1, len(th)):
                        side.append(th[q_][0])
                        side.append(th[q_ - 1][1])
                    side.append(th[-1][1])
                if pend_states:
                    merged = []
                    ps_ = list(pend_states)
                    while side or ps_:
                        if side:
                            merged.append(side.pop(0))
                        if side:
                            merged.append(side.pop(0))
                        if ps_:
                            merged.append(ps_.pop(0))
                    side = merged
                    pend_states = []
                pre, fcs, post = ssd_thunks(S, U, j, t, T, W, wS, 3072, 4608, [1, 2], 3, 4)
                main = [pre] + fcs

                def kq_group(isq, h, T=T, uT=uT, U=U, soff=soff, oi=oi):
                    cbase = (0 if isq else 1024) + h * 128
                    pb = 1 + (kn[0] % 2)
                    i = kn[0] % 3
                    kn[0] += 1
                    for kc in range(8):
                        P.op(PE, lambda e, kc=kc: e.matmul(banks[pb][:, 0:T], wS[:, kc, cbase:cbase + 128], uT[:, kc, 2:2 + T], start=(kc == 0), stop=(kc == 7)),
                             reads=[U["b_uT"], b_wres], writes=[bk[pb]])
                    if isq:
                        P.op(ACT, lambda e: e.activation(kst[i][:, 0:T], banks[pb][:, 0:T], AF.Copy, scale=0.125), reads=[bk[pb]], writes=[b_kst[i]])
                        for m in range(2):
                            P.dma(POOL, lambda e, m=m: e.dma_start(out=QT[h, m, :, oi * 512:oi * 512 + T], in_=kst[i][64 * m:64 * m + 64, 0:T]),
                                  reads=[b_kst[i]], writes=[b_QT], sbuf=b_kst[i])
                    else:
                        P.op(ACT, lambda e: e.copy(kst[i][:, 0:T], banks[pb][:, 0:T]), reads=[bk[pb]], writes=[b_kst[i]])
                        for m in range(2):
                            P.dma(POOL, lambda e, m=m: e.dma_start(out=KT[h, m, 0:64, soff:soff + T], in_=kst[i][64 * m:64 * m + 64, 0:T]),
                                  reads=[b_kst[i]], writes=[b_KT], sbuf=b_kst[i])

                for isq in ([False, True] if own else [False]):
                    for h in range(NH):
                        main.append(lambda isq=isq, h=h: kq_group(isq, h))

                def v_group(k, half, uT=uT, U=U, cs_=cs_):
                    pb = 1 + (kn[0] % 2)
                    kn[0] += 1
                    for kc in range(8):
                        P.op(PE, lambda e, kc=kc: e.matmul(banks[pb][0:cs_, :], uT[:, kc, 2 + 128 * k:2 + 128 * k + cs_], wS[:, kc, 2048 + half * 512:2048 + (half + 1) * 512],
                                                           start=(kc == 0), stop=(kc == 7)),
                             reads=[U["b_uT"], b_wres], writes=[bk[pb]])
                    P.op(ACT, lambda e: e.copy(vst[0:cs_, half * 4:(half + 1) * 4, k, 0:128], banks[pb][0:cs_, :].rearrange("p (h e) -> p h e", h=4)),
                         reads=[bk[pb]], writes=[b_vst])

                for k in range(nchk):
                    for half in range(2):
                        main.append(lambda k=k, half=half: v_group(k, half))

                def v_store(t=t, ch0=ch0):
                    if t == 0:
                        P.dma(POOL, lambda e: e.dma_start(out=VA[:, 0:16, 0, :].rearrange("h p e -> p h e"), in_=vst[0:16, :, 0, :]),
                              reads=[b_vst], writes=[b_VA], sbuf=b_vst)
                    else:
                        P.dma(POOL, lambda e: e.dma_start(out=VA[:, :, ch0:ch0 + 4, :].rearrange("h p c e -> p h c e"), in_=vst[:, :, :, :]),
                              reads=[b_vst], writes=[b_VA], sbuf=b_vst)
                main.append(v_store)
                stride = max(1, len(main) // (len(side) + 1)) if side else len(main)
                si_ = 0
                for mi, f in enumerate(main):
                    f()
                    if side and (mi + 1) % stride == 0 and si_ < len(side):
                        side[si_]()
                        si_ += 1
                while si_ < len(side):
                    side[si_]()
                    si_ += 1
                post()
                pend_states = make_states(S, t, own, oi, cs_, nchk)
            for f in pend_states:
                f()
            pend_states = []
            ldb = [sb(st, f"ldb{i}", [128, D], BF16) for i in range(2)]
            b_ldb = [P.buf() for _ in range(2)]
            for jj in range(4 * NOWN - 1, -1, -1):
                i = sn[0] % 2
                sn[0] += 1
                P.op(ACT, lambda e, i=i: e.copy(snap[i][:], CB[:]), reads=[b_CB], writes=[b_snap[i]])
                P.dma(POOL, lambda e, i=i, jj=jj: e.dma_start(out=PFB[1, jj, :, :], in_=snap[i][:]), reads=[b_snap[i]], writes=[b_PFB], sbuf=b_snap[i])
                P.dma(SP, lambda e, i=i, jj=jj: e.dma_start(out=ldb[i][:], in_=SBS[jj, :, :]), reads=[b_SBS], writes=[b_ldb[i]], sbuf=b_ldb[i])
                c3 = CB[:].rearrange("p (h d) -> p h d", h=16)
                P.op(DVE, lambda e, c3=c3, jj=jj: e.tensor_tensor(c3, c3, decs[:, jj, :].unsqueeze(2).to_broadcast([128, 16, 64]), ALU.mult), reads=[b_CB, b_decs], writes=[b_CB])
                P.op(DVE, lambda e, i=i: e.tensor_tensor(CB[:], CB[:], ldb[i][:], ALU.add), reads=[b_CB, b_ldb[i]], writes=[b_CB])
            P.barrier()
            P.release(mkS)

        with ExitStack() as stT:
            mkT = P.mark()
            wO = sb(stT, "wO", [128, 8, 2592], BF16)
            P.dma(SP, lambda e: e.dma_start(out=wO[:, :, 0:1024], in_=ws_in[:, :, 3072:4096]), reads=[b_ws["in"]], writes=[b_wres], sbuf=b_wres)
            P.dma(SP, lambda e: e.dma_start(out=wO[:, :, 1024:2592], in_=ws_in[:, :, 4096:5664]), reads=[b_ws["in"]], writes=[b_wres], sbuf=b_wres)
            mixT = sb(stT, "mixT", [128, 16, 512], BF16)
            b_mix = P.buf("mixT")
            wpn = [0]
            nk_chunks = cfg.nch(j)
            for oi in range(NOWN):
                t_own = nctx + 1 + oi
                with ExitStack() as st:
                    mkA = P.mark()
                    qv = [[sb(st, f"qv{r}_{v}", [72, 2, 512], BF16) for v in range(3)] for r in range(2)]
                    b_qv = [P.buf() for _ in range(2)]
                    NKV = 4
                    kp = [sb(st, f"kp{i}", [72, 2, 2048], BF16) for i in range(NKV)]
                    vp = [sb(st, f"vp{i}", [128, 16, 129], BF16) for i in range(NKV)]
                    b_kv = [P.buf() for _ in range(NKV)]
                    tmpb = [sb(st, f"tmpb{i}", [128, 1024]) for i in range(2)]
                    b_tmpb = [P.buf() for _ in range(2)]
                    pt = [sb(st, f"pt{i}", [128, 1024], BF16) for i in range(3)]
                    b_pt = [P.buf() for _ in range(3)]
                    ob = sb(st, "ob", [128, 8, 129])
                    rl = sb(st, "rl", [128, 8])
                    o1 = sb(st, "o1", [128, 4, 128])
                    o2 = sb(st, "o2", [128, 4, 128])
                    ssq = sb(st, "ssq", [128, 4])
                    junk = sb(st, "junk", [128, 128], BF16)
                    junkf = sb(st, "junkf", [128, 128])
                    attb = sb(st, "attb", [128, 4, 128], BF16)
                    b_ob, b_rl, b_o1, b_o2, b_ssq, b_junk, b_attb = (P.buf() for _ in range(7))
                    pn = [0]
                    sbn = [0]
                    pieces = [(0, 1)] + [(c0, min(16, nk_chunks - c0)) for c0 in range(1, nk_chunks, 16)]
                    acc_v = [banks[4 + a // 3][:, (a % 3) * 129:(a % 3) * 129 + 129] for a in range(8)]
                    epi_gen = [None]
                    for h in range(NH):
                        slope = 2.0 ** (-(h + 1))
                        r = h % 2
                        for v in range(3):
                            P.dma(SP, lambda e, r=r, v=v, h=h: e.dma_start(out=qv[r][v][0:64, :, :], in_=QT[h, :, :, oi * 512:(oi + 1) * 512].rearrange("m d q -> d m q")),
                                  reads=[b_QT], writes=[b_qv[r]], sbuf=b_qv[r])
                            for m in range(2):
                                P.dma(SP, lambda e, r=r, v=v, h=h, m=m: e.dma_start(out=qv[r][v][64:72, m, :], in_=qaug[j][v, h, :, oi * 512:(oi + 1) * 512]),
                                      writes=[b_qv[r]], sbuf=b_qv[r])
                        first_in_bank = {4: True, 5: True, 6: True}
                        steps = []
                        for (c0, ncn) in pieces:
                            for ci in range(ncn):
                                steps.append((c0, ncn, ci))
                        stinfo = {}

                        def emit_qk(si, h=h, r=r, slope=slope):
                            c0, ncn, ci = steps[si]
                            if ci == 0:
                                pi = pn[0] % NKV
                                pn[0] += 1
                                koff0 = 0 if c0 == 0 else 16 + 128 * (c0 - 1)
                                klen = 16 if c0 == 0 else 128 * ncn
                                P.dma(SP, lambda e: e.dma_start(out=kp[pi][:, :, 0:klen], in_=KT[h, :, :, koff0:koff0 + klen].rearrange("m r l -> r m l")),
                                      reads=[b_KT], writes=[b_kv[pi]], sbuf=b_kv[pi])
                                if c0 == 0:
                                    P.dma(SP, lambda e: e.dma_start(out=vp[pi][0:16, 0, :], in_=VA[h, 0:16, 0, :]), reads=[b_VA], writes=[b_kv[pi]], sbuf=b_kv[pi])
                                else:
                                    P.dma(SP, lambda e: e.dma_start(out=vp[pi][:, 0:ncn, :], in_=VA[h, :, c0:c0 + ncn, :]), reads=[b_VA], writes=[b_kv[pi]], sbuf=b_kv[pi])
                                stinfo["pi"] = pi
                            pi = stinfo["pi"]
                            c = c0 + ci
                            ks = 16 if c == 0 else 128
                            kt = 0 if c == 0 else 1 + (c - 1) // 4
                            kk = (c - 1) % 4
                            if kt <= nctx:
                                var, KR, ovl = 0, 72, False
                            elif kt < t_own:
                                var, KR, ovl = 1, 72, False
                            elif kt > t_own:
                                var, KR, ovl = 2, 72, False
                            else:
                                var, KR, ovl = 0, 64, True
                            sbi = sbn[0] % 2
                            pti = sbn[0] % 3
                            sbn[0] += 1
                            Bs = (2 * sbi, 2 * sbi + 1)
                            for m in range(2):
                                P.op(PE, lambda e, m=m: e.matmul(banks[Bs[m]][0:ks, :], kp[pi][0:KR, m, 128 * ci:128 * ci + ks], qv[r][var][0:KR, m, :], start=True, stop=True),
                                     reads=[b_kv[pi], b_qv[r]], writes=[bk[Bs[m]]])
                            if ovl:
                                tb = tmpb[sbi]
                                for m in range(2):
                                    P.op(DVE, lambda e, m=m: e.scalar_tensor_tensor(tb[:, m * 512:(m + 1) * 512], absd[kk], -slope, banks[Bs[m]][:, :], ALU.mult, ALU.add),
                                         reads=[b_cst, bk[Bs[m]]], writes=[b_tmpb[sbi]])
                                P.op(ACT, lambda e: e.activation(pt[pti][:, :], tb[:, :], AF.Exp), reads=[b_tmpb[sbi]], writes=[b_pt[pti]])
                            else:
                                P.op(ACT, lambda e: e.activation(pt[pti][0:ks, :], allbanks[0:ks, 512 * Bs[0]:512 * Bs[0] + 1024], AF.Exp),
                                     reads=[bk[Bs[0]], bk[Bs[1]]], writes=[b_pt[pti]])
                            return (pi, ci, ks, pti)

                        def emit_pv(info):
                            pi, ci, ks, pti = info
                            for a in range(8):
                                m, qc = a // 4, a % 4
                                bkn = 4 + a // 3
                                stt = first_in_bank[bkn]
                                first_in_bank[bkn] = False
                                P.op(PE, lambda e, a=a, m=m, qc=qc, stt=stt: e.matmul(acc_v[a], pt[pti][0:ks, m * 512 + qc * 128:m * 512 + (qc + 1) * 128], vp[pi][0:ks, ci, :],
                                                                                 start=stt, stop=False, skip_group_check=True),
                                     reads=[b_pt[pti], b_kv[pi]], writes=[bk[bkn]])

                        prev = None
                        for si in range(len(steps)):
                            cur = emit_qk(si)
                            if prev is not None:
                                emit_pv(prev)
                            prev = cur
                            if epi_gen[0] is not None and si % 2 == 1:
                                if next(epi_gen[0], "done") == "done":
                                    epi_gen[0] = None
                        emit_pv(prev)
                        while epi_gen[0] is not None:
                            if next(epi_gen[0], "done") == "done":
                                epi_gen[0] = None
                        for bi in range(3):
                            na = 3 if bi < 2 else 2
                            P.op(DVE, lambda e, bi=bi, na=na: e.tensor_copy(ob[:, bi * 3:bi * 3 + na, :], banks[4 + bi][:, 0:129 * na].rearrange("p (a c) -> p a c", c=129)),
                                 reads=[bk[4 + bi]], writes=[b_ob])

                        def epilogue(h=h):
                            P.op(DVE, lambda e: e.reciprocal(rl[:, :].unsqueeze(2), ob[:, :, 128:129]), reads=[b_ob], writes=[b_rl])
                            P.op(DVE, lambda e: e.tensor_tensor(o1[:, :, :], ob[:, 0:4, 0:128], rl[:, 0:4].unsqueeze(2).to_broadcast([128, 4, 128]), ALU.mult), reads=[b_ob, b_rl], writes=[b_o1])
                            P.op(DVE, lambda e: e.tensor_tensor(o2[:, :, :], ob[:, 4:8, 0:128], rl[:, 4:8].unsqueeze(2).to_broadcast([128, 4, 128]), ALU.mult), reads=[b_ob, b_rl], writes=[b_o2])
                            P.op(DVE, lambda e: e.scalar_tensor_tensor(o1[:, :, :], o2[:, :, :], neglam, o1[:, :, :], ALU.mult, ALU.add), reads=[b_o1, b_o2, b_lam], writes=[b_o1])
                            yield
                            for qc in range(4):
                                P.op(DVE, lambda e, qc=qc: e.scalar_tensor_tensor(junkf[:, :], o1[:, qc, :], 1.0, o1[:, qc, :], ALU.mult, ALU.mult, accum_out=ssq[:, qc:qc + 1]),
                                     reads=[b_o1], writes=[b_junk, b_ssq])
                            yield
                            P.op(ACT, lambda e: e.activation(ssq[:, :], ssq[:, :], AF.Ln, scale=1.0 / 128, bias=EPS), reads=[b_ssq], writes=[b_ssq])
                            P.op(ACT, lambda e: e.activation(ssq[:, :], ssq[:, :], AF.Exp, scale=-0.5), reads=[b_ssq], writes=[b_ssq])
                            yield
                            P.op(DVE, lambda e: e.tensor_tensor(o1[:, :, :], o1[:, :, :], ssq[:, :].unsqueeze(2).to_broadcast([128, 4, 128]), ALU.mult), reads=[b_o1, b_ssq], writes=[b_o1])
                            P.op(DVE, lambda e: e.tensor_tensor(attb[:, :, :], o1[:, :, :], attw.unsqueeze(1).to_broadcast([128, 4, 128]), ALU.mult), reads=[b_o1, b_gt], writes=[b_attb])
                            yield
                            tpv = banks[7][:, :].bitcast(BF16)
                            for qc in range(4):
                                P.op(PE, lambda e, qc=qc: e.transpose(tpv[:, qc * 128:(qc + 1) * 128], attb[:, qc, :], identb[:, :]), reads=[b_attb, b_idb], writes=[bk[7]])
                            yield
                            P.op(DVE, lambda e: e.tensor_copy(mixT[:, h, :], banks[7][:, :].bitcast(BF16)[:, 0:512]), reads=[bk[7]], writes=[b_mix])
                            yield

                        epi_gen[0] = epilogue()
                    while epi_gen[0] is not None:
                        if next(epi_gen[0], "done") == "done":
                            epi_gen[0] = None
                    P.barrier()
                    P.release(mkA)
                with ExitStack() as st:
                    mkO = P.mark()
                    U = make_uT(st, "o")
                    S = make_ssd(st, "o")
                    sz = sb(st, "sz", [128, 4, D], BF16)
                    b_sz = P.buf()
                    pfb = [sb(st, f"pfb{i}", [128, 2, D], BF16) for i in range(2)]
                    b_pfb = [P.buf() for _ in range(2)]
                    gtm2 = [sb(st, f"gtm{i}", [128, 4, 128]) for i in range(2)]
                    b_gtm2 = [P.buf() for _ in range(2)]
                    ef = [sb(st, f"ef{i}", [128, 128]) for i in range(6)]
                    b_ef = [P.buf() for _ in range(6)]
                    mt = [sb(st, f"mt{i}", [128, 128], BF16) for i in range(6)]
                    b_mt = [P.buf() for _ in range(6)]
                    xdt2 = [sb(st, f"xdt{i}", [128, 2, 16, 64], BF16) for i in range(2)]
                    b_xdt2 = [P.buf() for _ in range(2)]
                    bia2 = [sb(st, f"bia{i}", [128, 64]) for i in range(2)]
                    b_bia2 = [P.buf() for _ in range(2)]
                    yt_2 = [sb(st, f"yt{i}", [128, D]) for i in range(2)]
                    yt2_2 = [sb(st, f"ytt{i}", [128, D]) for i in range(2)]
                    b_yt_2 = [P.buf() for _ in range(2)]
                    b_yt2_2 = [P.buf() for _ in range(2)]
                    ssb2 = [sb(st, f"ssb{i}", [128, D], BF16) for i in range(2)]
                    b_ssb2 = [P.buf() for _ in range(2)]
                    sso = sb(st, "sso", [128, 4])
                    b_sso4 = [P.buf() for _ in range(4)]
                    jq = sb(st, "jq", [128, D], BF16)
                    b_jq = P.buf()
                    ahl = sb(st, "ahl", [128, 2, 32], BF16)
                    ahf = sb(st, "ahf", [128, 32])
                    b_ahl, b_ahf = P.buf(), P.buf()
                    SEGB = (0, 1, 2, 3, 5)
                    segq = [banks[bq][:, 0:128] for bq in SEGB]
                    b_segq = [bk[bq] for bq in SEGB]
                    build_uT(U, xw[j][t_own], 516, w1T, 0)
                    uT = U["uT"]
                    pre_, fcs_, post_ = ssd_thunks(S, U, j, t_own, 512, 516, wO, 1024, 2560, [1], 2, 3)

                    def z_group(k, half):
                        zb = 4 + (k * 2 + half) % 2
                        for kc in range(8):
                            P.op(PE, lambda e, kc=kc: e.matmul(banks[zb][:, :], uT[:, kc, 2 + 128 * k:130 + 128 * k], wO[:, kc, half * 512:(half + 1) * 512], start=(kc == 0), stop=(kc == 7)),
                                 reads=[U["b_uT"], b_wres], writes=[bk[zb]])
                        P.op(ACT, lambda e: e.activation(sz[:, k, half * 512:(half + 1) * 512], banks[zb][:, :], AF.Silu), reads=[bk[zb]], writes=[b_sz])

                    pre_()
                    for fc in range(12):
                        fcs_[fc]()
                        if fc < 8:
                            z_group(fc // 2, fc % 2)
                    post_()
                    en = [0]
                    pend = {"front": None, "back": None}
                    for k in range(4):
                        jj = oi * 4 + k
                        pi = k % 2
                        gtm, b_gtm, xdt, b_xdt, bia, b_bia = gtm2[pi], b_gtm2[pi], xdt2[pi], b_xdt2[pi], bia2[pi], b_bia2[pi]
                        yt, yt2, b_yt, b_yt2, ssb, b_ssb = yt_2[pi], yt2_2[pi], b_yt_2[pi], b_yt2_2[pi], ssb2[pi], b_ssb2[pi]
                        for d_ in range(2):
                            P.dma(SP, lambda e, pi=pi, d_=d_, jj=jj: e.dma_start(out=pfb[pi][:, d_, :], in_=PFB[d_, jj, :, :]), reads=[b_PFB], writes=[b_pfb[pi]], sbuf=b_pfb[pi])
                        dt, adt, cs, tot = S["dt"], S["adt"], S["cs"], S["tot"]
                        P.op(DVE, lambda e, k=k: e.tensor_scalar(bia[:, 0:16], cs[:, k, 0:16], -1.0, None, ALU.mult), reads=[S["b_dtq"]], writes=[b_bia])
                        P.op(DVE, lambda e, k=k: e.tensor_tensor(bia[:, 16:32], cs[:, k, 16:32], adt[:, k, 16:32], ALU.subtract), reads=[S["b_dtq"]], writes=[b_bia])
                        P.op(DVE, lambda e, k=k: e.tensor_tensor(bia[:, 48:64], tot[:, k, 16:32], bia[:, 16:32], ALU.subtract), reads=[S["b_dtq"], b_bia], writes=[b_bia])
                        P.op(ACT, lambda e, k=k: e.activation(bia[:, 32:48], cs[:, k, 0:16], AF.Exp), reads=[S["b_dtq"]], writes=[b_bia])
                        P.op(ACT, lambda e: e.activation(bia[:, 48:64], bia[:, 48:64], AF.Exp), reads=[b_bia], writes=[b_bia])
                        P.op(DVE, lambda e, k=k: e.tensor_copy(ahl[:, 0, :], adt[:, k, :]), reads=[S["b_dtq"]], writes=[b_ahl])
                        P.op(DVE, lambda e: e.tensor_copy(ahf[:, :], ahl[:, 0, :]), reads=[b_ahl], writes=[b_ahf])
                        P.op(DVE, lambda e, k=k: e.tensor_tensor(ahl[:, 1, :], adt[:, k, :], ahf[:, :], ALU.subtract), reads=[S["b_dtq"], b_ahf, b_ahl], writes=[b_ahl])
                        xt3 = S["xtm"][:, k, :].rearrange("p (h d) -> p h d", h=16)
                        for d_ in range(2):
                            P.op(POOL if d_ else DVE, lambda e, d_=d_, k=k, xt3=xt3: e.tensor_tensor(xdt[:, d_, :, :], xt3, dt[:, k, 16 * d_:16 * d_ + 16].unsqueeze(2).to_broadcast([128, 16, 64]), ALU.mult),
                                 reads=[S["b_xtm"], S["b_dtq"]], writes=[b_xdt])
                        for g in range(2):
                            P.op(PE, lambda e, g=g, k=k: e.matmul(banks[4][:, g * 128:(g + 1) * 128], S["xcT"][:, 8 + g, 128 * k:128 * k + 128], S["xcT"][:, 10 + g, 128 * k:128 * k + 128], start=True, stop=True),
                                 reads=[S["b_xcT"]], writes=[bk[4]])
                        for g in range(2):
                            P.op(DVE, lambda e, g=g: e.tensor_tensor(gtm[:, 2 * g, :], banks[4][:, g * 128:(g + 1) * 128], maskf, ALU.mult), reads=[bk[4], b_cst], writes=[b_gtm])
                            P.op(DVE, lambda e, g=g: e.tensor_tensor(gtm[:, 2 * g + 1, :], banks[4][:, g * 128:(g + 1) * 128], maskb, ALU.mult), reads=[bk[4], b_cst], writes=[b_gtm])
                        for g in range(2):
                            P.op(PE, lambda e, g=g, k=k, pi=pi: e.matmul(banks[g][:, :], S["xcT"][:, 10 + g, 128 * k:128 * k + 128], pfb[pi][:, 0, g * 512:(g + 1) * 512], start=True, stop=True),
                                 reads=[S["b_xcT"], b_pfb[pi]], writes=[bk[g]])
                        for g in range(2):
                            sl = slice(g * 512, (g + 1) * 512)
                            y3 = yt[:, sl].rearrange("p (h d) -> p h d", h=8)
                            P.op(DVE, lambda e, g=g, y3=y3: e.tensor_tensor(y3, banks[g][:, :].rearrange("p (h d) -> p h d", h=8), bia[:, 32 + 8 * g:40 + 8 * g].unsqueeze(2).to_broadcast([128, 8, 64]), ALU.mult),
                                 reads=[bk[g], b_bia], writes=[b_yt])
                        for g in range(2):
                            P.op(PE, lambda e, g=g, k=k, pi=pi: e.matmul(banks[g][:, :], S["xcT"][:, 10 + g, 128 * k:128 * k + 128], pfb[pi][:, 1, g * 512:(g + 1) * 512], start=True, stop=True),
                                 reads=[S["b_xcT"], b_pfb[pi]], writes=[bk[g]])
                        for g in range(2):
                            sl = slice(g * 512, (g + 1) * 512)
                            y23 = yt2[:, sl].rearrange("p (h d) -> p h d", h=8)
                            P.op(DVE, lambda e, g=g, y23=y23: e.tensor_tensor(y23, banks[g][:, :].rearrange("p (h d) -> p h d", h=8), bia[:, 48 + 8 * g:56 + 8 * g].unsqueeze(2).to_broadcast([128, 8, 64]), ALU.mult),
                                 reads=[bk[g], b_bia], writes=[b_yt2])
                        P.op(POOL, lambda e: e.tensor_tensor(yt[:, :], yt[:, :], yt2[:, :], ALU.add), reads=[b_yt, b_yt2], writes=[b_yt])
                        items = [(h, d_) for h in range(16) for d_ in range(2)]

                        def front(ii, k=k):
                            h, d_ = items[ii]
                            g = h // 8
                            q = ii % 6
                            rhs = umb[:, 0, :] if d_ == 0 else umb[:, 1, :]
                            sq_ = ii % 5
                            for hl in range(2):
                                P.op(PE, lambda e, hl=hl: e.matmul(segq[sq_], ahl[:, hl, 16 * d_ + h:16 * d_ + h + 1].to_broadcast([128, 128]), rhs, start=(hl == 0), stop=(hl == 1), skip_group_check=True),
                                     reads=[b_ahl, b_cstb], writes=[b_segq[sq_]])
                            P.op(ACT, lambda e: e.activation(ef[q][:, :], segq[sq_], AF.Exp, bias=bia[:, 16 * d_ + h:16 * d_ + h + 1]), reads=[b_segq[sq_], b_bia], writes=[b_ef[q]])
                            P.op(DVE, lambda e: e.scalar_tensor_tensor(mt[q][:, :], ef[q][:, :], 1.0, gtm[:, 2 * g + d_, :], ALU.min, ALU.mult),
                                 reads=[b_ef[q], b_gtm], writes=[b_mt[q]])

                        def back(ii):
                            h, d_ = items[ii]
                            q = ii % 6
                            P.op(PE, lambda e: e.matmul(banks[6 + h // 8][:, (h % 8) * 64:(h % 8) * 64 + 64], mt[q][:, :], xdt[:, d_, h, :], start=(d_ == 0), stop=(d_ == 1), skip_group_check=True),
                                 reads=[b_mt[q], b_xdt], writes=[bk[6 + h // 8]])

                        LA = 4
                        for ii in range(LA):
                            front(ii)
                        for ii in range(32):
                            if ii + LA < 32:
                                front(ii + LA)
                            back(ii)
                            if ii == 8 and pend["front"] is not None:
                                pend["front"]()
                                pend["front"] = None
                        if pend["back"] is not None:
                            pend["back"]()
                            pend["back"] = None
                        for g in range(2):
                            sl = slice(g * 512, (g + 1) * 512)
                            P.op(DVE, lambda e, g=g, sl=sl: e.tensor_tensor(yt[:, sl], yt[:, sl], banks[6 + g][:, :], ALU.add), reads=[b_yt, bk[6 + g]], writes=[b_yt])

                        def tail_front(k=k, yt=yt, yt2=yt2, b_yt=b_yt, b_yt2=b_yt2, ssb=ssb, b_ssb=b_ssb, xt3=xt3):
                            P.op(POOL, lambda e: e.tensor_tensor(yt2[:, :].rearrange("p (h d) -> p h d", h=16), xt3, Dbc.unsqueeze(2).to_broadcast([128, 16, 64]), ALU.mult),
                                 reads=[S["b_xtm"], b_gt, b_yt2], writes=[b_yt2])
                            P.op(POOL, lambda e: e.tensor_tensor(yt[:, :], yt[:, :], yt2[:, :], ALU.add), reads=[b_yt, b_yt2], writes=[b_yt])
                            P.op(DVE, lambda e: e.tensor_tensor(yt[:, :], yt[:, :], sz[:, k, :], ALU.mult), reads=[b_yt, b_sz], writes=[b_yt])
                            rms_rstd(ACT, yt[:, :], 128, D, jq[:, :], b_jq, sso[:, k:k + 1], b_sso4[k], [b_yt])
                            P.op(DVE, lambda e: e.scalar_tensor_tensor(ssb[:, :], yt[:, :], sso[:, k:k + 1], ssmw, ALU.mult, ALU.mult), reads=[b_yt, b_sso4[k], b_gt], writes=[b_ssb])

                        def tail_back(k=k, ssb=ssb, b_ssb=b_ssb):
                            tpv = banks[4][:, :].bitcast(BF16)
                            for fc in range(8):
                                P.op(PE, lambda e, fc=fc: e.transpose(tpv[:, fc * 128:(fc + 1) * 128], ssb[:, fc * 128:(fc + 1) * 128], identb[:, :]), reads=[b_ssb, b_idb], writes=[bk[4]])
                            P.op(ACT, lambda e: e.copy(mixT[:, 8:16, 128 * k:128 * k + 128], tpv[:, :].rearrange("p (f t) -> p f t", f=8)), reads=[bk[4]], writes=[b_mix])

                        pend["front"], pend["back"] = tail_front, tail_back
                    pend["front"]()
                    pend["back"]()
                    P.barrier()
                    P.release(mkO)
                with ExitStack() as st:
                    mkM = P.mark()
                    NWP = 6
                    wpool = [sb(st, f"wp{i}", [128, 4096], BF16) for i in range(NWP)]
                    b_wp = [P.buf() for _ in range(NWP)]
                    h1 = sb(st, "h1", [128, 4, D])
                    b_h1 = [P.buf() for _ in range(4)]
                    u2b = [sb(st, f"u2b{i}", [128, D], BF16) for i in range(2)]
                    b_u2b = [P.buf() for _ in range(2)]
                    u2T = sb(st, "u2T", [128, 8, 512], BF16)
                    b_u2T = P.buf()
                    rT = [sb(st, f"rT{i}", [128, 512], BF16) for i in range(2)]
                    b_rT = [P.buf() for _ in range(2)]
                    aT = [sb(st, f"aT{i}", [128, 4, 512], BF16) for i in range(2)]
                    b_aT = [P.buf() for _ in range(2)]
                    ss2 = sb(st, "ss2", [128, 4])
                    b_ss2 = [P.buf() for _ in range(4)]
                    jq = sb(st, "jq2", [128, D], BF16)
                    b_jq = P.buf()
                    ot = [sb(st, f"ot{i}", [128, D]) for i in range(2)]
                    b_ot = [P.buf() for _ in range(2)]
                    for k in range(4):
                        P.dma(SP, lambda e, k=k: e.dma_start(out=h1[:, k, :], in_=xw[j][t_own, 2 + 128 * k:130 + 128 * k, :]), writes=[b_h1[k]], sbuf=b_h1[k])
                    mn = [0]
                    wvs = []
                    for cb in range(4):
                        wi = wpn[0] % NWP
                        wpn[0] += 1
                        wv = wpool[wi][:, :].rearrange("p (k c) -> p k c", k=16)
                        P.dma(SP, lambda e, wv=wv, cb=cb: e.dma_start(out=wv, in_=ws_out[:, :, cb * 256:(cb + 1) * 256]), reads=[b_ws["out"]], writes=[b_wp[wi]], sbuf=b_wp[wi])
                        wvs.append((wv, wi))
                    tpv = banks[2][:, :].bitcast(BF16).rearrange("p (k t) -> p k t", k=8)

                    def n2_front(k):
                        rms_rstd(ACT, h1[:, k, :], 128, D, jq[:, :], b_jq, ss2[:, k:k + 1], b_ss2[k], [b_h1[k]])
                        P.op(POOL, lambda e: e.tensor_scalar(u2b[k % 2][:, :], h1[:, k, :], ss2[:, k:k + 1], None, ALU.mult), reads=[b_h1[k], b_ss2[k]], writes=[b_u2b[k % 2]])

                    def n2_back(k):
                        for kc in range(8):
                            P.op(PE, lambda e, kc=kc: e.transpose(tpv[:, kc, :], u2b[k % 2][:, kc * 128:(kc + 1) * 128], identb[:, :]), reads=[b_u2b[k % 2], b_idb], writes=[bk[2]])
                        P.op(DVE, lambda e: e.tensor_tensor(u2T[:, :, 128 * k:128 * k + 128], tpv, w2T.unsqueeze(2).to_broadcast([128, 8, 128]), ALU.mult),
                             reads=[bk[2], b_gt], writes=[b_u2T])

                    for k in range(4):
                        for cb in range(4):
                            wv, wi = wvs[cb]
                            pb = mn[0] % 2
                            mn[0] += 1
                            for kc in range(16):
                                P.op(PE, lambda e, kc=kc, wv=wv: e.matmul(banks[pb][:, 0:256], mixT[:, kc, 128 * k:128 * k + 128], wv[:, kc, :], start=(kc == 0), stop=(kc == 15)),
                                     reads=[b_mix, b_wp[wi]], writes=[bk[pb]])
                            P.op(DVE, lambda e, cb=cb: e.tensor_tensor(h1[:, k, cb * 256:(cb + 1) * 256], h1[:, k, cb * 256:(cb + 1) * 256], banks[pb][:, 0:256], ALU.add),
                                 reads=[b_h1[k], bk[pb]], writes=[b_h1[k]])
                        n2_front(k)
                        if k >= 1:
                            n2_back(k - 1)
                    n2_back(3)
                    un = [0]
                    dn_w = {}

                    def emit_up(p_):
                        wi = wpn[0] % NWP
                        wpn[0] += 1
                        wu = wpool[wi][:, :].rearrange("p (k c) -> p k c", k=8)
                        P.dma(SP, lambda e: e.dma_start(out=wu, in_=ws_up[:, :, p_ * 512:(p_ + 1) * 512]), reads=[b_ws["up"]], writes=[b_wp[wi]], sbuf=b_wp[wi])
                        wj = wpn[0] % NWP
                        wpn[0] += 1
                        wd = wpool[wj][:, :].rearrange("p (k c) -> p k c", k=4)
                        P.dma(SP, lambda e: e.dma_start(out=wd, in_=ws_dn[:, p_ * 4:(p_ + 1) * 4, :]), reads=[b_ws["dn"]], writes=[b_wp[wj]], sbuf=b_wp[wj])
                        dn_w[p_] = (wd, wj)
                        ai = p_ % 2
                        for fc in range(4):
                            pb = 3 + (un[0] % 2)
                            ri = un[0] % 2
                            un[0] += 1
                            for kc in range(8):
                                P.op(PE, lambda e, kc=kc: e.matmul(banks[pb][:, :], wu[:, kc, fc * 128:(fc + 1) * 128], u2T[:, kc, :], start=(kc == 0), stop=(kc == 7)),
                                     reads=[b_u2T, b_wp[wi]], writes=[bk[pb]])
                            P.op(ACT, lambda e: e.activation(rT[ri][:, :], banks[pb][:, :], AF.Relu), reads=[bk[pb]], writes=[b_rT[ri]])
                            P.op(POOL, lambda e: e.tensor_tensor(aT[ai][:, fc, :], rT[ri][:, :], rT[ri][:, :], ALU.mult), reads=[b_rT[ri]], writes=[b_aT[ai]])

                    def emit_down(p_):
                        wd, wj = dn_w[p_]
                        ai = p_ % 2
                        for k in range(4):
                            for ch in range(2):
                                pb = 5 + (un[0] % 2)
                                un[0] += 1
                                for fc in range(4):
                                    P.op(PE, lambda e, fc=fc: e.matmul(banks[pb][:, :], aT[ai][:, fc, 128 * k:128 * k + 128], wd[:, fc, ch * 512:(ch + 1) * 512], start=(fc == 0), stop=(fc == 3)),
                                         reads=[b_aT[ai], b_wp[wj]], writes=[bk[pb]])
                                P.op(DVE, lambda e: e.tensor_tensor(h1[:, k, ch * 512:(ch + 1) * 512], h1[:, k, ch * 512:(ch + 1) * 512], banks[pb][:, :], ALU.add),
                                     reads=[b_h1[k], bk[pb]], writes=[b_h1[k]])

                    emit_up(0)
                    for p_ in range(8):
                        if p_ + 1 < 8:
                            emit_up(p_ + 1)
                        emit_down(p_)
                    for k in range(4):
                        P.op(ACT, lambda e, k=k: e.activation(jq[:, :], h1[:, k, :], AF.Square, accum_out=ss2[:, k:k + 1]), reads=[b_h1[k]], writes=[b_jq, b_ss2[k]])
                    for k in range(4):
                        P.op(DVE, lambda e, k=k: e.tensor_scalar(ss2[:, k:k + 1], ss2[:, k:k + 1], 1.0 / D, EPS, ALU.mult, ALU.add), reads=[b_ss2[k]], writes=[b_ss2[k]])
                    for k in range(4):
                        P.op(ACT, lambda e, k=k: e.activation(ss2[:, k:k + 1], ss2[:, k:k + 1], AF.Ln), reads=[b_ss2[k]], writes=[b_ss2[k]])
                    for k in range(4):
                        P.op(ACT, lambda e, k=k: e.activation(ss2[:, k:k + 1], ss2[:, k:k + 1], AF.Exp, scale=-0.5), reads=[b_ss2[k]], writes=[b_ss2[k]])
                    for k in range(4):
                        oi2 = k % 2
                        P.op(DVE, lambda e, k=k, oi2=oi2: e.scalar_tensor_tensor(ot[oi2][:, :], h1[:, k, :], ss2[:, k:k + 1], finw, ALU.mult, ALU.mult), reads=[b_h1[k], b_ss2[k], b_gt], writes=[b_ot[oi2]])
                        P.dma(POOL, lambda e, k=k, oi2=oi2: e.dma_start(out=yout[j][oi * 512 + 128 * k:oi * 512 + 128 * k + 128, :], in_=ot[oi2][:, :]), reads=[b_ot[oi2]], sbuf=b_ot[oi2])
                    P.barrier()
                    P.release(mkM)
            P.release(mkT)
    P.finish()
    return nc, P


def _consts():
    c = np.zeros((128, 6 * 128 + 4 * 512), np.float32)
    i = np.arange(128)
    c[:, 0:128] = np.eye(128)
    c[:, 128:256] = (i[:, None] <= i[None, :])
    c[:, 256:384] = -1.0 * (i[:, None] < i[None, :])
    c[:, 384:512] = 1.0
    c[:, 512:640] = (i[None, :] >= i[:, None])
    c[:, 640:768] = (i[None, :] <= i[:, None])
    q = np.arange(512)
    for kk in range(4):
        c[:, 768 + 512 * kk:768 + 512 * (kk + 1)] = np.abs((128 * kk + i)[:, None] - q[None, :])
    return c


def _job_arrays(cfg, seq, own_start, params, is_prompt):
    meta = params["meta_tokens"]
    S = seq.shape[0]
    full = np.concatenate([meta, seq], axis=0)
    L = full.shape[0]
    NOWN = cfg.NOWN
    own_s0 = 16 + own_start
    own_s1 = own_s0 + NOWN * 512
    tiles = []
    kinds = []
    tiles.append(np.arange(-2, 18)); kinds.append("L")
    for s0 in range(16, own_s0, 512):
        tiles.append(np.arange(s0 - 2, s0 + 514)); kinds.append("L")
    for s0 in range(L - 512, own_s1 - 1, -512):
        tiles.append(np.arange(s0 + 513, s0 - 3, -1)); kinds.append("R")
    for s0 in range(own_s0, own_s1, 512):
        tiles.append(np.arange(s0 - 2, s0 + 514)); kinds.append("O")
    nt = len(tiles)
    xwin = np.zeros((nt, 516, D), np.float32)
    for t, idx in enumerate(tiles):
        ok = (idx >= 0) & (idx < L)
        xwin[t, np.nonzero(ok)[0]] = full[idx[ok]]
    pos = [tiles[0][2:18]] + [tl[2:514] for tl in tiles[1:]]
    kindtok = np.concatenate([np.full(len(p), {"L": 0, "R": 1, "O": 2}[k]) for p, k in zip(pos, kinds)])
    pos = np.concatenate(pos).astype(np.int64)
    Ls = len(pos)
    slopes = 2.0 ** (-(np.arange(NH) + 1.0))
    cpos, rpos = (pos // 128).astype(np.float32), (pos % 128).astype(np.float32)
    kaug = np.zeros((NH, 8, Ls), np.float32)
    for h in range(NH):
        sl = slopes[h]
        left = np.stack([-np.ones(Ls), -np.ones(Ls), sl * 128 * cpos, sl * rpos])
        right = -left
        ml = (kindtok != 1)[None, :]
        mr = (kindtok != 0)[None, :]
        kaug[h, 0:4] = left * ml
        kaug[h, 4:8] = right * mr
    qpos = np.arange(own_s0, own_s1)
    qc, qr = (qpos // 128).astype(np.float32), (qpos % 128).astype(np.float32)
    qaug = np.zeros((3, NH, 8, NOWN * 512), np.float32)
    for h in range(NH):
        sl = slopes[h]
        qa = np.stack([sl * 128 * qc, sl * qr, np.ones_like(qc), np.ones_like(qc)])
        qaug[0, h, 0:4] = qa; qaug[0, h, 4:8] = qa
        qaug[1, h, 0:4] = qa
        qaug[2, h, 4:8] = qa
    cw = params["conv_w"][0]
    cb = params["conv_b"][0]
    tab = np.zeros((nt, 128, TT), np.float32)
    cwT = cw.T.reshape(12, 128, 5).transpose(1, 0, 2)
    cbT = cb.reshape(12, 128).T
    last_left = max(t for t, k in enumerate(kinds) if k == "L")
    for t, k in enumerate(kinds):
        taps = cwT[:, :, ::-1] if k == "R" else cwT
        tab[t, :, 0:60] = taps.reshape(128, 60)
        tab[t, :, 60:72] = cbT
        if k == "R":
            prim_b, prim_a, sf, sb_ = params["dt_bias_b"][0], params["a_log_b"][0], 0.0, 1.0
        else:
            prim_b, prim_a, sf, sb_ = params["dt_bias_f"][0], params["a_log_f"][0], 1.0, 0.0
        tab[t, :, 72:88] = prim_b[None, :]
        tab[t, :, 88:104] = params["dt_bias_b"][0][None, :]
        tab[t, :, 104:120] = prim_a[None, :]
        tab[t, :, 120:136] = params["a_log_b"][0][None, :]
        tab[t, :, 136] = sf
        tab[t, :, 137] = sb_
        tab[t, :, 138] = 1.0 if t == last_left else 0.0
        tab[t, :, 139] = 0.0 if t == last_left else 1.0
    return dict(xw=xwin, kaug=kaug.astype(ml_dtypes.bfloat16), qaug=qaug.astype(ml_dtypes.bfloat16), ttab=tab)


_CACHE = {}


def run(cfg, inputs):
    p = {k: np.asarray(v, np.float32) for k, v in inputs.items()}
    key = (cfg.NC, cfg.SP, cfg.SS, cfg.NSEQ, cfg.NOWN)
    if key not in _CACHE:
        _CACHE[key] = build_program(cfg)
    nc, P = _CACHE[key]
    gt = np.zeros((128, 8 + 8 + 16 + 1024 + 1024 + 128 + 256), np.float32)
    gt[:, 0:8] = p["norm1_w"][0].reshape(8, 128).T
    gt[:, 8:16] = p["norm2_w"][0].reshape(8, 128).T
    gt[:, 16:32] = p["d_skip"][0][None, :]
    gt[:, 32:1056] = p["ssm_norm_w"][0][None, :]
    gt[:, 1056:2080] = p["final_norm_w"][None, :]
    gt[:, 2080:2208] = p["attn_norm_w"][0][None, :]
    gt[:, 2208:2272] = p["lambda_q1"][0][None, :]
    gt[:, 2272:2336] = p["lambda_k1"][0][None, :]
    gt[:, 2336:2400] = p["lambda_q2"][0][None, :]
    gt[:, 2400:2464] = p["lambda_k2"][0][None, :]
    cst = _consts()
    in_maps = []
    xp = p["x_prompt"][0]
    xs = p["x_sample"]
    for c in range(cfg.NC):
        m = {"w_in": p["w_in"][0], "w_out": p["w_out"][0], "w_up": p["w_up"][0], "w_dn": p["w_down"][0], "cst": cst, "gtab": gt}
        ja = [_job_arrays(cfg, xp, c * cfg.OWNT, p, True)]
        for s in range(cfg.NSEQ):
            ja.append(_job_arrays(cfg, xs[c * cfg.NSEQ + s], 0, p, False))
        for j, a in enumerate(ja):
            m[f"xw{j}"] = a["xw"]
            m[f"kaug{j}"] = a["kaug"]
            m[f"qaug{j}"] = a["qaug"]
            m[f"ttab{j}"] = a["ttab"]
        in_maps.append(m)
    res = run_bass_kernel_spmd(nc, in_maps, core_ids=list(range(cfg.NC)))
    _CACHE["last_exec_ns"] = getattr(res, "exec_time_ns", None)
    yp = np.concatenate([res.results[c]["y0"] for c in range(cfg.NC)], axis=0)[None]
    ys = np.stack([res.results[c][f"y{1 + s}"] for c in range(cfg.NC) for s in range(cfg.NSEQ)], axis=0)
    return yp.astype(np.float32), ys.astype(np.float32)


def kernel(**inputs):
    cfg = Cfg(ncores=8, sp=16384, ss=2048, nseq=4, nown=4)
    return run(cfg, inputs)
```

```python
import math
from contextlib import ExitStack
import numpy as np
import ml_dtypes
import concourse.bass as bass
import concourse.mybir as mybir
from concourse.bass_utils import run_bass_kernel_spmd

F32 = mybir.dt.float32
BF16 = mybir.dt.bfloat16
AF = mybir.ActivationFunctionType
ALU = mybir.AluOpType
PE, ACT, DVE, POOL, SP = "tensor", "scalar", "vector", "gpsimd", "sync"
COMPUTE = (PE, ACT, DVE, POOL)
ENGS = (PE, ACT, DVE, POOL, SP)

D = 1024
NH = 8
EPS = 1e-5
TT = 140
N_META = 16
LAM_INIT = 0.8 - 0.6 * math.exp(-0.3 * 0)
CQ, CK, CV, CZ, CX, CDT = 0, 1024, 2048, 3072, 4096, 5632


class Buf:
    __slots__ = ("name", "w", "rd", "dsem", "dcnt", "keep")

    def __init__(self, name=""):
        self.name = name
        self.keep = bool(name)
        self.w = None
        self.rd = []
        self.dsem = None
        self.dcnt = 0


class Op:
    __slots__ = ("eng", "fn", "deps", "is_dma", "flag", "sem", "val", "dbuf", "win", "pos")

    def __init__(self, eng, fn, is_dma):
        self.eng, self.fn, self.is_dma = eng, fn, is_dma
        self.deps = []
        self.flag = False
        self.sem = None
        self.val = 0
        self.dbuf = None
        self.win = 0
        self.pos = 0


class _Rec:
    def __init__(self):
        self.call = None

    def __getattr__(self, name):
        def f(*a, **k):
            self.call = (name, a, k)
            return None
        return f


class Prog:
    def __init__(self, nc):
        self.nc = nc
        self.pending = []
        self.bufs = []
        self.win = 0
        self.engs = {PE: nc.tensor, ACT: nc.scalar, DVE: nc.vector, POOL: nc.gpsimd, SP: nc.sync}
        self.esem = {e: nc.alloc_semaphore(f"prog_{e}") for e in ENGS}
        self.ecnt = {e: 0 for e in ENGS}
        self.seen = {e: {} for e in ENGS}
        self.winlast = {}
        self.lastop = {e: None for e in ENGS}
        self.n_ops = 0
        self.n_waits = 0
        self.sempool = []
        self.sempool_sw = []
        self.semq = {}
        self.nsem = 0

    def buf(self, name=""):
        b = Buf(name)
        self.bufs.append(b)
        return b

    def mark(self):
        return len(self.bufs)

    def release(self, mk):
        del self.bufs[mk:]

    def _add(self, op, reads, writes):
        deps = {}
        raw = set()
        for b in reads:
            if b.w is not None:
                deps[id(b.w)] = b.w
                raw.add(id(b.w))
        for b in writes:
            if b.w is not None and not (op.is_dma and b.w.is_dma and b.w.dbuf is op.dbuf):
                deps[id(b.w)] = b.w
            for r in b.rd:
                deps[id(r)] = r
        dl = []
        for d in deps.values():
            if d is op:
                continue
            if (not d.is_dma) and (not op.is_dma) and d.eng == op.eng:
                if op.eng == PE or id(d) not in raw:
                    continue
            dl.append(d)
        op.deps = dl
        for b in reads:
            b.rd.append(op)
        for b in writes:
            b.w = op
            b.rd = []
        op.win = self.win
        self.pending.append(op)
        if not op.is_dma:
            self.lastop[op.eng] = op
        self.n_ops += 1
        return op

    @staticmethod
    def _bind(fn):
        r = _Rec()
        fn(r)
        c = r.call
        return lambda e: getattr(e, c[0])(*c[1], **c[2])

    def op(self, eng, fn, reads=(), writes=()):
        return self._add(Op(eng, self._bind(fn), False), list(reads), list(writes))

    def dma(self, eng, fn, reads=(), writes=(), sbuf=None):
        o = Op(eng, self._bind(fn), True)
        o.dbuf = sbuf
        return self._add(o, list(reads), list(writes))

    def barrier(self):
        deps = {}
        for e in COMPUTE:
            if self.lastop[e] is not None:
                deps[id(self.lastop[e])] = self.lastop[e]
        for b in self.bufs:
            if b.w is not None and b.w.is_dma:
                deps[id(b.w)] = b.w
            for r in b.rd:
                if r.is_dma:
                    deps[id(r)] = r
        b0 = Op(SP, lambda e: e.nop(), False)
        b0.deps = list(deps.values())
        b0.win = self.win
        self.pending.append(b0)
        self.lastop[SP] = b0
        for e in COMPUTE + (SP,):
            o = Op(e, lambda en: en.nop(), False)
            o.deps = [b0]
            o.win = self.win
            self.pending.append(o)
            self.lastop[e] = o
        for b in self.bufs:
            b.w = None
            b.rd = []
        self.flush()
        for b in self.bufs:
            if b.dsem is not None:
                (self.sempool_sw if self.semq.get(id(b.dsem)) == POOL else self.sempool).append((b.dsem, b.dcnt))
                b.dsem = None

    def flush(self):
        nc = self.nc
        ops = self.pending
        self.pending = []
        last = {}
        for o in ops:
            for d in o.deps:
                d.flag = True
            if not o.is_dma:
                last[o.eng] = o
        for e, o in last.items():
            o.flag = True
            self.winlast[(self.win, e)] = o
        for o in ops:
            if o.is_dma:
                b = o.dbuf
                if b.dsem is None:
                    pool = self.sempool_sw if o.eng == POOL else self.sempool
                    if pool:
                        b.dsem, b.dcnt = pool.pop()
                    else:
                        self.nsem += 1
                        b.dsem = nc.alloc_semaphore(f"dma_{self.nsem}")
                        b.dcnt = 0
                    self.semq[id(b.dsem)] = o.eng
                b.dcnt += 16
                o.sem, o.val = b.dsem, b.dcnt
            elif o.flag:
                self.ecnt[o.eng] += 1
                o.sem, o.val = self.esem[o.eng], self.ecnt[o.eng]
        for o in ops:
            e = self.engs[o.eng]
            need = {}
            for d in o.deps:
                if d.sem is None:
                    d = self.winlast[(d.win, d.eng)]
                k = id(d.sem)
                if k not in need or need[k][1] < d.val:
                    need[k] = (d.sem, d.val)
            sn = self.seen[o.eng]
            for k, (s, v) in need.items():
                if sn.get(k, 0) >= v:
                    continue
                e.wait_ge(s, v)
                sn[k] = v
                self.n_waits += 1
            ins = o.fn(e)
            if o.is_dma:
                ins.then_inc(o.sem, 16)
            elif o.flag:
                ins.then_inc(o.sem, 1)
        self.win += 1

    def finish(self):
        self.flush()
        fe = self.engs[SP]
        for b in self.bufs:
            if b.dsem is not None:
                fe.wait_ge(b.dsem, b.dcnt)
        for (sm, cnt) in self.sempool + self.sempool_sw:
            if cnt > 0:
                fe.wait_ge(sm, cnt)


class Rot:
    def __init__(self, items):
        self.items = items
        self.i = 0

    def next(self):
        it = self.items[self.i % len(self.items)]
        self.i += 1
        return it


class Cfg:
    def __init__(self, ncores=8, sp=16384, ss=2048, nseq=4, nown=4):
        self.NC, self.SP, self.SS, self.NSEQ, self.NOWN = ncores, sp, ss, nseq, nown
        assert sp == ncores * nown * 512 and ss == nown * 512
        self.NCTX = sp // 512 - nown
        self.jobs = [self.NCTX] + [0] * nseq
        self.OWNT = nown * 512

    def ntiles(self, j):
        return 1 + self.jobs[j] + self.NOWN

    def L(self, j):
        return 16 + 512 * (self.ntiles(j) - 1)

    def nch(self, j):
        return 1 + 4 * (self.ntiles(j) - 1)


def build_program(cfg):
    nc = bass.Bass("TRN2", target_bir_lowering=False)
    P = Prog(nc)
    NOWN, OWNT = cfg.NOWN, cfg.OWNT
    NJ = len(cfg.jobs)
    Lmax = max(cfg.L(j) for j in range(NJ))
    NCHmax = max(cfg.nch(j) for j in range(NJ))

    def din(name, shape, dt=F32):
        return nc.dram_tensor(name, list(shape), dt, kind="ExternalInput").ap()

    def dscr(name, shape, dt=BF16):
        return nc.dram_tensor(name, list(shape), dt, kind="Internal").ap()

    xw = [din(f"xw{j}", [cfg.ntiles(j), 516, D]) for j in range(NJ)]
    kaug = [din(f"kaug{j}", [NH, 8, cfg.L(j)], BF16) for j in range(NJ)]
    qaug = [din(f"qaug{j}", [3, NH, 8, OWNT], BF16) for j in range(NJ)]
    ttab = [din(f"ttab{j}", [cfg.ntiles(j), 128, TT]) for j in range(NJ)]
    yout = [nc.dram_tensor(f"y{j}", [OWNT, D], F32, kind="ExternalOutput").ap() for j in range(NJ)]
    w_in = din("w_in", [D, 5664])
    w_out = din("w_out", [2048, D])
    w_up = din("w_up", [D, 4096])
    w_dn = din("w_dn", [4096, D])
    cst = din("cst", [128, 6 * 128 + 4 * 512])
    gtab = din("gtab", [128, 8 + 8 + 16 + 1024 + 1024 + 128 + 256])
    ws_in = dscr("ws_in", [128, 8, 5664])
    ws_out = dscr("ws_out", [128, 16, D])
    ws_up = dscr("ws_up", [128, 8, 4096])
    ws_dn = dscr("ws_dn", [128, 32, D])
    KT = dscr("KT", [NH, 2, 72, Lmax])
    QT = dscr("QT", [NH, 2, 64, OWNT])
    VA = dscr("VA", [NH, 128, NCHmax, 129])
    PFB = dscr("PFB", [2, 4 * NOWN, 128, D])
    SBS = dscr("SBS", [4 * NOWN, 128, D])
    b_ws = {k: P.buf("ws_" + k) for k in ("in", "out", "up", "dn")}
    b_KT, b_QT, b_VA, b_PFB, b_SBS = P.buf("KT"), P.buf("QT"), P.buf("VA"), P.buf("PFB"), P.buf("SBS")

    ES = ExitStack()
    G = ES

    _nm = [0]

    def sb(st, name, shape, dt=F32):
        _nm[0] += 1
        return st.enter_context(nc.sbuf_tensor(f"{name}_{_nm[0]}", list(shape), dt))

    allbanks = nc.alloc_psum_tensor("allbanks", [128, 4096], F32)
    banks = [allbanks[:, 512 * i:512 * (i + 1)] for i in range(8)]
    bk = [P.buf(f"bank{i}") for i in range(8)]

    cst_t = sb(G, "cst_t", [128, 6 * 128 + 4 * 512])
    gtab_t = sb(G, "gtab_t", [128, 8 + 8 + 16 + 1024 + 1024 + 128 + 256])
    identb = sb(G, "identb", [128, 128], BF16)
    lam_t = sb(G, "lam_t", [128, 8])
    CF = sb(G, "CF", [128, D])
    CB = sb(G, "CB", [128, D])
    decs = sb(G, "decs", [128, 4 * NOWN, 16])
    b_cst, b_gt, b_idb, b_lam, b_CF, b_CB, b_decs = (P.buf(n) for n in ("cst", "gt", "idb", "lam", "CF", "CB", "decs"))
    identf = cst_t[:, 0:128]
    uincl = cst_t[:, 128:256]
    negus = cst_t[:, 256:384]
    onesf = cst_t[:, 384:512]
    maskf = cst_t[:, 512:640]
    maskb = cst_t[:, 640:768]
    absd = [cst_t[:, 768 + 512 * i: 768 + 512 * (i + 1)] for i in range(4)]
    w1T = gtab_t[:, 0:8]
    w2T = gtab_t[:, 8:16]
    Dbc = gtab_t[:, 16:32]
    ssmw = gtab_t[:, 32:32 + 1024]
    finw = gtab_t[:, 1056:1056 + 1024]
    attw = gtab_t[:, 2080:2080 + 128]
    lamv = gtab_t[:, 2208:2208 + 256]

    P.dma(SP, lambda e: e.dma_start(out=cst_t[:], in_=cst[:, :]), writes=[b_cst], sbuf=b_cst)
    P.dma(SP, lambda e: e.dma_start(out=gtab_t[:], in_=gtab[:, :]), writes=[b_gt], sbuf=b_gt)
    P.op(DVE, lambda e: e.tensor_copy(identb[:], identf), reads=[b_cst], writes=[b_idb])
    umb = sb(G, "umb", [128, 2, 128], BF16)
    b_cstb = P.buf("cstb")
    P.op(DVE, lambda e: e.tensor_copy(umb[:, 0, :], uincl), reads=[b_cst], writes=[b_cstb])
    P.op(DVE, lambda e: e.tensor_copy(umb[:, 1, :], negus), reads=[b_cst], writes=[b_cstb])
    P.op(DVE, lambda e: e.tensor_tensor(lamv[:, 0:64], lamv[:, 0:64], lamv[:, 64:128], ALU.mult),
         reads=[b_gt], writes=[b_gt])
    P.op(DVE, lambda e: e.tensor_tensor(lamv[:, 128:192], lamv[:, 128:192], lamv[:, 192:256], ALU.mult),
         reads=[b_gt], writes=[b_gt])
    P.op(DVE, lambda e: e.reduce_sum(lam_t[:, 0:1], lamv[:, 0:64], mybir.AxisListType.X), reads=[b_gt], writes=[b_lam])
    P.op(DVE, lambda e: e.reduce_sum(lam_t[:, 1:2], lamv[:, 128:192], mybir.AxisListType.X), reads=[b_gt], writes=[b_lam])
    P.op(ACT, lambda e: e.activation(lam_t[:, 2:4], lam_t[:, 0:2], AF.Exp), reads=[b_lam], writes=[b_lam])
    P.op(DVE, lambda e: e.tensor_tensor(lam_t[:, 4:5], lam_t[:, 3:4], lam_t[:, 2:3], ALU.subtract), reads=[b_lam], writes=[b_lam])
    P.op(DVE, lambda e: e.tensor_scalar(lam_t[:, 4:5], lam_t[:, 4:5], -LAM_INIT, None, ALU.add), reads=[b_lam], writes=[b_lam])
    P.op(DVE, lambda e: e.tensor_scalar(attw, attw, 1.0 - LAM_INIT, None, ALU.mult), reads=[b_gt], writes=[b_gt])
    neglam = lam_t[:, 4:5]

    with ExitStack() as st:
        stf = [sb(st, f"stf{i}", [128, 2048]) for i in range(4)]
        stb = [sb(st, f"stb{i}", [128, 2048], BF16) for i in range(4)]
        b_stf = [P.buf(f"stf{i}") for i in range(4)]
        b_stb = [P.buf(f"stb{i}") for i in range(4)]
        cnt = [0]

        def conv_w(W, scr, bscr, K, N):
            for kc in range(K // 128):
                for c0 in range(0, N, 2048):
                    w = min(2048, N - c0)
                    i = cnt[0] % 4
                    ce = (DVE, ACT)[cnt[0] % 2]
                    cnt[0] += 1
                    P.dma(SP, lambda e, i=i, kc=kc, c0=c0, w=w: e.dma_start(out=stf[i][:, 0:w], in_=W[kc * 128:(kc + 1) * 128, c0:c0 + w]),
                          writes=[b_stf[i]], sbuf=b_stf[i])
                    if ce == ACT:
                        P.op(ACT, lambda e, i=i, w=w: e.copy(stb[i][:, 0:w], stf[i][:, 0:w]), reads=[b_stf[i]], writes=[b_stb[i]])
                    else:
                        P.op(ce, lambda e, i=i, w=w: e.tensor_copy(stb[i][:, 0:w], stf[i][:, 0:w]), reads=[b_stf[i]], writes=[b_stb[i]])
                    P.dma(POOL, lambda e, i=i, kc=kc, c0=c0, w=w: e.dma_start(out=scr[:, kc, c0:c0 + w], in_=stb[i][:, 0:w]),
                          reads=[b_stb[i]], writes=[bscr], sbuf=b_stb[i])

        conv_w(w_in, ws_in, b_ws["in"], D, 5664)
        conv_w(w_out, ws_out, b_ws["out"], 2048, D)
        conv_w(w_up, ws_up, b_ws["up"], D, 4096)
        conv_w(w_dn, ws_dn, b_ws["dn"], 4096, D)
        P.barrier()

    def make_uT(st, pfx):
        xs = [sb(st, f"{pfx}xs{i}", [128, D]) for i in range(2)]
        xb = [sb(st, f"{pfx}xb{i}", [128, D], BF16) for i in range(2)]
        sq = sb(st, f"{pfx}sq", [128, D], BF16)
        ss = sb(st, f"{pfx}ss", [128, 4])
        uT = sb(st, f"{pfx}uT", [128, 8, 516], BF16)
        return dict(xs=xs, xb=xb, sq=sq, ss=ss, uT=uT,
                    b_xs=[P.buf() for _ in range(2)], b_xb=[P.buf() for _ in range(2)],
                    b_sq=P.buf(), b_ss=[P.buf(), P.buf()], b_uT=P.buf(), cnt=[0])

    def rms_rstd(eng_sq, src_ap, n, width, sqjunk, b_junk, ssap, b_ss, src_bufs):
        P.op(ACT, lambda e: e.activation(sqjunk, src_ap, AF.Square, accum_out=ssap), reads=src_bufs, writes=[b_junk, b_ss])
        P.op(ACT, lambda e: e.activation(ssap, ssap, AF.Ln, scale=1.0 / width, bias=EPS), reads=[b_ss], writes=[b_ss])
        P.op(ACT, lambda e: e.activation(ssap, ssap, AF.Exp, scale=-0.5), reads=[b_ss], writes=[b_ss])

    def uT_thunks(U, src, W, wT, tp_bank):
        uT = U["uT"]
        tpv = banks[tp_bank][:, :].bitcast(BF16).rearrange("p (k t) -> p k t", k=8)
        out = []
        r0 = 0
        while r0 < W:
            n = min(128, W - r0)

            def mk(r0=r0, n=n):
                st_ = {}

                def stepA():
                    i = U["cnt"][0] % 2
                    U["cnt"][0] += 1
                    st_["i"] = i
                    xs, xb, bxs, bxb = U["xs"][i], U["xb"][i], U["b_xs"][i], U["b_xb"][i]
                    P.dma(SP, lambda e: e.dma_start(out=xs[0:n, :], in_=src[r0:r0 + n, :]), writes=[bxs], sbuf=bxs)
                    ssap = U["ss"][0:n, i:i + 1]
                    rms_rstd(ACT, xs[0:n, :], n, D, U["sq"][0:n, :], U["b_sq"], ssap, U["b_ss"][i], [bxs])
                    P.op(ACT, lambda e: e.activation(xb[0:n, :], xs[0:n, :], AF.Copy, scale=ssap), reads=[bxs, U["b_ss"][i]], writes=[bxb])

                def stepB():
                    i = st_["i"]
                    xb, bxb = U["xb"][i], U["b_xb"][i]
                    for kc in range(8):
                        P.op(PE, lambda e, kc=kc: e.transpose(tpv[:, kc, 0:n], xb[0:n, kc * 128:(kc + 1) * 128], identb[0:n, 0:n]),
                             reads=[bxb, b_idb], writes=[bk[tp_bank]])
                    P.op(DVE, lambda e: e.tensor_tensor(uT[:, :, r0:r0 + n], tpv[:, :, 0:n], wT.unsqueeze(2).to_broadcast([128, 8, n]), ALU.mult),
                         reads=[bk[tp_bank], b_gt], writes=[U["b_uT"]])
                return stepA, stepB
            out.append(mk())
            r0 += n
        return out

    def build_uT(U, src, W, wT, tp_bank):
        for (sa, sb_) in uT_thunks(U, src, W, wT, tp_bank):
            sa()
            sb_()

    def make_ssd(st, pfx):
        S = dict()
        S["tab"] = sb(st, f"{pfx}tab", [128, TT])
        S["raw"] = [sb(st, f"{pfx}raw{i}", [128, 516]) for i in range(2)]
        S["acc"] = [sb(st, f"{pfx}acc{i}", [128, 512]) for i in range(2)]
        S["xcT"] = sb(st, f"{pfx}xcT", [128, 12, 512], BF16)
        S["xtm"] = sb(st, f"{pfx}xtm", [128, 4, D], BF16)
        S["btm"] = sb(st, f"{pfx}btm", [128, 4, 256], BF16)
        S["dt"] = sb(st, f"{pfx}dt", [128, 4, 32])
        S["adt"] = sb(st, f"{pfx}adt", [128, 4, 32])
        S["cs"] = sb(st, f"{pfx}cs", [128, 4, 32])
        S["tot"] = sb(st, f"{pfx}tot", [128, 4, 32])
        S["A2"] = sb(st, f"{pfx}A2", [128, 32])
        S["tmp32"] = sb(st, f"{pfx}tmp32", [128, 32])
        for k in ("tab", "xcT", "xtm", "btm", "dtq", "A2", "tmp32"):
            S["b_" + k] = P.buf(pfx + k)
        S["b_raw"] = [P.buf() for _ in range(2)]
        S["b_acc"] = [P.buf() for _ in range(2)]
        S["n"] = 0
        return S

    def ssd_par(st, S, pfx):
        S2 = dict(S)
        S2["tab"] = sb(st, f"{pfx}tabB", [128, TT])
        S2["dt"] = sb(st, f"{pfx}dtB", [128, 4, 32])
        S2["adt"] = sb(st, f"{pfx}adtB", [128, 4, 32])
        S2["cs"] = sb(st, f"{pfx}csB", [128, 4, 32])
        S2["tot"] = sb(st, f"{pfx}totB", [128, 4, 32])
        S2["A2"] = sb(st, f"{pfx}A2B", [128, 32])
        for k in ("tab", "dtq", "A2"):
            S2["b_" + k] = P.buf()
        return S2

    def ssd_thunks(S, U, j, t, T, W, wv, cx, cdt, pb_main, pb_small, pb_tp):
        uT = U["uT"]
        tab = S["tab"]
        nchk = max(1, T // 128)
        cs_ = min(T, 128)
        def pre():
            P.dma(SP, lambda e: e.dma_start(out=tab[:], in_=ttab[j][t, :, :]), writes=[S["b_tab"]], sbuf=S["b_tab"])
            P.op(ACT, lambda e: e.activation(S["A2"][:], tab[:, 104:136], AF.Exp), reads=[S["b_tab"]], writes=[S["b_A2"]])
            P.op(DVE, lambda e: e.tensor_scalar(S["A2"][:], S["A2"][:], -1.0, None, ALU.mult), reads=[S["b_A2"]], writes=[S["b_A2"]])
            for k in range(nchk):
                for kc in range(8):
                    P.op(PE, lambda e, kc=kc, k=k: e.matmul(banks[pb_small][0:cs_, 64 + 32 * k:96 + 32 * k], uT[:, kc, 2 + 128 * k:2 + 128 * k + cs_], wv[:, kc, cdt:cdt + 32],
                                                            start=(kc == 0), stop=(kc == 7)),
                         reads=[U["b_uT"], b_wres], writes=[bk[pb_small]])
            dtr = banks[pb_small][0:cs_, 64:64 + 32 * nchk].rearrange("p (k c) -> p k c", c=32)
            dt, adt = S["dt"], S["adt"]
            P.op(DVE, lambda e: e.tensor_copy(dt[0:cs_, 0:nchk, 16:32], dtr[:, :, 16:32]), reads=[bk[pb_small]], writes=[S["b_dtq"]])
            P.op(DVE, lambda e: e.tensor_scalar(dt[0:cs_, 0:nchk, 0:16], dtr[:, :, 0:16], tab[0:cs_, 136:137], None, ALU.mult),
                 reads=[bk[pb_small], S["b_tab"]], writes=[S["b_dtq"]])
            P.op(DVE, lambda e: e.scalar_tensor_tensor(dt[0:cs_, 0:nchk, 0:16], dt[0:cs_, 0:nchk, 16:32], tab[0:cs_, 137:138], dt[0:cs_, 0:nchk, 0:16], ALU.mult, ALU.add),
                 reads=[S["b_dtq"], S["b_tab"]], writes=[S["b_dtq"]])
            P.op(DVE, lambda e: e.tensor_tensor(dt[0:cs_, 0:nchk, :], dt[0:cs_, 0:nchk, :], tab[0:cs_, 72:104].unsqueeze(1).to_broadcast([cs_, nchk, 32]), ALU.add),
                 reads=[S["b_dtq"], S["b_tab"]], writes=[S["b_dtq"]])
            P.op(ACT, lambda e: e.activation(dt[0:cs_, 0:nchk, :], dt[0:cs_, 0:nchk, :], AF.Exp), reads=[S["b_dtq"]], writes=[S["b_dtq"]])
            P.op(ACT, lambda e: e.activation(dt[0:cs_, 0:nchk, :], dt[0:cs_, 0:nchk, :], AF.Ln, bias=1.0), reads=[S["b_dtq"]], writes=[S["b_dtq"]])
            P.op(DVE, lambda e: e.tensor_tensor(adt[0:cs_, 0:nchk, :], dt[0:cs_, 0:nchk, :], S["A2"][0:cs_, :].unsqueeze(1).to_broadcast([cs_, nchk, 32]), ALU.mult),
                 reads=[S["b_dtq"], S["b_A2"]], writes=[S["b_dtq"]])
            for k in range(nchk):
                P.op(PE, lambda e, k=k: e.matmul(banks[pb_small][0:cs_, 192 + 32 * k:224 + 32 * k], uincl[0:cs_, 0:cs_], adt[0:cs_, k, :], start=True, stop=True),
                     reads=[S["b_dtq"], b_cst], writes=[bk[pb_small]])
                P.op(PE, lambda e, k=k: e.matmul(banks[pb_small][:, 320 + 32 * k:352 + 32 * k], onesf[0:cs_, :], adt[0:cs_, k, :], start=True, stop=True),
                     reads=[S["b_dtq"], b_cst], writes=[bk[pb_small]])
            P.op(DVE, lambda e: e.tensor_copy(S["cs"][0:cs_, 0:nchk, :], banks[pb_small][0:cs_, 192:192 + 32 * nchk].rearrange("p (k c) -> p k c", c=32)),
                 reads=[bk[pb_small]], writes=[S["b_dtq"]])
            P.op(DVE, lambda e: e.tensor_copy(S["tot"][:, 0:nchk, :], banks[pb_small][:, 320:320 + 32 * nchk].rearrange("p (k c) -> p k c", c=32)),
                 reads=[bk[pb_small]], writes=[S["b_dtq"]])

        pend_silu = []

        def one_fc(fc):
            i = S["n"] % 2
            S["n"] += 1
            raw, acc, braw, bacc = S["raw"][i], S["acc"][i], S["b_raw"][i], S["b_acc"][i]
            pbm = pb_main[fc % len(pb_main)]
            wa = min(W, 512)
            for kc in range(8):
                P.op(PE, lambda e, kc=kc, fc=fc, pbm=pbm, wa=wa: e.matmul(banks[pbm][:, 0:wa], wv[:, kc, cx + fc * 128: cx + (fc + 1) * 128], uT[:, kc, 0:wa],
                                                                   start=(kc == 0), stop=(kc == 7)),
                     reads=[U["b_uT"], b_wres], writes=[bk[pbm]])
            P.op(ACT, lambda e, raw=raw, pbm=pbm, wa=wa: e.copy(raw[:, 0:wa], banks[pbm][:, 0:wa]), reads=[bk[pbm]], writes=[braw])
            if W > 512:
                for kc in range(8):
                    P.op(PE, lambda e, kc=kc, fc=fc: e.matmul(banks[pb_small][:, 0:W - 512], wv[:, kc, cx + fc * 128: cx + (fc + 1) * 128], uT[:, kc, 512:W],
                                                             start=(kc == 0), stop=(kc == 7)),
                         reads=[U["b_uT"], b_wres], writes=[bk[pb_small]])
                P.op(ACT, lambda e, raw=raw: e.copy(raw[:, 512:W], banks[pb_small][:, 0:W - 512]), reads=[bk[pb_small]], writes=[braw])
            flush_silu()
            P.op(DVE, lambda e, raw=raw, acc=acc, fc=fc: e.tensor_scalar(acc[:, 0:T], raw[:, 0:T], tab[:, fc * 5:fc * 5 + 1], tab[:, 60 + fc:61 + fc], ALU.mult, ALU.add),
                 reads=[braw, S["b_tab"]], writes=[bacc])
            for jj in range(1, 5):
                P.op(DVE, lambda e, raw=raw, acc=acc, fc=fc, jj=jj: e.scalar_tensor_tensor(acc[:, 0:T], raw[:, jj:jj + T], tab[:, fc * 5 + jj:fc * 5 + jj + 1], acc[:, 0:T], ALU.mult, ALU.add),
                     reads=[braw, S["b_tab"], bacc], writes=[bacc])
            pend_silu.append((acc, bacc, fc))

        def flush_silu():
            while pend_silu:
                acc, bacc, fc = pend_silu.pop(0)
                P.op(ACT, lambda e: e.activation(S["xcT"][:, fc, 0:T], acc[:, 0:T], AF.Silu), reads=[bacc], writes=[S["b_xcT"]])

        def post():
            flush_silu()
            tpv = banks[pb_tp][:, :].bitcast(BF16)
            for k in range(nchk):
                for fc in range(8):
                    P.op(PE, lambda e, k=k, fc=fc: e.transpose(tpv[0:cs_, fc * 128:(fc + 1) * 128], S["xcT"][:, fc, 128 * k:128 * k + cs_], identb[:, :]),
                         reads=[S["b_xcT"], b_idb], writes=[bk[pb_tp]])
                P.op(ACT, lambda e, k=k: e.copy(S["xtm"][0:cs_, k, :], tpv[0:cs_, :]), reads=[bk[pb_tp]], writes=[S["b_xtm"]])
                for g in range(2):
                    P.op(PE, lambda e, k=k, g=g: e.transpose(tpv[0:cs_, g * 128:(g + 1) * 128], S["xcT"][:, 8 + g, 128 * k:128 * k + cs_], identb[:, :]),
                         reads=[S["b_xcT"], b_idb], writes=[bk[pb_tp]])
                P.op(ACT, lambda e, k=k: e.copy(S["btm"][0:cs_, k, :], tpv[0:cs_, 0:256]), reads=[bk[pb_tp]], writes=[S["b_btm"]])
        return pre, [(lambda fc=fc: one_fc(fc)) for fc in range(12)], post

    def ssd_prep(S, U, j, t, T, W, wv, cx, cdt, pb_main, pb_small, pb_tp):
        pre, fcs, post = ssd_thunks(S, U, j, t, T, W, wv, cx, cdt, pb_main, pb_small, pb_tp)
        pre()
        for f in fcs:
            f()
        post()

    b_wres = P.buf("wres")

    for j in range(NJ):
        nt = cfg.ntiles(j)
        nctx = cfg.jobs[j]
        with ExitStack() as st:
            mkS = P.mark()
            wS = sb(st, "wS", [128, 8, 4640], BF16)
            P.dma(SP, lambda e: e.dma_start(out=wS[:, :, 0:3072], in_=ws_in[:, :, 0:3072]), reads=[b_ws["in"]], writes=[b_wres], sbuf=b_wres)
            P.dma(SP, lambda e: e.dma_start(out=wS[:, :, 3072:4640], in_=ws_in[:, :, 4096:5664]), reads=[b_ws["in"]], writes=[b_wres], sbuf=b_wres)
            for m in range(2):
                P.dma(POOL, lambda e, m=m: e.dma_start(out=KT[:, m, 64:72, 0:cfg.L(j)], in_=kaug[j][:, :, :]), writes=[b_KT], sbuf=b_KT)
            U0 = make_uT(st, "s")
            U1 = dict(U0)
            U1["uT"] = sb(st, "suT1", [128, 8, 516], BF16)
            U1["b_uT"] = P.buf()
            Us = [U0, U1]
            S_0 = make_ssd(st, "s")
            S_1 = ssd_par(st, S_0, "s")
            Ss = [S_0, S_1]
            kst = [sb(st, f"kst{i}", [128, 512], BF16) for i in range(3)]
            b_kst = [P.buf() for _ in range(3)]
            vst = sb(st, "vst", [128, 8, 4, 129], BF16)
            b_vst = P.buf("vst")
            wgt = sb(st, "wgt", [128, 32])
            xwp = sb(st, "xwp", [128, 16, 64], BF16)
            xws = sb(st, "xws", [128, 16, 64], BF16)
            decp = sb(st, "decp", [128, 32])
            snap = [sb(st, f"snap{i}", [128, D], BF16) for i in range(2)]
            b_wgt, b_xwp, b_xws, b_decp = P.buf(), P.buf(), P.buf(), P.buf()
            b_snap = [P.buf() for _ in range(2)]
            P.op(POOL, lambda e: e.memset(vst[:], 1.0), writes=[b_vst])
            P.op(POOL, lambda e: e.memset(CF[:], 0.0), writes=[b_CF])
            P.op(POOL, lambda e: e.memset(CB[:], 0.0), writes=[b_CB])
            kn = [0]
            sn = [0]

            def make_states(S, t, own, oi, cs_, nchk):
                def one_chunk(k):
                    jj = oi * 4 + k
                    dt, cs, tot = S["dt"], S["cs"], S["tot"]
                    nw = 32 if own else 16
                    xt3 = S["xtm"][0:cs_, k, :].rearrange("p (h d) -> p h d", h=16)
                    P.op(DVE, lambda e: e.tensor_tensor(wgt[0:cs_, 0:16], tot[0:cs_, k, 0:16], cs[0:cs_, k, 0:16], ALU.subtract), reads=[S["b_dtq"]], writes=[b_wgt])
                    if own:
                        P.op(DVE, lambda e: e.tensor_tensor(wgt[0:cs_, 16:32], cs[0:cs_, k, 16:32], S["adt"][0:cs_, k, 16:32], ALU.subtract), reads=[S["b_dtq"], b_wgt], writes=[b_wgt])
                    yield
                    P.op(ACT, lambda e: e.activation(wgt[0:cs_, 0:nw], wgt[0:cs_, 0:nw], AF.Exp), reads=[b_wgt], writes=[b_wgt])
                    P.op(ACT, lambda e: e.activation(decp[:, :], tot[:, k, :], AF.Exp), reads=[S["b_dtq"]], writes=[b_decp])
                    yield
                    P.op(DVE, lambda e: e.tensor_tensor(wgt[0:cs_, 0:nw], wgt[0:cs_, 0:nw], dt[0:cs_, k, 0:nw], ALU.mult), reads=[b_wgt, S["b_dtq"]], writes=[b_wgt])
                    P.op(DVE, lambda e: e.tensor_tensor(xwp[0:cs_, :, :], xt3, wgt[0:cs_, 0:16].unsqueeze(2).to_broadcast([cs_, 16, 64]), ALU.mult),
                         reads=[S["b_xtm"], b_wgt], writes=[b_xwp])
                    if own:
                        P.op(DVE, lambda e: e.tensor_tensor(xws[0:cs_, :, :], xt3, wgt[0:cs_, 16:32].unsqueeze(2).to_broadcast([cs_, 16, 64]), ALU.mult),
                             reads=[S["b_xtm"], b_wgt], writes=[b_xws])
                    yield
                    for g in range(2):
                        P.op(PE, lambda e, g=g: e.matmul(banks[6 + g][:, :], S["btm"][0:cs_, k, g * 128:(g + 1) * 128], xwp[0:cs_, g * 8:(g + 1) * 8, :].rearrange("p h d -> p (h d)"),
                                                         start=True, stop=True),
                             reads=[S["b_btm"], b_xwp], writes=[bk[6 + g]])
                    if not own:
                        carry, bcar = CB, b_CB
                    else:
                        carry, bcar = CF, b_CF
                        i = sn[0] % 2
                        sn[0] += 1
                        P.op(ACT, lambda e: e.copy(snap[i][:], CF[:]), reads=[b_CF], writes=[b_snap[i]])
                        P.dma(POOL, lambda e: e.dma_start(out=PFB[0, jj, :, :], in_=snap[i][:]), reads=[b_snap[i]], writes=[b_PFB], sbuf=b_snap[i])
                    yield
                    c3 = carry[:].rearrange("p (h d) -> p h d", h=16)
                    P.op(DVE, lambda e: e.tensor_tensor(c3, c3, decp[:, 0:16].unsqueeze(2).to_broadcast([128, 16, 64]), ALU.mult), reads=[bcar, b_decp], writes=[bcar])
                    for g in range(2):
                        P.op(DVE, lambda e, g=g: e.tensor_tensor(carry[:, g * 512:(g + 1) * 512], carry[:, g * 512:(g + 1) * 512], banks[6 + g][:, :], ALU.add),
                             reads=[bcar, bk[6 + g]], writes=[bcar])
                    if own:
                        P.op(POOL, lambda e: e.tensor_copy(decs[:, jj, :], decp[:, 16:32]), reads=[b_decp], writes=[b_decs])
                    yield
                    if own:
                        i2 = sn[0] % 2
                        sn[0] += 1
                        for g in range(2):
                            P.op(PE, lambda e, g=g: e.matmul(banks[6 + g][:, :], S["btm"][0:cs_, k, g * 128:(g + 1) * 128], xws[0:cs_, g * 8:(g + 1) * 8, :].rearrange("p h d -> p (h d)"),
                                                             start=True, stop=True),
                                 reads=[S["b_btm"], b_xws], writes=[bk[6 + g]])
                            P.op(ACT, lambda e, g=g: e.copy(snap[i2][:, g * 512:(g + 1) * 512], banks[6 + g][:, :]), reads=[bk[6 + g]], writes=[b_snap[i2]])
                        P.dma(POOL, lambda e: e.dma_start(out=SBS[jj, :, :], in_=snap[i2][:]), reads=[b_snap[i2]], writes=[b_SBS], sbuf=b_snap[i2])
                    yield

                def stages(k):
                    gen = one_chunk(k)
                    return [(lambda: next(gen, None)) for _ in range(6)]

                def tile_end():
                    if not own:
                        P.op(DVE, lambda e: e.scalar_tensor_tensor(CF[:], CB[:], S["tab"][:, 138:139], CF[:], ALU.mult, ALU.add), reads=[b_CB, b_CF, S["b_tab"]], writes=[b_CF])
                        P.op(DVE, lambda e: e.tensor_scalar(CB[:], CB[:], S["tab"][:, 139:140], None, ALU.mult), reads=[b_CB, S["b_tab"]], writes=[b_CB])
                    return None
                out = []
                for k in range(nchk):
                    out += stages(k)
                return out + [tile_end]

            pend_states = []
            for t in range(nt):
                T = 16 if t == 0 else 512
                W = T + 4
                own = t > nctx
                oi = t - nctx - 1
                soff = 0 if t == 0 else 16 + 512 * (t - 1)
                ch0 = 0 if t == 0 else 1 + 4 * (t - 1)
                nchk = max(1, T // 128)
                cs_ = min(T, 128)
                U = Us[t % 2]
                S = Ss[t % 2]
                if t == 0:
                    build_uT(U, xw[j][0], W, w1T, 0)
                uT = U["uT"]
                side = []
                if t + 1 < nt:
                    th = uT_thunks(Us[(t + 1) % 2], xw[j][t + 1], 516, w1T, 0)
                    side.append(th[0][0])
                    for q_ in range(1, len(th)):
                        side.append(th[q_][0])
                        side.append(th[q_ - 1][1])
                    side.append(th[-1][1])
                if pend_states:
                    merged = []
                    ps_ = list(pend_states)
                    while side or ps_:
                        if side:
                            merged.append(side.pop(0))
                        if side:
                            merged.append(side.pop(0))
                        if ps_:
                            merged.append(ps_.pop(0))
                    side = merged
                    pend_states = []
                pre, fcs, post = ssd_thunks(S, U, j, t, T, W, wS, 3072, 4608, [1, 2], 3, 4)
                main = [pre] + fcs

                def kq_group(isq, h, T=T, uT=uT, U=U, soff=soff, oi=oi):
                    cbase = (0 if isq else 1024) + h * 128
                    pb = 1 + (kn[0] % 2)
                    i = kn[0] % 3
                    kn[0] += 1
                    for kc in range(8):
                        P.op(PE, lambda e, kc=kc: e.matmul(banks[pb][:, 0:T], wS[:, kc, cbase:cbase + 128], uT[:, kc, 2:2 + T], start=(kc == 0), stop=(kc == 7)),
                             reads=[U["b_uT"], b_wres], writes=[bk[pb]])
                    if isq:
                        P.op(ACT, lambda e: e.activation(kst[i][:, 0:T], banks[pb][:, 0:T], AF.Copy, scale=0.125), reads=[bk[pb]], writes=[b_kst[i]])
                        for m in range(2):
                            P.dma(POOL, lambda e, m=m: e.dma_start(out=QT[h, m, :, oi * 512:oi * 512 + T], in_=kst[i][64 * m:64 * m + 64, 0:T]),
                                  reads=[b_kst[i]], writes=[b_QT], sbuf=b_kst[i])
                    else:
                        P.op(ACT, lambda e: e.copy(kst[i][:, 0:T], banks[pb][:, 0:T]), reads=[bk[pb]], writes=[b_kst[i]])
                        for m in range(2):
                            P.dma(POOL, lambda e, m=m: e.dma_start(out=KT[h, m, 0:64, soff:soff + T], in_=kst[i][64 * m:64 * m + 64, 0:T]),
                                  reads=[b_kst[i]], writes=[b_KT], sbuf=b_kst[i])

                for isq in ([False, True] if own else [False]):
                    for h in range(NH):
                        main.append(lambda isq=isq, h=h: kq_group(isq, h))

                def v_group(k, half, uT=uT, U=U, cs_=cs_):
                    pb = 1 + (kn[0] % 2)
                    kn[0] += 1
                    for kc in range(8):
                        P.op(PE, lambda e, kc=kc: e.matmul(banks[pb][0:cs_, :], uT[:, kc, 2 + 128 * k:2 + 128 * k + cs_], wS[:, kc, 2048 + half * 512:2048 + (half + 1) * 512],
                                                           start=(kc == 0), stop=(kc == 7)),
                             reads=[U["b_uT"], b_wres], writes=[bk[pb]])
                    P.op(ACT, lambda e: e.copy(vst[0:cs_, half * 4:(half + 1) * 4, k, 0:128], banks[pb][0:cs_, :].rearrange("p (h e) -> p h e", h=4)),
                         reads=[bk[pb]], writes=[b_vst])

                for k in range(nchk):
                    for half in range(2):
                        main.append(lambda k=k, half=half: v_group(k, half))

                def v_store(t=t, ch0=ch0):
                    if t == 0:
                        P.dma(POOL, lambda e: e.dma_start(out=VA[:, 0:16, 0, :].rearrange("h p e -> p h e"), in_=vst[0:16, :, 0, :]),
                              reads=[b_vst], writes=[b_VA], sbuf=b_vst)
                    else:
                        P.dma(POOL, lambda e: e.dma_start(out=VA[:, :, ch0:ch0 + 4, :].rearrange("h p c e -> p h c e"), in_=vst[:, :, :, :]),
                              reads=[b_vst], writes=[b_VA], sbuf=b_vst)
                main.append(v_store)
                stride = max(1, len(main) // (len(side) + 1)) if side else len(main)
                si_ = 0
                for mi, f in enumerate(main):
                    f()
                    if side and (mi + 1) % stride == 0 and si_ < len(side):
                        side[si_]()
                        si_ += 1
                while si_ < len(side):
                    side[si_]()
                    si_ += 1
                post()
                pend_states = make_states(S, t, own, oi, cs_, nchk)
            for f in pend_states:
                f()
            pend_states = []
            ldb = [sb(st, f"ldb{i}", [128, D], BF16) for i in range(2)]
            b_ldb = [P.buf() for _ in range(2)]
            for jj in range(4 * NOWN - 1, -1, -1):
                i = sn[0] % 2
                sn[0] += 1
                P.op(ACT, lambda e, i=i: e.copy(snap[i][:], CB[:]), reads=[b_CB], writes=[b_snap[i]])
                P.dma(POOL, lambda e, i=i, jj=jj: e.dma_start(out=PFB[1, jj, :, :], in_=snap[i][:]), reads=[b_snap[i]], writes=[b_PFB], sbuf=b_snap[i])
                P.dma(SP, lambda e, i=i, jj=jj: e.dma_start(out=ldb[i][:], in_=SBS[jj, :, :]), reads=[b_SBS], writes=[b_ldb[i]], sbuf=b_ldb[i])
                c3 = CB[:].rearrange("p (h d) -> p h d", h=16)
                P.op(DVE, lambda e, c3=c3, jj=jj: e.tensor_tensor(c3, c3, decs[:, jj, :].unsqueeze(2).to_broadcast([128, 16, 64]), ALU.mult), reads=[b_CB, b_decs], writes=[b_CB])
                P.op(DVE, lambda e, i=i: e.tensor_tensor(CB[:], CB[:], ldb[i][:], ALU.add), reads=[b_CB, b_ldb[i]], writes=[b_CB])
            P.barrier()
            P.release(mkS)

        with ExitStack() as stT:
            mkT = P.mark()
            wO = sb(stT, "wO", [128, 8, 2592], BF16)
            P.dma(SP, lambda e: e.dma_start(out=wO[:, :, 0:1024], in_=ws_in[:, :, 3072:4096]), reads=[b_ws["in"]], writes=[b_wres], sbuf=b_wres)
            P.dma(SP, lambda e: e.dma_start(out=wO[:, :, 1024:2592], in_=ws_in[:, :, 4096:5664]), reads=[b_ws["in"]], writes=[b_wres], sbuf=b_wres)
            mixT = sb(stT, "mixT", [128, 16, 512], BF16)
            b_mix = P.buf("mixT")
            wpn = [0]
            nk_chunks = cfg.nch(j)
            for oi in range(NOWN):
                t_own = nctx + 1 + oi
                with ExitStack() as st:
                    mkA = P.mark()
                    qv = [[sb(st, f"qv{r}_{v}", [72, 2, 512], BF16) for v in range(3)] for r in range(2)]
                    b_qv = [P.buf() for _ in range(2)]
                    NKV = 4
                    kp = [sb(st, f"kp{i}", [72, 2, 2048], BF16) for i in range(NKV)]
                    vp = [sb(st, f"vp{i}", [128, 16, 129], BF16) for i in range(NKV)]
                    b_kv = [P.buf() for _ in range(NKV)]
                    tmpb = [sb(st, f"tmpb{i}", [128, 1024]) for i in range(2)]
                    b_tmpb = [P.buf() for _ in range(2)]
                    pt = [sb(st, f"pt{i}", [128, 1024], BF16) for i in range(3)]
                    b_pt = [P.buf() for _ in range(3)]
                    ob = sb(st, "ob", [128, 8, 129])
                    rl = sb(st, "rl", [128, 8])
                    o1 = sb(st, "o1", [128, 4, 128])
                    o2 = sb(st, "o2", [128, 4, 128])
                    ssq = sb(st, "ssq", [128, 4])
                    junk = sb(st, "junk", [128, 128], BF16)
                    junkf = sb(st, "junkf", [128, 128])
                    attb = sb(st, "attb", [128, 4, 128], BF16)
                    b_ob, b_rl, b_o1, b_o2, b_ssq, b_junk, b_attb = (P.buf() for _ in range(7))
                    pn = [0]
                    sbn = [0]
                    pieces = [(0, 1)] + [(c0, min(16, nk_chunks - c0)) for c0 in range(1, nk_chunks, 16)]
                    acc_v = [banks[4 + a // 3][:, (a % 3) * 129:(a % 3) * 129 + 129] for a in range(8)]
                    epi_gen = [None]
                    for h in range(NH):
                        slope = 2.0 ** (-(h + 1))
                        r = h % 2
                        for v in range(3):
                            P.dma(SP, lambda e, r=r, v=v, h=h: e.dma_start(out=qv[r][v][0:64, :, :], in_=QT[h, :, :, oi * 512:(oi + 1) * 512].rearrange("m d q -> d m q")),
                                  reads=[b_QT], writes=[b_qv[r]], sbuf=b_qv[r])
                            for m in range(2):
                                P.dma(SP, lambda e, r=r, v=v, h=h, m=m: e.dma_start(out=qv[r][v][64:72, m, :], in_=qaug[j][v, h, :, oi * 512:(oi + 1) * 512]),
                                      writes=[b_qv[r]], sbuf=b_qv[r])
                        first_in_bank = {4: True, 5: True, 6: True}
                        steps = []
                        for (c0, ncn) in pieces:
                            for ci in range(ncn):
                                steps.append((c0, ncn, ci))
                        stinfo = {}

                        def emit_qk(si, h=h, r=r, slope=slope):
                            c0, ncn, ci = steps[si]
                            if ci == 0:
                                pi = pn[0] % NKV
                                pn[0] += 1
                                koff0 = 0 if c0 == 0 else 16 + 128 * (c0 - 1)
                                klen = 16 if c0 == 0 else 128 * ncn
                                P.dma(SP, lambda e: e.dma_start(out=kp[pi][:, :, 0:klen], in_=KT[h, :, :, koff0:koff0 + klen].rearrange("m r l -> r m l")),
                                      reads=[b_KT], writes=[b_kv[pi]], sbuf=b_kv[pi])
                                if c0 == 0:
                                    P.dma(SP, lambda e: e.dma_start(out=vp[pi][0:16, 0, :], in_=VA[h, 0:16, 0, :]), reads=[b_VA], writes=[b_kv[pi]], sbuf=b_kv[pi])
                                else:
                                    P.dma(SP, lambda e: e.dma_start(out=vp[pi][:, 0:ncn, :], in_=VA[h, :, c0:c0 + ncn, :]), reads=[b_VA], writes=[b_kv[pi]], sbuf=b_kv[pi])
                                stinfo["pi"] = pi
                            pi = stinfo["pi"]
                            c = c0 + ci
                            ks = 16 if c == 0 else 128
                            kt = 0 if c == 0 else 1 + (c - 1) // 4
                            kk = (c - 1) % 4
                            if kt <= nctx:
                                var, KR, ovl = 0, 72, False
                            elif kt < t_own:
                                var, KR, ovl = 1, 72, False
                            elif kt > t_own:
                                var, KR, ovl = 2, 72, False
                            else:
                                var, KR, ovl = 0, 64, True
                            sbi = sbn[0] % 2
                            pti = sbn[0] % 3
                            sbn[0] += 1
                            Bs = (2 * sbi, 2 * sbi + 1)
                            for m in range(2):
                                P.op(PE, lambda e, m=m: e.matmul(banks[Bs[m]][0:ks, :], kp[pi][0:KR, m, 128 * ci:128 * ci + ks], qv[r][var][0:KR, m, :], start=True, stop=True),
                                     reads=[b_kv[pi], b_qv[r]], writes=[bk[Bs[m]]])
                            if ovl:
                                tb = tmpb[sbi]
                                for m in range(2):
                                    P.op(DVE, lambda e, m=m: e.scalar_tensor_tensor(tb[:, m * 512:(m + 1) * 512], absd[kk], -slope, banks[Bs[m]][:, :], ALU.mult, ALU.add),
                                         reads=[b_cst, bk[Bs[m]]], writes=[b_tmpb[sbi]])
                                P.op(ACT, lambda e: e.activation(pt[pti][:, :], tb[:, :], AF.Exp), reads=[b_tmpb[sbi]], writes=[b_pt[pti]])
                            else:
                                P.op(ACT, lambda e: e.activation(pt[pti][0:ks, :], allbanks[0:ks, 512 * Bs[0]:512 * Bs[0] + 1024], AF.Exp),
                                     reads=[bk[Bs[0]], bk[Bs[1]]], writes=[b_pt[pti]])
                            return (pi, ci, ks, pti)

                        def emit_pv(info):
                            pi, ci, ks, pti = info
                            for a in range(8):
                                m, qc = a // 4, a % 4
                                bkn = 4 + a // 3
                                stt = first_in_bank[bkn]
                                first_in_bank[bkn] = False
                                P.op(PE, lambda e, a=a, m=m, qc=qc, stt=stt: e.matmul(acc_v[a], pt[pti][0:ks, m * 512 + qc * 128:m * 512 + (qc + 1) * 128], vp[pi][0:ks, ci, :],
                                                                                 start=stt, stop=False, skip_group_check=True),
                                     reads=[b_pt[pti], b_kv[pi]], writes=[bk[bkn]])

                        prev = None
                        for si in range(len(steps)):
                            cur = emit_qk(si)
                            if prev is not None:
                                emit_pv(prev)
                            prev = cur
                            if epi_gen[0] is not None and si % 2 == 1:
                                if next(epi_gen[0], "done") == "done":
                                    epi_gen[0] = None
                        emit_pv(prev)
                        while epi_gen[0] is not None:
                            if next(epi_gen[0], "done") == "done":
                                epi_gen[0] = None
                        for bi in range(3):
                            na = 3 if bi < 2 else 2
                            P.op(DVE, lambda e, bi=bi, na=na: e.tensor_copy(ob[:, bi * 3:bi * 3 + na, :], banks[4 + bi][:, 0:129 * na].rearrange("p (a c) -> p a c", c=129)),
                                 reads=[bk[4 + bi]], writes=[b_ob])

                        def epilogue(h=h):
                            P.op(DVE, lambda e: e.reciprocal(rl[:, :].unsqueeze(2), ob[:, :, 128:129]), reads=[b_ob], writes=[b_rl])
                            P.op(DVE, lambda e: e.tensor_tensor(o1[:, :, :], ob[:, 0:4, 0:128], rl[:, 0:4].unsqueeze(2).to_broadcast([128, 4, 128]), ALU.mult), reads=[b_ob, b_rl], writes=[b_o1])
                            P.op(DVE, lambda e: e.tensor_tensor(o2[:, :, :], ob[:, 4:8, 0:128], rl[:, 4:8].unsqueeze(2).to_broadcast([128, 4, 128]), ALU.mult), reads=[b_ob, b_rl], writes=[b_o2])
                            P.op(DVE, lambda e: e.scalar_tensor_tensor(o1[:, :, :], o2[:, :, :], neglam, o1[:, :, :], ALU.mult, ALU.add), reads=[b_o1, b_o2, b_lam], writes=[b_o1])
                            yield
                            for qc in range(4):
                                P.op(DVE, lambda e, qc=qc: e.scalar_tensor_tensor(junkf[:, :], o1[:, qc, :], 1.0, o1[:, qc, :], ALU.mult, ALU.mult, accum_out=ssq[:, qc:qc + 1]),
                                     reads=[b_o1], writes=[b_junk, b_ssq])
                            yield
                            P.op(ACT, lambda e: e.activation(ssq[:, :], ssq[:, :], AF.Ln, scale=1.0 / 128, bias=EPS), reads=[b_ssq], writes=[b_ssq])
                            P.op(ACT, lambda e: e.activation(ssq[:, :], ssq[:, :], AF.Exp, scale=-0.5), reads=[b_ssq], writes=[b_ssq])
                            yield
                            P.op(DVE, lambda e: e.tensor_tensor(o1[:, :, :], o1[:, :, :], ssq[:, :].unsqueeze(2).to_broadcast([128, 4, 128]), ALU.mult), reads=[b_o1, b_ssq], writes=[b_o1])
                            P.op(DVE, lambda e: e.tensor_tensor(attb[:, :, :], o1[:, :, :], attw.unsqueeze(1).to_broadcast([128, 4, 128]), ALU.mult), reads=[b_o1, b_gt], writes=[b_attb])
                            yield
                            tpv = banks[7][:, :].bitcast(BF16)
                            for qc in range(4):
                                P.op(PE, lambda e, qc=qc: e.transpose(tpv[:, qc * 128:(qc + 1) * 128], attb[:, qc, :], identb[:, :]), reads=[b_attb, b_idb], writes=[bk[7]])
                            yield
                            P.op(DVE, lambda e: e.tensor_copy(mixT[:, h, :], banks[7][:, :].bitcast(BF16)[:, 0:512]), reads=[bk[7]], writes=[b_mix])
                            yield

                        epi_gen[0] = epilogue()
                    while epi_gen[0] is not None:
                        if next(epi_gen[0], "done") == "done":
                            epi_gen[0] = None
                    P.barrier()
                    P.release(mkA)
                with ExitStack() as st:
                    mkO = P.mark()
                    U = make_uT(st, "o")
                    S = make_ssd(st, "o")
                    sz = sb(st, "sz", [128, 4, D], BF16)
                    b_sz = P.buf()
                    pfb = [sb(st, f"pfb{i}", [128, 2, D], BF16) for i in range(2)]
                    b_pfb = [P.buf() for _ in range(2)]
                    gtm2 = [sb(st, f"gtm{i}", [128, 4, 128]) for i in range(2)]
                    b_gtm2 = [P.buf() for _ in range(2)]
                    ef = [sb(st, f"ef{i}", [128, 128]) for i in range(6)]
                    b_ef = [P.buf() for _ in range(6)]
                    mt = [sb(st, f"mt{i}", [128, 128], BF16) for i in range(6)]
                    b_mt = [P.buf() for _ in range(6)]
                    xdt2 = [sb(st, f"xdt{i}", [128, 2, 16, 64], BF16) for i in range(2)]
                    b_xdt2 = [P.buf() for _ in range(2)]
                    bia2 = [sb(st, f"bia{i}", [128, 64]) for i in range(2)]
                    b_bia2 = [P.buf() for _ in range(2)]
                    yt_2 = [sb(st, f"yt{i}", [128, D]) for i in range(2)]
                    yt2_2 = [sb(st, f"ytt{i}", [128, D]) for i in range(2)]
                    b_yt_2 = [P.buf() for _ in range(2)]
                    b_yt2_2 = [P.buf() for _ in range(2)]
                    ssb2 = [sb(st, f"ssb{i}", [128, D], BF16) for i in range(2)]
                    b_ssb2 = [P.buf() for _ in range(2)]
                    sso = sb(st, "sso", [128, 4])
                    b_sso4 = [P.buf() for _ in range(4)]
                    jq = sb(st, "jq", [128, D], BF16)
                    b_jq = P.buf()
                    ahl = sb(st, "ahl", [128, 2, 32], BF16)
                    ahf = sb(st, "ahf", [128, 32])
                    b_ahl, b_ahf = P.buf(), P.buf()
                    SEGB = (0, 1, 2, 3, 5)
                    segq = [banks[bq][:, 0:128] for bq in SEGB]
                    b_segq = [bk[bq] for bq in SEGB]
                    build_uT(U, xw[j][t_own], 516, w1T, 0)
                    uT = U["uT"]
                    pre_, fcs_, post_ = ssd_thunks(S, U, j, t_own, 512, 516, wO, 1024, 2560, [1], 2, 3)

                    def z_group(k, half):
                        zb = 4 + (k * 2 + half) % 2
                        for kc in range(8):
                            P.op(PE, lambda e, kc=kc: e.matmul(banks[zb][:, :], uT[:, kc, 2 + 128 * k:130 + 128 * k], wO[:, kc, half * 512:(half + 1) * 512], start=(kc == 0), stop=(kc == 7)),
                                 reads=[U["b_uT"], b_wres], writes=[bk[zb]])
                        P.op(ACT, lambda e: e.activation(sz[:, k, half * 512:(half + 1) * 512], banks[zb][:, :], AF.Silu), reads=[bk[zb]], writes=[b_sz])

                    pre_()
                    for fc in range(12):
                        fcs_[fc]()
                        if fc < 8:
                            z_group(fc // 2, fc % 2)
                    post_()
                    en = [0]
                    pend = {"front": None, "back": None}
                    for k in range(4):
                        jj = oi * 4 + k
                        pi = k % 2
                        gtm, b_gtm, xdt, b_xdt, bia, b_bia = gtm2[pi], b_gtm2[pi], xdt2[pi], b_xdt2[pi], bia2[pi], b_bia2[pi]
                        yt, yt2, b_yt, b_yt2, ssb, b_ssb = yt_2[pi], yt2_2[pi], b_yt_2[pi], b_yt2_2[pi], ssb2[pi], b_ssb2[pi]
                        for d_ in range(2):
                            P.dma(SP, lambda e, pi=pi, d_=d_, jj=jj: e.dma_start(out=pfb[pi][:, d_, :], in_=PFB[d_, jj, :, :]), reads=[b_PFB], writes=[b_pfb[pi]], sbuf=b_pfb[pi])
                        dt, adt, cs, tot = S["dt"], S["adt"], S["cs"], S["tot"]
                        P.op(DVE, lambda e, k=k: e.tensor_scalar(bia[:, 0:16], cs[:, k, 0:16], -1.0, None, ALU.mult), reads=[S["b_dtq"]], writes=[b_bia])
                        P.op(DVE, lambda e, k=k: e.tensor_tensor(bia[:, 16:32], cs[:, k, 16:32], adt[:, k, 16:32], ALU.subtract), reads=[S["b_dtq"]], writes=[b_bia])
                        P.op(DVE, lambda e, k=k: e.tensor_tensor(bia[:, 48:64], tot[:, k, 16:32], bia[:, 16:32], ALU.subtract), reads=[S["b_dtq"], b_bia], writes=[b_bia])
                        P.op(ACT, lambda e, k=k: e.activation(bia[:, 32:48], cs[:, k, 0:16], AF.Exp), reads=[S["b_dtq"]], writes=[b_bia])
                        P.op(ACT, lambda e: e.activation(bia[:, 48:64], bia[:, 48:64], AF.Exp), reads=[b_bia], writes=[b_bia])
                        P.op(DVE, lambda e, k=k: e.tensor_copy(ahl[:, 0, :], adt[:, k, :]), reads=[S["b_dtq"]], writes=[b_ahl])
                        P.op(DVE, lambda e: e.tensor_copy(ahf[:, :], ahl[:, 0, :]), reads=[b_ahl], writes=[b_ahf])
                        P.op(DVE, lambda e, k=k: e.tensor_tensor(ahl[:, 1, :], adt[:, k, :], ahf[:, :], ALU.subtract), reads=[S["b_dtq"], b_ahf, b_ahl], writes=[b_ahl])
                        xt3 = S["xtm"][:, k, :].rearrange("p (h d) -> p h d", h=16)
                        for d_ in range(2):
                            P.op(POOL if d_ else DVE, lambda e, d_=d_, k=k, xt3=xt3: e.tensor_tensor(xdt[:, d_, :, :], xt3, dt[:, k, 16 * d_:16 * d_ + 16].unsqueeze(2).to_broadcast([128, 16, 64]), ALU.mult),
                                 reads=[S["b_xtm"], S["b_dtq"]], writes=[b_xdt])
                        for g in range(2):
                            P.op(PE, lambda e, g=g, k=k: e.matmul(banks[4][:, g * 128:(g + 1) * 128], S["xcT"][:, 8 + g, 128 * k:128 * k + 128], S["xcT"][:, 10 + g, 128 * k:128 * k + 128], start=True, stop=True),
                                 reads=[S["b_xcT"]], writes=[bk[4]])
                        for g in range(2):
                            P.op(DVE, lambda e, g=g: e.tensor_tensor(gtm[:, 2 * g, :], banks[4][:, g * 128:(g + 1) * 128], maskf, ALU.mult), reads=[bk[4], b_cst], writes=[b_gtm])
                            P.op(DVE, lambda e, g=g: e.tensor_tensor(gtm[:, 2 * g + 1, :], banks[4][:, g * 128:(g + 1) * 128], maskb, ALU.mult), reads=[bk[4], b_cst], writes=[b_gtm])
                        for g in range(2):
                            P.op(PE, lambda e, g=g, k=k, pi=pi: e.matmul(banks[g][:, :], S["xcT"][:, 10 + g, 128 * k:128 * k + 128], pfb[pi][:, 0, g * 512:(g + 1) * 512], start=True, stop=True),
                                 reads=[S["b_xcT"], b_pfb[pi]], writes=[bk[g]])
                        for g in range(2):
                            sl = slice(g * 512, (g + 1) * 512)
                            y3 = yt[:, sl].rearrange("p (h d) -> p h d", h=8)
                            P.op(DVE, lambda e, g=g, y3=y3: e.tensor_tensor(y3, banks[g][:, :].rearrange("p (h d) -> p h d", h=8), bia[:, 32 + 8 * g:40 + 8 * g].unsqueeze(2).to_broadcast([128, 8, 64]), ALU.mult),
                                 reads=[bk[g], b_bia], writes=[b_yt])
                        for g in range(2):
                            P.op(PE, lambda e, g=g, k=k, pi=pi: e.matmul(banks[g][:, :], S["xcT"][:, 10 + g, 128 * k:128 * k + 128], pfb[pi][:, 1, g * 512:(g + 1) * 512], start=True, stop=True),
                                 reads=[S["b_xcT"], b_pfb[pi]], writes=[bk[g]])
                        for g in range(2):
                            sl = slice(g * 512, (g + 1) * 512)
                            y23 = yt2[:, sl].rearrange("p (h d) -> p h d", h=8)
                            P.op(DVE, lambda e, g=g, y23=y23: e.tensor_tensor(y23, banks[g][:, :].rearrange("p (h d) -> p h d", h=8), bia[:, 48 + 8 * g:56 + 8 * g].unsqueeze(2).to_broadcast([128, 8, 64]), ALU.mult),
                                 reads=[bk[g], b_bia], writes=[b_yt2])
                        P.op(POOL, lambda e: e.tensor_tensor(yt[:, :], yt[:, :], yt2[:, :], ALU.add), reads=[b_yt, b_yt2], writes=[b_yt])
                        items = [(h, d_) for h in range(16) for d_ in range(2)]

                        def front(ii, k=k):
                            h, d_ = items[ii]
                            g = h // 8
                            q = ii % 6
                            rhs = umb[:, 0, :] if d_ == 0 else umb[:, 1, :]
                            sq_ = ii % 5
                            for hl in range(2):
                                P.op(PE, lambda e, hl=hl: e.matmul(segq[sq_], ahl[:, hl, 16 * d_ + h:16 * d_ + h + 1].to_broadcast([128, 128]), rhs, start=(hl == 0), stop=(hl == 1), skip_group_check=True),
                                     reads=[b_ahl, b_cstb], writes=[b_segq[sq_]])
                            P.op(ACT, lambda e: e.activation(ef[q][:, :], segq[sq_], AF.Exp, bias=bia[:, 16 * d_ + h:16 * d_ + h + 1]), reads=[b_segq[sq_], b_bia], writes=[b_ef[q]])
                            P.op(DVE, lambda e: e.scalar_tensor_tensor(mt[q][:, :], ef[q][:, :], 1.0, gtm[:, 2 * g + d_, :], ALU.min, ALU.mult),
                                 reads=[b_ef[q], b_gtm], writes=[b_mt[q]])

                        def back(ii):
                            h, d_ = items[ii]
                            q = ii % 6
                            P.op(PE, lambda e: e.matmul(banks[6 + h // 8][:, (h % 8) * 64:(h % 8) * 64 + 64], mt[q][:, :], xdt[:, d_, h, :], start=(d_ == 0), stop=(d_ == 1), skip_group_check=True),
                                 reads=[b_mt[q], b_xdt], writes=[bk[6 + h // 8]])

                        LA = 4
                        for ii in range(LA):
                            front(ii)
                        for ii in range(32):
                            if ii + LA < 32:
                                front(ii + LA)
                            back(ii)
                            if ii == 8 and pend["front"] is not None:
                                pend["front"]()
                                pend["front"] = None
                        if pend["back"] is not None:
                            pend["back"]()
                            pend["back"] = None
                        for g in range(2):
                            sl = slice(g * 512, (g + 1) * 512)
                            P.op(DVE, lambda e, g=g, sl=sl: e.tensor_tensor(yt[:, sl], yt[:, sl], banks[6 + g][:, :], ALU.add), reads=[b_yt, bk[6 + g]], writes=[b_yt])

                        def tail_front(k=k, yt=yt, yt2=yt2, b_yt=b_yt, b_yt2=b_yt2, ssb=ssb, b_ssb=b_ssb, xt3=xt3):
                            P.op(POOL, lambda e: e.tensor_tensor(yt2[:, :].rearrange("p (h d) -> p h d", h=16), xt3, Dbc.unsqueeze(2).to_broadcast([128, 16, 64]), ALU.mult),
                                 reads=[S["b_xtm"], b_gt, b_yt2], writes=[b_yt2])
                            P.op(POOL, lambda e: e.tensor_tensor(yt[:, :], yt[:, :], yt2[:, :], ALU.add), reads=[b_yt, b_yt2], writes=[b_yt])
                            P.op(DVE, lambda e: e.tensor_tensor(yt[:, :], yt[:, :], sz[:, k, :], ALU.mult), reads=[b_yt, b_sz], writes=[b_yt])
                            rms_rstd(ACT, yt[:, :], 128, D, jq[:, :], b_jq, sso[:, k:k + 1], b_sso4[k], [b_yt])
                            P.op(DVE, lambda e: e.scalar_tensor_tensor(ssb[:, :], yt[:, :], sso[:, k:k + 1], ssmw, ALU.mult, ALU.mult), reads=[b_yt, b_sso4[k], b_gt], writes=[b_ssb])

                        def tail_back(k=k, ssb=ssb, b_ssb=b_ssb):
                            tpv = banks[4][:, :].bitcast(BF16)
                            for fc in range(8):
                                P.op(PE, lambda e, fc=fc: e.transpose(tpv[:, fc * 128:(fc + 1) * 128], ssb[:, fc * 128:(fc + 1) * 128], identb[:, :]), reads=[b_ssb, b_idb], writes=[bk[4]])
                            P.op(ACT, lambda e: e.copy(mixT[:, 8:16, 128 * k:128 * k + 128], tpv[:, :].rearrange("p (f t) -> p f t", f=8)), reads=[bk[4]], writes=[b_mix])

                        pend["front"], pend["back"] = tail_front, tail_back
                    pend["front"]()
                    pend["back"]()
                    P.barrier()
                    P.release(mkO)
                with ExitStack() as st:
                    mkM = P.mark()
                    NWP = 6
                    wpool = [sb(st, f"wp{i}", [128, 4096], BF16) for i in range(NWP)]
                    b_wp = [P.buf() for _ in range(NWP)]
                    h1 = sb(st, "h1", [128, 4, D])
                    b_h1 = [P.buf() for _ in range(4)]
                    u2b = [sb(st, f"u2b{i}", [128, D], BF16) for i in range(2)]
                    b_u2b = [P.buf() for _ in range(2)]
                    u2T = sb(st, "u2T", [128, 8, 512], BF16)
                    b_u2T = P.buf()
                    rT = [sb(st, f"rT{i}", [128, 512], BF16) for i in range(2)]
                    b_rT = [P.buf() for _ in range(2)]
                    aT = [sb(st, f"aT{i}", [128, 4, 512], BF16) for i in range(2)]
                    b_aT = [P.buf() for _ in range(2)]
                    ss2 = sb(st, "ss2", [128, 4])
                    b_ss2 = [P.buf() for _ in range(4)]
                    jq = sb(st, "jq2", [128, D], BF16)
                    b_jq = P.buf()
                    ot = [sb(st, f"ot{i}", [128, D]) for i in range(2)]
                    b_ot = [P.buf() for _ in range(2)]
                    for k in range(4):
                        P.dma(SP, lambda e, k=k: e.dma_start(out=h1[:, k, :], in_=xw[j][t_own, 2 + 128 * k:130 + 128 * k, :]), writes=[b_h1[k]], sbuf=b_h1[k])
                    mn = [0]
                    wvs = []
                    for cb in range(4):
                        wi = wpn[0] % NWP
                        wpn[0] += 1
                        wv = wpool[wi][:, :].rearrange("p (k c) -> p k c", k=16)
                        P.dma(SP, lambda e, wv=wv, cb=cb: e.dma_start(out=wv, in_=ws_out[:, :, cb * 256:(cb + 1) * 256]), reads=[b_ws["out"]], writes=[b_wp[wi]], sbuf=b_wp[wi])
                        wvs.append((wv, wi))
                    tpv = banks[2][:, :].bitcast(BF16).rearrange("p (k t) -> p k t", k=8)

                    def n2_front(k):
                        rms_rstd(ACT, h1[:, k, :], 128, D, jq[:, :], b_jq, ss2[:, k:k + 1], b_ss2[k], [b_h1[k]])
                        P.op(ACT, lambda e: e.activation(u2b[k % 2][:, :], h1[:, k, :], AF.Copy, scale=ss2[:, k:k + 1]), reads=[b_h1[k], b_ss2[k]], writes=[b_u2b[k % 2]])

                    def n2_back(k):
                        for kc in range(8):
                            P.op(PE, lambda e, kc=kc: e.transpose(tpv[:, kc, :], u2b[k % 2][:, kc * 128:(kc + 1) * 128], identb[:, :]), reads=[b_u2b[k % 2], b_idb], writes=[bk[2]])
                        P.op(DVE, lambda e: e.tensor_tensor(u2T[:, :, 128 * k:128 * k + 128], tpv, w2T.unsqueeze(2).to_broadcast([128, 8, 128]), ALU.mult),
                             reads=[bk[2], b_gt], writes=[b_u2T])

                    for k in range(4):
                        for cb in range(4):
                            wv, wi = wvs[cb]
                            pb = mn[0] % 2
                            mn[0] += 1
                            for kc in range(16):
                                P.op(PE, lambda e, kc=kc, wv=wv: e.matmul(banks[pb][:, 0:256], mixT[:, kc, 128 * k:128 * k + 128], wv[:, kc, :], start=(kc == 0), stop=(kc == 15)),
                                     reads=[b_mix, b_wp[wi]], writes=[bk[pb]])
                            P.op(DVE, lambda e, cb=cb: e.tensor_tensor(h1[:, k, cb * 256:(cb + 1) * 256], h1[:, k, cb * 256:(cb + 1) * 256], banks[pb][:, 0:256], ALU.add),
                                 reads=[b_h1[k], bk[pb]], writes=[b_h1[k]])
                        n2_front(k)
                        if k >= 1:
                            n2_back(k - 1)
                    n2_back(3)
                    un = [0]
                    dn_w = {}

                    def emit_up(p_):
                        wi = wpn[0] % NWP
                        wpn[0] += 1
                        wu = wpool[wi][:, :].rearrange("p (k c) -> p k c", k=8)
                        P.dma(SP, lambda e: e.dma_start(out=wu, in_=ws_up[:, :, p_ * 512:(p_ + 1) * 512]), reads=[b_ws["up"]], writes=[b_wp[wi]], sbuf=b_wp[wi])
                        wj = wpn[0] % NWP
                        wpn[0] += 1
                        wd = wpool[wj][:, :].rearrange("p (k c) -> p k c", k=4)
                        P.dma(SP, lambda e: e.dma_start(out=wd, in_=ws_dn[:, p_ * 4:(p_ + 1) * 4, :]), reads=[b_ws["dn"]], writes=[b_wp[wj]], sbuf=b_wp[wj])
                        dn_w[p_] = (wd, wj)
                        ai = p_ % 2
                        for fc in range(4):
                            pb = 3 + (un[0] % 2)
                            ri = un[0] % 2
                            un[0] += 1
                            for kc in range(8):
                                P.op(PE, lambda e, kc=kc: e.matmul(banks[pb][:, :], wu[:, kc, fc * 128:(fc + 1) * 128], u2T[:, kc, :], start=(kc == 0), stop=(kc == 7)),
                                     reads=[b_u2T, b_wp[wi]], writes=[bk[pb]])
                            P.op(ACT, lambda e: e.activation(rT[ri][:, :], banks[pb][:, :], AF.Relu), reads=[bk[pb]], writes=[b_rT[ri]])
                            P.op(DVE, lambda e: e.tensor_tensor(aT[ai][:, fc, :], rT[ri][:, :], rT[ri][:, :], ALU.mult), reads=[b_rT[ri]], writes=[b_aT[ai]])

                    def emit_down(p_):
                        wd, wj = dn_w[p_]
                        ai = p_ % 2
                        for k in range(4):
                            for ch in range(2):
                                pb = 5 + (un[0] % 2)
                                un[0] += 1
                                for fc in range(4):
                                    P.op(PE, lambda e, fc=fc: e.matmul(banks[pb][:, :], aT[ai][:, fc, 128 * k:128 * k + 128], wd[:, fc, ch * 512:(ch + 1) * 512], start=(fc == 0), stop=(fc == 3)),
                                         reads=[b_aT[ai], b_wp[wj]], writes=[bk[pb]])
                                P.op(DVE, lambda e: e.tensor_tensor(h1[:, k, ch * 512:(ch + 1) * 512], h1[:, k, ch * 512:(ch + 1) * 512], banks[pb][:, :], ALU.add),
                                     reads=[b_h1[k], bk[pb]], writes=[b_h1[k]])

                    emit_up(0)
                    for p_ in range(8):
                        if p_ + 1 < 8:
                            emit_up(p_ + 1)
                        emit_down(p_)
                    for k in range(4):
                        P.op(ACT, lambda e, k=k: e.activation(jq[:, :], h1[:, k, :], AF.Square, accum_out=ss2[:, k:k + 1]), reads=[b_h1[k]], writes=[b_jq, b_ss2[k]])
                    for k in range(4):
                        P.op(DVE, lambda e, k=k: e.tensor_scalar(ss2[:, k:k + 1], ss2[:, k:k + 1], 1.0 / D, EPS, ALU.mult, ALU.add), reads=[b_ss2[k]], writes=[b_ss2[k]])
                    for k in range(4):
                        P.op(ACT, lambda e, k=k: e.activation(ss2[:, k:k + 1], ss2[:, k:k + 1], AF.Ln), reads=[b_ss2[k]], writes=[b_ss2[k]])
                    for k in range(4):
                        P.op(ACT, lambda e, k=k: e.activation(ss2[:, k:k + 1], ss2[:, k:k + 1], AF.Exp, scale=-0.5), reads=[b_ss2[k]], writes=[b_ss2[k]])
                    for k in range(4):
                        oi2 = k % 2
                        P.op(DVE, lambda e, k=k, oi2=oi2: e.scalar_tensor_tensor(ot[oi2][:, :], h1[:, k, :], ss2[:, k:k + 1], finw, ALU.mult, ALU.mult), reads=[b_h1[k], b_ss2[k], b_gt], writes=[b_ot[oi2]])
                        P.dma(POOL, lambda e, k=k, oi2=oi2: e.dma_start(out=yout[j][oi * 512 + 128 * k:oi * 512 + 128 * k + 128, :], in_=ot[oi2][:, :]), reads=[b_ot[oi2]], sbuf=b_ot[oi2])
                    P.barrier()
                    P.release(mkM)
            P.release(mkT)
    P.finish()
    return nc, P


def _consts():
    c = np.zeros((128, 6 * 128 + 4 * 512), np.float32)
    i = np.arange(128)
    c[:, 0:128] = np.eye(128)
    c[:, 128:256] = (i[:, None] <= i[None, :])
    c[:, 256:384] = -1.0 * (i[:, None] < i[None, :])
    c[:, 384:512] = 1.0
    c[:, 512:640] = (i[None, :] >= i[:, None])
    c[:, 640:768] = (i[None, :] <= i[:, None])
    q = np.arange(512)
    for kk in range(4):
        c[:, 768 + 512 * kk:768 + 512 * (kk + 1)] = np.abs((128 * kk + i)[:, None] - q[None, :])
    return c


def _job_arrays(cfg, seq, own_start, params, is_prompt):
    meta = params["meta_tokens"]
    S = seq.shape[0]
    full = np.concatenate([meta, seq], axis=0)
    L = full.shape[0]
    NOWN = cfg.NOWN
    own_s0 = 16 + own_start
    own_s1 = own_s0 + NOWN * 512
    tiles = []
    kinds = []
    tiles.append(np.arange(-2, 18)); kinds.append("L")
    for s0 in range(16, own_s0, 512):
        tiles.append(np.arange(s0 - 2, s0 + 514)); kinds.append("L")
    for s0 in range(L - 512, own_s1 - 1, -512):
        tiles.append(np.arange(s0 + 513, s0 - 3, -1)); kinds.append("R")
    for s0 in range(own_s0, own_s1, 512):
        tiles.append(np.arange(s0 - 2, s0 + 514)); kinds.append("O")
    nt = len(tiles)
    xwin = np.zeros((nt, 516, D), np.float32)
    for t, idx in enumerate(tiles):
        ok = (idx >= 0) & (idx < L)
        xwin[t, np.nonzero(ok)[0]] = full[idx[ok]]
    pos = [tiles[0][2:18]] + [tl[2:514] for tl in tiles[1:]]
    kindtok = np.concatenate([np.full(len(p), {"L": 0, "R": 1, "O": 2}[k]) for p, k in zip(pos, kinds)])
    pos = np.concatenate(pos).astype(np.int64)
    Ls = len(pos)
    slopes = 2.0 ** (-(np.arange(NH) + 1.0))
    cpos, rpos = (pos // 128).astype(np.float32), (pos % 128).astype(np.float32)
    kaug = np.zeros((NH, 8, Ls), np.float32)
    for h in range(NH):
        sl = slopes[h]
        left = np.stack([-np.ones(Ls), -np.ones(Ls), sl * 128 * cpos, sl * rpos])
        right = -left
        ml = (kindtok != 1)[None, :]
        mr = (kindtok != 0)[None, :]
        kaug[h, 0:4] = left * ml
        kaug[h, 4:8] = right * mr
    qpos = np.arange(own_s0, own_s1)
    qc, qr = (qpos // 128).astype(np.float32), (qpos % 128).astype(np.float32)
    qaug = np.zeros((3, NH, 8, NOWN * 512), np.float32)
    for h in range(NH):
        sl = slopes[h]
        qa = np.stack([sl * 128 * qc, sl * qr, np.ones_like(qc), np.ones_like(qc)])
        qaug[0, h, 0:4] = qa; qaug[0, h, 4:8] = qa
        qaug[1, h, 0:4] = qa
        qaug[2, h, 4:8] = qa
    cw = params["conv_w"][0]
    cb = params["conv_b"][0]
    tab = np.zeros((nt, 128, TT), np.float32)
    cwT = cw.T.reshape(12, 128, 5).transpose(1, 0, 2)
    cbT = cb.reshape(12, 128).T
    last_left = max(t for t, k in enumerate(kinds) if k == "L")
    for t, k in enumerate(kinds):
        taps = cwT[:, :, ::-1] if k == "R" else cwT
        tab[t, :, 0:60] = taps.reshape(128, 60)
        tab[t, :, 60:72] = cbT
        if k == "R":
            prim_b, prim_a, sf, sb_ = params["dt_bias_b"][0], params["a_log_b"][0], 0.0, 1.0
        else:
            prim_b, prim_a, sf, sb_ = params["dt_bias_f"][0], params["a_log_f"][0], 1.0, 0.0
        tab[t, :, 72:88] = prim_b[None, :]
        tab[t, :, 88:104] = params["dt_bias_b"][0][None, :]
        tab[t, :, 104:120] = prim_a[None, :]
        tab[t, :, 120:136] = params["a_log_b"][0][None, :]
        tab[t, :, 136] = sf
        tab[t, :, 137] = sb_
        tab[t, :, 138] = 1.0 if t == last_left else 0.0
        tab[t, :, 139] = 0.0 if t == last_left else 1.0
    return dict(xw=xwin, kaug=kaug.astype(ml_dtypes.bfloat16), qaug=qaug.astype(ml_dtypes.bfloat16), ttab=tab)


_CACHE = {}


def run(cfg, inputs):
    p = {k: np.asarray(v, np.float32) for k, v in inputs.items()}
    key = (cfg.NC, cfg.SP, cfg.SS, cfg.NSEQ, cfg.NOWN)
    if key not in _CACHE:
        _CACHE[key] = build_program(cfg)
    nc, P = _CACHE[key]
    gt = np.zeros((128, 8 + 8 + 16 + 1024 + 1024 + 128 + 256), np.float32)
    gt[:, 0:8] = p["norm1_w"][0].reshape(8, 128).T
    gt[:, 8:16] = p["norm2_w"][0].reshape(8, 128).T
    gt[:, 16:32] = p["d_skip"][0][None, :]
    gt[:, 32:1056] = p["ssm_norm_w"][0][None, :]
    gt[:, 1056:2080] = p["final_norm_w"][None, :]
    gt[:, 2080:2208] = p["attn_norm_w"][0][None, :]
    gt[:, 2208:2272] = p["lambda_q1"][0][None, :]
    gt[:, 2272:2336] = p["lambda_k1"][0][None, :]
    gt[:, 2336:2400] = p["lambda_q2"][0][None, :]
    gt[:, 2400:2464] = p["lambda_k2"][0][None, :]
    cst = _consts()
    in_maps = []
    xp = p["x_prompt"][0]
    xs = p["x_sample"]
    for c in range(cfg.NC):
        m = {"w_in": p["w_in"][0], "w_out": p["w_out"][0], "w_up": p["w_up"][0], "w_dn": p["w_down"][0], "cst": cst, "gtab": gt}
        ja = [_job_arrays(cfg, xp, c * cfg.OWNT, p, True)]
        for s in range(cfg.NSEQ):
            ja.append(_job_arrays(cfg, xs[c * cfg.NSEQ + s], 0, p, False))
        for j, a in enumerate(ja):
            m[f"xw{j}"] = a["xw"]
            m[f"kaug{j}"] = a["kaug"]
            m[f"qaug{j}"] = a["qaug"]
            m[f"ttab{j}"] = a["ttab"]
        in_maps.append(m)
    res = run_bass_kernel_spmd(nc, in_maps, core_ids=list(range(cfg.NC)))
    _CACHE["last_exec_ns"] = getattr(res, "exec_time_ns", None)
    yp = np.concatenate([res.results[c]["y0"] for c in range(cfg.NC)], axis=0)[None]
    ys = np.stack([res.results[c][f"y{1 + s}"] for c in range(cfg.NC) for s in range(cfg.NSEQ)], axis=0)
    return yp.astype(np.float32), ys.astype(np.float32)


def kernel(**inputs):
    cfg = Cfg(ncores=8, sp=16384, ss=2048, nseq=4, nown=4)
    return run(cfg, inputs)
```

```python
import math
from contextlib import ExitStack
import numpy as np
import ml_dtypes
import concourse.bass as bass
import concourse.mybir as mybir
from concourse.bass_utils import run_bass_kernel_spmd

F32 = mybir.dt.float32
BF16 = mybir.dt.bfloat16
AF = mybir.ActivationFunctionType
ALU = mybir.AluOpType
PE, ACT, DVE, POOL, SP = "tensor", "scalar", "vector", "gpsimd", "sync"
COMPUTE = (PE, ACT, DVE, POOL)
ENGS = (PE, ACT, DVE, POOL, SP)

D = 1024
NH = 8
EPS = 1e-5
TT = 140
N_META = 16
LAM_INIT = 0.8 - 0.6 * math.exp(-0.3 * 0)
CQ, CK, CV, CZ, CX, CDT = 0, 1024, 2048, 3072, 4096, 5632


class Buf:
    __slots__ = ("name", "w", "rd", "dsem", "dcnt", "keep")

    def __init__(self, name=""):
        self.name = name
        self.keep = bool(name)
        self.w = None
        self.rd = []
        self.dsem = None
        self.dcnt = 0


class Op:
    __slots__ = ("eng", "fn", "deps", "is_dma", "flag", "sem", "val", "dbuf", "win", "pos")

    def __init__(self, eng, fn, is_dma):
        self.eng, self.fn, self.is_dma = eng, fn, is_dma
        self.deps = []
        self.flag = False
        self.sem = None
        self.val = 0
        self.dbuf = None
        self.win = 0
        self.pos = 0


class _Rec:
    def __init__(self):
        self.call = None

    def __getattr__(self, name):
        def f(*a, **k):
            self.call = (name, a, k)
            return None
        return f


class Prog:
    def __init__(self, nc):
        self.nc = nc
        self.pending = []
        self.bufs = []
        self.win = 0
        self.engs = {PE: nc.tensor, ACT: nc.scalar, DVE: nc.vector, POOL: nc.gpsimd, SP: nc.sync}
        self.esem = {e: nc.alloc_semaphore(f"prog_{e}") for e in ENGS}
        self.ecnt = {e: 0 for e in ENGS}
        self.seen = {e: {} for e in ENGS}
        self.winlast = {}
        self.lastop = {e: None for e in ENGS}
        self.n_ops = 0
        self.n_waits = 0
        self.sempool = []
        self.sempool_sw = []
        self.semq = {}
        self.nsem = 0

    def buf(self, name=""):
        b = Buf(name)
        self.bufs.append(b)
        return b

    def mark(self):
        return len(self.bufs)

    def release(self, mk):
        del self.bufs[mk:]

    def _add(self, op, reads, writes):
        deps = {}
        raw = set()
        for b in reads:
            if b.w is not None:
                deps[id(b.w)] = b.w
                raw.add(id(b.w))
        for b in writes:
            if b.w is not None and not (op.is_dma and b.w.is_dma and b.w.dbuf is op.dbuf):
                deps[id(b.w)] = b.w
            for r in b.rd:
                deps[id(r)] = r
        dl = []
        for d in deps.values():
            if d is op:
                continue
            if (not d.is_dma) and (not op.is_dma) and d.eng == op.eng:
                if op.eng == PE or id(d) not in raw:
                    continue
            dl.append(d)
        op.deps = dl
        for b in reads:
            b.rd.append(op)
        for b in writes:
            b.w = op
            b.rd = []
        op.win = self.win
        self.pending.append(op)
        if not op.is_dma:
            self.lastop[op.eng] = op
        self.n_ops += 1
        return op

    @staticmethod
    def _bind(fn):
        r = _Rec()
        fn(r)
        c = r.call
        return lambda e: getattr(e, c[0])(*c[1], **c[2])

    def op(self, eng, fn, reads=(), writes=()):
        return self._add(Op(eng, self._bind(fn), False), list(reads), list(writes))

    def dma(self, eng, fn, reads=(), writes=(), sbuf=None):
        o = Op(eng, self._bind(fn), True)
        o.dbuf = sbuf
        return self._add(o, list(reads), list(writes))

    def barrier(self):
        deps = {}
        for e in COMPUTE:
            if self.lastop[e] is not None:
                deps[id(self.lastop[e])] = self.lastop[e]
        for b in self.bufs:
            if b.w is not None and b.w.is_dma:
                deps[id(b.w)] = b.w
            for r in b.rd:
                if r.is_dma:
                    deps[id(r)] = r
        b0 = Op(SP, lambda e: e.nop(), False)
        b0.deps = list(deps.values())
        b0.win = self.win
        self.pending.append(b0)
        self.lastop[SP] = b0
        for e in COMPUTE + (SP,):
            o = Op(e, lambda en: en.nop(), False)
            o.deps = [b0]
            o.win = self.win
            self.pending.append(o)
            self.lastop[e] = o
        for b in self.bufs:
            b.w = None
            b.rd = []
        self.flush()
        for b in self.bufs:
            if b.dsem is not None:
                (self.sempool_sw if self.semq.get(id(b.dsem)) == POOL else self.sempool).append((b.dsem, b.dcnt))
                b.dsem = None

    def flush(self):
        nc = self.nc
        ops = self.pending
        self.pending = []
        last = {}
        for o in ops:
            for d in o.deps:
                d.flag = True
            if not o.is_dma:
                last[o.eng] = o
        for e, o in last.items():
            o.flag = True
            self.winlast[(self.win, e)] = o
        for o in ops:
            if o.is_dma:
                b = o.dbuf
                if b.dsem is None:
                    pool = self.sempool_sw if o.eng == POOL else self.sempool
                    if pool:
                        b.dsem, b.dcnt = pool.pop()
                    else:
                        self.nsem += 1
                        b.dsem = nc.alloc_semaphore(f"dma_{self.nsem}")
                        b.dcnt = 0
                    self.semq[id(b.dsem)] = o.eng
                b.dcnt += 16
                o.sem, o.val = b.dsem, b.dcnt
            elif o.flag:
                self.ecnt[o.eng] += 1
                o.sem, o.val = self.esem[o.eng], self.ecnt[o.eng]
        for o in ops:
            e = self.engs[o.eng]
            need = {}
            for d in o.deps:
                if d.sem is None:
                    d = self.winlast[(d.win, d.eng)]
                k = id(d.sem)
                if k not in need or need[k][1] < d.val:
                    need[k] = (d.sem, d.val)
            sn = self.seen[o.eng]
            for k, (s, v) in need.items():
                if sn.get(k, 0) >= v:
                    continue
                e.wait_ge(s, v)
                sn[k] = v
                self.n_waits += 1
            ins = o.fn(e)
            if o.is_dma:
                ins.then_inc(o.sem, 16)
            elif o.flag:
                ins.then_inc(o.sem, 1)
        self.win += 1

    def finish(self):
        self.flush()
        fe = self.engs[SP]
        for b in self.bufs:
            if b.dsem is not None:
                fe.wait_ge(b.dsem, b.dcnt)
        for (sm, cnt) in self.sempool + self.sempool_sw:
            if cnt > 0:
                fe.wait_ge(sm, cnt)


class Rot:
    def __init__(self, items):
        self.items = items
        self.i = 0

    def next(self):
        it = self.items[self.i % len(self.items)]
        self.i += 1
        return it


class Cfg:
    def __init__(self, ncores=8, sp=16384, ss=2048, nseq=4, nown=4):
        self.NC, self.SP, self.SS, self.NSEQ, self.NOWN = ncores, sp, ss, nseq, nown
        assert sp == ncores * nown * 512 and ss == nown * 512
        self.NCTX = sp // 512 - nown
        self.jobs = [self.NCTX] + [0] * nseq
        self.OWNT = nown * 512

    def ntiles(self, j):
        return 1 + self.jobs[j] + self.NOWN

    def L(self, j):
        return 16 + 512 * (self.ntiles(j) - 1)

    def nch(self, j):
        return 1 + 4 * (self.ntiles(j) - 1)


def build_program(cfg):
    nc = bass.Bass("TRN2", target_bir_lowering=False)
    P = Prog(nc)
    NOWN, OWNT = cfg.NOWN, cfg.OWNT
    NJ = len(cfg.jobs)
    Lmax = max(cfg.L(j) for j in range(NJ))
    NCHmax = max(cfg.nch(j) for j in range(NJ))

    def din(name, shape, dt=F32):
        return nc.dram_tensor(name, list(shape), dt, kind="ExternalInput").ap()

    def dscr(name, shape, dt=BF16):
        return nc.dram_tensor(name, list(shape), dt, kind="Internal").ap()

    xw = [din(f"xw{j}", [cfg.ntiles(j), 516, D]) for j in range(NJ)]
    kaug = [din(f"kaug{j}", [NH, 8, cfg.L(j)], BF16) for j in range(NJ)]
    qaug = [din(f"qaug{j}", [3, NH, 8, OWNT], BF16) for j in range(NJ)]
    ttab = [din(f"ttab{j}", [cfg.ntiles(j), 128, TT]) for j in range(NJ)]
    yout = [nc.dram_tensor(f"y{j}", [OWNT, D], F32, kind="ExternalOutput").ap() for j in range(NJ)]
    w_in = din("w_in", [D, 5664])
    w_out = din("w_out", [2048, D])
    w_up = din("w_up", [D, 4096])
    w_dn = din("w_dn", [4096, D])
    cst = din("cst", [128, 6 * 128 + 4 * 512])
    gtab = din("gtab", [128, 8 + 8 + 16 + 1024 + 1024 + 128 + 256])
    ws_in = dscr("ws_in", [128, 8, 5664])
    ws_out = dscr("ws_out", [128, 16, D])
    ws_up = dscr("ws_up", [128, 8, 4096])
    ws_dn = dscr("ws_dn", [128, 32, D])
    KT = dscr("KT", [NH, 2, 72, Lmax])
    QT = dscr("QT", [NH, 2, 64, OWNT])
    VA = dscr("VA", [NH, 128, NCHmax, 129])
    PFB = dscr("PFB", [2, 4 * NOWN, 128, D])
    SBS = dscr("SBS", [4 * NOWN, 128, D])
    b_ws = {k: P.buf("ws_" + k) for k in ("in", "out", "up", "dn")}
    b_KT, b_QT, b_VA, b_PFB, b_SBS = P.buf("KT"), P.buf("QT"), P.buf("VA"), P.buf("PFB"), P.buf("SBS")

    ES = ExitStack()
    G = ES

    _nm = [0]

    def sb(st, name, shape, dt=F32):
        _nm[0] += 1
        return st.enter_context(nc.sbuf_tensor(f"{name}_{_nm[0]}", list(shape), dt))

    allbanks = nc.alloc_psum_tensor("allbanks", [128, 4096], F32)
    banks = [allbanks[:, 512 * i:512 * (i + 1)] for i in range(8)]
    bk = [P.buf(f"bank{i}") for i in range(8)]

    cst_t = sb(G, "cst_t", [128, 6 * 128 + 4 * 512])
    gtab_t = sb(G, "gtab_t", [128, 8 + 8 + 16 + 1024 + 1024 + 128 + 256])
    identb = sb(G, "identb", [128, 128], BF16)
    lam_t = sb(G, "lam_t", [128, 8])
    CF = sb(G, "CF", [128, D])
    CB = sb(G, "CB", [128, D])
    decs = sb(G, "decs", [128, 4 * NOWN, 16])
    b_cst, b_gt, b_idb, b_lam, b_CF, b_CB, b_decs = (P.buf(n) for n in ("cst", "gt", "idb", "lam", "CF", "CB", "decs"))
    identf = cst_t[:, 0:128]
    uincl = cst_t[:, 128:256]
    negus = cst_t[:, 256:384]
    onesf = cst_t[:, 384:512]
    maskf = cst_t[:, 512:640]
    maskb = cst_t[:, 640:768]
    absd = [cst_t[:, 768 + 512 * i: 768 + 512 * (i + 1)] for i in range(4)]
    w1T = gtab_t[:, 0:8]
    w2T = gtab_t[:, 8:16]
    Dbc = gtab_t[:, 16:32]
    ssmw = gtab_t[:, 32:32 + 1024]
    finw = gtab_t[:, 1056:1056 + 1024]
    attw = gtab_t[:, 2080:2080 + 128]
    lamv = gtab_t[:, 2208:2208 + 256]

    P.dma(SP, lambda e: e.dma_start(out=cst_t[:], in_=cst[:, :]), writes=[b_cst], sbuf=b_cst)
    P.dma(SP, lambda e: e.dma_start(out=gtab_t[:], in_=gtab[:, :]), writes=[b_gt], sbuf=b_gt)
    P.op(DVE, lambda e: e.tensor_copy(identb[:], identf), reads=[b_cst], writes=[b_idb])
    umb = sb(G, "umb", [128, 2, 128], BF16)
    b_cstb = P.buf("cstb")
    P.op(DVE, lambda e: e.tensor_copy(umb[:, 0, :], uincl), reads=[b_cst], writes=[b_cstb])
    P.op(DVE, lambda e: e.tensor_copy(umb[:, 1, :], negus), reads=[b_cst], writes=[b_cstb])
    P.op(DVE, lambda e: e.tensor_tensor(lamv[:, 0:64], lamv[:, 0:64], lamv[:, 64:128], ALU.mult),
         reads=[b_gt], writes=[b_gt])
    P.op(DVE, lambda e: e.tensor_tensor(lamv[:, 128:192], lamv[:, 128:192], lamv[:, 192:256], ALU.mult),
         reads=[b_gt], writes=[b_gt])
    P.op(DVE, lambda e: e.reduce_sum(lam_t[:, 0:1], lamv[:, 0:64], mybir.AxisListType.X), reads=[b_gt], writes=[b_lam])
    P.op(DVE, lambda e: e.reduce_sum(lam_t[:, 1:2], lamv[:, 128:192], mybir.AxisListType.X), reads=[b_gt], writes=[b_lam])
    P.op(ACT, lambda e: e.activation(lam_t[:, 2:4], lam_t[:, 0:2], AF.Exp), reads=[b_lam], writes=[b_lam])
    P.op(DVE, lambda e: e.tensor_tensor(lam_t[:, 4:5], lam_t[:, 3:4], lam_t[:, 2:3], ALU.subtract), reads=[b_lam], writes=[b_lam])
    P.op(DVE, lambda e: e.tensor_scalar(lam_t[:, 4:5], lam_t[:, 4:5], -LAM_INIT, None, ALU.add), reads=[b_lam], writes=[b_lam])
    P.op(DVE, lambda e: e.tensor_scalar(attw, attw, 1.0 - LAM_INIT, None, ALU.mult), reads=[b_gt], writes=[b_gt])
    neglam = lam_t[:, 4:5]

    with ExitStack() as st:
        stf = [sb(st, f"stf{i}", [128, 2048]) for i in range(4)]
        stb = [sb(st, f"stb{i}", [128, 2048], BF16) for i in range(4)]
        b_stf = [P.buf(f"stf{i}") for i in range(4)]
        b_stb = [P.buf(f"stb{i}") for i in range(4)]
        cnt = [0]

        def conv_w(W, scr, bscr, K, N):
            for kc in range(K // 128):
                for c0 in range(0, N, 2048):
                    w = min(2048, N - c0)
                    i = cnt[0] % 4
                    ce = (DVE, ACT)[cnt[0] % 2]
                    cnt[0] += 1
                    P.dma(SP, lambda e, i=i, kc=kc, c0=c0, w=w: e.dma_start(out=stf[i][:, 0:w], in_=W[kc * 128:(kc + 1) * 128, c0:c0 + w]),
                          writes=[b_stf[i]], sbuf=b_stf[i])
                    if ce == ACT:
                        P.op(ACT, lambda e, i=i, w=w: e.copy(stb[i][:, 0:w], stf[i][:, 0:w]), reads=[b_stf[i]], writes=[b_stb[i]])
                    else:
                        P.op(ce, lambda e, i=i, w=w: e.tensor_copy(stb[i][:, 0:w], stf[i][:, 0:w]), reads=[b_stf[i]], writes=[b_stb[i]])
                    P.dma(POOL, lambda e, i=i, kc=kc, c0=c0, w=w: e.dma_start(out=scr[:, kc, c0:c0 + w], in_=stb[i][:, 0:w]),
                          reads=[b_stb[i]], writes=[bscr], sbuf=b_stb[i])

        conv_w(w_in, ws_in, b_ws["in"], D, 5664)
        conv_w(w_out, ws_out, b_ws["out"], 2048, D)
        conv_w(w_up, ws_up, b_ws["up"], D, 4096)
        conv_w(w_dn, ws_dn, b_ws["dn"], 4096, D)
        P.barrier()

    def make_uT(st, pfx):
        xs = [sb(st, f"{pfx}xs{i}", [128, D]) for i in range(2)]
        xb = [sb(st, f"{pfx}xb{i}", [128, D], BF16) for i in range(2)]
        sq = sb(st, f"{pfx}sq", [128, D], BF16)
        ss = sb(st, f"{pfx}ss", [128, 4])
        uT = sb(st, f"{pfx}uT", [128, 8, 516], BF16)
        return dict(xs=xs, xb=xb, sq=sq, ss=ss, uT=uT,
                    b_xs=[P.buf() for _ in range(2)], b_xb=[P.buf() for _ in range(2)],
                    b_sq=P.buf(), b_ss=[P.buf(), P.buf()], b_uT=P.buf(), cnt=[0])

    def rms_rstd(eng_sq, src_ap, n, width, sqjunk, b_junk, ssap, b_ss, src_bufs):
        P.op(ACT, lambda e: e.activation(sqjunk, src_ap, AF.Square, accum_out=ssap), reads=src_bufs, writes=[b_junk, b_ss])
        P.op(ACT, lambda e: e.activation(ssap, ssap, AF.Ln, scale=1.0 / width, bias=EPS), reads=[b_ss], writes=[b_ss])
        P.op(ACT, lambda e: e.activation(ssap, ssap, AF.Exp, scale=-0.5), reads=[b_ss], writes=[b_ss])

    def uT_thunks(U, src, W, wT, tp_bank):
        uT = U["uT"]
        tpv = banks[tp_bank][:, :].bitcast(BF16).rearrange("p (k t) -> p k t", k=8)
        out = []
        r0 = 0
        while r0 < W:
            n = min(128, W - r0)

            def mk(r0=r0, n=n):
                st_ = {}

                def stepA():
                    i = U["cnt"][0] % 2
                    U["cnt"][0] += 1
                    st_["i"] = i
                    xs, xb, bxs, bxb = U["xs"][i], U["xb"][i], U["b_xs"][i], U["b_xb"][i]
                    P.dma(SP, lambda e: e.dma_start(out=xs[0:n, :], in_=src[r0:r0 + n, :]), writes=[bxs], sbuf=bxs)
                    ssap = U["ss"][0:n, i:i + 1]
                    rms_rstd(ACT, xs[0:n, :], n, D, U["sq"][0:n, :], U["b_sq"], ssap, U["b_ss"][i], [bxs])
                    P.op(ACT, lambda e: e.activation(xb[0:n, :], xs[0:n, :], AF.Copy, scale=ssap), reads=[bxs, U["b_ss"][i]], writes=[bxb])

                def stepB():
                    i = st_["i"]
                    xb, bxb = U["xb"][i], U["b_xb"][i]
                    for kc in range(8):
                        P.op(PE, lambda e, kc=kc: e.transpose(tpv[:, kc, 0:n], xb[0:n, kc * 128:(kc + 1) * 128], identb[0:n, 0:n]),
                             reads=[bxb, b_idb], writes=[bk[tp_bank]])
                    P.op(DVE, lambda e: e.tensor_tensor(uT[:, :, r0:r0 + n], tpv[:, :, 0:n], wT.unsqueeze(2).to_broadcast([128, 8, n]), ALU.mult),
                         reads=[bk[tp_bank], b_gt], writes=[U["b_uT"]])
                return stepA, stepB
            out.append(mk())
            r0 += n
        return out

    def build_uT(U, src, W, wT, tp_bank):
        for (sa, sb_) in uT_thunks(U, src, W, wT, tp_bank):
            sa()
            sb_()

    def make_ssd(st, pfx):
        S = dict()
        S["tab"] = sb(st, f"{pfx}tab", [128, TT])
        S["raw"] = [sb(st, f"{pfx}raw{i}", [128, 516]) for i in range(2)]
        S["acc"] = [sb(st, f"{pfx}acc{i}", [128, 512]) for i in range(2)]
        S["xcT"] = sb(st, f"{pfx}xcT", [128, 12, 512], BF16)
        S["xtm"] = sb(st, f"{pfx}xtm", [128, 4, D], BF16)
        S["btm"] = sb(st, f"{pfx}btm", [128, 4, 256], BF16)
        S["dt"] = sb(st, f"{pfx}dt", [128, 4, 32])
        S["adt"] = sb(st, f"{pfx}adt", [128, 4, 32])
        S["cs"] = sb(st, f"{pfx}cs", [128, 4, 32])
        S["tot"] = sb(st, f"{pfx}tot", [128, 4, 32])
        S["A2"] = sb(st, f"{pfx}A2", [128, 32])
        S["tmp32"] = sb(st, f"{pfx}tmp32", [128, 32])
        for k in ("tab", "xcT", "xtm", "btm", "dtq", "A2", "tmp32"):
            S["b_" + k] = P.buf(pfx + k)
        S["b_raw"] = [P.buf() for _ in range(2)]
        S["b_acc"] = [P.buf() for _ in range(2)]
        S["n"] = 0
        return S

    def ssd_par(st, S, pfx):
        S2 = dict(S)
        S2["tab"] = sb(st, f"{pfx}tabB", [128, TT])
        S2["dt"] = sb(st, f"{pfx}dtB", [128, 4, 32])
        S2["adt"] = sb(st, f"{pfx}adtB", [128, 4, 32])
        S2["cs"] = sb(st, f"{pfx}csB", [128, 4, 32])
        S2["tot"] = sb(st, f"{pfx}totB", [128, 4, 32])
        S2["A2"] = sb(st, f"{pfx}A2B", [128, 32])
        for k in ("tab", "dtq", "A2"):
            S2["b_" + k] = P.buf()
        return S2

    def ssd_thunks(S, U, j, t, T, W, wv, cx, cdt, pb_main, pb_small, pb_tp):
        uT = U["uT"]
        tab = S["tab"]
        nchk = max(1, T // 128)
        cs_ = min(T, 128)
        def pre():
            P.dma(SP, lambda e: e.dma_start(out=tab[:], in_=ttab[j][t, :, :]), writes=[S["b_tab"]], sbuf=S["b_tab"])
            P.op(ACT, lambda e: e.activation(S["A2"][:], tab[:, 104:136], AF.Exp), reads=[S["b_tab"]], writes=[S["b_A2"]])
            P.op(DVE, lambda e: e.tensor_scalar(S["A2"][:], S["A2"][:], -1.0, None, ALU.mult), reads=[S["b_A2"]], writes=[S["b_A2"]])
            for k in range(nchk):
                for kc in range(8):
                    P.op(PE, lambda e, kc=kc, k=k: e.matmul(banks[pb_small][0:cs_, 64 + 32 * k:96 + 32 * k], uT[:, kc, 2 + 128 * k:2 + 128 * k + cs_], wv[:, kc, cdt:cdt + 32],
                                                            start=(kc == 0), stop=(kc == 7)),
                         reads=[U["b_uT"], b_wres], writes=[bk[pb_small]])
            dtr = banks[pb_small][0:cs_, 64:64 + 32 * nchk].rearrange("p (k c) -> p k c", c=32)
            dt, adt = S["dt"], S["adt"]
            P.op(DVE, lambda e: e.tensor_copy(dt[0:cs_, 0:nchk, 16:32], dtr[:, :, 16:32]), reads=[bk[pb_small]], writes=[S["b_dtq"]])
            P.op(DVE, lambda e: e.tensor_scalar(dt[0:cs_, 0:nchk, 0:16], dtr[:, :, 0:16], tab[0:cs_, 136:137], None, ALU.mult),
                 reads=[bk[pb_small], S["b_tab"]], writes=[S["b_dtq"]])
            P.op(DVE, lambda e: e.scalar_tensor_tensor(dt[0:cs_, 0:nchk, 0:16], dt[0:cs_, 0:nchk, 16:32], tab[0:cs_, 137:138], dt[0:cs_, 0:nchk, 0:16], ALU.mult, ALU.add),
                 reads=[S["b_dtq"], S["b_tab"]], writes=[S["b_dtq"]])
            P.op(DVE, lambda e: e.tensor_tensor(dt[0:cs_, 0:nchk, :], dt[0:cs_, 0:nchk, :], tab[0:cs_, 72:104].unsqueeze(1).to_broadcast([cs_, nchk, 32]), ALU.add),
                 reads=[S["b_dtq"], S["b_tab"]], writes=[S["b_dtq"]])
            P.op(ACT, lambda e: e.activation(dt[0:cs_, 0:nchk, :], dt[0:cs_, 0:nchk, :], AF.Exp), reads=[S["b_dtq"]], writes=[S["b_dtq"]])
            P.op(ACT, lambda e: e.activation(dt[0:cs_, 0:nchk, :], dt[0:cs_, 0:nchk, :], AF.Ln, bias=1.0), reads=[S["b_dtq"]], writes=[S["b_dtq"]])
            P.op(DVE, lambda e: e.tensor_tensor(adt[0:cs_, 0:nchk, :], dt[0:cs_, 0:nchk, :], S["A2"][0:cs_, :].unsqueeze(1).to_broadcast([cs_, nchk, 32]), ALU.mult),
                 reads=[S["b_dtq"], S["b_A2"]], writes=[S["b_dtq"]])
            for k in range(nchk):
                P.op(PE, lambda e, k=k: e.matmul(banks[pb_small][0:cs_, 192 + 32 * k:224 + 32 * k], uincl[0:cs_, 0:cs_], adt[0:cs_, k, :], start=True, stop=True),
                     reads=[S["b_dtq"], b_cst], writes=[bk[pb_small]])
                P.op(PE, lambda e, k=k: e.matmul(banks[pb_small][:, 320 + 32 * k:352 + 32 * k], onesf[0:cs_, :], adt[0:cs_, k, :], start=True, stop=True),
                     reads=[S["b_dtq"], b_cst], writes=[bk[pb_small]])
            P.op(DVE, lambda e: e.tensor_copy(S["cs"][0:cs_, 0:nchk, :], banks[pb_small][0:cs_, 192:192 + 32 * nchk].rearrange("p (k c) -> p k c", c=32)),
                 reads=[bk[pb_small]], writes=[S["b_dtq"]])
            P.op(DVE, lambda e: e.tensor_copy(S["tot"][:, 0:nchk, :], banks[pb_small][:, 320:320 + 32 * nchk].rearrange("p (k c) -> p k c", c=32)),
                 reads=[bk[pb_small]], writes=[S["b_dtq"]])

        pend_silu = []

        def one_fc(fc):
            i = S["n"] % 2
            S["n"] += 1
            raw, acc, braw, bacc = S["raw"][i], S["acc"][i], S["b_raw"][i], S["b_acc"][i]
            pbm = pb_main[fc % len(pb_main)]
            wa = min(W, 512)
            for kc in range(8):
                P.op(PE, lambda e, kc=kc, fc=fc, pbm=pbm, wa=wa: e.matmul(banks[pbm][:, 0:wa], wv[:, kc, cx + fc * 128: cx + (fc + 1) * 128], uT[:, kc, 0:wa],
                                                                   start=(kc == 0), stop=(kc == 7)),
                     reads=[U["b_uT"], b_wres], writes=[bk[pbm]])
            P.op(ACT, lambda e, raw=raw, pbm=pbm, wa=wa: e.copy(raw[:, 0:wa], banks[pbm][:, 0:wa]), reads=[bk[pbm]], writes=[braw])
            if W > 512:
                for kc in range(8):
                    P.op(PE, lambda e, kc=kc, fc=fc: e.matmul(banks[pb_small][:, 0:W - 512], wv[:, kc, cx + fc * 128: cx + (fc + 1) * 128], uT[:, kc, 512:W],
                                                             start=(kc == 0), stop=(kc == 7)),
                         reads=[U["b_uT"], b_wres], writes=[bk[pb_small]])
                P.op(ACT, lambda e, raw=raw: e.copy(raw[:, 512:W], banks[pb_small][:, 0:W - 512]), reads=[bk[pb_small]], writes=[braw])
            flush_silu()
            P.op(DVE, lambda e, raw=raw, acc=acc, fc=fc: e.tensor_scalar(acc[:, 0:T], raw[:, 0:T], tab[:, fc * 5:fc * 5 + 1], tab[:, 60 + fc:61 + fc], ALU.mult, ALU.add),
                 reads=[braw, S["b_tab"]], writes=[bacc])
            for jj in range(1, 5):
                P.op(DVE, lambda e, raw=raw, acc=acc, fc=fc, jj=jj: e.scalar_tensor_tensor(acc[:, 0:T], raw[:, jj:jj + T], tab[:, fc * 5 + jj:fc * 5 + jj + 1], acc[:, 0:T], ALU.mult, ALU.add),
                     reads=[braw, S["b_tab"], bacc], writes=[bacc])
            pend_silu.append((acc, bacc, fc))

        def flush_silu():
            while pend_silu:
                acc, bacc, fc = pend_silu.pop(0)
                P.op(ACT, lambda e: e.activation(S["xcT"][:, fc, 0:T], acc[:, 0:T], AF.Silu), reads=[bacc], writes=[S["b_xcT"]])

        def post():
            flush_silu()
            tpv = banks[pb_tp][:, :].bitcast(BF16)
            for k in range(nchk):
                for fc in range(8):
                    P.op(PE, lambda e, k=k, fc=fc: e.transpose(tpv[0:cs_, fc * 128:(fc + 1) * 128], S["xcT"][:, fc, 128 * k:128 * k + cs_], identb[:, :]),
                         reads=[S["b_xcT"], b_idb], writes=[bk[pb_tp]])
                P.op(ACT, lambda e, k=k: e.copy(S["xtm"][0:cs_, k, :], tpv[0:cs_, :]), reads=[bk[pb_tp]], writes=[S["b_xtm"]])
                for g in range(2):
                    P.op(PE, lambda e, k=k, g=g: e.transpose(tpv[0:cs_, g * 128:(g + 1) * 128], S["xcT"][:, 8 + g, 128 * k:128 * k + cs_], identb[:, :]),
                         reads=[S["b_xcT"], b_idb], writes=[bk[pb_tp]])
                P.op(ACT, lambda e, k=k: e.copy(S["btm"][0:cs_, k, :], tpv[0:cs_, 0:256]), reads=[bk[pb_tp]], writes=[S["b_btm"]])
        return pre, [(lambda fc=fc: one_fc(fc)) for fc in range(12)], post

    def ssd_prep(S, U, j, t, T, W, wv, cx, cdt, pb_main, pb_small, pb_tp):
        pre, fcs, post = ssd_thunks(S, U, j, t, T, W, wv, cx, cdt, pb_main, pb_small, pb_tp)
        pre()
        for f in fcs:
            f()
        post()

    b_wres = P.buf("wres")

    for j in range(NJ):
        nt = cfg.ntiles(j)
        nctx = cfg.jobs[j]
        with ExitStack() as st:
            mkS = P.mark()
            wS = sb(st, "wS", [128, 8, 4640], BF16)
            P.dma(SP, lambda e: e.dma_start(out=wS[:, :, 0:3072], in_=ws_in[:, :, 0:3072]), reads=[b_ws["in"]], writes=[b_wres], sbuf=b_wres)
            P.dma(SP, lambda e: e.dma_start(out=wS[:, :, 3072:4640], in_=ws_in[:, :, 4096:5664]), reads=[b_ws["in"]], writes=[b_wres], sbuf=b_wres)
            for m in range(2):
                P.dma(POOL, lambda e, m=m: e.dma_start(out=KT[:, m, 64:72, 0:cfg.L(j)], in_=kaug[j][:, :, :]), writes=[b_KT], sbuf=b_KT)
            U0 = make_uT(st, "s")
            U1 = dict(U0)
            U1["uT"] = sb(st, "suT1", [128, 8, 516], BF16)
            U1["b_uT"] = P.buf()
            Us = [U0, U1]
            S_0 = make_ssd(st, "s")
            S_1 = ssd_par(st, S_0, "s")
            Ss = [S_0, S_1]
            kst = [sb(st, f"kst{i}", [128, 512], BF16) for i in range(3)]
            b_kst = [P.buf() for _ in range(3)]
            vst = sb(st, "vst", [128, 8, 4, 129], BF16)
            b_vst = P.buf("vst")
            wgt = sb(st, "wgt", [128, 32])
            xwp = sb(st, "xwp", [128, 16, 64], BF16)
            xws = sb(st, "xws", [128, 16, 64], BF16)
            decp = sb(st, "decp", [128, 32])
            snap = [sb(st, f"snap{i}", [128, D], BF16) for i in range(2)]
            b_wgt, b_xwp, b_xws, b_decp = P.buf(), P.buf(), P.buf(), P.buf()
            b_snap = [P.buf() for _ in range(2)]
            P.op(POOL, lambda e: e.memset(vst[:], 1.0), writes=[b_vst])
            P.op(POOL, lambda e: e.memset(CF[:], 0.0), writes=[b_CF])
            P.op(POOL, lambda e: e.memset(CB[:], 0.0), writes=[b_CB])
            kn = [0]
            sn = [0]

            def make_states(S, t, own, oi, cs_, nchk):
                def one_chunk(k):
                    jj = oi * 4 + k
                    dt, cs, tot = S["dt"], S["cs"], S["tot"]
                    nw = 32 if own else 16
                    xt3 = S["xtm"][0:cs_, k, :].rearrange("p (h d) -> p h d", h=16)
                    P.op(DVE, lambda e: e.tensor_tensor(wgt[0:cs_, 0:16], tot[0:cs_, k, 0:16], cs[0:cs_, k, 0:16], ALU.subtract), reads=[S["b_dtq"]], writes=[b_wgt])
                    if own:
                        P.op(DVE, lambda e: e.tensor_tensor(wgt[0:cs_, 16:32], cs[0:cs_, k, 16:32], S["adt"][0:cs_, k, 16:32], ALU.subtract), reads=[S["b_dtq"], b_wgt], writes=[b_wgt])
                    yield
                    P.op(ACT, lambda e: e.activation(wgt[0:cs_, 0:nw], wgt[0:cs_, 0:nw], AF.Exp), reads=[b_wgt], writes=[b_wgt])
                    P.op(ACT, lambda e: e.activation(decp[:, :], tot[:, k, :], AF.Exp), reads=[S["b_dtq"]], writes=[b_decp])
                    yield
                    P.op(DVE, lambda e: e.tensor_tensor(wgt[0:cs_, 0:nw], wgt[0:cs_, 0:nw], dt[0:cs_, k, 0:nw], ALU.mult), reads=[b_wgt, S["b_dtq"]], writes=[b_wgt])
                    P.op(DVE, lambda e: e.tensor_tensor(xwp[0:cs_, :, :], xt3, wgt[0:cs_, 0:16].unsqueeze(2).to_broadcast([cs_, 16, 64]), ALU.mult),
                         reads=[S["b_xtm"], b_wgt], writes=[b_xwp])
                    if own:
                        P.op(DVE, lambda e: e.tensor_tensor(xws[0:cs_, :, :], xt3, wgt[0:cs_, 16:32].unsqueeze(2).to_broadcast([cs_, 16, 64]), ALU.mult),
                             reads=[S["b_xtm"], b_wgt], writes=[b_xws])
                    yield
                    for g in range(2):
                        P.op(PE, lambda e, g=g: e.matmul(banks[6 + g][:, :], S["btm"][0:cs_, k, g * 128:(g + 1) * 128], xwp[0:cs_, g * 8:(g + 1) * 8, :].rearrange("p h d -> p (h d)"),
                                                         start=True, stop=True),
                             reads=[S["b_btm"], b_xwp], writes=[bk[6 + g]])
                    if not own:
                        carry, bcar = CB, b_CB
                    else:
                        carry, bcar = CF, b_CF
                        i = sn[0] % 2
                        sn[0] += 1
                        P.op(ACT, lambda e: e.copy(snap[i][:], CF[:]), reads=[b_CF], writes=[b_snap[i]])
                        P.dma(POOL, lambda e: e.dma_start(out=PFB[0, jj, :, :], in_=snap[i][:]), reads=[b_snap[i]], writes=[b_PFB], sbuf=b_snap[i])
                    yield
                    c3 = carry[:].rearrange("p (h d) -> p h d", h=16)
                    P.op(DVE, lambda e: e.tensor_tensor(c3, c3, decp[:, 0:16].unsqueeze(2).to_broadcast([128, 16, 64]), ALU.mult), reads=[bcar, b_decp], writes=[bcar])
                    for g in range(2):
                        P.op(DVE, lambda e, g=g: e.tensor_tensor(carry[:, g * 512:(g + 1) * 512], carry[:, g * 512:(g + 1) * 512], banks[6 + g][:, :], ALU.add),
                             reads=[bcar, bk[6 + g]], writes=[bcar])
                    if own:
                        P.op(POOL, lambda e: e.tensor_copy(decs[:, jj, :], decp[:, 16:32]), reads=[b_decp], writes=[b_decs])
                    yield
                    if own:
                        i2 = sn[0] % 2
                        sn[0] += 1
                        for g in range(2):
                            P.op(PE, lambda e, g=g: e.matmul(banks[6 + g][:, :], S["btm"][0:cs_, k, g * 128:(g + 1) * 128], xws[0:cs_, g * 8:(g + 1) * 8, :].rearrange("p h d -> p (h d)"),
                                                             start=True, stop=True),
                                 reads=[S["b_btm"], b_xws], writes=[bk[6 + g]])
                            P.op(ACT, lambda e, g=g: e.copy(snap[i2][:, g * 512:(g + 1) * 512], banks[6 + g][:, :]), reads=[bk[6 + g]], writes=[b_snap[i2]])
                        P.dma(POOL, lambda e: e.dma_start(out=SBS[jj, :, :], in_=snap[i2][:]), reads=[b_snap[i2]], writes=[b_SBS], sbuf=b_snap[i2])
                    yield

                def stages(k):
                    gen = one_chunk(k)
                    return [(lambda: next(gen, None)) for _ in range(6)]

                def tile_end():
                    if not own:
                        P.op(DVE, lambda e: e.scalar_tensor_tensor(CF[:], CB[:], S["tab"][:, 138:139], CF[:], ALU.mult, ALU.add), reads=[b_CB, b_CF, S["b_tab"]], writes=[b_CF])
                        P.op(DVE, lambda e: e.tensor_scalar(CB[:], CB[:], S["tab"][:, 139:140], None, ALU.mult), reads=[b_CB, S["b_tab"]], writes=[b_CB])
                    return None
                out = []
                for k in range(nchk):
                    out += stages(k)
                return out + [tile_end]

            pend_states = []
            for t in range(nt):
                T = 16 if t == 0 else 512
                W = T + 4
                own = t > nctx
                oi = t - nctx - 1
                soff = 0 if t == 0 else 16 + 512 * (t - 1)
                ch0 = 0 if t == 0 else 1 + 4 * (t - 1)
                nchk = max(1, T // 128)
                cs_ = min(T, 128)
                U = Us[t % 2]
                S = Ss[t % 2]
                if t == 0:
                    build_uT(U, xw[j][0], W, w1T, 0)
                uT = U["uT"]
                side = []
                if t + 1 < nt:
                    th = uT_thunks(Us[(t + 1) % 2], xw[j][t + 1], 516, w1T, 0)
                    side.append(th[0][0])
                    for q_ in range(1, len(th)):
                        side.append(th[q_][0])
                        side.append(th[q_ - 1][1])
                    side.append(th[-1][1])
                if pend_states:
                    merged = []
                    ps_ = list(pend_states)
                    while side or ps_:
                        if side:
                            merged.append(side.pop(0))
                        if side:
                            merged.append(side.pop(0))
                        if ps_:
                            merged.append(ps_.pop(0))
                    side = merged
                    pend_states = []
                pre, fcs, post = ssd_thunks(S, U, j, t, T, W, wS, 3072, 4608, [1, 2], 3, 4)
                main = [pre] + fcs

                def kq_group(isq, h, T=T, uT=uT, U=U, soff=soff, oi=oi):
                    cbase = (0 if isq else 1024) + h * 128
                    pb = 1 + (kn[0] % 2)
                    i = kn[0] % 3
                    kn[0] += 1
                    for kc in range(8):
                        P.op(PE, lambda e, kc=kc: e.matmul(banks[pb][:, 0:T], wS[:, kc, cbase:cbase + 128], uT[:, kc, 2:2 + T], start=(kc == 0), stop=(kc == 7)),
                             reads=[U["b_uT"], b_wres], writes=[bk[pb]])
                    if isq:
                        P.op(ACT, lambda e: e.activation(kst[i][:, 0:T], banks[pb][:, 0:T], AF.Copy, scale=0.125), reads=[bk[pb]], writes=[b_kst[i]])
                        for m in range(2):
                            P.dma(POOL, lambda e, m=m: e.dma_start(out=QT[h, m, :, oi * 512:oi * 512 + T], in_=kst[i][64 * m:64 * m + 64, 0:T]),
                                  reads=[b_kst[i]], writes=[b_QT], sbuf=b_kst[i])
                    else:
                        P.op(ACT, lambda e: e.copy(kst[i][:, 0:T], banks[pb][:, 0:T]), reads=[bk[pb]], writes=[b_kst[i]])
                        for m in range(2):
                            P.dma(POOL, lambda e, m=m: e.dma_start(out=KT[h, m, 0:64, soff:soff + T], in_=kst[i][64 * m:64 * m + 64, 0:T]),
                                  reads=[b_kst[i]], writes=[b_KT], sbuf=b_kst[i])

                for isq in ([False, True] if own else [False]):
                    for h in range(NH):
                        main.append(lambda isq=isq, h=h: kq_group(isq, h))

                def v_group(k, half, uT=uT, U=U, cs_=cs_):
                    pb = 1 + (kn[0] % 2)
                    kn[0] += 1
                    for kc in range(8):
                        P.op(PE, lambda e, kc=kc: e.matmul(banks[pb][0:cs_, :], uT[:, kc, 2 + 128 * k:2 + 128 * k + cs_], wS[:, kc, 2048 + half * 512:2048 + (half + 1) * 512],
                                                           start=(kc == 0), stop=(kc == 7)),
                             reads=[U["b_uT"], b_wres], writes=[bk[pb]])
                    P.op(ACT, lambda e: e.copy(vst[0:cs_, half * 4:(half + 1) * 4, k, 0:128], banks[pb][0:cs_, :].rearrange("p (h e) -> p h e", h=4)),
                         reads=[bk[pb]], writes=[b_vst])

                for k in range(nchk):
                    for half in range(2):
                        main.append(lambda k=k, half=half: v_group(k, half))

                def v_store(t=t, ch0=ch0):
                    if t == 0:
                        P.dma(POOL, lambda e: e.dma_start(out=VA[:, 0:16, 0, :].rearrange("h p e -> p h e"), in_=vst[0:16, :, 0, :]),
                              reads=[b_vst], writes=[b_VA], sbuf=b_vst)
                    else:
                        P.dma(POOL, lambda e: e.dma_start(out=VA[:, :, ch0:ch0 + 4, :].rearrange("h p c e -> p h c e"), in_=vst[:, :, :, :]),
                              reads=[b_vst], writes=[b_VA], sbuf=b_vst)
                main.append(v_store)
                stride = max(1, len(main) // (len(side) + 1)) if side else len(main)
                si_ = 0
                for mi, f in enumerate(main):
                    f()
                    if side and (mi + 1) % stride == 0 and si_ < len(side):
                        side[si_]()
                        si_ += 1
                while si_ < len(side):
                    side[si_]()
                    si_ += 1
                post()
                pend_states = make_states(S, t, own, oi, cs_, nchk)
            for f in pend_states:
                f()
            pend_states = []
            ldb = [sb(st, f"ldb{i}", [128, D], BF16) for i in range(2)]
            b_ldb = [P.buf() for _ in range(2)]
            for jj in range(4 * NOWN - 1, -1, -1):
                i = sn[0] % 2
                sn[0] += 1
                P.op(ACT, lambda e, i=i: e.copy(snap[i][:], CB[:]), reads=[b_CB], writes=[b_snap[i]])
                P.dma(POOL, lambda e, i=i, jj=jj: e.dma_start(out=PFB[1, jj, :, :], in_=snap[i][:]), reads=[b_snap[i]], writes=[b_PFB], sbuf=b_snap[i])
                P.dma(SP, lambda e, i=i, jj=jj: e.dma_start(out=ldb[i][:], in_=SBS[jj, :, :]), reads=[b_SBS], writes=[b_ldb[i]], sbuf=b_ldb[i])
                c3 = CB[:].rearrange("p (h d) -> p h d", h=16)
                P.op(DVE, lambda e, c3=c3, jj=jj: e.tensor_tensor(c3, c3, decs[:, jj, :].unsqueeze(2).to_broadcast([128, 16, 64]), ALU.mult), reads=[b_CB, b_decs], writes=[b_CB])
                P.op(DVE, lambda e, i=i: e.tensor_tensor(CB[:], CB[:], ldb[i][:], ALU.add), reads=[b_CB, b_ldb[i]], writes=[b_CB])
            P.barrier()
            P.release(mkS)

        with ExitStack() as stT:
            mkT = P.mark()
            wO = sb(stT, "wO", [128, 8, 2592], BF16)
            P.dma(SP, lambda e: e.dma_start(out=wO[:, :, 0:1024], in_=ws_in[:, :, 3072:4096]), reads=[b_ws["in"]], writes=[b_wres], sbuf=b_wres)
            P.dma(SP, lambda e: e.dma_start(out=wO[:, :, 1024:2592], in_=ws_in[:, :, 4096:5664]), reads=[b_ws["in"]], writes=[b_wres], sbuf=b_wres)
            mixT = sb(stT, "mixT", [128, 16, 512], BF16)
            b_mix = P.buf("mixT")
            wpn = [0]
            nk_chunks = cfg.nch(j)
            for oi in range(NOWN):
                t_own = nctx + 1 + oi
                with ExitStack() as st:
                    mkA = P.mark()
                    qv = [[sb(st, f"qv{r}_{v}", [72, 2, 512], BF16) for v in range(3)] for r in range(2)]
                    b_qv = [P.buf() for _ in range(2)]
                    NKV = 4
                    kp = [sb(st, f"kp{i}", [72, 2, 2048], BF16) for i in range(NKV)]
                    vp = [sb(st, f"vp{i}", [128, 16, 129], BF16) for i in range(NKV)]
                    b_kv = [P.buf() for _ in range(NKV)]
                    tmpb = [sb(st, f"tmpb{i}", [128, 1024]) for i in range(2)]
                    b_tmpb = [P.buf() for _ in range(2)]
                    pt = [sb(st, f"pt{i}", [128, 1024], BF16) for i in range(3)]
                    b_pt = [P.buf() for _ in range(3)]
                    ob = sb(st, "ob", [128, 8, 129])
                    rl = sb(st, "rl", [128, 8])
                    o1 = sb(st, "o1", [128, 4, 128])
                    o2 = sb(st, "o2", [128, 4, 128])
                    ssq = sb(st, "ssq", [128, 4])
                    junk = sb(st, "junk", [128, 128], BF16)
                    junkf = sb(st, "junkf", [128, 128])
                    attb = sb(st, "attb", [128, 4, 128], BF16)
                    b_ob, b_rl, b_o1, b_o2, b_ssq, b_junk, b_attb = (P.buf() for _ in range(7))
                    pn = [0]
                    sbn = [0]
                    pieces = [(0, 1)] + [(c0, min(16, nk_chunks - c0)) for c0 in range(1, nk_chunks, 16)]
                    acc_v = [banks[4 + a // 3][:, (a % 3) * 129:(a % 3) * 129 + 129] for a in range(8)]
                    epi_gen = [None]
                    carry = [None]

                    def epilogue(h):
                        P.op(DVE, lambda e: e.reciprocal(rl[:, :].unsqueeze(2), ob[:, :, 128:129]), reads=[b_ob], writes=[b_rl])
                        P.op(DVE, lambda e: e.tensor_tensor(o1[:, :, :], ob[:, 0:4, 0:128], rl[:, 0:4].unsqueeze(2).to_broadcast([128, 4, 128]), ALU.mult), reads=[b_ob, b_rl], writes=[b_o1])
                        P.op(DVE, lambda e: e.tensor_tensor(o2[:, :, :], ob[:, 4:8, 0:128], rl[:, 4:8].unsqueeze(2).to_broadcast([128, 4, 128]), ALU.mult), reads=[b_ob, b_rl], writes=[b_o2])
                        P.op(DVE, lambda e: e.scalar_tensor_tensor(o1[:, :, :], o2[:, :, :], neglam, o1[:, :, :], ALU.mult, ALU.add), reads=[b_o1, b_o2, b_lam], writes=[b_o1])
                        yield
                        for qc in range(4):
                            P.op(DVE, lambda e, qc=qc: e.scalar_tensor_tensor(junkf[:, :], o1[:, qc, :], 1.0, o1[:, qc, :], ALU.mult, ALU.mult, accum_out=ssq[:, qc:qc + 1]),
                                 reads=[b_o1], writes=[b_junk, b_ssq])
                        yield
                        P.op(ACT, lambda e: e.activation(ssq[:, :], ssq[:, :], AF.Ln, scale=1.0 / 128, bias=EPS), reads=[b_ssq], writes=[b_ssq])
                        P.op(ACT, lambda e: e.activation(ssq[:, :], ssq[:, :], AF.Exp, scale=-0.5), reads=[b_ssq], writes=[b_ssq])
                        yield
                        P.op(DVE, lambda e: e.tensor_tensor(o1[:, :, :], o1[:, :, :], ssq[:, :].unsqueeze(2).to_broadcast([128, 4, 128]), ALU.mult), reads=[b_o1, b_ssq], writes=[b_o1])
                        P.op(DVE, lambda e: e.tensor_tensor(attb[:, :, :], o1[:, :, :], attw.unsqueeze(1).to_broadcast([128, 4, 128]), ALU.mult), reads=[b_o1, b_gt], writes=[b_attb])
                        yield
                        tpv = banks[7][:, :].bitcast(BF16)
                        for qc in range(4):
                            P.op(PE, lambda e, qc=qc: e.transpose(tpv[:, qc * 128:(qc + 1) * 128], attb[:, qc, :], identb[:, :]), reads=[b_attb, b_idb], writes=[bk[7]])
                        yield
                        P.op(DVE, lambda e: e.tensor_copy(mixT[:, h, :], banks[7][:, :].bitcast(BF16)[:, 0:512]), reads=[bk[7]], writes=[b_mix])
                        yield


                    for h in range(NH):
                        slope = 2.0 ** (-(h + 1))
                        r = h % 2
                        for v in range(3):
                            P.dma(SP, lambda e, r=r, v=v, h=h: e.dma_start(out=qv[r][v][0:64, :, :], in_=QT[h, :, :, oi * 512:(oi + 1) * 512].rearrange("m d q -> d m q")),
                                  reads=[b_QT], writes=[b_qv[r]], sbuf=b_qv[r])
                            for m in range(2):
                                P.dma(SP, lambda e, r=r, v=v, h=h, m=m: e.dma_start(out=qv[r][v][64:72, m, :], in_=qaug[j][v, h, :, oi * 512:(oi + 1) * 512]),
                                      writes=[b_qv[r]], sbuf=b_qv[r])
                        first_in_bank = {4: True, 5: True, 6: True}
                        steps = []
                        for (c0, ncn) in pieces:
                            for ci in range(ncn):
                                steps.append((c0, ncn, ci))
                        stinfo = {}

                        def emit_qk(si, h=h, r=r, slope=slope):
                            c0, ncn, ci = steps[si]
                            if ci == 0:
                                pi = pn[0] % NKV
                                pn[0] += 1
                                koff0 = 0 if c0 == 0 else 16 + 128 * (c0 - 1)
                                klen = 16 if c0 == 0 else 128 * ncn
                                P.dma(SP, lambda e: e.dma_start(out=kp[pi][:, :, 0:klen], in_=KT[h, :, :, koff0:koff0 + klen].rearrange("m r l -> r m l")),
                                      reads=[b_KT], writes=[b_kv[pi]], sbuf=b_kv[pi])
                                if c0 == 0:
                                    P.dma(SP, lambda e: e.dma_start(out=vp[pi][0:16, 0, :], in_=VA[h, 0:16, 0, :]), reads=[b_VA], writes=[b_kv[pi]], sbuf=b_kv[pi])
                                else:
                                    P.dma(SP, lambda e: e.dma_start(out=vp[pi][:, 0:ncn, :], in_=VA[h, :, c0:c0 + ncn, :]), reads=[b_VA], writes=[b_kv[pi]], sbuf=b_kv[pi])
                                stinfo["pi"] = pi
                            pi = stinfo["pi"]
                            c = c0 + ci
                            ks = 16 if c == 0 else 128
                            kt = 0 if c == 0 else 1 + (c - 1) // 4
                            kk = (c - 1) % 4
                            if kt <= nctx:
                                var, KR, ovl = 0, 72, False
                            elif kt < t_own:
                                var, KR, ovl = 1, 72, False
                            elif kt > t_own:
                                var, KR, ovl = 2, 72, False
                            else:
                                var, KR, ovl = 0, 64, True
                            sbi = sbn[0] % 2
                            pti = sbn[0] % 3
                            sbn[0] += 1
                            Bs = (2 * sbi, 2 * sbi + 1)
                            for m in range(2):
                                P.op(PE, lambda e, m=m: e.matmul(banks[Bs[m]][0:ks, :], kp[pi][0:KR, m, 128 * ci:128 * ci + ks], qv[r][var][0:KR, m, :], start=True, stop=True),
                                     reads=[b_kv[pi], b_qv[r]], writes=[bk[Bs[m]]])
                            if ovl:
                                tb = tmpb[sbi]
                                for m in range(2):
                                    P.op(DVE, lambda e, m=m: e.scalar_tensor_tensor(tb[:, m * 512:(m + 1) * 512], absd[kk], -slope, banks[Bs[m]][:, :], ALU.mult, ALU.add),
                                         reads=[b_cst, bk[Bs[m]]], writes=[b_tmpb[sbi]])
                                P.op(ACT, lambda e: e.activation(pt[pti][:, :], tb[:, :], AF.Exp), reads=[b_tmpb[sbi]], writes=[b_pt[pti]])
                            else:
                                P.op(ACT, lambda e: e.activation(pt[pti][0:ks, :], allbanks[0:ks, 512 * Bs[0]:512 * Bs[0] + 1024], AF.Exp),
                                     reads=[bk[Bs[0]], bk[Bs[1]]], writes=[b_pt[pti]])
                            return (pi, ci, ks, pti)

                        def emit_pv(info, first_in_bank=first_in_bank):
                            pi, ci, ks, pti = info
                            for a in range(8):
                                m, qc = a // 4, a % 4
                                bkn = 4 + a // 3
                                stt = first_in_bank[bkn]
                                first_in_bank[bkn] = False
                                P.op(PE, lambda e, a=a, m=m, qc=qc, stt=stt: e.matmul(acc_v[a], pt[pti][0:ks, m * 512 + qc * 128:m * 512 + (qc + 1) * 128], vp[pi][0:ks, ci, :],
                                                                                 start=stt, stop=False, skip_group_check=True),
                                     reads=[b_pt[pti], b_kv[pi]], writes=[bk[bkn]])

                        prev = None
                        for si in range(len(steps)):
                            cur = emit_qk(si)
                            if si == 0 and carry[0] is not None:
                                carry[0]()
                                carry[0] = None
                            if prev is not None:
                                emit_pv(prev)
                            prev = cur
                            if epi_gen[0] is not None and si % 2 == 1:
                                if next(epi_gen[0], "done") == "done":
                                    epi_gen[0] = None

                        def finish(prev=prev, emit_pv=emit_pv, h=h):
                            emit_pv(prev)
                            while epi_gen[0] is not None:
                                if next(epi_gen[0], "done") == "done":
                                    epi_gen[0] = None
                            for bi in range(3):
                                na = 3 if bi < 2 else 2
                                P.op(DVE, lambda e, bi=bi, na=na: e.tensor_copy(ob[:, bi * 3:bi * 3 + na, :], banks[4 + bi][:, 0:129 * na].rearrange("p (a c) -> p a c", c=129)),
                                     reads=[bk[4 + bi]], writes=[b_ob])
                            epi_gen[0] = epilogue(h)

                        carry[0] = finish
                    carry[0]()
                    while epi_gen[0] is not None:
                        if next(epi_gen[0], "done") == "done":
                            epi_gen[0] = None
                    P.barrier()
                    P.release(mkA)
                with ExitStack() as st:
                    mkO = P.mark()
                    U = make_uT(st, "o")
                    S = make_ssd(st, "o")
                    sz = sb(st, "sz", [128, 4, D], BF16)
                    b_sz = P.buf()
                    pfb = [sb(st, f"pfb{i}", [128, 2, D], BF16) for i in range(2)]
                    b_pfb = [P.buf() for _ in range(2)]
                    gtm2 = [sb(st, f"gtm{i}", [128, 4, 128]) for i in range(2)]
                    b_gtm2 = [P.buf() for _ in range(2)]
                    ef = [sb(st, f"ef{i}", [128, 128]) for i in range(6)]
                    b_ef = [P.buf() for _ in range(6)]
                    mt = [sb(st, f"mt{i}", [128, 128], BF16) for i in range(6)]
                    b_mt = [P.buf() for _ in range(6)]
                    xdt2 = [sb(st, f"xdt{i}", [128, 2, 16, 64], BF16) for i in range(2)]
                    b_xdt2 = [P.buf() for _ in range(2)]
                    bia2 = [sb(st, f"bia{i}", [128, 64]) for i in range(2)]
                    b_bia2 = [P.buf() for _ in range(2)]
                    yt_2 = [sb(st, f"yt{i}", [128, D]) for i in range(2)]
                    yt2_2 = [sb(st, f"ytt{i}", [128, D]) for i in range(2)]
                    b_yt_2 = [P.buf() for _ in range(2)]
                    b_yt2_2 = [P.buf() for _ in range(2)]
                    ssb2 = [sb(st, f"ssb{i}", [128, D], BF16) for i in range(2)]
                    b_ssb2 = [P.buf() for _ in range(2)]
                    sso = sb(st, "sso", [128, 4])
                    b_sso4 = [P.buf() for _ in range(4)]
                    jq = sb(st, "jq", [128, D], BF16)
                    b_jq = P.buf()
                    ahl = sb(st, "ahl", [128, 2, 32], BF16)
                    ahf = sb(st, "ahf", [128, 32])
                    b_ahl, b_ahf = P.buf(), P.buf()
                    SEGB = (0, 1, 2, 3, 5)
                    segq = [banks[bq][:, 0:128] for bq in SEGB]
                    b_segq = [bk[bq] for bq in SEGB]
                    build_uT(U, xw[j][t_own], 516, w1T, 0)
                    uT = U["uT"]
                    pre_, fcs_, post_ = ssd_thunks(S, U, j, t_own, 512, 516, wO, 1024, 2560, [1], 2, 3)

                    def z_group(k, half):
                        zb = 4 + (k * 2 + half) % 2
                        for kc in range(8):
                            P.op(PE, lambda e, kc=kc: e.matmul(banks[zb][:, :], uT[:, kc, 2 + 128 * k:130 + 128 * k], wO[:, kc, half * 512:(half + 1) * 512], start=(kc == 0), stop=(kc == 7)),
                                 reads=[U["b_uT"], b_wres], writes=[bk[zb]])
                        P.op(ACT, lambda e: e.activation(sz[:, k, half * 512:(half + 1) * 512], banks[zb][:, :], AF.Silu), reads=[bk[zb]], writes=[b_sz])

                    pre_()
                    for fc in range(12):
                        fcs_[fc]()
                        if fc < 8:
                            z_group(fc // 2, fc % 2)
                    post_()
                    en = [0]
                    pend = {"front": None, "back": None}
                    for k in range(4):
                        jj = oi * 4 + k
                        pi = k % 2
                        gtm, b_gtm, xdt, b_xdt, bia, b_bia = gtm2[pi], b_gtm2[pi], xdt2[pi], b_xdt2[pi], bia2[pi], b_bia2[pi]
                        yt, yt2, b_yt, b_yt2, ssb, b_ssb = yt_2[pi], yt2_2[pi], b_yt_2[pi], b_yt2_2[pi], ssb2[pi], b_ssb2[pi]
                        for d_ in range(2):
                            P.dma(SP, lambda e, pi=pi, d_=d_, jj=jj: e.dma_start(out=pfb[pi][:, d_, :], in_=PFB[d_, jj, :, :]), reads=[b_PFB], writes=[b_pfb[pi]], sbuf=b_pfb[pi])
                        dt, adt, cs, tot = S["dt"], S["adt"], S["cs"], S["tot"]
                        P.op(DVE, lambda e, k=k: e.tensor_scalar(bia[:, 0:16], cs[:, k, 0:16], -1.0, None, ALU.mult), reads=[S["b_dtq"]], writes=[b_bia])
                        P.op(DVE, lambda e, k=k: e.tensor_tensor(bia[:, 16:32], cs[:, k, 16:32], adt[:, k, 16:32], ALU.subtract), reads=[S["b_dtq"]], writes=[b_bia])
                        P.op(DVE, lambda e, k=k: e.tensor_tensor(bia[:, 48:64], tot[:, k, 16:32], bia[:, 16:32], ALU.subtract), reads=[S["b_dtq"], b_bia], writes=[b_bia])
                        P.op(ACT, lambda e, k=k: e.activation(bia[:, 32:48], cs[:, k, 0:16], AF.Exp), reads=[S["b_dtq"]], writes=[b_bia])
                        P.op(ACT, lambda e: e.activation(bia[:, 48:64], bia[:, 48:64], AF.Exp), reads=[b_bia], writes=[b_bia])
                        P.op(DVE, lambda e, k=k: e.tensor_copy(ahl[:, 0, :], adt[:, k, :]), reads=[S["b_dtq"]], writes=[b_ahl])
                        P.op(DVE, lambda e: e.tensor_copy(ahf[:, :], ahl[:, 0, :]), reads=[b_ahl], writes=[b_ahf])
                        P.op(DVE, lambda e, k=k: e.tensor_tensor(ahl[:, 1, :], adt[:, k, :], ahf[:, :], ALU.subtract), reads=[S["b_dtq"], b_ahf, b_ahl], writes=[b_ahl])
                        xt3 = S["xtm"][:, k, :].rearrange("p (h d) -> p h d", h=16)
                        for d_ in range(2):
                            P.op(POOL if d_ else DVE, lambda e, d_=d_, k=k, xt3=xt3: e.tensor_tensor(xdt[:, d_, :, :], xt3, dt[:, k, 16 * d_:16 * d_ + 16].unsqueeze(2).to_broadcast([128, 16, 64]), ALU.mult),
                                 reads=[S["b_xtm"], S["b_dtq"]], writes=[b_xdt])
                        for g in range(2):
                            P.op(PE, lambda e, g=g, k=k: e.matmul(banks[4][:, g * 128:(g + 1) * 128], S["xcT"][:, 8 + g, 128 * k:128 * k + 128], S["xcT"][:, 10 + g, 128 * k:128 * k + 128], start=True, stop=True),
                                 reads=[S["b_xcT"]], writes=[bk[4]])
                        for g in range(2):
                            P.op(DVE, lambda e, g=g: e.tensor_tensor(gtm[:, 2 * g, :], banks[4][:, g * 128:(g + 1) * 128], maskf, ALU.mult), reads=[bk[4], b_cst], writes=[b_gtm])
                            P.op(DVE, lambda e, g=g: e.tensor_tensor(gtm[:, 2 * g + 1, :], banks[4][:, g * 128:(g + 1) * 128], maskb, ALU.mult), reads=[bk[4], b_cst], writes=[b_gtm])
                        for g in range(2):
                            P.op(PE, lambda e, g=g, k=k, pi=pi: e.matmul(banks[g][:, :], S["xcT"][:, 10 + g, 128 * k:128 * k + 128], pfb[pi][:, 0, g * 512:(g + 1) * 512], start=True, stop=True),
                                 reads=[S["b_xcT"], b_pfb[pi]], writes=[bk[g]])
                        for g in range(2):
                            sl = slice(g * 512, (g + 1) * 512)
                            y3 = yt[:, sl].rearrange("p (h d) -> p h d", h=8)
                            P.op(DVE, lambda e, g=g, y3=y3: e.tensor_tensor(y3, banks[g][:, :].rearrange("p (h d) -> p h d", h=8), bia[:, 32 + 8 * g:40 + 8 * g].unsqueeze(2).to_broadcast([128, 8, 64]), ALU.mult),
                                 reads=[bk[g], b_bia], writes=[b_yt])
                        for g in range(2):
                            P.op(PE, lambda e, g=g, k=k, pi=pi: e.matmul(banks[g][:, :], S["xcT"][:, 10 + g, 128 * k:128 * k + 128], pfb[pi][:, 1, g * 512:(g + 1) * 512], start=True, stop=True),
                                 reads=[S["b_xcT"], b_pfb[pi]], writes=[bk[g]])
                        for g in range(2):
                            sl = slice(g * 512, (g + 1) * 512)
                            y23 = yt2[:, sl].rearrange("p (h d) -> p h d", h=8)
                            P.op(DVE, lambda e, g=g, y23=y23: e.tensor_tensor(y23, banks[g][:, :].rearrange("p (h d) -> p h d", h=8), bia[:, 48 + 8 * g:56 + 8 * g].unsqueeze(2).to_broadcast([128, 8, 64]), ALU.mult),
                                 reads=[bk[g], b_bia], writes=[b_yt2])
                        P.op(POOL, lambda e: e.tensor_tensor(yt[:, :], yt[:, :], yt2[:, :], ALU.add), reads=[b_yt, b_yt2], writes=[b_yt])
                        items = [(h, d_) for h in range(16) for d_ in range(2)]

                        def front(ii, k=k):
                            h, d_ = items[ii]
                            g = h // 8
                            q = ii % 6
                            rhs = umb[:, 0, :] if d_ == 0 else umb[:, 1, :]
                            sq_ = ii % 5
                            for hl in range(2):
                                P.op(PE, lambda e, hl=hl: e.matmul(segq[sq_], ahl[:, hl, 16 * d_ + h:16 * d_ + h + 1].to_broadcast([128, 128]), rhs, start=(hl == 0), stop=(hl == 1), skip_group_check=True),
                                     reads=[b_ahl, b_cstb], writes=[b_segq[sq_]])
                            P.op(ACT, lambda e: e.activation(ef[q][:, :], segq[sq_], AF.Exp, bias=bia[:, 16 * d_ + h:16 * d_ + h + 1]), reads=[b_segq[sq_], b_bia], writes=[b_ef[q]])
                            P.op(DVE, lambda e: e.scalar_tensor_tensor(mt[q][:, :], ef[q][:, :], 1.0, gtm[:, 2 * g + d_, :], ALU.min, ALU.mult),
                                 reads=[b_ef[q], b_gtm], writes=[b_mt[q]])

                        def back(ii):
                            h, d_ = items[ii]
                            q = ii % 6
                            P.op(PE, lambda e: e.matmul(banks[6 + h // 8][:, (h % 8) * 64:(h % 8) * 64 + 64], mt[q][:, :], xdt[:, d_, h, :], start=(d_ == 0), stop=(d_ == 1), skip_group_check=True),
                                 reads=[b_mt[q], b_xdt], writes=[bk[6 + h // 8]])

                        LA = 4
                        for ii in range(LA):
                            front(ii)
                        for ii in range(32):
                            if ii + LA < 32:
                                front(ii + LA)
                            back(ii)
                            if ii == 8 and pend["front"] is not None:
                                pend["front"]()
                                pend["front"] = None
                        if pend["back"] is not None:
                            pend["back"]()
                            pend["back"] = None
                        for g in range(2):
                            sl = slice(g * 512, (g + 1) * 512)
                            P.op(DVE, lambda e, g=g, sl=sl: e.tensor_tensor(yt[:, sl], yt[:, sl], banks[6 + g][:, :], ALU.add), reads=[b_yt, bk[6 + g]], writes=[b_yt])

                        def tail_front(k=k, yt=yt, yt2=yt2, b_yt=b_yt, b_yt2=b_yt2, ssb=ssb, b_ssb=b_ssb, xt3=xt3):
                            P.op(POOL, lambda e: e.tensor_tensor(yt2[:, :].rearrange("p (h d) -> p h d", h=16), xt3, Dbc.unsqueeze(2).to_broadcast([128, 16, 64]), ALU.mult),
                                 reads=[S["b_xtm"], b_gt, b_yt2], writes=[b_yt2])
                            P.op(POOL, lambda e: e.tensor_tensor(yt[:, :], yt[:, :], yt2[:, :], ALU.add), reads=[b_yt, b_yt2], writes=[b_yt])
                            P.op(DVE, lambda e: e.tensor_tensor(yt[:, :], yt[:, :], sz[:, k, :], ALU.mult), reads=[b_yt, b_sz], writes=[b_yt])
                            rms_rstd(ACT, yt[:, :], 128, D, jq[:, :], b_jq, sso[:, k:k + 1], b_sso4[k], [b_yt])
                            P.op(DVE, lambda e: e.scalar_tensor_tensor(ssb[:, :], yt[:, :], sso[:, k:k + 1], ssmw, ALU.mult, ALU.mult), reads=[b_yt, b_sso4[k], b_gt], writes=[b_ssb])

                        def tail_back(k=k, ssb=ssb, b_ssb=b_ssb):
                            tpv = banks[4][:, :].bitcast(BF16)
                            for fc in range(8):
                                P.op(PE, lambda e, fc=fc: e.transpose(tpv[:, fc * 128:(fc + 1) * 128], ssb[:, fc * 128:(fc + 1) * 128], identb[:, :]), reads=[b_ssb, b_idb], writes=[bk[4]])
                            P.op(ACT, lambda e: e.copy(mixT[:, 8:16, 128 * k:128 * k + 128], tpv[:, :].rearrange("p (f t) -> p f t", f=8)), reads=[bk[4]], writes=[b_mix])

                        pend["front"], pend["back"] = tail_front, tail_back
                    pend["front"]()
                    pend["back"]()
                    P.barrier()
                    P.release(mkO)
                with ExitStack() as st:
                    mkM = P.mark()
                    NWP = 6
                    wpool = [sb(st, f"wp{i}", [128, 4096], BF16) for i in range(NWP)]
                    b_wp = [P.buf() for _ in range(NWP)]
                    h1 = sb(st, "h1", [128, 4, D])
                    b_h1 = [P.buf() for _ in range(4)]
                    u2b = [sb(st, f"u2b{i}", [128, D], BF16) for i in range(2)]
                    b_u2b = [P.buf() for _ in range(2)]
                    u2T = sb(st, "u2T", [128, 8, 512], BF16)
                    b_u2T = P.buf()
                    rT = [sb(st, f"rT{i}", [128, 512], BF16) for i in range(2)]
                    b_rT = [P.buf() for _ in range(2)]
                    aT = [sb(st, f"aT{i}", [128, 4, 512], BF16) for i in range(2)]
                    b_aT = [P.buf() for _ in range(2)]
                    ss2 = sb(st, "ss2", [128, 4])
                    b_ss2 = [P.buf() for _ in range(4)]
                    jq = sb(st, "jq2", [128, D], BF16)
                    b_jq = P.buf()
                    ot = [sb(st, f"ot{i}", [128, D]) for i in range(2)]
                    b_ot = [P.buf() for _ in range(2)]
                    for k in range(4):
                        P.dma(SP, lambda e, k=k: e.dma_start(out=h1[:, k, :], in_=xw[j][t_own, 2 + 128 * k:130 + 128 * k, :]), writes=[b_h1[k]], sbuf=b_h1[k])
                    mn = [0]
                    wvs = []
                    for cb in range(4):
                        wi = wpn[0] % NWP
                        wpn[0] += 1
                        wv = wpool[wi][:, :].rearrange("p (k c) -> p k c", k=16)
                        P.dma(SP, lambda e, wv=wv, cb=cb: e.dma_start(out=wv, in_=ws_out[:, :, cb * 256:(cb + 1) * 256]), reads=[b_ws["out"]], writes=[b_wp[wi]], sbuf=b_wp[wi])
                        wvs.append((wv, wi))
                    tpv = banks[2][:, :].bitcast(BF16).rearrange("p (k t) -> p k t", k=8)

                    def n2_front(k):
                        rms_rstd(ACT, h1[:, k, :], 128, D, jq[:, :], b_jq, ss2[:, k:k + 1], b_ss2[k], [b_h1[k]])
                        P.op(ACT, lambda e: e.activation(u2b[k % 2][:, :], h1[:, k, :], AF.Copy, scale=ss2[:, k:k + 1]), reads=[b_h1[k], b_ss2[k]], writes=[b_u2b[k % 2]])

                    def n2_back(k):
                        for kc in range(8):
                            P.op(PE, lambda e, kc=kc: e.transpose(tpv[:, kc, :], u2b[k % 2][:, kc * 128:(kc + 1) * 128], identb[:, :]), reads=[b_u2b[k % 2], b_idb], writes=[bk[2]])
                        P.op(DVE, lambda e: e.tensor_tensor(u2T[:, :, 128 * k:128 * k + 128], tpv, w2T.unsqueeze(2).to_broadcast([128, 8, 128]), ALU.mult),
                             reads=[bk[2], b_gt], writes=[b_u2T])

                    for k in range(4):
                        for cb in range(4):
                            wv, wi = wvs[cb]
                            pb = mn[0] % 2
                            mn[0] += 1
                            for kc in range(16):
                                P.op(PE, lambda e, kc=kc, wv=wv: e.matmul(banks[pb][:, 0:256], mixT[:, kc, 128 * k:128 * k + 128], wv[:, kc, :], start=(kc == 0), stop=(kc == 15)),
                                     reads=[b_mix, b_wp[wi]], writes=[bk[pb]])
                            P.op(DVE, lambda e, cb=cb: e.tensor_tensor(h1[:, k, cb * 256:(cb + 1) * 256], h1[:, k, cb * 256:(cb + 1) * 256], banks[pb][:, 0:256], ALU.add),
                                 reads=[b_h1[k], bk[pb]], writes=[b_h1[k]])
                        n2_front(k)
                        if k >= 1:
                            n2_back(k - 1)
                    n2_back(3)
                    un = [0]
                    dn_w = {}

                    def emit_up(p_):
                        wi = wpn[0] % NWP
                        wpn[0] += 1
                        wu = wpool[wi][:, :].rearrange("p (k c) -> p k c", k=8)
                        P.dma(SP, lambda e: e.dma_start(out=wu, in_=ws_up[:, :, p_ * 512:(p_ + 1) * 512]), reads=[b_ws["up"]], writes=[b_wp[wi]], sbuf=b_wp[wi])
                        wj = wpn[0] % NWP
                        wpn[0] += 1
                        wd = wpool[wj][:, :].rearrange("p (k c) -> p k c", k=4)
                        P.dma(SP, lambda e: e.dma_start(out=wd, in_=ws_dn[:, p_ * 4:(p_ + 1) * 4, :]), reads=[b_ws["dn"]], writes=[b_wp[wj]], sbuf=b_wp[wj])
                        dn_w[p_] = (wd, wj)
                        ai = p_ % 2
                        for fc in range(4):
                            pb = 3 + (un[0] % 2)
                            ri = un[0] % 2
                            un[0] += 1
                            for kc in range(8):
                                P.op(PE, lambda e, kc=kc: e.matmul(banks[pb][:, :], wu[:, kc, fc * 128:(fc + 1) * 128], u2T[:, kc, :], start=(kc == 0), stop=(kc == 7)),
                                     reads=[b_u2T, b_wp[wi]], writes=[bk[pb]])
                            P.op(ACT, lambda e: e.activation(rT[ri][:, :], banks[pb][:, :], AF.Relu), reads=[bk[pb]], writes=[b_rT[ri]])
                            P.op(DVE, lambda e: e.tensor_tensor(aT[ai][:, fc, :], rT[ri][:, :], rT[ri][:, :], ALU.mult), reads=[b_rT[ri]], writes=[b_aT[ai]])

                    def emit_down(p_):
                        wd, wj = dn_w[p_]
                        ai = p_ % 2
                        for k in range(4):
                            for ch in range(2):
                                pb = 5 + (un[0] % 2)
                                un[0] += 1
                                for fc in range(4):
                                    P.op(PE, lambda e, fc=fc: e.matmul(banks[pb][:, :], aT[ai][:, fc, 128 * k:128 * k + 128], wd[:, fc, ch * 512:(ch + 1) * 512], start=(fc == 0), stop=(fc == 3)),
                                         reads=[b_aT[ai], b_wp[wj]], writes=[bk[pb]])
                                P.op(DVE, lambda e: e.tensor_tensor(h1[:, k, ch * 512:(ch + 1) * 512], h1[:, k, ch * 512:(ch + 1) * 512], banks[pb][:, :], ALU.add),
                                     reads=[b_h1[k], bk[pb]], writes=[b_h1[k]])

                    emit_up(0)
                    for p_ in range(8):
                        if p_ + 1 < 8:
                            emit_up(p_ + 1)
                        emit_down(p_)
                    for k in range(4):
                        P.op(ACT, lambda e, k=k: e.activation(jq[:, :], h1[:, k, :], AF.Square, accum_out=ss2[:, k:k + 1]), reads=[b_h1[k]], writes=[b_jq, b_ss2[k]])
                    for k in range(4):
                        P.op(DVE, lambda e, k=k: e.tensor_scalar(ss2[:, k:k + 1], ss2[:, k:k + 1], 1.0 / D, EPS, ALU.mult, ALU.add), reads=[b_ss2[k]], writes=[b_ss2[k]])
                    for k in range(4):
                        P.op(ACT, lambda e, k=k: e.activation(ss2[:, k:k + 1], ss2[:, k:k + 1], AF.Ln), reads=[b_ss2[k]], writes=[b_ss2[k]])
                    for k in range(4):
                        P.op(ACT, lambda e, k=k: e.activation(ss2[:, k:k + 1], ss2[:, k:k + 1], AF.Exp, scale=-0.5), reads=[b_ss2[k]], writes=[b_ss2[k]])
                    for k in range(4):
                        oi2 = k % 2
                        P.op(DVE, lambda e, k=k, oi2=oi2: e.scalar_tensor_tensor(ot[oi2][:, :], h1[:, k, :], ss2[:, k:k + 1], finw, ALU.mult, ALU.mult), reads=[b_h1[k], b_ss2[k], b_gt], writes=[b_ot[oi2]])
                        P.dma(POOL, lambda e, k=k, oi2=oi2: e.dma_start(out=yout[j][oi * 512 + 128 * k:oi * 512 + 128 * k + 128, :], in_=ot[oi2][:, :]), reads=[b_ot[oi2]], sbuf=b_ot[oi2])
                    P.barrier()
                    P.release(mkM)
            P.release(mkT)
    P.finish()
    return nc, P


def _consts():
    c = np.zeros((128, 6 * 128 + 4 * 512), np.float32)
    i = np.arange(128)
    c[:, 0:128] = np.eye(128)
    c[:, 128:256] = (i[:, None] <= i[None, :])
    c[:, 256:384] = -1.0 * (i[:, None] < i[None, :])
    c[:, 384:512] = 1.0
    c[:, 512:640] = (i[None, :] >= i[:, None])
    c[:, 640:768] = (i[None, :] <= i[:, None])
    q = np.arange(512)
    for kk in range(4):
        c[:, 768 + 512 * kk:768 + 512 * (kk + 1)] = np.abs((128 * kk + i)[:, None] - q[None, :])
    return c


def _job_arrays(cfg, seq, own_start, params, is_prompt):
    meta = params["meta_tokens"]
    S = seq.shape[0]
    full = np.concatenate([meta, seq], axis=0)
    L = full.shape[0]
    NOWN = cfg.NOWN
    own_s0 = 16 + own_start
    own_s1 = own_s0 + NOWN * 512
    tiles = []
    kinds = []
    tiles.append(np.arange(-2, 18)); kinds.append("L")
    for s0 in range(16, own_s0, 512):
        tiles.append(np.arange(s0 - 2, s0 + 514)); kinds.append("L")
    for s0 in range(L - 512, own_s1 - 1, -512):
        tiles.append(np.arange(s0 + 513, s0 - 3, -1)); kinds.append("R")
    for s0 in range(own_s0, own_s1, 512):
        tiles.append(np.arange(s0 - 2, s0 + 514)); kinds.append("O")
    nt = len(tiles)
    xwin = np.zeros((nt, 516, D), np.float32)
    for t, idx in enumerate(tiles):
        ok = (idx >= 0) & (idx < L)
        xwin[t, np.nonzero(ok)[0]] = full[idx[ok]]
    pos = [tiles[0][2:18]] + [tl[2:514] for tl in tiles[1:]]
    kindtok = np.concatenate([np.full(len(p), {"L": 0, "R": 1, "O": 2}[k]) for p, k in zip(pos, kinds)])
    pos = np.concatenate(pos).astype(np.int64)
    Ls = len(pos)
    slopes = 2.0 ** (-(np.arange(NH) + 1.0))
    cpos, rpos = (pos // 128).astype(np.float32), (pos % 128).astype(np.float32)
    kaug = np.zeros((NH, 8, Ls), np.float32)
    for h in range(NH):
        sl = slopes[h]
        left = np.stack([-np.ones(Ls), -np.ones(Ls), sl * 128 * cpos, sl * rpos])
        right = -left
        ml = (kindtok != 1)[None, :]
        mr = (kindtok != 0)[None, :]
        kaug[h, 0:4] = left * ml
        kaug[h, 4:8] = right * mr
    qpos = np.arange(own_s0, own_s1)
    qc, qr = (qpos // 128).astype(np.float32), (qpos % 128).astype(np.float32)
    qaug = np.zeros((3, NH, 8, NOWN * 512), np.float32)
    for h in range(NH):
        sl = slopes[h]
        qa = np.stack([sl * 128 * qc, sl * qr, np.ones_like(qc), np.ones_like(qc)])
        qaug[0, h, 0:4] = qa; qaug[0, h, 4:8] = qa
        qaug[1, h, 0:4] = qa
        qaug[2, h, 4:8] = qa
    cw = params["conv_w"][0]
    cb = params["conv_b"][0]
    tab = np.zeros((nt, 128, TT), np.float32)
    cwT = cw.T.reshape(12, 128, 5).transpose(1, 0, 2)
    cbT = cb.reshape(12, 128).T
    last_left = max(t for t, k in enumerate(kinds) if k == "L")
    for t, k in enumerate(kinds):
        taps = cwT[:, :, ::-1] if k == "R" else cwT
        tab[t, :, 0:60] = taps.reshape(128, 60)
        tab[t, :, 60:72] = cbT
        if k == "R":
            prim_b, prim_a, sf, sb_ = params["dt_bias_b"][0], params["a_log_b"][0], 0.0, 1.0
        else:
            prim_b, prim_a, sf, sb_ = params["dt_bias_f"][0], params["a_log_f"][0], 1.0, 0.0
        tab[t, :, 72:88] = prim_b[None, :]
        tab[t, :, 88:104] = params["dt_bias_b"][0][None, :]
        tab[t, :, 104:120] = prim_a[None, :]
        tab[t, :, 120:136] = params["a_log_b"][0][None, :]
        tab[t, :, 136] = sf
        tab[t, :, 137] = sb_
        tab[t, :, 138] = 1.0 if t == last_left else 0.0
        tab[t, :, 139] = 0.0 if t == last_left else 1.0
    return dict(xw=xwin, kaug=kaug.astype(ml_dtypes.bfloat16), qaug=qaug.astype(ml_dtypes.bfloat16), ttab=tab)


_CACHE = {}


def run(cfg, inputs):
    p = {k: np.asarray(v, np.float32) for k, v in inputs.items()}
    key = (cfg.NC, cfg.SP, cfg.SS, cfg.NSEQ, cfg.NOWN)
    if key not in _CACHE:
        _CACHE[key] = build_program(cfg)
    nc, P = _CACHE[key]
    gt = np.zeros((128, 8 + 8 + 16 + 1024 + 1024 + 128 + 256), np.float32)
    gt[:, 0:8] = p["norm1_w"][0].reshape(8, 128).T
    gt[:, 8:16] = p["norm2_w"][0].reshape(8, 128).T
    gt[:, 16:32] = p["d_skip"][0][None, :]
    gt[:, 32:1056] = p["ssm_norm_w"][0][None, :]
    gt[:, 1056:2080] = p["final_norm_w"][None, :]
    gt[:, 2080:2208] = p["attn_norm_w"][0][None, :]
    gt[:, 2208:2272] = p["lambda_q1"][0][None, :]
    gt[:, 2272:2336] = p["lambda_k1"][0][None, :]
    gt[:, 2336:2400] = p["lambda_q2"][0][None, :]
    gt[:, 2400:2464] = p["lambda_k2"][0][None, :]
    cst = _consts()
    in_maps = []
    xp = p["x_prompt"][0]
    xs = p["x_sample"]
    for c in range(cfg.NC):
        m = {"w_in": p["w_in"][0], "w_out": p["w_out"][0], "w_up": p["w_up"][0], "w_dn": p["w_down"][0], "cst": cst, "gtab": gt}
        ja = [_job_arrays(cfg, xp, c * cfg.OWNT, p, True)]
        for s in range(cfg.NSEQ):
            ja.append(_job_arrays(cfg, xs[c * cfg.NSEQ + s], 0, p, False))
        for j, a in enumerate(ja):
            m[f"xw{j}"] = a["xw"]
            m[f"kaug{j}"] = a["kaug"]
            m[f"qaug{j}"] = a["qaug"]
            m[f"ttab{j}"] = a["ttab"]
        in_maps.append(m)
    res = run_bass_kernel_spmd(nc, in_maps, core_ids=list(range(cfg.NC)))
    _CACHE["last_exec_ns"] = getattr(res, "exec_time_ns", None)
    yp = np.concatenate([res.results[c]["y0"] for c in range(cfg.NC)], axis=0)[None]
    ys = np.stack([res.results[c][f"y{1 + s}"] for c in range(cfg.NC) for s in range(cfg.NSEQ)], axis=0)
    return yp.astype(np.float32), ys.astype(np.float32)


def kernel(**inputs):
    cfg = Cfg(ncores=8, sp=16384, ss=2048, nseq=4, nown=4)
    return run(cfg, inputs)
```

```python
import math
from contextlib import ExitStack
import numpy as np
import ml_dtypes
import concourse.bass as bass
import concourse.mybir as mybir
from concourse.bass_utils import run_bass_kernel_spmd

F32 = mybir.dt.float32
BF16 = mybir.dt.bfloat16
AF = mybir.ActivationFunctionType
ALU = mybir.AluOpType
PE, ACT, DVE, POOL, SP = "tensor", "scalar", "vector", "gpsimd", "sync"
COMPUTE = (PE, ACT, DVE, POOL)
ENGS = (PE, ACT, DVE, POOL, SP)

D = 1024
NH = 8
EPS = 1e-5
TT = 140
N_META = 16
LAM_INIT = 0.8 - 0.6 * math.exp(-0.3 * 0)
CQ, CK, CV, CZ, CX, CDT = 0, 1024, 2048, 3072, 4096, 5632


class Buf:
    __slots__ = ("name", "w", "rd", "dsem", "dcnt", "keep")

    def __init__(self, name=""):
        self.name = name
        self.keep = bool(name)
        self.w = None
        self.rd = []
        self.dsem = None
        self.dcnt = 0


class Op:
    __slots__ = ("eng", "fn", "deps", "is_dma", "flag", "sem", "val", "dbuf", "win", "pos")

    def __init__(self, eng, fn, is_dma):
        self.eng, self.fn, self.is_dma = eng, fn, is_dma
        self.deps = []
        self.flag = False
        self.sem = None
        self.val = 0
        self.dbuf = None
        self.win = 0
        self.pos = 0


class _Rec:
    def __init__(self):
        self.call = None

    def __getattr__(self, name):
        def f(*a, **k):
            self.call = (name, a, k)
            return None
        return f


class Prog:
    def __init__(self, nc):
        self.nc = nc
        self.pending = []
        self.bufs = []
        self.win = 0
        self.engs = {PE: nc.tensor, ACT: nc.scalar, DVE: nc.vector, POOL: nc.gpsimd, SP: nc.sync}
        self.esem = {e: nc.alloc_semaphore(f"prog_{e}") for e in ENGS}
        self.ecnt = {e: 0 for e in ENGS}
        self.seen = {e: {} for e in ENGS}
        self.winlast = {}
        self.lastop = {e: None for e in ENGS}
        self.n_ops = 0
        self.n_waits = 0
        self.sempool = []
        self.sempool_sw = []
        self.semq = {}
        self.nsem = 0

    def buf(self, name=""):
        b = Buf(name)
        self.bufs.append(b)
        return b

    def mark(self):
        return len(self.bufs)

    def release(self, mk):
        del self.bufs[mk:]

    def _add(self, op, reads, writes):
        deps = {}
        raw = set()
        for b in reads:
            if b.w is not None:
                deps[id(b.w)] = b.w
                raw.add(id(b.w))
        for b in writes:
            if b.w is not None and not (op.is_dma and b.w.is_dma and b.w.dbuf is op.dbuf):
                deps[id(b.w)] = b.w
            for r in b.rd:
                deps[id(r)] = r
        dl = []
        for d in deps.values():
            if d is op:
                continue
            if (not d.is_dma) and (not op.is_dma) and d.eng == op.eng:
                if op.eng == PE or id(d) not in raw:
                    continue
            dl.append(d)
        op.deps = dl
        for b in reads:
            b.rd.append(op)
        for b in writes:
            b.w = op
            b.rd = []
        op.win = self.win
        self.pending.append(op)
        if not op.is_dma:
            self.lastop[op.eng] = op
        self.n_ops += 1
        return op

    @staticmethod
    def _bind(fn):
        r = _Rec()
        fn(r)
        c = r.call
        return lambda e: getattr(e, c[0])(*c[1], **c[2])

    def op(self, eng, fn, reads=(), writes=()):
        return self._add(Op(eng, self._bind(fn), False), list(reads), list(writes))

    def dma(self, eng, fn, reads=(), writes=(), sbuf=None):
        o = Op(eng, self._bind(fn), True)
        o.dbuf = sbuf
        return self._add(o, list(reads), list(writes))

    def barrier(self):
        deps = {}
        for e in COMPUTE:
            if self.lastop[e] is not None:
                deps[id(self.lastop[e])] = self.lastop[e]
        for b in self.bufs:
            if b.w is not None and b.w.is_dma:
                deps[id(b.w)] = b.w
            for r in b.rd:
                if r.is_dma:
                    deps[id(r)] = r
        b0 = Op(SP, lambda e: e.nop(), False)
        b0.deps = list(deps.values())
        b0.win = self.win
        self.pending.append(b0)
        self.lastop[SP] = b0
        for e in COMPUTE + (SP,):
            o = Op(e, lambda en: en.nop(), False)
            o.deps = [b0]
            o.win = self.win
            self.pending.append(o)
            self.lastop[e] = o
        for b in self.bufs:
            b.w = None
            b.rd = []
        self.flush()
        for b in self.bufs:
            if b.dsem is not None:
                (self.sempool_sw if self.semq.get(id(b.dsem)) == POOL else self.sempool).append((b.dsem, b.dcnt))
                b.dsem = None

    def flush(self):
        nc = self.nc
        ops = self.pending
        self.pending = []
        last = {}
        for o in ops:
            for d in o.deps:
                d.flag = True
            if not o.is_dma:
                last[o.eng] = o
        for e, o in last.items():
            o.flag = True
            self.winlast[(self.win, e)] = o
        for o in ops:
            if o.is_dma:
                b = o.dbuf
                if b.dsem is None:
                    pool = self.sempool_sw if o.eng == POOL else self.sempool
                    if pool:
                        b.dsem, b.dcnt = pool.pop()
                    else:
                        self.nsem += 1
                        b.dsem = nc.alloc_semaphore(f"dma_{self.nsem}")
                        b.dcnt = 0
                    self.semq[id(b.dsem)] = o.eng
                b.dcnt += 16
                o.sem, o.val = b.dsem, b.dcnt
            elif o.flag:
                self.ecnt[o.eng] += 1
                o.sem, o.val = self.esem[o.eng], self.ecnt[o.eng]
        for o in ops:
            e = self.engs[o.eng]
            need = {}
            for d in o.deps:
                if d.sem is None:
                    d = self.winlast[(d.win, d.eng)]
                k = id(d.sem)
                if k not in need or need[k][1] < d.val:
                    need[k] = (d.sem, d.val)
            sn = self.seen[o.eng]
            for k, (s, v) in need.items():
                if sn.get(k, 0) >= v:
                    continue
                e.wait_ge(s, v)
                sn[k] = v
                self.n_waits += 1
            ins = o.fn(e)
            if o.is_dma:
                ins.then_inc(o.sem, 16)
            elif o.flag:
                ins.then_inc(o.sem, 1)
        self.win += 1

    def finish(self):
        self.flush()
        fe = self.engs[SP]
        for b in self.bufs:
            if b.dsem is not None:
                fe.wait_ge(b.dsem, b.dcnt)
        for (sm, cnt) in self.sempool + self.sempool_sw:
            if cnt > 0:
                fe.wait_ge(sm, cnt)


class Rot:
    def __init__(self, items):
        self.items = items
        self.i = 0

    def next(self):
        it = self.items[self.i % len(self.items)]
        self.i += 1
        return it


class Cfg:
    def __init__(self, ncores=8, sp=16384, ss=2048, nseq=4, nown=4):
        self.NC, self.SP, self.SS, self.NSEQ, self.NOWN = ncores, sp, ss, nseq, nown
        assert sp == ncores * nown * 512 and ss == nown * 512
        self.NCTX = sp // 512 - nown
        self.jobs = [self.NCTX] + [0] * nseq
        self.OWNT = nown * 512

    def ntiles(self, j):
        return 1 + self.jobs[j] + self.NOWN

    def L(self, j):
        return 16 + 512 * (self.ntiles(j) - 1)

    def nch(self, j):
        return 1 + 4 * (self.ntiles(j) - 1)


def build_program(cfg):
    nc = bass.Bass("TRN2", target_bir_lowering=False)
    P = Prog(nc)
    NOWN, OWNT = cfg.NOWN, cfg.OWNT
    NJ = len(cfg.jobs)
    Lmax = max(cfg.L(j) for j in range(NJ))
    NCHmax = max(cfg.nch(j) for j in range(NJ))

    def din(name, shape, dt=F32):
        return nc.dram_tensor(name, list(shape), dt, kind="ExternalInput").ap()

    def dscr(name, shape, dt=BF16):
        return nc.dram_tensor(name, list(shape), dt, kind="Internal").ap()

    xw = [din(f"xw{j}", [cfg.ntiles(j), 516, D]) for j in range(NJ)]
    kaug = [din(f"kaug{j}", [NH, 8, cfg.L(j)], BF16) for j in range(NJ)]
    qaug = [din(f"qaug{j}", [3, NH, 8, OWNT], BF16) for j in range(NJ)]
    ttab = [din(f"ttab{j}", [cfg.ntiles(j), 128, TT]) for j in range(NJ)]
    yout = [nc.dram_tensor(f"y{j}", [OWNT, D], F32, kind="ExternalOutput").ap() for j in range(NJ)]
    w_in = din("w_in", [D, 5664])
    w_out = din("w_out", [2048, D])
    w_up = din("w_up", [D, 4096])
    w_dn = din("w_dn", [4096, D])
    cst = din("cst", [128, 6 * 128 + 4 * 512])
    gtab = din("gtab", [128, 8 + 8 + 16 + 1024 + 1024 + 128 + 256])
    ws_in = dscr("ws_in", [128, 8, 5664])
    ws_out = dscr("ws_out", [128, 16, D])
    ws_up = dscr("ws_up", [128, 8, 4096])
    ws_dn = dscr("ws_dn", [128, 32, D])
    KT = dscr("KT", [NH, 2, 72, Lmax])
    QT = dscr("QT", [NH, 2, 64, OWNT])
    VA = dscr("VA", [NH, 128, NCHmax, 129])
    PFB = dscr("PFB", [2, 4 * NOWN, 128, D])
    SBS = dscr("SBS", [4 * NOWN, 128, D])
    b_ws = {k: P.buf("ws_" + k) for k in ("in", "out", "up", "dn")}
    b_KT, b_QT, b_VA, b_PFB, b_SBS = P.buf("KT"), P.buf("QT"), P.buf("VA"), P.buf("PFB"), P.buf("SBS")

    ES = ExitStack()
    G = ES

    _nm = [0]

    def sb(st, name, shape, dt=F32):
        _nm[0] += 1
        return st.enter_context(nc.sbuf_tensor(f"{name}_{_nm[0]}", list(shape), dt))

    allbanks = nc.alloc_psum_tensor("allbanks", [128, 4096], F32)
    banks = [allbanks[:, 512 * i:512 * (i + 1)] for i in range(8)]
    bk = [P.buf(f"bank{i}") for i in range(8)]

    cst_t = sb(G, "cst_t", [128, 6 * 128 + 4 * 512])
    gtab_t = sb(G, "gtab_t", [128, 8 + 8 + 16 + 1024 + 1024 + 128 + 256])
    identb = sb(G, "identb", [128, 128], BF16)
    lam_t = sb(G, "lam_t", [128, 8])
    CF = sb(G, "CF", [128, D])
    CB = sb(G, "CB", [128, D])
    decs = sb(G, "decs", [128, 4 * NOWN, 16])
    b_cst, b_gt, b_idb, b_lam, b_CF, b_CB, b_decs = (P.buf(n) for n in ("cst", "gt", "idb", "lam", "CF", "CB", "decs"))
    identf = cst_t[:, 0:128]
    uincl = cst_t[:, 128:256]
    negus = cst_t[:, 256:384]
    onesf = cst_t[:, 384:512]
    maskf = cst_t[:, 512:640]
    maskb = cst_t[:, 640:768]
    absd = [cst_t[:, 768 + 512 * i: 768 + 512 * (i + 1)] for i in range(4)]
    w1T = gtab_t[:, 0:8]
    w2T = gtab_t[:, 8:16]
    Dbc = gtab_t[:, 16:32]
    ssmw = gtab_t[:, 32:32 + 1024]
    finw = gtab_t[:, 1056:1056 + 1024]
    attw = gtab_t[:, 2080:2080 + 128]
    lamv = gtab_t[:, 2208:2208 + 256]

    P.dma(SP, lambda e: e.dma_start(out=cst_t[:], in_=cst[:, :]), writes=[b_cst], sbuf=b_cst)
    P.dma(SP, lambda e: e.dma_start(out=gtab_t[:], in_=gtab[:, :]), writes=[b_gt], sbuf=b_gt)
    P.op(DVE, lambda e: e.tensor_copy(identb[:], identf), reads=[b_cst], writes=[b_idb])
    umb = sb(G, "umb", [128, 2, 128], BF16)
    b_cstb = P.buf("cstb")
    P.op(DVE, lambda e: e.tensor_copy(umb[:, 0, :], uincl), reads=[b_cst], writes=[b_cstb])
    P.op(DVE, lambda e: e.tensor_copy(umb[:, 1, :], negus), reads=[b_cst], writes=[b_cstb])
    P.op(DVE, lambda e: e.tensor_tensor(lamv[:, 0:64], lamv[:, 0:64], lamv[:, 64:128], ALU.mult),
         reads=[b_gt], writes=[b_gt])
    P.op(DVE, lambda e: e.tensor_tensor(lamv[:, 128:192], lamv[:, 128:192], lamv[:, 192:256], ALU.mult),
         reads=[b_gt], writes=[b_gt])
    P.op(DVE, lambda e: e.reduce_sum(lam_t[:, 0:1], lamv[:, 0:64], mybir.AxisListType.X), reads=[b_gt], writes=[b_lam])
    P.op(DVE, lambda e: e.reduce_sum(lam_t[:, 1:2], lamv[:, 128:192], mybir.AxisListType.X), reads=[b_gt], writes=[b_lam])
    P.op(ACT, lambda e: e.activation(lam_t[:, 2:4], lam_t[:, 0:2], AF.Exp), reads=[b_lam], writes=[b_lam])
    P.op(DVE, lambda e: e.tensor_tensor(lam_t[:, 4:5], lam_t[:, 3:4], lam_t[:, 2:3], ALU.subtract), reads=[b_lam], writes=[b_lam])
    P.op(DVE, lambda e: e.tensor_scalar(lam_t[:, 4:5], lam_t[:, 4:5], -LAM_INIT, None, ALU.add), reads=[b_lam], writes=[b_lam])
    P.op(DVE, lambda e: e.tensor_scalar(attw, attw, 1.0 - LAM_INIT, None, ALU.mult), reads=[b_gt], writes=[b_gt])
    neglam = lam_t[:, 4:5]

    with ExitStack() as st:
        stf = [sb(st, f"stf{i}", [128, 2048]) for i in range(4)]
        stb = [sb(st, f"stb{i}", [128, 2048], BF16) for i in range(4)]
        b_stf = [P.buf(f"stf{i}") for i in range(4)]
        b_stb = [P.buf(f"stb{i}") for i in range(4)]
        cnt = [0]

        def conv_w(W, scr, bscr, K, N):
            for kc in range(K // 128):
                for c0 in range(0, N, 2048):
                    w = min(2048, N - c0)
                    i = cnt[0] % 4
                    ce = (DVE, ACT)[cnt[0] % 2]
                    cnt[0] += 1
                    P.dma(SP, lambda e, i=i, kc=kc, c0=c0, w=w: e.dma_start(out=stf[i][:, 0:w], in_=W[kc * 128:(kc + 1) * 128, c0:c0 + w]),
                          writes=[b_stf[i]], sbuf=b_stf[i])
                    if ce == ACT:
                        P.op(ACT, lambda e, i=i, w=w: e.copy(stb[i][:, 0:w], stf[i][:, 0:w]), reads=[b_stf[i]], writes=[b_stb[i]])
                    else:
                        P.op(ce, lambda e, i=i, w=w: e.tensor_copy(stb[i][:, 0:w], stf[i][:, 0:w]), reads=[b_stf[i]], writes=[b_stb[i]])
                    P.dma(POOL, lambda e, i=i, kc=kc, c0=c0, w=w: e.dma_start(out=scr[:, kc, c0:c0 + w], in_=stb[i][:, 0:w]),
                          reads=[b_stb[i]], writes=[bscr], sbuf=b_stb[i])

        conv_w(w_in, ws_in, b_ws["in"], D, 5664)
        conv_w(w_out, ws_out, b_ws["out"], 2048, D)
        conv_w(w_up, ws_up, b_ws["up"], D, 4096)
        conv_w(w_dn, ws_dn, b_ws["dn"], 4096, D)
        P.barrier()

    def make_uT(st, pfx):
        xs = [sb(st, f"{pfx}xs{i}", [128, D]) for i in range(2)]
        xb = [sb(st, f"{pfx}xb{i}", [128, D], BF16) for i in range(2)]
        sq = sb(st, f"{pfx}sq", [128, D], BF16)
        ss = sb(st, f"{pfx}ss", [128, 4])
        uT = sb(st, f"{pfx}uT", [128, 8, 516], BF16)
        return dict(xs=xs, xb=xb, sq=sq, ss=ss, uT=uT,
                    b_xs=[P.buf() for _ in range(2)], b_xb=[P.buf() for _ in range(2)],
                    b_sq=P.buf(), b_ss=[P.buf(), P.buf()], b_uT=P.buf(), cnt=[0])

    def rms_rstd(eng_sq, src_ap, n, width, sqjunk, b_junk, ssap, b_ss, src_bufs):
        P.op(ACT, lambda e: e.activation(sqjunk, src_ap, AF.Square, accum_out=ssap), reads=src_bufs, writes=[b_junk, b_ss])
        P.op(ACT, lambda e: e.activation(ssap, ssap, AF.Ln, scale=1.0 / width, bias=EPS), reads=[b_ss], writes=[b_ss])
        P.op(ACT, lambda e: e.activation(ssap, ssap, AF.Exp, scale=-0.5), reads=[b_ss], writes=[b_ss])

    def uT_thunks(U, src, W, wT, tp_bank):
        uT = U["uT"]
        tpv = banks[tp_bank][:, :].bitcast(BF16).rearrange("p (k t) -> p k t", k=8)
        out = []
        r0 = 0
        while r0 < W:
            n = min(128, W - r0)

            def mk(r0=r0, n=n):
                st_ = {}

                def stepA():
                    i = U["cnt"][0] % 2
                    U["cnt"][0] += 1
                    st_["i"] = i
                    xs, xb, bxs, bxb = U["xs"][i], U["xb"][i], U["b_xs"][i], U["b_xb"][i]
                    P.dma(SP, lambda e: e.dma_start(out=xs[0:n, :], in_=src[r0:r0 + n, :]), writes=[bxs], sbuf=bxs)
                    ssap = U["ss"][0:n, i:i + 1]
                    rms_rstd(ACT, xs[0:n, :], n, D, U["sq"][0:n, :], U["b_sq"], ssap, U["b_ss"][i], [bxs])
                    P.op(ACT, lambda e: e.activation(xb[0:n, :], xs[0:n, :], AF.Copy, scale=ssap), reads=[bxs, U["b_ss"][i]], writes=[bxb])

                def stepB():
                    i = st_["i"]
                    xb, bxb = U["xb"][i], U["b_xb"][i]
                    for kc in range(8):
                        P.op(PE, lambda e, kc=kc: e.transpose(tpv[:, kc, 0:n], xb[0:n, kc * 128:(kc + 1) * 128], identb[0:n, 0:n]),
                             reads=[bxb, b_idb], writes=[bk[tp_bank]])
                    P.op(DVE, lambda e: e.tensor_tensor(uT[:, :, r0:r0 + n], tpv[:, :, 0:n], wT.unsqueeze(2).to_broadcast([128, 8, n]), ALU.mult),
                         reads=[bk[tp_bank], b_gt], writes=[U["b_uT"]])
                return stepA, stepB
            out.append(mk())
            r0 += n
        return out

    def build_uT(U, src, W, wT, tp_bank):
        for (sa, sb_) in uT_thunks(U, src, W, wT, tp_bank):
            sa()
            sb_()

    def make_ssd(st, pfx):
        S = dict()
        S["tab"] = sb(st, f"{pfx}tab", [128, TT])
        S["raw"] = [sb(st, f"{pfx}raw{i}", [128, 516]) for i in range(2)]
        S["acc"] = [sb(st, f"{pfx}acc{i}", [128, 512]) for i in range(2)]
        S["xcT"] = sb(st, f"{pfx}xcT", [128, 12, 512], BF16)
        S["xtm"] = sb(st, f"{pfx}xtm", [128, 4, D], BF16)
        S["btm"] = sb(st, f"{pfx}btm", [128, 4, 256], BF16)
        S["dt"] = sb(st, f"{pfx}dt", [128, 4, 32])
        S["adt"] = sb(st, f"{pfx}adt", [128, 4, 32])
        S["cs"] = sb(st, f"{pfx}cs", [128, 4, 32])
        S["tot"] = sb(st, f"{pfx}tot", [128, 4, 32])
        S["A2"] = sb(st, f"{pfx}A2", [128, 32])
        S["tmp32"] = sb(st, f"{pfx}tmp32", [128, 32])
        for k in ("tab", "xcT", "xtm", "btm", "dtq", "A2", "tmp32"):
            S["b_" + k] = P.buf(pfx + k)
        S["b_raw"] = [P.buf() for _ in range(2)]
        S["b_acc"] = [P.buf() for _ in range(2)]
        S["n"] = 0
        return S

    def ssd_par(st, S, pfx):
        S2 = dict(S)
        S2["tab"] = sb(st, f"{pfx}tabB", [128, TT])
        S2["dt"] = sb(st, f"{pfx}dtB", [128, 4, 32])
        S2["adt"] = sb(st, f"{pfx}adtB", [128, 4, 32])
        S2["cs"] = sb(st, f"{pfx}csB", [128, 4, 32])
        S2["tot"] = sb(st, f"{pfx}totB", [128, 4, 32])
        S2["A2"] = sb(st, f"{pfx}A2B", [128, 32])
        for k in ("tab", "dtq", "A2"):
            S2["b_" + k] = P.buf()
        return S2

    def ssd_thunks(S, U, j, t, T, W, wv, cx, cdt, pb_main, pb_small, pb_tp):
        uT = U["uT"]
        tab = S["tab"]
        nchk = max(1, T // 128)
        cs_ = min(T, 128)
        def pre():
            P.dma(SP, lambda e: e.dma_start(out=tab[:], in_=ttab[j][t, :, :]), writes=[S["b_tab"]], sbuf=S["b_tab"])
            P.op(ACT, lambda e: e.activation(S["A2"][:], tab[:, 104:136], AF.Exp), reads=[S["b_tab"]], writes=[S["b_A2"]])
            P.op(DVE, lambda e: e.tensor_scalar(S["A2"][:], S["A2"][:], -1.0, None, ALU.mult), reads=[S["b_A2"]], writes=[S["b_A2"]])
            for k in range(nchk):
                for kc in range(8):
                    P.op(PE, lambda e, kc=kc, k=k: e.matmul(banks[pb_small][0:cs_, 64 + 32 * k:96 + 32 * k], uT[:, kc, 2 + 128 * k:2 + 128 * k + cs_], wv[:, kc, cdt:cdt + 32],
                                                            start=(kc == 0), stop=(kc == 7)),
                         reads=[U["b_uT"], b_wres], writes=[bk[pb_small]])
            dtr = banks[pb_small][0:cs_, 64:64 + 32 * nchk].rearrange("p (k c) -> p k c", c=32)
            dt, adt = S["dt"], S["adt"]
            P.op(DVE, lambda e: e.tensor_copy(dt[0:cs_, 0:nchk, 16:32], dtr[:, :, 16:32]), reads=[bk[pb_small]], writes=[S["b_dtq"]])
            P.op(DVE, lambda e: e.tensor_scalar(dt[0:cs_, 0:nchk, 0:16], dtr[:, :, 0:16], tab[0:cs_, 136:137], None, ALU.mult),
                 reads=[bk[pb_small], S["b_tab"]], writes=[S["b_dtq"]])
            P.op(DVE, lambda e: e.scalar_tensor_tensor(dt[0:cs_, 0:nchk, 0:16], dt[0:cs_, 0:nchk, 16:32], tab[0:cs_, 137:138], dt[0:cs_, 0:nchk, 0:16], ALU.mult, ALU.add),
                 reads=[S["b_dtq"], S["b_tab"]], writes=[S["b_dtq"]])
            P.op(DVE, lambda e: e.tensor_tensor(dt[0:cs_, 0:nchk, :], dt[0:cs_, 0:nchk, :], tab[0:cs_, 72:104].unsqueeze(1).to_broadcast([cs_, nchk, 32]), ALU.add),
                 reads=[S["b_dtq"], S["b_tab"]], writes=[S["b_dtq"]])
            P.op(ACT, lambda e: e.activation(dt[0:cs_, 0:nchk, :], dt[0:cs_, 0:nchk, :], AF.Exp), reads=[S["b_dtq"]], writes=[S["b_dtq"]])
            P.op(ACT, lambda e: e.activation(dt[0:cs_, 0:nchk, :], dt[0:cs_, 0:nchk, :], AF.Ln, bias=1.0), reads=[S["b_dtq"]], writes=[S["b_dtq"]])
            P.op(DVE, lambda e: e.tensor_tensor(adt[0:cs_, 0:nchk, :], dt[0:cs_, 0:nchk, :], S["A2"][0:cs_, :].unsqueeze(1).to_broadcast([cs_, nchk, 32]), ALU.mult),
                 reads=[S["b_dtq"], S["b_A2"]], writes=[S["b_dtq"]])
            for k in range(nchk):
                P.op(PE, lambda e, k=k: e.matmul(banks[pb_small][0:cs_, 192 + 32 * k:224 + 32 * k], uincl[0:cs_, 0:cs_], adt[0:cs_, k, :], start=True, stop=True),
                     reads=[S["b_dtq"], b_cst], writes=[bk[pb_small]])
                P.op(PE, lambda e, k=k: e.matmul(banks[pb_small][:, 320 + 32 * k:352 + 32 * k], onesf[0:cs_, :], adt[0:cs_, k, :], start=True, stop=True),
                     reads=[S["b_dtq"], b_cst], writes=[bk[pb_small]])
            P.op(DVE, lambda e: e.tensor_copy(S["cs"][0:cs_, 0:nchk, :], banks[pb_small][0:cs_, 192:192 + 32 * nchk].rearrange("p (k c) -> p k c", c=32)),
                 reads=[bk[pb_small]], writes=[S["b_dtq"]])
            P.op(DVE, lambda e: e.tensor_copy(S["tot"][:, 0:nchk, :], banks[pb_small][:, 320:320 + 32 * nchk].rearrange("p (k c) -> p k c", c=32)),
                 reads=[bk[pb_small]], writes=[S["b_dtq"]])

        pend_silu = []

        def one_fc(fc):
            i = S["n"] % 2
            S["n"] += 1
            raw, acc, braw, bacc = S["raw"][i], S["acc"][i], S["b_raw"][i], S["b_acc"][i]
            pbm = pb_main[fc % len(pb_main)]
            wa = min(W, 512)
            for kc in range(8):
                P.op(PE, lambda e, kc=kc, fc=fc, pbm=pbm, wa=wa: e.matmul(banks[pbm][:, 0:wa], wv[:, kc, cx + fc * 128: cx + (fc + 1) * 128], uT[:, kc, 0:wa],
                                                                   start=(kc == 0), stop=(kc == 7)),
                     reads=[U["b_uT"], b_wres], writes=[bk[pbm]])
            P.op(ACT, lambda e, raw=raw, pbm=pbm, wa=wa: e.copy(raw[:, 0:wa], banks[pbm][:, 0:wa]), reads=[bk[pbm]], writes=[braw])
            if W > 512:
                for kc in range(8):
                    P.op(PE, lambda e, kc=kc, fc=fc: e.matmul(banks[pb_small][:, 0:W - 512], wv[:, kc, cx + fc * 128: cx + (fc + 1) * 128], uT[:, kc, 512:W],
                                                             start=(kc == 0), stop=(kc == 7)),
                         reads=[U["b_uT"], b_wres], writes=[bk[pb_small]])
                P.op(ACT, lambda e, raw=raw: e.copy(raw[:, 512:W], banks[pb_small][:, 0:W - 512]), reads=[bk[pb_small]], writes=[braw])
            flush_silu()
            P.op(DVE, lambda e, raw=raw, acc=acc, fc=fc: e.tensor_scalar(acc[:, 0:T], raw[:, 0:T], tab[:, fc * 5:fc * 5 + 1], tab[:, 60 + fc:61 + fc], ALU.mult, ALU.add),
                 reads=[braw, S["b_tab"]], writes=[bacc])
            for jj in range(1, 5):
                P.op(DVE, lambda e, raw=raw, acc=acc, fc=fc, jj=jj: e.scalar_tensor_tensor(acc[:, 0:T], raw[:, jj:jj + T], tab[:, fc * 5 + jj:fc * 5 + jj + 1], acc[:, 0:T], ALU.mult, ALU.add),
                     reads=[braw, S["b_tab"], bacc], writes=[bacc])
            pend_silu.append((acc, bacc, fc))

        def flush_silu():
            while pend_silu:
                acc, bacc, fc = pend_silu.pop(0)
                P.op(ACT, lambda e: e.activation(S["xcT"][:, fc, 0:T], acc[:, 0:T], AF.Silu), reads=[bacc], writes=[S["b_xcT"]])

        def post():
            flush_silu()
            tpv = banks[pb_tp][:, :].bitcast(BF16)
            for k in range(nchk):
                for fc in range(8):
                    P.op(PE, lambda e, k=k, fc=fc: e.transpose(tpv[0:cs_, fc * 128:(fc + 1) * 128], S["xcT"][:, fc, 128 * k:128 * k + cs_], identb[:, :]),
                         reads=[S["b_xcT"], b_idb], writes=[bk[pb_tp]])
                P.op(ACT, lambda e, k=k: e.copy(S["xtm"][0:cs_, k, :], tpv[0:cs_, :]), reads=[bk[pb_tp]], writes=[S["b_xtm"]])
                for g in range(2):
                    P.op(PE, lambda e, k=k, g=g: e.transpose(tpv[0:cs_, g * 128:(g + 1) * 128], S["xcT"][:, 8 + g, 128 * k:128 * k + cs_], identb[:, :]),
                         reads=[S["b_xcT"], b_idb], writes=[bk[pb_tp]])
                P.op(ACT, lambda e, k=k: e.copy(S["btm"][0:cs_, k, :], tpv[0:cs_, 0:256]), reads=[bk[pb_tp]], writes=[S["b_btm"]])
        return pre, [(lambda fc=fc: one_fc(fc)) for fc in range(12)], post

    def ssd_prep(S, U, j, t, T, W, wv, cx, cdt, pb_main, pb_small, pb_tp):
        pre, fcs, post = ssd_thunks(S, U, j, t, T, W, wv, cx, cdt, pb_main, pb_small, pb_tp)
        pre()
        for f in fcs:
            f()
        post()

    b_wres = P.buf("wres")

    for j in range(NJ):
        nt = cfg.ntiles(j)
        nctx = cfg.jobs[j]
        with ExitStack() as st:
            mkS = P.mark()
            wS = sb(st, "wS", [128, 8, 4640], BF16)
            b_wq = P.buf()
            P.dma(SP, lambda e: e.dma_start(out=wS[:, :, 3072:4640], in_=ws_in[:, :, 4096:5664]), reads=[b_ws["in"]], writes=[b_wres], sbuf=b_wres)
            P.dma(SP, lambda e: e.dma_start(out=wS[:, :, 0:3072], in_=ws_in[:, :, 0:3072]), reads=[b_ws["in"]], writes=[b_wq], sbuf=b_wq)
            for m in range(2):
                P.dma(POOL, lambda e, m=m: e.dma_start(out=KT[:, m, 64:72, 0:cfg.L(j)], in_=kaug[j][:, :, :]), writes=[b_KT], sbuf=b_KT)
            U0 = make_uT(st, "s")
            U1 = dict(U0)
            U1["uT"] = sb(st, "suT1", [128, 8, 516], BF16)
            U1["b_uT"] = P.buf()
            Us = [U0, U1]
            S_0 = make_ssd(st, "s")
            S_1 = ssd_par(st, S_0, "s")
            Ss = [S_0, S_1]
            kst = [sb(st, f"kst{i}", [128, 512], BF16) for i in range(3)]
            b_kst = [P.buf() for _ in range(3)]
            vst = sb(st, "vst", [128, 8, 4, 129], BF16)
            b_vst = P.buf("vst")
            wgt = sb(st, "wgt", [128, 32])
            xwp = sb(st, "xwp", [128, 16, 64], BF16)
            xws = sb(st, "xws", [128, 16, 64], BF16)
            decp = sb(st, "decp", [128, 32])
            snap = [sb(st, f"snap{i}", [128, D], BF16) for i in range(2)]
            b_wgt, b_xwp, b_xws, b_decp = P.buf(), P.buf(), P.buf(), P.buf()
            b_snap = [P.buf() for _ in range(2)]
            P.op(POOL, lambda e: e.memset(vst[:], 1.0), writes=[b_vst])
            P.op(POOL, lambda e: e.memset(CF[:], 0.0), writes=[b_CF])
            P.op(POOL, lambda e: e.memset(CB[:], 0.0), writes=[b_CB])
            kn = [0]
            sn = [0]

            def make_states(S, t, own, oi, cs_, nchk):
                def one_chunk(k):
                    jj = oi * 4 + k
                    dt, cs, tot = S["dt"], S["cs"], S["tot"]
                    nw = 32 if own else 16
                    xt3 = S["xtm"][0:cs_, k, :].rearrange("p (h d) -> p h d", h=16)
                    P.op(DVE, lambda e: e.tensor_tensor(wgt[0:cs_, 0:16], tot[0:cs_, k, 0:16], cs[0:cs_, k, 0:16], ALU.subtract), reads=[S["b_dtq"]], writes=[b_wgt])
                    if own:
                        P.op(DVE, lambda e: e.tensor_tensor(wgt[0:cs_, 16:32], cs[0:cs_, k, 16:32], S["adt"][0:cs_, k, 16:32], ALU.subtract), reads=[S["b_dtq"], b_wgt], writes=[b_wgt])
                    yield
                    P.op(ACT, lambda e: e.activation(wgt[0:cs_, 0:nw], wgt[0:cs_, 0:nw], AF.Exp), reads=[b_wgt], writes=[b_wgt])
                    P.op(ACT, lambda e: e.activation(decp[:, :], tot[:, k, :], AF.Exp), reads=[S["b_dtq"]], writes=[b_decp])
                    yield
                    P.op(DVE, lambda e: e.tensor_tensor(wgt[0:cs_, 0:nw], wgt[0:cs_, 0:nw], dt[0:cs_, k, 0:nw], ALU.mult), reads=[b_wgt, S["b_dtq"]], writes=[b_wgt])
                    P.op(DVE, lambda e: e.tensor_tensor(xwp[0:cs_, :, :], xt3, wgt[0:cs_, 0:16].unsqueeze(2).to_broadcast([cs_, 16, 64]), ALU.mult),
                         reads=[S["b_xtm"], b_wgt], writes=[b_xwp])
                    if own:
                        P.op(DVE, lambda e: e.tensor_tensor(xws[0:cs_, :, :], xt3, wgt[0:cs_, 16:32].unsqueeze(2).to_broadcast([cs_, 16, 64]), ALU.mult),
                             reads=[S["b_xtm"], b_wgt], writes=[b_xws])
                    yield
                    for g in range(2):
                        P.op(PE, lambda e, g=g: e.matmul(banks[6 + g][:, :], S["btm"][0:cs_, k, g * 128:(g + 1) * 128], xwp[0:cs_, g * 8:(g + 1) * 8, :].rearrange("p h d -> p (h d)"),
                                                         start=True, stop=True),
                             reads=[S["b_btm"], b_xwp], writes=[bk[6 + g]])
                    if not own:
                        carry, bcar = CB, b_CB
                    else:
                        carry, bcar = CF, b_CF
                        i = sn[0] % 2
                        sn[0] += 1
                        P.op(ACT, lambda e: e.copy(snap[i][:], CF[:]), reads=[b_CF], writes=[b_snap[i]])
                        P.dma(POOL, lambda e: e.dma_start(out=PFB[0, jj, :, :], in_=snap[i][:]), reads=[b_snap[i]], writes=[b_PFB], sbuf=b_snap[i])
                    yield
                    c3 = carry[:].rearrange("p (h d) -> p h d", h=16)
                    P.op(DVE, lambda e: e.tensor_tensor(c3, c3, decp[:, 0:16].unsqueeze(2).to_broadcast([128, 16, 64]), ALU.mult), reads=[bcar, b_decp], writes=[bcar])
                    for g in range(2):
                        P.op(DVE, lambda e, g=g: e.tensor_tensor(carry[:, g * 512:(g + 1) * 512], carry[:, g * 512:(g + 1) * 512], banks[6 + g][:, :], ALU.add),
                             reads=[bcar, bk[6 + g]], writes=[bcar])
                    if own:
                        P.op(POOL, lambda e: e.tensor_copy(decs[:, jj, :], decp[:, 16:32]), reads=[b_decp], writes=[b_decs])
                    yield
                    if own:
                        i2 = sn[0] % 2
                        sn[0] += 1
                        for g in range(2):
                            P.op(PE, lambda e, g=g: e.matmul(banks[6 + g][:, :], S["btm"][0:cs_, k, g * 128:(g + 1) * 128], xws[0:cs_, g * 8:(g + 1) * 8, :].rearrange("p h d -> p (h d)"),
                                                             start=True, stop=True),
                                 reads=[S["b_btm"], b_xws], writes=[bk[6 + g]])
                            P.op(ACT, lambda e, g=g: e.copy(snap[i2][:, g * 512:(g + 1) * 512], banks[6 + g][:, :]), reads=[bk[6 + g]], writes=[b_snap[i2]])
                        P.dma(POOL, lambda e: e.dma_start(out=SBS[jj, :, :], in_=snap[i2][:]), reads=[b_snap[i2]], writes=[b_SBS], sbuf=b_snap[i2])
                    yield

                def stages(k):
                    gen = one_chunk(k)
                    return [(lambda: next(gen, None)) for _ in range(6)]

                def tile_end():
                    if not own:
                        P.op(DVE, lambda e: e.scalar_tensor_tensor(CF[:], CB[:], S["tab"][:, 138:139], CF[:], ALU.mult, ALU.add), reads=[b_CB, b_CF, S["b_tab"]], writes=[b_CF])
                        P.op(DVE, lambda e: e.tensor_scalar(CB[:], CB[:], S["tab"][:, 139:140], None, ALU.mult), reads=[b_CB, S["b_tab"]], writes=[b_CB])
                    return None
                out = []
                for k in range(nchk):
                    out += stages(k)
                return out + [tile_end]

            pend_states = []
            for t in range(nt):
                T = 16 if t == 0 else 512
                W = T + 4
                own = t > nctx
                oi = t - nctx - 1
                soff = 0 if t == 0 else 16 + 512 * (t - 1)
                ch0 = 0 if t == 0 else 1 + 4 * (t - 1)
                nchk = max(1, T // 128)
                cs_ = min(T, 128)
                U = Us[t % 2]
                S = Ss[t % 2]
                if t == 0:
                    build_uT(U, xw[j][0], W, w1T, 0)
                uT = U["uT"]
                side = []
                if t + 1 < nt:
                    th = uT_thunks(Us[(t + 1) % 2], xw[j][t + 1], 516, w1T, 0)
                    side.append(th[0][0])
                    for q_ in range(1, len(th)):
                        side.append(th[q_][0])
                        side.append(th[q_ - 1][1])
                    side.append(th[-1][1])
                if pend_states:
                    merged = []
                    ps_ = list(pend_states)
                    while side or ps_:
                        if side:
                            merged.append(side.pop(0))
                        if side:
                            merged.append(side.pop(0))
                        if ps_:
                            merged.append(ps_.pop(0))
                    side = merged
                    pend_states = []
                pre, fcs, post = ssd_thunks(S, U, j, t, T, W, wS, 3072, 4608, [1, 2], 3, 4)
                main = [pre] + fcs

                def kq_group(isq, h, T=T, uT=uT, U=U, soff=soff, oi=oi):
                    cbase = (0 if isq else 1024) + h * 128
                    pb = 1 + (kn[0] % 2)
                    i = kn[0] % 3
                    kn[0] += 1
                    for kc in range(8):
                        P.op(PE, lambda e, kc=kc: e.matmul(banks[pb][:, 0:T], wS[:, kc, cbase:cbase + 128], uT[:, kc, 2:2 + T], start=(kc == 0), stop=(kc == 7)),
                             reads=[U["b_uT"], b_wq], writes=[bk[pb]])
                    if isq:
                        P.op(ACT, lambda e: e.activation(kst[i][:, 0:T], banks[pb][:, 0:T], AF.Copy, scale=0.125), reads=[bk[pb]], writes=[b_kst[i]])
                        for m in range(2):
                            P.dma(POOL, lambda e, m=m: e.dma_start(out=QT[h, m, :, oi * 512:oi * 512 + T], in_=kst[i][64 * m:64 * m + 64, 0:T]),
                                  reads=[b_kst[i]], writes=[b_QT], sbuf=b_kst[i])
                    else:
                        P.op(ACT, lambda e: e.copy(kst[i][:, 0:T], banks[pb][:, 0:T]), reads=[bk[pb]], writes=[b_kst[i]])
                        for m in range(2):
                            P.dma(POOL, lambda e, m=m: e.dma_start(out=KT[h, m, 0:64, soff:soff + T], in_=kst[i][64 * m:64 * m + 64, 0:T]),
                                  reads=[b_kst[i]], writes=[b_KT], sbuf=b_kst[i])

                for isq in ([False, True] if own else [False]):
                    for h in range(NH):
                        main.append(lambda isq=isq, h=h: kq_group(isq, h))

                def v_group(k, half, uT=uT, U=U, cs_=cs_):
                    pb = 1 + (kn[0] % 2)
                    kn[0] += 1
                    for kc in range(8):
                        P.op(PE, lambda e, kc=kc: e.matmul(banks[pb][0:cs_, :], uT[:, kc, 2 + 128 * k:2 + 128 * k + cs_], wS[:, kc, 2048 + half * 512:2048 + (half + 1) * 512],
                                                           start=(kc == 0), stop=(kc == 7)),
                             reads=[U["b_uT"], b_wq], writes=[bk[pb]])
                    P.op(ACT, lambda e: e.copy(vst[0:cs_, half * 4:(half + 1) * 4, k, 0:128], banks[pb][0:cs_, :].rearrange("p (h e) -> p h e", h=4)),
                         reads=[bk[pb]], writes=[b_vst])

                for k in range(nchk):
                    for half in range(2):
                        main.append(lambda k=k, half=half: v_group(k, half))

                def v_store(t=t, ch0=ch0):
                    if t == 0:
                        P.dma(POOL, lambda e: e.dma_start(out=VA[:, 0:16, 0, :].rearrange("h p e -> p h e"), in_=vst[0:16, :, 0, :]),
                              reads=[b_vst], writes=[b_VA], sbuf=b_vst)
                    else:
                        P.dma(POOL, lambda e: e.dma_start(out=VA[:, :, ch0:ch0 + 4, :].rearrange("h p c e -> p h c e"), in_=vst[:, :, :, :]),
                              reads=[b_vst], writes=[b_VA], sbuf=b_vst)
                main.append(v_store)
                stride = max(1, len(main) // (len(side) + 1)) if side else len(main)
                si_ = 0
                for mi, f in enumerate(main):
                    f()
                    if side and (mi + 1) % stride == 0 and si_ < len(side):
                        side[si_]()
                        si_ += 1
                while si_ < len(side):
                    side[si_]()
                    si_ += 1
                post()
                pend_states = make_states(S, t, own, oi, cs_, nchk)
            for f in pend_states:
                f()
            pend_states = []
            ldb = [sb(st, f"ldb{i}", [128, D], BF16) for i in range(2)]
            b_ldb = [P.buf() for _ in range(2)]
            for jj in range(4 * NOWN - 1, -1, -1):
                i = sn[0] % 2
                sn[0] += 1
                P.op(ACT, lambda e, i=i: e.copy(snap[i][:], CB[:]), reads=[b_CB], writes=[b_snap[i]])
                P.dma(POOL, lambda e, i=i, jj=jj: e.dma_start(out=PFB[1, jj, :, :], in_=snap[i][:]), reads=[b_snap[i]], writes=[b_PFB], sbuf=b_snap[i])
                P.dma(SP, lambda e, i=i, jj=jj: e.dma_start(out=ldb[i][:], in_=SBS[jj, :, :]), reads=[b_SBS], writes=[b_ldb[i]], sbuf=b_ldb[i])
                c3 = CB[:].rearrange("p (h d) -> p h d", h=16)
                P.op(DVE, lambda e, c3=c3, jj=jj: e.tensor_tensor(c3, c3, decs[:, jj, :].unsqueeze(2).to_broadcast([128, 16, 64]), ALU.mult), reads=[b_CB, b_decs], writes=[b_CB])
                P.op(DVE, lambda e, i=i: e.tensor_tensor(CB[:], CB[:], ldb[i][:], ALU.add), reads=[b_CB, b_ldb[i]], writes=[b_CB])
            P.barrier()
            P.release(mkS)

        with ExitStack() as stT:
            mkT = P.mark()
            wO = sb(stT, "wO", [128, 8, 2592], BF16)
            P.dma(SP, lambda e: e.dma_start(out=wO[:, :, 0:1024], in_=ws_in[:, :, 3072:4096]), reads=[b_ws["in"]], writes=[b_wres], sbuf=b_wres)
            P.dma(SP, lambda e: e.dma_start(out=wO[:, :, 1024:2592], in_=ws_in[:, :, 4096:5664]), reads=[b_ws["in"]], writes=[b_wres], sbuf=b_wres)
            mixT = sb(stT, "mixT", [128, 16, 512], BF16)
            b_mix = P.buf("mixT")
            UT = make_uT(stT, "t")
            wpn = [0]
            nk_chunks = cfg.nch(j)
            for oi in range(NOWN):
                t_own = nctx + 1 + oi
                with ExitStack() as st:
                    mkA = P.mark()
                    qv = [[sb(st, f"qv{r}_{v}", [72, 2, 512], BF16) for v in range(3)] for r in range(2)]
                    b_qv = [P.buf() for _ in range(2)]
                    NKV = 4
                    kp = [sb(st, f"kp{i}", [72, 2, 2048], BF16) for i in range(NKV)]
                    vp = [sb(st, f"vp{i}", [128, 16, 129], BF16) for i in range(NKV)]
                    b_kv = [P.buf() for _ in range(NKV)]
                    tmpb = [sb(st, f"tmpb{i}", [128, 1024]) for i in range(2)]
                    b_tmpb = [P.buf() for _ in range(2)]
                    pt = [sb(st, f"pt{i}", [128, 1024], BF16) for i in range(3)]
                    b_pt = [P.buf() for _ in range(3)]
                    ob = sb(st, "ob", [128, 8, 129])
                    rl = sb(st, "rl", [128, 8])
                    o1 = sb(st, "o1", [128, 4, 128])
                    o2 = sb(st, "o2", [128, 4, 128])
                    ssq = sb(st, "ssq", [128, 4])
                    junk = sb(st, "junk", [128, 128], BF16)
                    junkf = sb(st, "junkf", [128, 128])
                    attb = sb(st, "attb", [128, 4, 128], BF16)
                    b_ob, b_rl, b_o1, b_o2, b_ssq, b_junk, b_attb = (P.buf() for _ in range(7))
                    pn = [0]
                    sbn = [0]
                    pieces = [(0, 1)] + [(c0, min(16, nk_chunks - c0)) for c0 in range(1, nk_chunks, 16)]
                    acc_v = [banks[4 + a // 3][:, (a % 3) * 129:(a % 3) * 129 + 129] for a in range(8)]
                    epi_gen = [None]
                    carry = [None]
                    th_ = uT_thunks(UT, xw[j][t_own], 516, w1T, 7)
                    ut_side = [th_[0][0]]
                    for q_ in range(1, len(th_)):
                        ut_side.append(th_[q_][0])
                        ut_side.append(th_[q_ - 1][1])
                    ut_side.append(th_[-1][1])

                    def epilogue(h):
                        P.op(DVE, lambda e: e.reciprocal(rl[:, :].unsqueeze(2), ob[:, :, 128:129]), reads=[b_ob], writes=[b_rl])
                        P.op(DVE, lambda e: e.tensor_tensor(o1[:, :, :], ob[:, 0:4, 0:128], rl[:, 0:4].unsqueeze(2).to_broadcast([128, 4, 128]), ALU.mult), reads=[b_ob, b_rl], writes=[b_o1])
                        P.op(DVE, lambda e: e.tensor_tensor(o2[:, :, :], ob[:, 4:8, 0:128], rl[:, 4:8].unsqueeze(2).to_broadcast([128, 4, 128]), ALU.mult), reads=[b_ob, b_rl], writes=[b_o2])
                        P.op(DVE, lambda e: e.scalar_tensor_tensor(o1[:, :, :], o2[:, :, :], neglam, o1[:, :, :], ALU.mult, ALU.add), reads=[b_o1, b_o2, b_lam], writes=[b_o1])
                        yield
                        for qc in range(4):
                            P.op(DVE, lambda e, qc=qc: e.scalar_tensor_tensor(junkf[:, :], o1[:, qc, :], 1.0, o1[:, qc, :], ALU.mult, ALU.mult, accum_out=ssq[:, qc:qc + 1]),
                                 reads=[b_o1], writes=[b_junk, b_ssq])
                        yield
                        P.op(ACT, lambda e: e.activation(ssq[:, :], ssq[:, :], AF.Ln, scale=1.0 / 128, bias=EPS), reads=[b_ssq], writes=[b_ssq])
                        P.op(ACT, lambda e: e.activation(ssq[:, :], ssq[:, :], AF.Exp, scale=-0.5), reads=[b_ssq], writes=[b_ssq])
                        yield
                        P.op(DVE, lambda e: e.tensor_tensor(o1[:, :, :], o1[:, :, :], ssq[:, :].unsqueeze(2).to_broadcast([128, 4, 128]), ALU.mult), reads=[b_o1, b_ssq], writes=[b_o1])
                        P.op(DVE, lambda e: e.tensor_tensor(attb[:, :, :], o1[:, :, :], attw.unsqueeze(1).to_broadcast([128, 4, 128]), ALU.mult), reads=[b_o1, b_gt], writes=[b_attb])
                        yield
                        tpv = banks[7][:, :].bitcast(BF16)
                        for qc in range(4):
                            P.op(PE, lambda e, qc=qc: e.transpose(tpv[:, qc * 128:(qc + 1) * 128], attb[:, qc, :], identb[:, :]), reads=[b_attb, b_idb], writes=[bk[7]])
                        yield
                        P.op(DVE, lambda e: e.tensor_copy(mixT[:, h, :], banks[7][:, :].bitcast(BF16)[:, 0:512]), reads=[bk[7]], writes=[b_mix])
                        yield


                    for h in range(NH):
                        slope = 2.0 ** (-(h + 1))
                        r = h % 2
                        for v in range(3):
                            P.dma(SP, lambda e, r=r, v=v, h=h: e.dma_start(out=qv[r][v][0:64, :, :], in_=QT[h, :, :, oi * 512:(oi + 1) * 512].rearrange("m d q -> d m q")),
                                  reads=[b_QT], writes=[b_qv[r]], sbuf=b_qv[r])
                            for m in range(2):
                                P.dma(SP, lambda e, r=r, v=v, h=h, m=m: e.dma_start(out=qv[r][v][64:72, m, :], in_=qaug[j][v, h, :, oi * 512:(oi + 1) * 512]),
                                      writes=[b_qv[r]], sbuf=b_qv[r])
                        first_in_bank = {4: True, 5: True, 6: True}
                        steps = []
                        for (c0, ncn) in pieces:
                            for ci in range(ncn):
                                steps.append((c0, ncn, ci))
                        stinfo = {}

                        def emit_qk(si, h=h, r=r, slope=slope):
                            c0, ncn, ci = steps[si]
                            if ci == 0:
                                pi = pn[0] % NKV
                                pn[0] += 1
                                koff0 = 0 if c0 == 0 else 16 + 128 * (c0 - 1)
                                klen = 16 if c0 == 0 else 128 * ncn
                                P.dma(SP, lambda e: e.dma_start(out=kp[pi][:, :, 0:klen], in_=KT[h, :, :, koff0:koff0 + klen].rearrange("m r l -> r m l")),
                                      reads=[b_KT], writes=[b_kv[pi]], sbuf=b_kv[pi])
                                if c0 == 0:
                                    P.dma(SP, lambda e: e.dma_start(out=vp[pi][0:16, 0, :], in_=VA[h, 0:16, 0, :]), reads=[b_VA], writes=[b_kv[pi]], sbuf=b_kv[pi])
                                else:
                                    P.dma(SP, lambda e: e.dma_start(out=vp[pi][:, 0:ncn, :], in_=VA[h, :, c0:c0 + ncn, :]), reads=[b_VA], writes=[b_kv[pi]], sbuf=b_kv[pi])
                                stinfo["pi"] = pi
                            pi = stinfo["pi"]
                            c = c0 + ci
                            ks = 16 if c == 0 else 128
                            kt = 0 if c == 0 else 1 + (c - 1) // 4
                            kk = (c - 1) % 4
                            if kt <= nctx:
                                var, KR, ovl = 0, 72, False
                            elif kt < t_own:
                                var, KR, ovl = 1, 72, False
                            elif kt > t_own:
                                var, KR, ovl = 2, 72, False
                            else:
                                var, KR, ovl = 0, 64, True
                            sbi = sbn[0] % 2
                            pti = sbn[0] % 3
                            sbn[0] += 1
                            Bs = (2 * sbi, 2 * sbi + 1)
                            for m in range(2):
                                P.op(PE, lambda e, m=m: e.matmul(banks[Bs[m]][0:ks, :], kp[pi][0:KR, m, 128 * ci:128 * ci + ks], qv[r][var][0:KR, m, :], start=True, stop=True),
                                     reads=[b_kv[pi], b_qv[r]], writes=[bk[Bs[m]]])
                            if ovl:
                                tb = tmpb[sbi]
                                for m in range(2):
                                    P.op(DVE, lambda e, m=m: e.scalar_tensor_tensor(tb[:, m * 512:(m + 1) * 512], absd[kk], -slope, banks[Bs[m]][:, :], ALU.mult, ALU.add),
                                         reads=[b_cst, bk[Bs[m]]], writes=[b_tmpb[sbi]])
                                P.op(ACT, lambda e: e.activation(pt[pti][:, :], tb[:, :], AF.Exp), reads=[b_tmpb[sbi]], writes=[b_pt[pti]])
                            else:
                                P.op(ACT, lambda e: e.activation(pt[pti][0:ks, :], allbanks[0:ks, 512 * Bs[0]:512 * Bs[0] + 1024], AF.Exp),
                                     reads=[bk[Bs[0]], bk[Bs[1]]], writes=[b_pt[pti]])
                            return (pi, ci, ks, pti)

                        def emit_pv(info, first_in_bank=first_in_bank):
                            pi, ci, ks, pti = info
                            for a in range(8):
                                m, qc = a // 4, a % 4
                                bkn = 4 + a // 3
                                stt = first_in_bank[bkn]
                                first_in_bank[bkn] = False
                                P.op(PE, lambda e, a=a, m=m, qc=qc, stt=stt: e.matmul(acc_v[a], pt[pti][0:ks, m * 512 + qc * 128:m * 512 + (qc + 1) * 128], vp[pi][0:ks, ci, :],
                                                                                 start=stt, stop=False, skip_group_check=True),
                                     reads=[b_pt[pti], b_kv[pi]], writes=[bk[bkn]])

                        prev = None
                        for si in range(len(steps)):
                            cur = emit_qk(si)
                            if si == 0 and carry[0] is not None:
                                carry[0]()
                                carry[0] = None
                            if prev is not None:
                                emit_pv(prev)
                            prev = cur
                            if epi_gen[0] is not None and si % 2 == 1:
                                if next(epi_gen[0], "done") == "done":
                                    epi_gen[0] = None
                            if ut_side and si % 2 == 0 and si >= 2:
                                ut_side.pop(0)()

                        def finish(prev=prev, emit_pv=emit_pv, h=h):
                            emit_pv(prev)
                            while epi_gen[0] is not None:
                                if next(epi_gen[0], "done") == "done":
                                    epi_gen[0] = None
                            for bi in range(3):
                                na = 3 if bi < 2 else 2
                                P.op(DVE, lambda e, bi=bi, na=na: e.tensor_copy(ob[:, bi * 3:bi * 3 + na, :], banks[4 + bi][:, 0:129 * na].rearrange("p (a c) -> p a c", c=129)),
                                     reads=[bk[4 + bi]], writes=[b_ob])
                            epi_gen[0] = epilogue(h)

                        carry[0] = finish
                    carry[0]()
                    while epi_gen[0] is not None:
                        if next(epi_gen[0], "done") == "done":
                            epi_gen[0] = None
                    while ut_side:
                        ut_side.pop(0)()
                    P.barrier()
                    P.release(mkA)
                with ExitStack() as st:
                    mkO = P.mark()
                    U = UT
                    S = make_ssd(st, "o")
                    sz = sb(st, "sz", [128, 4, D], BF16)
                    b_sz = P.buf()
                    pfb = [sb(st, f"pfb{i}", [128, 2, D], BF16) for i in range(2)]
                    b_pfb = [P.buf() for _ in range(2)]
                    gtm2 = [sb(st, f"gtm{i}", [128, 4, 128]) for i in range(2)]
                    b_gtm2 = [P.buf() for _ in range(2)]
                    ef = [sb(st, f"ef{i}", [128, 128]) for i in range(6)]
                    b_ef = [P.buf() for _ in range(6)]
                    mt = [sb(st, f"mt{i}", [128, 128], BF16) for i in range(6)]
                    b_mt = [P.buf() for _ in range(6)]
                    xdt2 = [sb(st, f"xdt{i}", [128, 2, 16, 64], BF16) for i in range(2)]
                    b_xdt2 = [P.buf() for _ in range(2)]
                    bia2 = [sb(st, f"bia{i}", [128, 64]) for i in range(2)]
                    b_bia2 = [P.buf() for _ in range(2)]
                    yt_2 = [sb(st, f"yt{i}", [128, D]) for i in range(2)]
                    yt2_2 = [sb(st, f"ytt{i}", [128, D]) for i in range(2)]
                    b_yt_2 = [P.buf() for _ in range(2)]
                    b_yt2_2 = [P.buf() for _ in range(2)]
                    ssb2 = [sb(st, f"ssb{i}", [128, D], BF16) for i in range(2)]
                    b_ssb2 = [P.buf() for _ in range(2)]
                    sso = sb(st, "sso", [128, 4])
                    b_sso4 = [P.buf() for _ in range(4)]
                    jq = sb(st, "jq", [128, D], BF16)
                    b_jq = P.buf()
                    ahl = sb(st, "ahl", [128, 2, 32], BF16)
                    ahf = sb(st, "ahf", [128, 32])
                    b_ahl, b_ahf = P.buf(), P.buf()
                    SEGB = (0, 1, 2, 3, 5)
                    segq = [banks[bq][:, 0:128] for bq in SEGB]
                    b_segq = [bk[bq] for bq in SEGB]
                    uT = U["uT"]
                    pre_, fcs_, post_ = ssd_thunks(S, U, j, t_own, 512, 516, wO, 1024, 2560, [1], 2, 3)

                    def z_group(k, half):
                        zb = 4 + (k * 2 + half) % 2
                        for kc in range(8):
                            P.op(PE, lambda e, kc=kc: e.matmul(banks[zb][:, :], uT[:, kc, 2 + 128 * k:130 + 128 * k], wO[:, kc, half * 512:(half + 1) * 512], start=(kc == 0), stop=(kc == 7)),
                                 reads=[U["b_uT"], b_wres], writes=[bk[zb]])
                        P.op(ACT, lambda e: e.activation(sz[:, k, half * 512:(half + 1) * 512], banks[zb][:, :], AF.Silu), reads=[bk[zb]], writes=[b_sz])

                    pre_()
                    for fc in range(12):
                        fcs_[fc]()
                        if fc < 8:
                            z_group(fc // 2, fc % 2)
                    post_()
                    en = [0]
                    pend = {"front": None, "back": None}
                    for k in range(4):
                        jj = oi * 4 + k
                        pi = k % 2
                        gtm, b_gtm, xdt, b_xdt, bia, b_bia = gtm2[pi], b_gtm2[pi], xdt2[pi], b_xdt2[pi], bia2[pi], b_bia2[pi]
                        yt, yt2, b_yt, b_yt2, ssb, b_ssb = yt_2[pi], yt2_2[pi], b_yt_2[pi], b_yt2_2[pi], ssb2[pi], b_ssb2[pi]
                        for d_ in range(2):
                            P.dma(SP, lambda e, pi=pi, d_=d_, jj=jj: e.dma_start(out=pfb[pi][:, d_, :], in_=PFB[d_, jj, :, :]), reads=[b_PFB], writes=[b_pfb[pi]], sbuf=b_pfb[pi])
                        dt, adt, cs, tot = S["dt"], S["adt"], S["cs"], S["tot"]
                        P.op(DVE, lambda e, k=k: e.tensor_scalar(bia[:, 0:16], cs[:, k, 0:16], -1.0, None, ALU.mult), reads=[S["b_dtq"]], writes=[b_bia])
                        P.op(DVE, lambda e, k=k: e.tensor_tensor(bia[:, 16:32], cs[:, k, 16:32], adt[:, k, 16:32], ALU.subtract), reads=[S["b_dtq"]], writes=[b_bia])
                        P.op(DVE, lambda e, k=k: e.tensor_tensor(bia[:, 48:64], tot[:, k, 16:32], bia[:, 16:32], ALU.subtract), reads=[S["b_dtq"], b_bia], writes=[b_bia])
                        P.op(ACT, lambda e, k=k: e.activation(bia[:, 32:48], cs[:, k, 0:16], AF.Exp), reads=[S["b_dtq"]], writes=[b_bia])
                        P.op(ACT, lambda e: e.activation(bia[:, 48:64], bia[:, 48:64], AF.Exp), reads=[b_bia], writes=[b_bia])
                        P.op(DVE, lambda e, k=k: e.tensor_copy(ahl[:, 0, :], adt[:, k, :]), reads=[S["b_dtq"]], writes=[b_ahl])
                        P.op(DVE, lambda e: e.tensor_copy(ahf[:, :], ahl[:, 0, :]), reads=[b_ahl], writes=[b_ahf])
                        P.op(DVE, lambda e, k=k: e.tensor_tensor(ahl[:, 1, :], adt[:, k, :], ahf[:, :], ALU.subtract), reads=[S["b_dtq"], b_ahf, b_ahl], writes=[b_ahl])
                        xt3 = S["xtm"][:, k, :].rearrange("p (h d) -> p h d", h=16)
                        for d_ in range(2):
                            P.op(POOL if d_ else DVE, lambda e, d_=d_, k=k, xt3=xt3: e.tensor_tensor(xdt[:, d_, :, :], xt3, dt[:, k, 16 * d_:16 * d_ + 16].unsqueeze(2).to_broadcast([128, 16, 64]), ALU.mult),
                                 reads=[S["b_xtm"], S["b_dtq"]], writes=[b_xdt])
                        for g in range(2):
                            P.op(PE, lambda e, g=g, k=k: e.matmul(banks[4][:, g * 128:(g + 1) * 128], S["xcT"][:, 8 + g, 128 * k:128 * k + 128], S["xcT"][:, 10 + g, 128 * k:128 * k + 128], start=True, stop=True),
                                 reads=[S["b_xcT"]], writes=[bk[4]])
                        for g in range(2):
                            P.op(DVE, lambda e, g=g: e.tensor_tensor(gtm[:, 2 * g, :], banks[4][:, g * 128:(g + 1) * 128], maskf, ALU.mult), reads=[bk[4], b_cst], writes=[b_gtm])
                            P.op(DVE, lambda e, g=g: e.tensor_tensor(gtm[:, 2 * g + 1, :], banks[4][:, g * 128:(g + 1) * 128], maskb, ALU.mult), reads=[bk[4], b_cst], writes=[b_gtm])
                        for g in range(2):
                            P.op(PE, lambda e, g=g, k=k, pi=pi: e.matmul(banks[g][:, :], S["xcT"][:, 10 + g, 128 * k:128 * k + 128], pfb[pi][:, 0, g * 512:(g + 1) * 512], start=True, stop=True),
                                 reads=[S["b_xcT"], b_pfb[pi]], writes=[bk[g]])
                        for g in range(2):
                            sl = slice(g * 512, (g + 1) * 512)
                            y3 = yt[:, sl].rearrange("p (h d) -> p h d", h=8)
                            P.op(DVE, lambda e, g=g, y3=y3: e.tensor_tensor(y3, banks[g][:, :].rearrange("p (h d) -> p h d", h=8), bia[:, 32 + 8 * g:40 + 8 * g].unsqueeze(2).to_broadcast([128, 8, 64]), ALU.mult),
                                 reads=[bk[g], b_bia], writes=[b_yt])
                        for g in range(2):
                            P.op(PE, lambda e, g=g, k=k, pi=pi: e.matmul(banks[g][:, :], S["xcT"][:, 10 + g, 128 * k:128 * k + 128], pfb[pi][:, 1, g * 512:(g + 1) * 512], start=True, stop=True),
                                 reads=[S["b_xcT"], b_pfb[pi]], writes=[bk[g]])
                        for g in range(2):
                            sl = slice(g * 512, (g + 1) * 512)
                            y23 = yt2[:, sl].rearrange("p (h d) -> p h d", h=8)
                            P.op(DVE, lambda e, g=g, y23=y23: e.tensor_tensor(y23, banks[g][:, :].rearrange("p (h d) -> p h d", h=8), bia[:, 48 + 8 * g:56 + 8 * g].unsqueeze(2).to_broadcast([128, 8, 64]), ALU.mult),
                                 reads=[bk[g], b_bia], writes=[b_yt2])
                        P.op(POOL, lambda e: e.tensor_tensor(yt[:, :], yt[:, :], yt2[:, :], ALU.add), reads=[b_yt, b_yt2], writes=[b_yt])
                        items = [(h, d_) for h in range(16) for d_ in range(2)]

                        def front(ii, k=k):
                            h, d_ = items[ii]
                            g = h // 8
                            q = ii % 6
                            rhs = umb[:, 0, :] if d_ == 0 else umb[:, 1, :]
                            sq_ = ii % 5
                            for hl in range(2):
                                P.op(PE, lambda e, hl=hl: e.matmul(segq[sq_], ahl[:, hl, 16 * d_ + h:16 * d_ + h + 1].to_broadcast([128, 128]), rhs, start=(hl == 0), stop=(hl == 1), skip_group_check=True),
                                     reads=[b_ahl, b_cstb], writes=[b_segq[sq_]])
                            P.op(ACT, lambda e: e.activation(ef[q][:, :], segq[sq_], AF.Exp, bias=bia[:, 16 * d_ + h:16 * d_ + h + 1]), reads=[b_segq[sq_], b_bia], writes=[b_ef[q]])
                            P.op(DVE, lambda e: e.scalar_tensor_tensor(mt[q][:, :], ef[q][:, :], 1.0, gtm[:, 2 * g + d_, :], ALU.min, ALU.mult),
                                 reads=[b_ef[q], b_gtm], writes=[b_mt[q]])

                        def back(ii):
                            h, d_ = items[ii]
                            q = ii % 6
                            P.op(PE, lambda e: e.matmul(banks[6 + h // 8][:, (h % 8) * 64:(h % 8) * 64 + 64], mt[q][:, :], xdt[:, d_, h, :], start=(d_ == 0), stop=(d_ == 1), skip_group_check=True),
                                 reads=[b_mt[q], b_xdt], writes=[bk[6 + h // 8]])

                        LA = 4
                        for ii in range(LA):
                            front(ii)
                        for ii in range(32):
                            if ii + LA < 32:
                                front(ii + LA)
                            back(ii)
                            if ii == 8 and pend["front"] is not None:
                                pend["front"]()
                                pend["front"] = None
                        if pend["back"] is not None:
                            pend["back"]()
                            pend["back"] = None
                        for g in range(2):
                            sl = slice(g * 512, (g + 1) * 512)
                            P.op(DVE, lambda e, g=g, sl=sl: e.tensor_tensor(yt[:, sl], yt[:, sl], banks[6 + g][:, :], ALU.add), reads=[b_yt, bk[6 + g]], writes=[b_yt])

                        def tail_front(k=k, yt=yt, yt2=yt2, b_yt=b_yt, b_yt2=b_yt2, ssb=ssb, b_ssb=b_ssb, xt3=xt3):
                            P.op(POOL, lambda e: e.tensor_tensor(yt2[:, :].rearrange("p (h d) -> p h d", h=16), xt3, Dbc.unsqueeze(2).to_broadcast([128, 16, 64]), ALU.mult),
                                 reads=[S["b_xtm"], b_gt, b_yt2], writes=[b_yt2])
                            P.op(POOL, lambda e: e.tensor_tensor(yt[:, :], yt[:, :], yt2[:, :], ALU.add), reads=[b_yt, b_yt2], writes=[b_yt])
                            P.op(DVE, lambda e: e.tensor_tensor(yt[:, :], yt[:, :], sz[:, k, :], ALU.mult), reads=[b_yt, b_sz], writes=[b_yt])
                            rms_rstd(ACT, yt[:, :], 128, D, jq[:, :], b_jq, sso[:, k:k + 1], b_sso4[k], [b_yt])
                            P.op(DVE, lambda e: e.scalar_tensor_tensor(ssb[:, :], yt[:, :], sso[:, k:k + 1], ssmw, ALU.mult, ALU.mult), reads=[b_yt, b_sso4[k], b_gt], writes=[b_ssb])

                        def tail_back(k=k, ssb=ssb, b_ssb=b_ssb):
                            tpv = banks[4][:, :].bitcast(BF16)
                            for fc in range(8):
                                P.op(PE, lambda e, fc=fc: e.transpose(tpv[:, fc * 128:(fc + 1) * 128], ssb[:, fc * 128:(fc + 1) * 128], identb[:, :]), reads=[b_ssb, b_idb], writes=[bk[4]])
                            P.op(ACT, lambda e: e.copy(mixT[:, 8:16, 128 * k:128 * k + 128], tpv[:, :].rearrange("p (f t) -> p f t", f=8)), reads=[bk[4]], writes=[b_mix])

                        pend["front"], pend["back"] = tail_front, tail_back
                    pend["front"]()
                    pend["back"]()
                    P.barrier()
                    P.release(mkO)
                with ExitStack() as st:
                    mkM = P.mark()
                    NWP = 6
                    wpool = [sb(st, f"wp{i}", [128, 4096], BF16) for i in range(NWP)]
                    b_wp = [P.buf() for _ in range(NWP)]
                    h1 = sb(st, "h1", [128, 4, D])
                    b_h1 = [P.buf() for _ in range(4)]
                    u2b = [sb(st, f"u2b{i}", [128, D], BF16) for i in range(2)]
                    b_u2b = [P.buf() for _ in range(2)]
                    u2T = sb(st, "u2T", [128, 8, 512], BF16)
                    b_u2T = P.buf()
                    rT = [sb(st, f"rT{i}", [128, 512], BF16) for i in range(2)]
                    b_rT = [P.buf() for _ in range(2)]
                    aT = [sb(st, f"aT{i}", [128, 4, 512], BF16) for i in range(2)]
                    b_aT = [P.buf() for _ in range(2)]
                    ss2 = sb(st, "ss2", [128, 4])
                    b_ss2 = [P.buf() for _ in range(4)]
                    jq = sb(st, "jq2", [128, D], BF16)
                    b_jq = P.buf()
                    ot = [sb(st, f"ot{i}", [128, D]) for i in range(2)]
                    b_ot = [P.buf() for _ in range(2)]
                    for k in range(4):
                        P.dma(SP, lambda e, k=k: e.dma_start(out=h1[:, k, :], in_=xw[j][t_own, 2 + 128 * k:130 + 128 * k, :]), writes=[b_h1[k]], sbuf=b_h1[k])
                    mn = [0]
                    wvs = []
                    for cb in range(4):
                        wi = wpn[0] % NWP
                        wpn[0] += 1
                        wv = wpool[wi][:, :].rearrange("p (k c) -> p k c", k=16)
                        P.dma(SP, lambda e, wv=wv, cb=cb: e.dma_start(out=wv, in_=ws_out[:, :, cb * 256:(cb + 1) * 256]), reads=[b_ws["out"]], writes=[b_wp[wi]], sbuf=b_wp[wi])
                        wvs.append((wv, wi))
                    tpv = banks[2][:, :].bitcast(BF16).rearrange("p (k t) -> p k t", k=8)

                    def n2_front(k):
                        rms_rstd(ACT, h1[:, k, :], 128, D, jq[:, :], b_jq, ss2[:, k:k + 1], b_ss2[k], [b_h1[k]])
                        P.op(ACT, lambda e: e.activation(u2b[k % 2][:, :], h1[:, k, :], AF.Copy, scale=ss2[:, k:k + 1]), reads=[b_h1[k], b_ss2[k]], writes=[b_u2b[k % 2]])

                    def n2_back(k):
                        for kc in range(8):
                            P.op(PE, lambda e, kc=kc: e.transpose(tpv[:, kc, :], u2b[k % 2][:, kc * 128:(kc + 1) * 128], identb[:, :]), reads=[b_u2b[k % 2], b_idb], writes=[bk[2]])
                        P.op(DVE, lambda e: e.tensor_tensor(u2T[:, :, 128 * k:128 * k + 128], tpv, w2T.unsqueeze(2).to_broadcast([128, 8, 128]), ALU.mult),
                             reads=[bk[2], b_gt], writes=[b_u2T])

                    for k in range(4):
                        for cb in range(4):
                            wv, wi = wvs[cb]
                            pb = mn[0] % 2
                            mn[0] += 1
                            for kc in range(16):
                                P.op(PE, lambda e, kc=kc, wv=wv: e.matmul(banks[pb][:, 0:256], mixT[:, kc, 128 * k:128 * k + 128], wv[:, kc, :], start=(kc == 0), stop=(kc == 15)),
                                     reads=[b_mix, b_wp[wi]], writes=[bk[pb]])
                            P.op(DVE, lambda e, cb=cb: e.tensor_tensor(h1[:, k, cb * 256:(cb + 1) * 256], h1[:, k, cb * 256:(cb + 1) * 256], banks[pb][:, 0:256], ALU.add),
                                 reads=[b_h1[k], bk[pb]], writes=[b_h1[k]])
                        n2_front(k)
                        if k >= 1:
                            n2_back(k - 1)
                    n2_back(3)
                    un = [0]
                    dn_w = {}

                    def emit_up(p_):
                        wi = wpn[0] % NWP
                        wpn[0] += 1
                        wu = wpool[wi][:, :].rearrange("p (k c) -> p k c", k=8)
                        P.dma(SP, lambda e: e.dma_start(out=wu, in_=ws_up[:, :, p_ * 512:(p_ + 1) * 512]), reads=[b_ws["up"]], writes=[b_wp[wi]], sbuf=b_wp[wi])
                        wj = wpn[0] % NWP
                        wpn[0] += 1
                        wd = wpool[wj][:, :].rearrange("p (k c) -> p k c", k=4)
                        P.dma(SP, lambda e: e.dma_start(out=wd, in_=ws_dn[:, p_ * 4:(p_ + 1) * 4, :]), reads=[b_ws["dn"]], writes=[b_wp[wj]], sbuf=b_wp[wj])
                        dn_w[p_] = (wd, wj)
                        ai = p_ % 2
                        for fc in range(4):
                            pb = 3 + (un[0] % 2)
                            ri = un[0] % 2
                            un[0] += 1
                            for kc in range(8):
                                P.op(PE, lambda e, kc=kc: e.matmul(banks[pb][:, :], wu[:, kc, fc * 128:(fc + 1) * 128], u2T[:, kc, :], start=(kc == 0), stop=(kc == 7)),
                                     reads=[b_u2T, b_wp[wi]], writes=[bk[pb]])
                            P.op(ACT, lambda e: e.activation(rT[ri][:, :], banks[pb][:, :], AF.Relu), reads=[bk[pb]], writes=[b_rT[ri]])
                            P.op(DVE, lambda e: e.tensor_tensor(aT[ai][:, fc, :], rT[ri][:, :], rT[ri][:, :], ALU.mult), reads=[b_rT[ri]], writes=[b_aT[ai]])

                    def emit_down(p_):
                        wd, wj = dn_w[p_]
                        ai = p_ % 2
                        for k in range(4):
                            for ch in range(2):
                                pb = 5 + (un[0] % 2)
                                un[0] += 1
                                for fc in range(4):
                                    P.op(PE, lambda e, fc=fc: e.matmul(banks[pb][:, :], aT[ai][:, fc, 128 * k:128 * k + 128], wd[:, fc, ch * 512:(ch + 1) * 512], start=(fc == 0), stop=(fc == 3)),
                                         reads=[b_aT[ai], b_wp[wj]], writes=[bk[pb]])
                                P.op(DVE, lambda e: e.tensor_tensor(h1[:, k, ch * 512:(ch + 1) * 512], h1[:, k, ch * 512:(ch + 1) * 512], banks[pb][:, :], ALU.add),
                                     reads=[b_h1[k], bk[pb]], writes=[b_h1[k]])

                    emit_up(0)
                    for p_ in range(8):
                        if p_ + 1 < 8:
                            emit_up(p_ + 1)
                        emit_down(p_)
                    for k in range(4):
                        P.op(ACT, lambda e, k=k: e.activation(jq[:, :], h1[:, k, :], AF.Square, accum_out=ss2[:, k:k + 1]), reads=[b_h1[k]], writes=[b_jq, b_ss2[k]])
                    for k in range(4):
                        P.op(DVE, lambda e, k=k: e.tensor_scalar(ss2[:, k:k + 1], ss2[:, k:k + 1], 1.0 / D, EPS, ALU.mult, ALU.add), reads=[b_ss2[k]], writes=[b_ss2[k]])
                    for k in range(4):
                        P.op(ACT, lambda e, k=k: e.activation(ss2[:, k:k + 1], ss2[:, k:k + 1], AF.Ln), reads=[b_ss2[k]], writes=[b_ss2[k]])
                    for k in range(4):
                        P.op(ACT, lambda e, k=k: e.activation(ss2[:, k:k + 1], ss2[:, k:k + 1], AF.Exp, scale=-0.5), reads=[b_ss2[k]], writes=[b_ss2[k]])
                    for k in range(4):
                        oi2 = k % 2
                        P.op(DVE, lambda e, k=k, oi2=oi2: e.scalar_tensor_tensor(ot[oi2][:, :], h1[:, k, :], ss2[:, k:k + 1], finw, ALU.mult, ALU.mult), reads=[b_h1[k], b_ss2[k], b_gt], writes=[b_ot[oi2]])
                        P.dma(POOL, lambda e, k=k, oi2=oi2: e.dma_start(out=yout[j][oi * 512 + 128 * k:oi * 512 + 128 * k + 128, :], in_=ot[oi2][:, :]), reads=[b_ot[oi2]], sbuf=b_ot[oi2])
                    P.barrier()
                    P.release(mkM)
            P.release(mkT)
    P.finish()
    return nc, P


def _consts():
    c = np.zeros((128, 6 * 128 + 4 * 512), np.float32)
    i = np.arange(128)
    c[:, 0:128] = np.eye(128)
    c[:, 128:256] = (i[:, None] <= i[None, :])
    c[:, 256:384] = -1.0 * (i[:, None] < i[None, :])
    c[:, 384:512] = 1.0
    c[:, 512:640] = (i[None, :] >= i[:, None])
    c[:, 640:768] = (i[None, :] <= i[:, None])
    q = np.arange(512)
    for kk in range(4):
        c[:, 768 + 512 * kk:768 + 512 * (kk + 1)] = np.abs((128 * kk + i)[:, None] - q[None, :])
    return c


def _job_arrays(cfg, seq, own_start, params, is_prompt):
    meta = params["meta_tokens"]
    S = seq.shape[0]
    full = np.concatenate([meta, seq], axis=0)
    L = full.shape[0]
    NOWN = cfg.NOWN
    own_s0 = 16 + own_start
    own_s1 = own_s0 + NOWN * 512
    tiles = []
    kinds = []
    tiles.append(np.arange(-2, 18)); kinds.append("L")
    for s0 in range(16, own_s0, 512):
        tiles.append(np.arange(s0 - 2, s0 + 514)); kinds.append("L")
    for s0 in range(L - 512, own_s1 - 1, -512):
        tiles.append(np.arange(s0 + 513, s0 - 3, -1)); kinds.append("R")
    for s0 in range(own_s0, own_s1, 512):
        tiles.append(np.arange(s0 - 2, s0 + 514)); kinds.append("O")
    nt = len(tiles)
    xwin = np.zeros((nt, 516, D), np.float32)
    for t, idx in enumerate(tiles):
        ok = (idx >= 0) & (idx < L)
        xwin[t, np.nonzero(ok)[0]] = full[idx[ok]]
    pos = [tiles[0][2:18]] + [tl[2:514] for tl in tiles[1:]]
    kindtok = np.concatenate([np.full(len(p), {"L": 0, "R": 1, "O": 2}[k]) for p, k in zip(pos, kinds)])
    pos = np.concatenate(pos).astype(np.int64)
    Ls = len(pos)
    slopes = 2.0 ** (-(np.arange(NH) + 1.0))
    cpos, rpos = (pos // 128).astype(np.float32), (pos % 128).astype(np.float32)
    kaug = np.zeros((NH, 8, Ls), np.float32)
    for h in range(NH):
        sl = slopes[h]
        left = np.stack([-np.ones(Ls), -np.ones(Ls), sl * 128 * cpos, sl * rpos])
        right = -left
        ml = (kindtok != 1)[None, :]
        mr = (kindtok != 0)[None, :]
        kaug[h, 0:4] = left * ml
        kaug[h, 4:8] = right * mr
    qpos = np.arange(own_s0, own_s1)
    qc, qr = (qpos // 128).astype(np.float32), (qpos % 128).astype(np.float32)
    qaug = np.zeros((3, NH, 8, NOWN * 512), np.float32)
    for h in range(NH):
        sl = slopes[h]
        qa = np.stack([sl * 128 * qc, sl * qr, np.ones_like(qc), np.ones_like(qc)])
        qaug[0, h, 0:4] = qa; qaug[0, h, 4:8] = qa
        qaug[1, h, 0:4] = qa
        qaug[2, h, 4:8] = qa
    cw = params["conv_w"][0]
    cb = params["conv_b"][0]
    tab = np.zeros((nt, 128, TT), np.float32)
    cwT = cw.T.reshape(12, 128, 5).transpose(1, 0, 2)
    cbT = cb.reshape(12, 128).T
    last_left = max(t for t, k in enumerate(kinds) if k == "L")
    for t, k in enumerate(kinds):
        taps = cwT[:, :, ::-1] if k == "R" else cwT
        tab[t, :, 0:60] = taps.reshape(128, 60)
        tab[t, :, 60:72] = cbT
        if k == "R":
            prim_b, prim_a, sf, sb_ = params["dt_bias_b"][0], params["a_log_b"][0], 0.0, 1.0
        else:
            prim_b, prim_a, sf, sb_ = params["dt_bias_f"][0], params["a_log_f"][0], 1.0, 0.0
        tab[t, :, 72:88] = prim_b[None, :]
        tab[t, :, 88:104] = params["dt_bias_b"][0][None, :]
        tab[t, :, 104:120] = prim_a[None, :]
        tab[t, :, 120:136] = params["a_log_b"][0][None, :]
        tab[t, :, 136] = sf
        tab[t, :, 137] = sb_
        tab[t, :, 138] = 1.0 if t == last_left else 0.0
        tab[t, :, 139] = 0.0 if t == last_left else 1.0
    return dict(xw=xwin, kaug=kaug.astype(ml_dtypes.bfloat16), qaug=qaug.astype(ml_dtypes.bfloat16), ttab=tab)


_CACHE = {}


def run(cfg, inputs):
    p = {k: np.asarray(v, np.float32) for k, v in inputs.items()}
    key = (cfg.NC, cfg.SP, cfg.SS, cfg.NSEQ, cfg.NOWN)
    if key not in _CACHE:
        _CACHE[key] = build_program(cfg)
    nc, P = _CACHE[key]
    gt = np.zeros((128, 8 + 8 + 16 + 1024 + 1024 + 128 + 256), np.float32)
    gt[:, 0:8] = p["norm1_w"][0].reshape(8, 128).T
    gt[:, 8:16] = p["norm2_w"][0].reshape(8, 128).T
    gt[:, 16:32] = p["d_skip"][0][None, :]
    gt[:, 32:1056] = p["ssm_norm_w"][0][None, :]
    gt[:, 1056:2080] = p["final_norm_w"][None, :]
    gt[:, 2080:2208] = p["attn_norm_w"][0][None, :]
    gt[:, 2208:2272] = p["lambda_q1"][0][None, :]
    gt[:, 2272:2336] = p["lambda_k1"][0][None, :]
    gt[:, 2336:2400] = p["lambda_q2"][0][None, :]
    gt[:, 2400:2464] = p["lambda_k2"][0][None, :]
    cst = _consts()
    in_maps = []
    xp = p["x_prompt"][0]
    xs = p["x_sample"]
    for c in range(cfg.NC):
        m = {"w_in": p["w_in"][0], "w_out": p["w_out"][0], "w_up": p["w_up"][0], "w_dn": p["w_down"][0], "cst": cst, "gtab": gt}
        ja = [_job_arrays(cfg, xp, c * cfg.OWNT, p, True)]
        for s in range(cfg.NSEQ):
            ja.append(_job_arrays(cfg, xs[c * cfg.NSEQ + s], 0, p, False))
        for j, a in enumerate(ja):
            m[f"xw{j}"] = a["xw"]
            m[f"kaug{j}"] = a["kaug"]
            m[f"qaug{j}"] = a["qaug"]
            m[f"ttab{j}"] = a["ttab"]
        in_maps.append(m)
    res = run_bass_kernel_spmd(nc, in_maps, core_ids=list(range(cfg.NC)))
    _CACHE["last_exec_ns"] = getattr(res, "exec_time_ns", None)
    yp = np.concatenate([res.results[c]["y0"] for c in range(cfg.NC)], axis=0)[None]
    ys = np.stack([res.results[c][f"y{1 + s}"] for c in range(cfg.NC) for s in range(cfg.NSEQ)], axis=0)
    return yp.astype(np.float32), ys.astype(np.float32)


def kernel(**inputs):
    cfg = Cfg(ncores=8, sp=16384, ss=2048, nseq=4, nown=4)
    return run(cfg, inputs)
```

```python
import math
from contextlib import ExitStack
import numpy as np
import ml_dtypes
import concourse.bass as bass
import concourse.mybir as mybir
from concourse.bass_utils import run_bass_kernel_spmd

F32 = mybir.dt.float32
BF16 = mybir.dt.bfloat16
AF = mybir.ActivationFunctionType
ALU = mybir.AluOpType
PE, ACT, DVE, POOL, SP = "tensor", "scalar", "vector", "gpsimd", "sync"
COMPUTE = (PE, ACT, DVE, POOL)
ENGS = (PE, ACT, DVE, POOL, SP)

D = 1024
NH = 8
EPS = 1e-5
TT = 140
N_META = 16
LAM_INIT = 0.8 - 0.6 * math.exp(-0.3 * 0)
CQ, CK, CV, CZ, CX, CDT = 0, 1024, 2048, 3072, 4096, 5632


class Buf:
    __slots__ = ("name", "w", "rd", "dsem", "dcnt", "keep")

    def __init__(self, name=""):
        self.name = name
        self.keep = bool(name)
        self.w = None
        self.rd = []
        self.dsem = None
        self.dcnt = 0


class Op:
    __slots__ = ("eng", "fn", "deps", "is_dma", "flag", "sem", "val", "dbuf", "win", "pos")

    def __init__(self, eng, fn, is_dma):
        self.eng, self.fn, self.is_dma = eng, fn, is_dma
        self.deps = []
        self.flag = False
        self.sem = None
        self.val = 0
        self.dbuf = None
        self.win = 0
        self.pos = 0


class _Rec:
    def __init__(self):
        self.call = None

    def __getattr__(self, name):
        def f(*a, **k):
            self.call = (name, a, k)
            return None
        return f


class Prog:
    def __init__(self, nc):
        self.nc = nc
        self.pending = []
        self.bufs = []
        self.win = 0
        self.engs = {PE: nc.tensor, ACT: nc.scalar, DVE: nc.vector, POOL: nc.gpsimd, SP: nc.sync}
        self.esem = {e: nc.alloc_semaphore(f"prog_{e}") for e in ENGS}
        self.ecnt = {e: 0 for e in ENGS}
        self.seen = {e: {} for e in ENGS}
        self.winlast = {}
        self.lastop = {e: None for e in ENGS}
        self.n_ops = 0
        self.n_waits = 0
        self.sempool = []
        self.sempool_sw = []
        self.semq = {}
        self.nsem = 0

    def buf(self, name=""):
        b = Buf(name)
        self.bufs.append(b)
        return b

    def mark(self):
        return len(self.bufs)

    def release(self, mk):
        del self.bufs[mk:]

    def _add(self, op, reads, writes):
        deps = {}
        raw = set()
        for b in reads:
            if b.w is not None:
                deps[id(b.w)] = b.w
                raw.add(id(b.w))
        for b in writes:
            if b.w is not None and not (op.is_dma and b.w.is_dma and b.w.dbuf is op.dbuf):
                deps[id(b.w)] = b.w
            for r in b.rd:
                deps[id(r)] = r
        dl = []
        for d in deps.values():
            if d is op:
                continue
            if (not d.is_dma) and (not op.is_dma) and d.eng == op.eng:
                if op.eng == PE or id(d) not in raw:
                    continue
            dl.append(d)
        op.deps = dl
        for b in reads:
            b.rd.append(op)
        for b in writes:
            b.w = op
            b.rd = []
        op.win = self.win
        self.pending.append(op)
        if not op.is_dma:
            self.lastop[op.eng] = op
        self.n_ops += 1
        return op

    @staticmethod
    def _bind(fn):
        r = _Rec()
        fn(r)
        c = r.call
        return lambda e: getattr(e, c[0])(*c[1], **c[2])

    def op(self, eng, fn, reads=(), writes=()):
        return self._add(Op(eng, self._bind(fn), False), list(reads), list(writes))

    def dma(self, eng, fn, reads=(), writes=(), sbuf=None):
        o = Op(eng, self._bind(fn), True)
        o.dbuf = sbuf
        return self._add(o, list(reads), list(writes))

    def barrier(self):
        deps = {}
        for e in COMPUTE:
            if self.lastop[e] is not None:
                deps[id(self.lastop[e])] = self.lastop[e]
        for b in self.bufs:
            if b.w is not None and b.w.is_dma:
                deps[id(b.w)] = b.w
            for r in b.rd:
                if r.is_dma:
                    deps[id(r)] = r
        b0 = Op(SP, lambda e: e.nop(), False)
        b0.deps = list(deps.values())
        b0.win = self.win
        self.pending.append(b0)
        self.lastop[SP] = b0
        for e in COMPUTE + (SP,):
            o = Op(e, lambda en: en.nop(), False)
            o.deps = [b0]
            o.win = self.win
            self.pending.append(o)
            self.lastop[e] = o
        for b in self.bufs:
            b.w = None
            b.rd = []
        self.flush()
        for b in self.bufs:
            if b.dsem is not None:
                (self.sempool_sw if self.semq.get(id(b.dsem)) == POOL else self.sempool).append((b.dsem, b.dcnt))
                b.dsem = None

    def flush(self):
        nc = self.nc
        ops = self.pending
        self.pending = []
        last = {}
        for o in ops:
            for d in o.deps:
                d.flag = True
            if not o.is_dma:
                last[o.eng] = o
        for e, o in last.items():
            o.flag = True
            self.winlast[(self.win, e)] = o
        for o in ops:
            if o.is_dma:
                b = o.dbuf
                if b.dsem is None:
                    pool = self.sempool_sw if o.eng == POOL else self.sempool
                    if pool:
                        b.dsem, b.dcnt = pool.pop()
                    else:
                        self.nsem += 1
                        b.dsem = nc.alloc_semaphore(f"dma_{self.nsem}")
                        b.dcnt = 0
                    self.semq[id(b.dsem)] = o.eng
                b.dcnt += 16
                o.sem, o.val = b.dsem, b.dcnt
            elif o.flag:
                self.ecnt[o.eng] += 1
                o.sem, o.val = self.esem[o.eng], self.ecnt[o.eng]
        for o in ops:
            e = self.engs[o.eng]
            need = {}
            for d in o.deps:
                if d.sem is None:
                    d = self.winlast[(d.win, d.eng)]
                k = id(d.sem)
                if k not in need or need[k][1] < d.val:
                    need[k] = (d.sem, d.val)
            sn = self.seen[o.eng]
            for k, (s, v) in need.items():
                if sn.get(k, 0) >= v:
                    continue
                e.wait_ge(s, v)
                sn[k] = v
                self.n_waits += 1
            ins = o.fn(e)
            if o.is_dma:
                ins.then_inc(o.sem, 16)
            elif o.flag:
                ins.then_inc(o.sem, 1)
        self.win += 1

    def finish(self):
        self.flush()
        fe = self.engs[SP]
        for b in self.bufs:
            if b.dsem is not None:
                fe.wait_ge(b.dsem, b.dcnt)
        for (sm, cnt) in self.sempool + self.sempool_sw:
            if cnt > 0:
                fe.wait_ge(sm, cnt)


class Rot:
    def __init__(self, items):
        self.items = items
        self.i = 0

    def next(self):
        it = self.items[self.i % len(self.items)]
        self.i += 1
        return it


class Cfg:
    def __init__(self, ncores=8, sp=16384, ss=2048, nseq=4, nown=4):
        self.NC, self.SP, self.SS, self.NSEQ, self.NOWN = ncores, sp, ss, nseq, nown
        assert sp == ncores * nown * 512 and ss == nown * 512
        self.NCTX = sp // 512 - nown
        self.jobs = [self.NCTX] + [0] * nseq
        self.OWNT = nown * 512

    def ntiles(self, j):
        return 1 + self.jobs[j] + self.NOWN

    def L(self, j):
        return 16 + 512 * (self.ntiles(j) - 1)

    def nch(self, j):
        return 1 + 4 * (self.ntiles(j) - 1)


def build_program(cfg):
    nc = bass.Bass("TRN2", target_bir_lowering=False)
    P = Prog(nc)
    NOWN, OWNT = cfg.NOWN, cfg.OWNT
    NJ = len(cfg.jobs)
    Lmax = max(cfg.L(j) for j in range(NJ))
    NCHmax = max(cfg.nch(j) for j in range(NJ))

    def din(name, shape, dt=F32):
        return nc.dram_tensor(name, list(shape), dt, kind="ExternalInput").ap()

    def dscr(name, shape, dt=BF16):
        return nc.dram_tensor(name, list(shape), dt, kind="Internal").ap()

    xw = [din(f"xw{j}", [cfg.ntiles(j), 516, D]) for j in range(NJ)]
    kaug = [din(f"kaug{j}", [NH, 8, cfg.L(j)], BF16) for j in range(NJ)]
    qaug = [din(f"qaug{j}", [3, NH, 8, OWNT], BF16) for j in range(NJ)]
    ttab = [din(f"ttab{j}", [cfg.ntiles(j), 128, TT]) for j in range(NJ)]
    yout = [nc.dram_tensor(f"y{j}", [OWNT, D], F32, kind="ExternalOutput").ap() for j in range(NJ)]
    w_in = din("w_in", [D, 5664])
    w_out = din("w_out", [2048, D])
    w_up = din("w_up", [D, 4096])
    w_dn = din("w_dn", [4096, D])
    cst = din("cst", [128, 6 * 128 + 4 * 512])
    gtab = din("gtab", [128, 8 + 8 + 16 + 1024 + 1024 + 128 + 256])
    ws_in = dscr("ws_in", [128, 8, 5664])
    ws_out = dscr("ws_out", [128, 16, D])
    ws_up = dscr("ws_up", [128, 8, 4096])
    ws_dn = dscr("ws_dn", [128, 32, D])
    KT = dscr("KT", [NH, 2, 72, Lmax])
    QT = dscr("QT", [NH, 2, 64, OWNT])
    VA = dscr("VA", [NH, 128, NCHmax, 129])
    PFB = dscr("PFB", [2, 4 * NOWN, 128, D])
    SBS = dscr("SBS", [4 * NOWN, 128, D])
    b_ws = {k: P.buf("ws_" + k) for k in ("in", "out", "up", "dn")}
    b_KT, b_QT, b_VA, b_PFB, b_SBS = P.buf("KT"), P.buf("QT"), P.buf("VA"), P.buf("PFB"), P.buf("SBS")

    ES = ExitStack()
    G = ES

    _nm = [0]

    def sb(st, name, shape, dt=F32):
        _nm[0] += 1
        return st.enter_context(nc.sbuf_tensor(f"{name}_{_nm[0]}", list(shape), dt))

    allbanks = nc.alloc_psum_tensor("allbanks", [128, 4096], F32)
    banks = [allbanks[:, 512 * i:512 * (i + 1)] for i in range(8)]
    bk = [P.buf(f"bank{i}") for i in range(8)]

    cst_t = sb(G, "cst_t", [128, 6 * 128 + 4 * 512])
    gtab_t = sb(G, "gtab_t", [128, 8 + 8 + 16 + 1024 + 1024 + 128 + 256])
    identb = sb(G, "identb", [128, 128], BF16)
    lam_t = sb(G, "lam_t", [128, 8])
    CF = sb(G, "CF", [128, D])
    CB = sb(G, "CB", [128, D])
    decs = sb(G, "decs", [128, 4 * NOWN, 16])
    b_cst, b_gt, b_idb, b_lam, b_CF, b_CB, b_decs = (P.buf(n) for n in ("cst", "gt", "idb", "lam", "CF", "CB", "decs"))
    identf = cst_t[:, 0:128]
    uincl = cst_t[:, 128:256]
    negus = cst_t[:, 256:384]
    onesf = cst_t[:, 384:512]
    maskf = cst_t[:, 512:640]
    maskb = cst_t[:, 640:768]
    absd = [cst_t[:, 768 + 512 * i: 768 + 512 * (i + 1)] for i in range(4)]
    w1T = gtab_t[:, 0:8]
    w2T = gtab_t[:, 8:16]
    Dbc = gtab_t[:, 16:32]
    ssmw = gtab_t[:, 32:32 + 1024]
    finw = gtab_t[:, 1056:1056 + 1024]
    attw = gtab_t[:, 2080:2080 + 128]
    lamv = gtab_t[:, 2208:2208 + 256]

    P.dma(SP, lambda e: e.dma_start(out=cst_t[:], in_=cst[:, :]), writes=[b_cst], sbuf=b_cst)
    P.dma(SP, lambda e: e.dma_start(out=gtab_t[:], in_=gtab[:, :]), writes=[b_gt], sbuf=b_gt)
    P.op(DVE, lambda e: e.tensor_copy(identb[:], identf), reads=[b_cst], writes=[b_idb])
    umb = sb(G, "umb", [128, 2, 128], BF16)
    b_cstb = P.buf("cstb")
    P.op(DVE, lambda e: e.tensor_copy(umb[:, 0, :], uincl), reads=[b_cst], writes=[b_cstb])
    P.op(DVE, lambda e: e.tensor_copy(umb[:, 1, :], negus), reads=[b_cst], writes=[b_cstb])
    P.op(DVE, lambda e: e.tensor_tensor(lamv[:, 0:64], lamv[:, 0:64], lamv[:, 64:128], ALU.mult),
         reads=[b_gt], writes=[b_gt])
    P.op(DVE, lambda e: e.tensor_tensor(lamv[:, 128:192], lamv[:, 128:192], lamv[:, 192:256], ALU.mult),
         reads=[b_gt], writes=[b_gt])
    P.op(DVE, lambda e: e.reduce_sum(lam_t[:, 0:1], lamv[:, 0:64], mybir.AxisListType.X), reads=[b_gt], writes=[b_lam])
    P.op(DVE, lambda e: e.reduce_sum(lam_t[:, 1:2], lamv[:, 128:192], mybir.AxisListType.X), reads=[b_gt], writes=[b_lam])
    P.op(ACT, lambda e: e.activation(lam_t[:, 2:4], lam_t[:, 0:2], AF.Exp), reads=[b_lam], writes=[b_lam])
    P.op(DVE, lambda e: e.tensor_tensor(lam_t[:, 4:5], lam_t[:, 3:4], lam_t[:, 2:3], ALU.subtract), reads=[b_lam], writes=[b_lam])
    P.op(DVE, lambda e: e.tensor_scalar(lam_t[:, 4:5], lam_t[:, 4:5], -LAM_INIT, None, ALU.add), reads=[b_lam], writes=[b_lam])
    P.op(DVE, lambda e: e.tensor_scalar(attw, attw, 1.0 - LAM_INIT, None, ALU.mult), reads=[b_gt], writes=[b_gt])
    neglam = lam_t[:, 4:5]

    with ExitStack() as st:
        stf = [sb(st, f"stf{i}", [128, 2048]) for i in range(4)]
        stb = [sb(st, f"stb{i}", [128, 2048], BF16) for i in range(4)]
        b_stf = [P.buf(f"stf{i}") for i in range(4)]
        b_stb = [P.buf(f"stb{i}") for i in range(4)]
        cnt = [0]

        def conv_w(W, scr, bscr, K, N):
            for kc in range(K // 128):
                for c0 in range(0, N, 2048):
                    w = min(2048, N - c0)
                    i = cnt[0] % 4
                    ce = (DVE, ACT)[cnt[0] % 2]
                    cnt[0] += 1
                    P.dma(SP, lambda e, i=i, kc=kc, c0=c0, w=w: e.dma_start(out=stf[i][:, 0:w], in_=W[kc * 128:(kc + 1) * 128, c0:c0 + w]),
                          writes=[b_stf[i]], sbuf=b_stf[i])
                    if ce == ACT:
                        P.op(ACT, lambda e, i=i, w=w: e.copy(stb[i][:, 0:w], stf[i][:, 0:w]), reads=[b_stf[i]], writes=[b_stb[i]])
                    else:
                        P.op(ce, lambda e, i=i, w=w: e.tensor_copy(stb[i][:, 0:w], stf[i][:, 0:w]), reads=[b_stf[i]], writes=[b_stb[i]])
                    P.dma(POOL, lambda e, i=i, kc=kc, c0=c0, w=w: e.dma_start(out=scr[:, kc, c0:c0 + w], in_=stb[i][:, 0:w]),
                          reads=[b_stb[i]], writes=[bscr], sbuf=b_stb[i])

        conv_w(w_in, ws_in, b_ws["in"], D, 5664)
        conv_w(w_out, ws_out, b_ws["out"], 2048, D)
        conv_w(w_up, ws_up, b_ws["up"], D, 4096)
        conv_w(w_dn, ws_dn, b_ws["dn"], 4096, D)
        P.barrier()

    def make_uT(st, pfx):
        xs = [sb(st, f"{pfx}xs{i}", [128, D]) for i in range(2)]
        xb = [sb(st, f"{pfx}xb{i}", [128, D], BF16) for i in range(2)]
        sq = sb(st, f"{pfx}sq", [128, D], BF16)
        ss = sb(st, f"{pfx}ss", [128, 4])
        uT = sb(st, f"{pfx}uT", [128, 8, 516], BF16)
        return dict(xs=xs, xb=xb, sq=sq, ss=ss, uT=uT,
                    b_xs=[P.buf() for _ in range(2)], b_xb=[P.buf() for _ in range(2)],
                    b_sq=P.buf(), b_ss=[P.buf(), P.buf()], b_uT=P.buf(), cnt=[0])

    def rms_rstd(eng_sq, src_ap, n, width, sqjunk, b_junk, ssap, b_ss, src_bufs):
        P.op(ACT, lambda e: e.activation(sqjunk, src_ap, AF.Square, accum_out=ssap), reads=src_bufs, writes=[b_junk, b_ss])
        P.op(ACT, lambda e: e.activation(ssap, ssap, AF.Ln, scale=1.0 / width, bias=EPS), reads=[b_ss], writes=[b_ss])
        P.op(ACT, lambda e: e.activation(ssap, ssap, AF.Exp, scale=-0.5), reads=[b_ss], writes=[b_ss])

    def uT_thunks(U, src, W, wT, tp_bank):
        uT = U["uT"]
        tpv = banks[tp_bank][:, :].bitcast(BF16).rearrange("p (k t) -> p k t", k=8)
        out = []
        r0 = 0
        while r0 < W:
            n = min(128, W - r0)

            def mk(r0=r0, n=n):
                st_ = {}

                def stepA():
                    i = U["cnt"][0] % 2
                    U["cnt"][0] += 1
                    st_["i"] = i
                    xs, xb, bxs, bxb = U["xs"][i], U["xb"][i], U["b_xs"][i], U["b_xb"][i]
                    P.dma(SP, lambda e: e.dma_start(out=xs[0:n, :], in_=src[r0:r0 + n, :]), writes=[bxs], sbuf=bxs)
                    ssap = U["ss"][0:n, i:i + 1]
                    rms_rstd(ACT, xs[0:n, :], n, D, U["sq"][0:n, :], U["b_sq"], ssap, U["b_ss"][i], [bxs])
                    P.op(ACT, lambda e: e.activation(xb[0:n, :], xs[0:n, :], AF.Copy, scale=ssap), reads=[bxs, U["b_ss"][i]], writes=[bxb])

                def stepB():
                    i = st_["i"]
                    xb, bxb = U["xb"][i], U["b_xb"][i]
                    for kc in range(8):
                        P.op(PE, lambda e, kc=kc: e.transpose(tpv[:, kc, 0:n], xb[0:n, kc * 128:(kc + 1) * 128], identb[0:n, 0:n]),
                             reads=[bxb, b_idb], writes=[bk[tp_bank]])
                    P.op(DVE, lambda e: e.tensor_tensor(uT[:, :, r0:r0 + n], tpv[:, :, 0:n], wT.unsqueeze(2).to_broadcast([128, 8, n]), ALU.mult),
                         reads=[bk[tp_bank], b_gt], writes=[U["b_uT"]])
                return stepA, stepB
            out.append(mk())
            r0 += n
        return out

    def build_uT(U, src, W, wT, tp_bank):
        for (sa, sb_) in uT_thunks(U, src, W, wT, tp_bank):
            sa()
            sb_()

    def make_ssd(st, pfx):
        S = dict()
        S["tab"] = sb(st, f"{pfx}tab", [128, TT])
        S["raw"] = [sb(st, f"{pfx}raw{i}", [128, 516]) for i in range(2)]
        S["acc"] = [sb(st, f"{pfx}acc{i}", [128, 512]) for i in range(2)]
        S["xcT"] = sb(st, f"{pfx}xcT", [128, 12, 512], BF16)
        S["xtm"] = sb(st, f"{pfx}xtm", [128, 4, D], BF16)
        S["btm"] = sb(st, f"{pfx}btm", [128, 4, 256], BF16)
        S["dt"] = sb(st, f"{pfx}dt", [128, 4, 32])
        S["adt"] = sb(st, f"{pfx}adt", [128, 4, 32])
        S["cs"] = sb(st, f"{pfx}cs", [128, 4, 32])
        S["tot"] = sb(st, f"{pfx}tot", [128, 4, 32])
        S["A2"] = sb(st, f"{pfx}A2", [128, 32])
        S["tmp32"] = sb(st, f"{pfx}tmp32", [128, 32])
        for k in ("tab", "xcT", "xtm", "btm", "dtq", "A2", "tmp32"):
            S["b_" + k] = P.buf(pfx + k)
        S["b_raw"] = [P.buf() for _ in range(2)]
        S["b_acc"] = [P.buf() for _ in range(2)]
        S["n"] = 0
        return S

    def ssd_par(st, S, pfx):
        S2 = dict(S)
        S2["tab"] = sb(st, f"{pfx}tabB", [128, TT])
        S2["dt"] = sb(st, f"{pfx}dtB", [128, 4, 32])
        S2["adt"] = sb(st, f"{pfx}adtB", [128, 4, 32])
        S2["cs"] = sb(st, f"{pfx}csB", [128, 4, 32])
        S2["tot"] = sb(st, f"{pfx}totB", [128, 4, 32])
        S2["A2"] = sb(st, f"{pfx}A2B", [128, 32])
        for k in ("tab", "dtq", "A2"):
            S2["b_" + k] = P.buf()
        return S2

    def ssd_thunks(S, U, j, t, T, W, wv, cx, cdt, pb_main, pb_small, pb_tp):
        uT = U["uT"]
        tab = S["tab"]
        nchk = max(1, T // 128)
        cs_ = min(T, 128)
        def pre():
            P.dma(SP, lambda e: e.dma_start(out=tab[:], in_=ttab[j][t, :, :]), writes=[S["b_tab"]], sbuf=S["b_tab"])
            P.op(ACT, lambda e: e.activation(S["A2"][:], tab[:, 104:136], AF.Exp), reads=[S["b_tab"]], writes=[S["b_A2"]])
            P.op(DVE, lambda e: e.tensor_scalar(S["A2"][:], S["A2"][:], -1.0, None, ALU.mult), reads=[S["b_A2"]], writes=[S["b_A2"]])
            for k in range(nchk):
                for kc in range(8):
                    P.op(PE, lambda e, kc=kc, k=k: e.matmul(banks[pb_small][0:cs_, 64 + 32 * k:96 + 32 * k], uT[:, kc, 2 + 128 * k:2 + 128 * k + cs_], wv[:, kc, cdt:cdt + 32],
                                                            start=(kc == 0), stop=(kc == 7)),
                         reads=[U["b_uT"], b_wres], writes=[bk[pb_small]])
            dtr = banks[pb_small][0:cs_, 64:64 + 32 * nchk].rearrange("p (k c) -> p k c", c=32)
            dt, adt = S["dt"], S["adt"]
            P.op(DVE, lambda e: e.tensor_copy(dt[0:cs_, 0:nchk, 16:32], dtr[:, :, 16:32]), reads=[bk[pb_small]], writes=[S["b_dtq"]])
            P.op(DVE, lambda e: e.tensor_scalar(dt[0:cs_, 0:nchk, 0:16], dtr[:, :, 0:16], tab[0:cs_, 136:137], None, ALU.mult),
                 reads=[bk[pb_small], S["b_tab"]], writes=[S["b_dtq"]])
            P.op(DVE, lambda e: e.scalar_tensor_tensor(dt[0:cs_, 0:nchk, 0:16], dt[0:cs_, 0:nchk, 16:32], tab[0:cs_, 137:138], dt[0:cs_, 0:nchk, 0:16], ALU.mult, ALU.add),
                 reads=[S["b_dtq"], S["b_tab"]], writes=[S["b_dtq"]])
            P.op(DVE, lambda e: e.tensor_tensor(dt[0:cs_, 0:nchk, :], dt[0:cs_, 0:nchk, :], tab[0:cs_, 72:104].unsqueeze(1).to_broadcast([cs_, nchk, 32]), ALU.add),
                 reads=[S["b_dtq"], S["b_tab"]], writes=[S["b_dtq"]])
            P.op(ACT, lambda e: e.activation(dt[0:cs_, 0:nchk, :], dt[0:cs_, 0:nchk, :], AF.Exp), reads=[S["b_dtq"]], writes=[S["b_dtq"]])
            P.op(ACT, lambda e: e.activation(dt[0:cs_, 0:nchk, :], dt[0:cs_, 0:nchk, :], AF.Ln, bias=1.0), reads=[S["b_dtq"]], writes=[S["b_dtq"]])
            P.op(DVE, lambda e: e.tensor_tensor(adt[0:cs_, 0:nchk, :], dt[0:cs_, 0:nchk, :], S["A2"][0:cs_, :].unsqueeze(1).to_broadcast([cs_, nchk, 32]), ALU.mult),
                 reads=[S["b_dtq"], S["b_A2"]], writes=[S["b_dtq"]])
        def pre2():
            dt, adt = S["dt"], S["adt"]
            for k in range(nchk):
                P.op(PE, lambda e, k=k: e.matmul(banks[pb_small][0:cs_, 192 + 32 * k:224 + 32 * k], uincl[0:cs_, 0:cs_], adt[0:cs_, k, :], start=True, stop=True),
                     reads=[S["b_dtq"], b_cst], writes=[bk[pb_small]])
                P.op(PE, lambda e, k=k: e.matmul(banks[pb_small][:, 320 + 32 * k:352 + 32 * k], onesf[0:cs_, :], adt[0:cs_, k, :], start=True, stop=True),
                     reads=[S["b_dtq"], b_cst], writes=[bk[pb_small]])
            P.op(DVE, lambda e: e.tensor_copy(S["cs"][0:cs_, 0:nchk, :], banks[pb_small][0:cs_, 192:192 + 32 * nchk].rearrange("p (k c) -> p k c", c=32)),
                 reads=[bk[pb_small]], writes=[S["b_dtq"]])
            P.op(DVE, lambda e: e.tensor_copy(S["tot"][:, 0:nchk, :], banks[pb_small][:, 320:320 + 32 * nchk].rearrange("p (k c) -> p k c", c=32)),
                 reads=[bk[pb_small]], writes=[S["b_dtq"]])

        pend_silu = []

        def one_fc(fc):
            i = S["n"] % 2
            S["n"] += 1
            raw, acc, braw, bacc = S["raw"][i], S["acc"][i], S["b_raw"][i], S["b_acc"][i]
            pbm = pb_main[fc % len(pb_main)]
            wa = min(W, 512)
            for kc in range(8):
                P.op(PE, lambda e, kc=kc, fc=fc, pbm=pbm, wa=wa: e.matmul(banks[pbm][:, 0:wa], wv[:, kc, cx + fc * 128: cx + (fc + 1) * 128], uT[:, kc, 0:wa],
                                                                   start=(kc == 0), stop=(kc == 7)),
                     reads=[U["b_uT"], b_wres], writes=[bk[pbm]])
            P.op(ACT, lambda e, raw=raw, pbm=pbm, wa=wa: e.copy(raw[:, 0:wa], banks[pbm][:, 0:wa]), reads=[bk[pbm]], writes=[braw])
            if W > 512:
                for kc in range(8):
                    P.op(PE, lambda e, kc=kc, fc=fc: e.matmul(banks[pb_small][:, 0:W - 512], wv[:, kc, cx + fc * 128: cx + (fc + 1) * 128], uT[:, kc, 512:W],
                                                             start=(kc == 0), stop=(kc == 7)),
                         reads=[U["b_uT"], b_wres], writes=[bk[pb_small]])
                P.op(ACT, lambda e, raw=raw: e.copy(raw[:, 512:W], banks[pb_small][:, 0:W - 512]), reads=[bk[pb_small]], writes=[braw])
            flush_silu()
            P.op(DVE, lambda e, raw=raw, acc=acc, fc=fc: e.tensor_scalar(acc[:, 0:T], raw[:, 0:T], tab[:, fc * 5:fc * 5 + 1], tab[:, 60 + fc:61 + fc], ALU.mult, ALU.add),
                 reads=[braw, S["b_tab"]], writes=[bacc])
            for jj in range(1, 5):
                P.op(DVE, lambda e, raw=raw, acc=acc, fc=fc, jj=jj: e.scalar_tensor_tensor(acc[:, 0:T], raw[:, jj:jj + T], tab[:, fc * 5 + jj:fc * 5 + jj + 1], acc[:, 0:T], ALU.mult, ALU.add),
                     reads=[braw, S["b_tab"], bacc], writes=[bacc])
            pend_silu.append((acc, bacc, fc))

        def flush_silu():
            while pend_silu:
                acc, bacc, fc = pend_silu.pop(0)
                P.op(ACT, lambda e: e.activation(S["xcT"][:, fc, 0:T], acc[:, 0:T], AF.Silu), reads=[bacc], writes=[S["b_xcT"]])

        def post():
            flush_silu()
            tpv = banks[pb_tp][:, :].bitcast(BF16)
            for k in range(nchk):
                for fc in range(8):
                    P.op(PE, lambda e, k=k, fc=fc: e.transpose(tpv[0:cs_, fc * 128:(fc + 1) * 128], S["xcT"][:, fc, 128 * k:128 * k + cs_], identb[:, :]),
                         reads=[S["b_xcT"], b_idb], writes=[bk[pb_tp]])
                P.op(ACT, lambda e, k=k: e.copy(S["xtm"][0:cs_, k, :], tpv[0:cs_, :]), reads=[bk[pb_tp]], writes=[S["b_xtm"]])
                for g in range(2):
                    P.op(PE, lambda e, k=k, g=g: e.transpose(tpv[0:cs_, g * 128:(g + 1) * 128], S["xcT"][:, 8 + g, 128 * k:128 * k + cs_], identb[:, :]),
                         reads=[S["b_xcT"], b_idb], writes=[bk[pb_tp]])
                P.op(ACT, lambda e, k=k: e.copy(S["btm"][0:cs_, k, :], tpv[0:cs_, 0:256]), reads=[bk[pb_tp]], writes=[S["b_btm"]])
        def fc_then_pre2(fc):
            one_fc(fc)
            pre2()

        return pre, [((lambda fc=fc: fc_then_pre2(fc)) if fc == 1 else (lambda fc=fc: one_fc(fc))) for fc in range(12)], post

    def ssd_prep(S, U, j, t, T, W, wv, cx, cdt, pb_main, pb_small, pb_tp):
        pre, fcs, post = ssd_thunks(S, U, j, t, T, W, wv, cx, cdt, pb_main, pb_small, pb_tp)
        pre()
        for f in fcs:
            f()
        post()

    b_wres = P.buf("wres")

    for j in range(NJ):
        nt = cfg.ntiles(j)
        nctx = cfg.jobs[j]
        with ExitStack() as st:
            mkS = P.mark()
            wS = sb(st, "wS", [128, 8, 4640], BF16)
            b_wq = P.buf()
            P.dma(SP, lambda e: e.dma_start(out=wS[:, :, 3072:4640], in_=ws_in[:, :, 4096:5664]), reads=[b_ws["in"]], writes=[b_wres], sbuf=b_wres)
            P.dma(SP, lambda e: e.dma_start(out=wS[:, :, 0:3072], in_=ws_in[:, :, 0:3072]), reads=[b_ws["in"]], writes=[b_wq], sbuf=b_wq)
            for m in range(2):
                P.dma(POOL, lambda e, m=m: e.dma_start(out=KT[:, m, 64:72, 0:cfg.L(j)], in_=kaug[j][:, :, :]), writes=[b_KT], sbuf=b_KT)
            U0 = make_uT(st, "s")
            U1 = dict(U0)
            U1["uT"] = sb(st, "suT1", [128, 8, 516], BF16)
            U1["b_uT"] = P.buf()
            Us = [U0, U1]
            S_0 = make_ssd(st, "s")
            S_1 = ssd_par(st, S_0, "s")
            Ss = [S_0, S_1]
            kst = [sb(st, f"kst{i}", [128, 512], BF16) for i in range(3)]
            b_kst = [P.buf() for _ in range(3)]
            vst = sb(st, "vst", [128, 8, 4, 129], BF16)
            b_vst = P.buf("vst")
            wgt = sb(st, "wgt", [128, 32])
            xwp = sb(st, "xwp", [128, 16, 64], BF16)
            xws = sb(st, "xws", [128, 16, 64], BF16)
            decp = sb(st, "decp", [128, 32])
            snap = [sb(st, f"snap{i}", [128, D], BF16) for i in range(2)]
            b_wgt, b_xwp, b_xws, b_decp = P.buf(), P.buf(), P.buf(), P.buf()
            b_snap = [P.buf() for _ in range(2)]
            P.op(POOL, lambda e: e.memset(vst[:], 1.0), writes=[b_vst])
            P.op(POOL, lambda e: e.memset(CF[:], 0.0), writes=[b_CF])
            P.op(POOL, lambda e: e.memset(CB[:], 0.0), writes=[b_CB])
            kn = [0]
            sn = [0]

            def make_states(S, t, own, oi, cs_, nchk):
                def one_chunk(k):
                    jj = oi * 4 + k
                    dt, cs, tot = S["dt"], S["cs"], S["tot"]
                    nw = 32 if own else 16
                    xt3 = S["xtm"][0:cs_, k, :].rearrange("p (h d) -> p h d", h=16)
                    P.op(DVE, lambda e: e.tensor_tensor(wgt[0:cs_, 0:16], tot[0:cs_, k, 0:16], cs[0:cs_, k, 0:16], ALU.subtract), reads=[S["b_dtq"]], writes=[b_wgt])
                    if own:
                        P.op(DVE, lambda e: e.tensor_tensor(wgt[0:cs_, 16:32], cs[0:cs_, k, 16:32], S["adt"][0:cs_, k, 16:32], ALU.subtract), reads=[S["b_dtq"], b_wgt], writes=[b_wgt])
                    yield
                    P.op(ACT, lambda e: e.activation(wgt[0:cs_, 0:nw], wgt[0:cs_, 0:nw], AF.Exp), reads=[b_wgt], writes=[b_wgt])
                    P.op(ACT, lambda e: e.activation(decp[:, :], tot[:, k, :], AF.Exp), reads=[S["b_dtq"]], writes=[b_decp])
                    yield
                    P.op(DVE, lambda e: e.tensor_tensor(wgt[0:cs_, 0:nw], wgt[0:cs_, 0:nw], dt[0:cs_, k, 0:nw], ALU.mult), reads=[b_wgt, S["b_dtq"]], writes=[b_wgt])
                    P.op(DVE, lambda e: e.tensor_tensor(xwp[0:cs_, :, :], xt3, wgt[0:cs_, 0:16].unsqueeze(2).to_broadcast([cs_, 16, 64]), ALU.mult),
                         reads=[S["b_xtm"], b_wgt], writes=[b_xwp])
                    if own:
                        P.op(DVE, lambda e: e.tensor_tensor(xws[0:cs_, :, :], xt3, wgt[0:cs_, 16:32].unsqueeze(2).to_broadcast([cs_, 16, 64]), ALU.mult),
                             reads=[S["b_xtm"], b_wgt], writes=[b_xws])
                    yield
                    for g in range(2):
                        P.op(PE, lambda e, g=g: e.matmul(banks[6 + g][:, :], S["btm"][0:cs_, k, g * 128:(g + 1) * 128], xwp[0:cs_, g * 8:(g + 1) * 8, :].rearrange("p h d -> p (h d)"),
                                                         start=True, stop=True),
                             reads=[S["b_btm"], b_xwp], writes=[bk[6 + g]])
                    if not own:
                        carry, bcar = CB, b_CB
                    else:
                        carry, bcar = CF, b_CF
                        i = sn[0] % 2
                        sn[0] += 1
                        P.op(ACT, lambda e: e.copy(snap[i][:], CF[:]), reads=[b_CF], writes=[b_snap[i]])
                        P.dma(POOL, lambda e: e.dma_start(out=PFB[0, jj, :, :], in_=snap[i][:]), reads=[b_snap[i]], writes=[b_PFB], sbuf=b_snap[i])
                    yield
                    c3 = carry[:].rearrange("p (h d) -> p h d", h=16)
                    P.op(DVE, lambda e: e.tensor_tensor(c3, c3, decp[:, 0:16].unsqueeze(2).to_broadcast([128, 16, 64]), ALU.mult), reads=[bcar, b_decp], writes=[bcar])
                    for g in range(2):
                        P.op(DVE, lambda e, g=g: e.tensor_tensor(carry[:, g * 512:(g + 1) * 512], carry[:, g * 512:(g + 1) * 512], banks[6 + g][:, :], ALU.add),
                             reads=[bcar, bk[6 + g]], writes=[bcar])
                    if own:
                        P.op(POOL, lambda e: e.tensor_copy(decs[:, jj, :], decp[:, 16:32]), reads=[b_decp], writes=[b_decs])
                    yield
                    if own:
                        i2 = sn[0] % 2
                        sn[0] += 1
                        for g in range(2):
                            P.op(PE, lambda e, g=g: e.matmul(banks[6 + g][:, :], S["btm"][0:cs_, k, g * 128:(g + 1) * 128], xws[0:cs_, g * 8:(g + 1) * 8, :].rearrange("p h d -> p (h d)"),
                                                             start=True, stop=True),
                                 reads=[S["b_btm"], b_xws], writes=[bk[6 + g]])
                            P.op(ACT, lambda e, g=g: e.copy(snap[i2][:, g * 512:(g + 1) * 512], banks[6 + g][:, :]), reads=[bk[6 + g]], writes=[b_snap[i2]])
                        P.dma(POOL, lambda e: e.dma_start(out=SBS[jj, :, :], in_=snap[i2][:]), reads=[b_snap[i2]], writes=[b_SBS], sbuf=b_snap[i2])
                    yield

                def stages(k):
                    gen = one_chunk(k)
                    return [(lambda: next(gen, None)) for _ in range(6)]

                def tile_end():
                    if not own:
                        P.op(DVE, lambda e: e.scalar_tensor_tensor(CF[:], CB[:], S["tab"][:, 138:139], CF[:], ALU.mult, ALU.add), reads=[b_CB, b_CF, S["b_tab"]], writes=[b_CF])
                        P.op(DVE, lambda e: e.tensor_scalar(CB[:], CB[:], S["tab"][:, 139:140], None, ALU.mult), reads=[b_CB, S["b_tab"]], writes=[b_CB])
                    return None
                out = []
                for k in range(nchk):
                    out += stages(k)
                return out + [tile_end]

            pend_states = []
            for t in range(nt):
                T = 16 if t == 0 else 512
                W = T + 4
                own = t > nctx
                oi = t - nctx - 1
                soff = 0 if t == 0 else 16 + 512 * (t - 1)
                ch0 = 0 if t == 0 else 1 + 4 * (t - 1)
                nchk = max(1, T // 128)
                cs_ = min(T, 128)
                U = Us[t % 2]
                S = Ss[t % 2]
                if t == 0:
                    build_uT(U, xw[j][0], W, w1T, 0)
                uT = U["uT"]
                side = []
                if t + 1 < nt:
                    th = uT_thunks(Us[(t + 1) % 2], xw[j][t + 1], 516, w1T, 0)
                    side.append(th[0][0])
                    for q_ in range(1, len(th)):
                        side.append(th[q_][0])
                        side.append(th[q_ - 1][1])
                    side.append(th[-1][1])
                if pend_states:
                    merged = []
                    ps_ = list(pend_states)
                    while side or ps_:
                        if side:
                            merged.append(side.pop(0))
                        if side:
                            merged.append(side.pop(0))
                        if ps_:
                            merged.append(ps_.pop(0))
                    side = merged
                    pend_states = []
                pre, fcs, post = ssd_thunks(S, U, j, t, T, W, wS, 3072, 4608, [5], 3, 4)
                main = [pre]
                groups = []

                def kq_group(isq, h, T=T, uT=uT, U=U, soff=soff, oi=oi):
                    cbase = (0 if isq else 1024) + h * 128
                    pb = 1 + (kn[0] % 2)
                    i = kn[0] % 3
                    kn[0] += 1
                    for kc in range(8):
                        P.op(PE, lambda e, kc=kc: e.matmul(banks[pb][:, 0:T], wS[:, kc, cbase:cbase + 128], uT[:, kc, 2:2 + T], start=(kc == 0), stop=(kc == 7)),
                             reads=[U["b_uT"], b_wq], writes=[bk[pb]])
                    if isq:
                        P.op(ACT, lambda e: e.activation(kst[i][:, 0:T], banks[pb][:, 0:T], AF.Copy, scale=0.125), reads=[bk[pb]], writes=[b_kst[i]])
                        for m in range(2):
                            P.dma(POOL, lambda e, m=m: e.dma_start(out=QT[h, m, :, oi * 512:oi * 512 + T], in_=kst[i][64 * m:64 * m + 64, 0:T]),
                                  reads=[b_kst[i]], writes=[b_QT], sbuf=b_kst[i])
                    else:
                        P.op(ACT, lambda e: e.copy(kst[i][:, 0:T], banks[pb][:, 0:T]), reads=[bk[pb]], writes=[b_kst[i]])
                        for m in range(2):
                            P.dma(POOL, lambda e, m=m: e.dma_start(out=KT[h, m, 0:64, soff:soff + T], in_=kst[i][64 * m:64 * m + 64, 0:T]),
                                  reads=[b_kst[i]], writes=[b_KT], sbuf=b_kst[i])

                for isq in ([False, True] if own else [False]):
                    for h in range(NH):
                        groups.append(lambda isq=isq, h=h: kq_group(isq, h))

                def v_group(k, half, uT=uT, U=U, cs_=cs_):
                    pb = 1 + (kn[0] % 2)
                    kn[0] += 1
                    for kc in range(8):
                        P.op(PE, lambda e, kc=kc: e.matmul(banks[pb][0:cs_, :], uT[:, kc, 2 + 128 * k:2 + 128 * k + cs_], wS[:, kc, 2048 + half * 512:2048 + (half + 1) * 512],
                                                           start=(kc == 0), stop=(kc == 7)),
                             reads=[U["b_uT"], b_wq], writes=[bk[pb]])
                    P.op(ACT, lambda e: e.copy(vst[0:cs_, half * 4:(half + 1) * 4, k, 0:128], banks[pb][0:cs_, :].rearrange("p (h e) -> p h e", h=4)),
                         reads=[bk[pb]], writes=[b_vst])

                for k in range(nchk):
                    for half in range(2):
                        groups.append(lambda k=k, half=half: v_group(k, half))

                def v_store(t=t, ch0=ch0):
                    if t == 0:
                        P.dma(POOL, lambda e: e.dma_start(out=VA[:, 0:16, 0, :].rearrange("h p e -> p h e"), in_=vst[0:16, :, 0, :]),
                              reads=[b_vst], writes=[b_VA], sbuf=b_vst)
                    else:
                        P.dma(POOL, lambda e: e.dma_start(out=VA[:, :, ch0:ch0 + 4, :].rearrange("h p c e -> p h c e"), in_=vst[:, :, :, :]),
                              reads=[b_vst], writes=[b_VA], sbuf=b_vst)
                groups.append(v_store)
                ng_, done_ = len(groups), 0
                for fi_, f_ in enumerate(fcs):
                    main.append(f_)
                    upto_ = ((fi_ + 1) * ng_) // len(fcs)
                    main += groups[done_:upto_]
                    done_ = upto_
                main += groups[done_:]
                stride = max(1, len(main) // (len(side) + 1)) if side else len(main)
                si_ = 0
                for mi, f in enumerate(main):
                    f()
                    if side and (mi + 1) % stride == 0 and si_ < len(side):
                        side[si_]()
                        si_ += 1
                while si_ < len(side):
                    side[si_]()
                    si_ += 1
                post()
                pend_states = make_states(S, t, own, oi, cs_, nchk)
            for f in pend_states:
                f()
            pend_states = []
            ldb = [sb(st, f"ldb{i}", [128, D], BF16) for i in range(2)]
            b_ldb = [P.buf() for _ in range(2)]
            for jj in range(4 * NOWN - 1, -1, -1):
                i = sn[0] % 2
                sn[0] += 1
                P.op(ACT, lambda e, i=i: e.copy(snap[i][:], CB[:]), reads=[b_CB], writes=[b_snap[i]])
                P.dma(POOL, lambda e, i=i, jj=jj: e.dma_start(out=PFB[1, jj, :, :], in_=snap[i][:]), reads=[b_snap[i]], writes=[b_PFB], sbuf=b_snap[i])
                P.dma(SP, lambda e, i=i, jj=jj: e.dma_start(out=ldb[i][:], in_=SBS[jj, :, :]), reads=[b_SBS], writes=[b_ldb[i]], sbuf=b_ldb[i])
                c3 = CB[:].rearrange("p (h d) -> p h d", h=16)
                P.op(DVE, lambda e, c3=c3, jj=jj: e.tensor_tensor(c3, c3, decs[:, jj, :].unsqueeze(2).to_broadcast([128, 16, 64]), ALU.mult), reads=[b_CB, b_decs], writes=[b_CB])
                P.op(DVE, lambda e, i=i: e.tensor_tensor(CB[:], CB[:], ldb[i][:], ALU.add), reads=[b_CB, b_ldb[i]], writes=[b_CB])
            P.barrier()
            P.release(mkS)

        with ExitStack() as stT:
            mkT = P.mark()
            wO = sb(stT, "wO", [128, 8, 2592], BF16)
            P.dma(SP, lambda e: e.dma_start(out=wO[:, :, 0:1024], in_=ws_in[:, :, 3072:4096]), reads=[b_ws["in"]], writes=[b_wres], sbuf=b_wres)
            P.dma(SP, lambda e: e.dma_start(out=wO[:, :, 1024:2592], in_=ws_in[:, :, 4096:5664]), reads=[b_ws["in"]], writes=[b_wres], sbuf=b_wres)
            mixT = sb(stT, "mixT", [128, 16, 512], BF16)
            b_mix = P.buf("mixT")
            UT = make_uT(stT, "t")
            wpn = [0]
            nk_chunks = cfg.nch(j)
            for oi in range(NOWN):
                t_own = nctx + 1 + oi
                with ExitStack() as st:
                    mkA = P.mark()
                    qv = [[sb(st, f"qv{r}_{v}", [72, 2, 512], BF16) for v in range(3)] for r in range(2)]
                    b_qv = [P.buf() for _ in range(2)]
                    NKV = 4
                    kp = [sb(st, f"kp{i}", [72, 2, 2048], BF16) for i in range(NKV)]
                    vp = [sb(st, f"vp{i}", [128, 16, 129], BF16) for i in range(NKV)]
                    b_kv = [P.buf() for _ in range(NKV)]
                    tmpb = [sb(st, f"tmpb{i}", [128, 1024]) for i in range(2)]
                    b_tmpb = [P.buf() for _ in range(2)]
                    pt = [sb(st, f"pt{i}", [128, 1024], BF16) for i in range(3)]
                    b_pt = [P.buf() for _ in range(3)]
                    ob = sb(st, "ob", [128, 8, 129])
                    rl = sb(st, "rl", [128, 8])
                    o1 = sb(st, "o1", [128, 4, 128])
                    o2 = sb(st, "o2", [128, 4, 128])
                    ssq = sb(st, "ssq", [128, 4])
                    junk = sb(st, "junk", [128, 128], BF16)
                    junkf = sb(st, "junkf", [128, 128])
                    attb = sb(st, "attb", [128, 4, 128], BF16)
                    b_ob, b_rl, b_o1, b_o2, b_ssq, b_junk, b_attb = (P.buf() for _ in range(7))
                    pn = [0]
                    sbn = [0]
                    pieces = [(0, 1)] + [(c0, min(16, nk_chunks - c0)) for c0 in range(1, nk_chunks, 16)]
                    acc_v = [banks[4 + a // 3][:, (a % 3) * 129:(a % 3) * 129 + 129] for a in range(8)]
                    epi_gen = [None]
                    carry = [None]
                    th_ = uT_thunks(UT, xw[j][t_own], 516, w1T, 7)
                    ut_side = [th_[0][0]]
                    for q_ in range(1, len(th_)):
                        ut_side.append(th_[q_][0])
                        ut_side.append(th_[q_ - 1][1])
                    ut_side.append(th_[-1][1])

                    def epilogue(h):
                        P.op(DVE, lambda e: e.reciprocal(rl[:, :].unsqueeze(2), ob[:, :, 128:129]), reads=[b_ob], writes=[b_rl])
                        P.op(DVE, lambda e: e.tensor_tensor(o1[:, :, :], ob[:, 0:4, 0:128], rl[:, 0:4].unsqueeze(2).to_broadcast([128, 4, 128]), ALU.mult), reads=[b_ob, b_rl], writes=[b_o1])
                        P.op(DVE, lambda e: e.tensor_tensor(o2[:, :, :], ob[:, 4:8, 0:128], rl[:, 4:8].unsqueeze(2).to_broadcast([128, 4, 128]), ALU.mult), reads=[b_ob, b_rl], writes=[b_o2])
                        P.op(DVE, lambda e: e.scalar_tensor_tensor(o1[:, :, :], o2[:, :, :], neglam, o1[:, :, :], ALU.mult, ALU.add), reads=[b_o1, b_o2, b_lam], writes=[b_o1])
                        yield
                        for qc in range(4):
                            P.op(DVE, lambda e, qc=qc: e.scalar_tensor_tensor(junkf[:, :], o1[:, qc, :], 1.0, o1[:, qc, :], ALU.mult, ALU.mult, accum_out=ssq[:, qc:qc + 1]),
                                 reads=[b_o1], writes=[b_junk, b_ssq])
                        yield
                        P.op(ACT, lambda e: e.activation(ssq[:, :], ssq[:, :], AF.Ln, scale=1.0 / 128, bias=EPS), reads=[b_ssq], writes=[b_ssq])
                        P.op(ACT, lambda e: e.activation(ssq[:, :], ssq[:, :], AF.Exp, scale=-0.5), reads=[b_ssq], writes=[b_ssq])
                        yield
                        P.op(DVE, lambda e: e.tensor_tensor(o1[:, :, :], o1[:, :, :], ssq[:, :].unsqueeze(2).to_broadcast([128, 4, 128]), ALU.mult), reads=[b_o1, b_ssq], writes=[b_o1])
                        P.op(DVE, lambda e: e.tensor_tensor(attb[:, :, :], o1[:, :, :], attw.unsqueeze(1).to_broadcast([128, 4, 128]), ALU.mult), reads=[b_o1, b_gt], writes=[b_attb])
                        yield
                        tpv = banks[7][:, :].bitcast(BF16)
                        for qc in range(4):
                            P.op(PE, lambda e, qc=qc: e.transpose(tpv[:, qc * 128:(qc + 1) * 128], attb[:, qc, :], identb[:, :]), reads=[b_attb, b_idb], writes=[bk[7]])
                        yield
                        P.op(DVE, lambda e: e.tensor_copy(mixT[:, h, :], banks[7][:, :].bitcast(BF16)[:, 0:512]), reads=[bk[7]], writes=[b_mix])
                        yield


                    for h in range(NH):
                        slope = 2.0 ** (-(h + 1))
                        r = h % 2
                        for v in range(3):
                            P.dma(SP, lambda e, r=r, v=v, h=h: e.dma_start(out=qv[r][v][0:64, :, :], in_=QT[h, :, :, oi * 512:(oi + 1) * 512].rearrange("m d q -> d m q")),
                                  reads=[b_QT], writes=[b_qv[r]], sbuf=b_qv[r])
                            for m in range(2):
                                P.dma(SP, lambda e, r=r, v=v, h=h, m=m: e.dma_start(out=qv[r][v][64:72, m, :], in_=qaug[j][v, h, :, oi * 512:(oi + 1) * 512]),
                                      writes=[b_qv[r]], sbuf=b_qv[r])
                        first_in_bank = {4: True, 5: True, 6: True}
                        steps = []
                        for (c0, ncn) in pieces:
                            for ci in range(ncn):
                                steps.append((c0, ncn, ci))
                        stinfo = {}

                        def emit_qk(si, h=h, r=r, slope=slope):
                            c0, ncn, ci = steps[si]
                            if ci == 0:
                                pi = pn[0] % NKV
                                pn[0] += 1
                                koff0 = 0 if c0 == 0 else 16 + 128 * (c0 - 1)
                                klen = 16 if c0 == 0 else 128 * ncn
                                P.dma(SP, lambda e: e.dma_start(out=kp[pi][:, :, 0:klen], in_=KT[h, :, :, koff0:koff0 + klen].rearrange("m r l -> r m l")),
                                      reads=[b_KT], writes=[b_kv[pi]], sbuf=b_kv[pi])
                                if c0 == 0:
                                    P.dma(SP, lambda e: e.dma_start(out=vp[pi][0:16, 0, :], in_=VA[h, 0:16, 0, :]), reads=[b_VA], writes=[b_kv[pi]], sbuf=b_kv[pi])
                                else:
                                    P.dma(SP, lambda e: e.dma_start(out=vp[pi][:, 0:ncn, :], in_=VA[h, :, c0:c0 + ncn, :]), reads=[b_VA], writes=[b_kv[pi]], sbuf=b_kv[pi])
                                stinfo["pi"] = pi
                            pi = stinfo["pi"]
                            c = c0 + ci
                            ks = 16 if c == 0 else 128
                            kt = 0 if c == 0 else 1 + (c - 1) // 4
                            kk = (c - 1) % 4
                            if kt <= nctx:
                                var, KR, ovl = 0, 72, False
                            elif kt < t_own:
                                var, KR, ovl = 1, 72, False
                            elif kt > t_own:
                                var, KR, ovl = 2, 72, False
                            else:
                                var, KR, ovl = 0, 64, True
                            sbi = sbn[0] % 2
                            pti = sbn[0] % 3
                            sbn[0] += 1
                            Bs = (2 * sbi, 2 * sbi + 1)
                            for m in range(2):
                                P.op(PE, lambda e, m=m: e.matmul(banks[Bs[m]][0:ks, :], kp[pi][0:KR, m, 128 * ci:128 * ci + ks], qv[r][var][0:KR, m, :], start=True, stop=True),
                                     reads=[b_kv[pi], b_qv[r]], writes=[bk[Bs[m]]])
                            if ovl:
                                tb = tmpb[sbi]
                                for m in range(2):
                                    P.op(DVE, lambda e, m=m: e.scalar_tensor_tensor(tb[:, m * 512:(m + 1) * 512], absd[kk], -slope, banks[Bs[m]][:, :], ALU.mult, ALU.add),
                                         reads=[b_cst, bk[Bs[m]]], writes=[b_tmpb[sbi]])
                                P.op(ACT, lambda e: e.activation(pt[pti][:, :], tb[:, :], AF.Exp), reads=[b_tmpb[sbi]], writes=[b_pt[pti]])
                            else:
                                P.op(ACT, lambda e: e.activation(pt[pti][0:ks, :], allbanks[0:ks, 512 * Bs[0]:512 * Bs[0] + 1024], AF.Exp),
                                     reads=[bk[Bs[0]], bk[Bs[1]]], writes=[b_pt[pti]])
                            return (pi, ci, ks, pti)

                        def emit_pv(info, first_in_bank=first_in_bank):
                            pi, ci, ks, pti = info
                            for a in range(8):
                                m, qc = a // 4, a % 4
                                bkn = 4 + a // 3
                                stt = first_in_bank[bkn]
                                first_in_bank[bkn] = False
                                P.op(PE, lambda e, a=a, m=m, qc=qc, stt=stt: e.matmul(acc_v[a], pt[pti][0:ks, m * 512 + qc * 128:m * 512 + (qc + 1) * 128], vp[pi][0:ks, ci, :],
                                                                                 start=stt, stop=False, skip_group_check=True),
                                     reads=[b_pt[pti], b_kv[pi]], writes=[bk[bkn]])

                        prev = None
                        for si in range(len(steps)):
                            cur = emit_qk(si)
                            if si == 0 and carry[0] is not None:
                                carry[0]()
                                carry[0] = None
                            if prev is not None:
                                emit_pv(prev)
                            prev = cur
                            if epi_gen[0] is not None and si % 2 == 1:
                                if next(epi_gen[0], "done") == "done":
                                    epi_gen[0] = None
                            if ut_side and si % 2 == 0 and si >= 2:
                                ut_side.pop(0)()

                        def finish(prev=prev, emit_pv=emit_pv, h=h):
                            emit_pv(prev)
                            while epi_gen[0] is not None:
                                if next(epi_gen[0], "done") == "done":
                                    epi_gen[0] = None
                            for bi in range(3):
                                na = 3 if bi < 2 else 2
                                P.op(DVE, lambda e, bi=bi, na=na: e.tensor_copy(ob[:, bi * 3:bi * 3 + na, :], banks[4 + bi][:, 0:129 * na].rearrange("p (a c) -> p a c", c=129)),
                                     reads=[bk[4 + bi]], writes=[b_ob])
                            epi_gen[0] = epilogue(h)

                        carry[0] = finish
                    carry[0]()
                    while epi_gen[0] is not None:
                        if next(epi_gen[0], "done") == "done":
                            epi_gen[0] = None
                    while ut_side:
                        ut_side.pop(0)()
                    P.barrier()
                    P.release(mkA)
                with ExitStack() as st:
                    mkO = P.mark()
                    U = UT
                    S = make_ssd(st, "o")
                    sz = sb(st, "sz", [128, 4, D], BF16)
                    b_sz = P.buf()
                    pfb = [sb(st, f"pfb{i}", [128, 2, D], BF16) for i in range(2)]
                    b_pfb = [P.buf() for _ in range(2)]
                    gtm2 = [sb(st, f"gtm{i}", [128, 4, 128]) for i in range(2)]
                    b_gtm2 = [P.buf() for _ in range(2)]
                    ef = [sb(st, f"ef{i}", [128, 128]) for i in range(6)]
                    b_ef = [P.buf() for _ in range(6)]
                    mt = [sb(st, f"mt{i}", [128, 128], BF16) for i in range(6)]
                    b_mt = [P.buf() for _ in range(6)]
                    xdt2 = [sb(st, f"xdt{i}", [128, 2, 16, 64], BF16) for i in range(2)]
                    b_xdt2 = [P.buf() for _ in range(2)]
                    bia2 = [sb(st, f"bia{i}", [128, 64]) for i in range(2)]
                    b_bia2 = [P.buf() for _ in range(2)]
                    yt_2 = [sb(st, f"yt{i}", [128, D]) for i in range(2)]
                    yt2_2 = [sb(st, f"ytt{i}", [128, D]) for i in range(2)]
                    b_yt_2 = [P.buf() for _ in range(2)]
                    b_yt2_2 = [P.buf() for _ in range(2)]
                    ssb2 = [sb(st, f"ssb{i}", [128, D], BF16) for i in range(2)]
                    b_ssb2 = [P.buf() for _ in range(2)]
                    sso = sb(st, "sso", [128, 4])
                    b_sso4 = [P.buf() for _ in range(4)]
                    jq = sb(st, "jq", [128, D], BF16)
                    b_jq = P.buf()
                    ahl = sb(st, "ahl", [128, 2, 32], BF16)
                    ahf = sb(st, "ahf", [128, 32])
                    b_ahl, b_ahf = P.buf(), P.buf()
                    SEGB = (0, 1, 2, 3, 5)
                    segq = [banks[bq][:, 0:128] for bq in SEGB]
                    b_segq = [bk[bq] for bq in SEGB]
                    uT = U["uT"]
                    pre_, fcs_, post_ = ssd_thunks(S, U, j, t_own, 512, 516, wO, 1024, 2560, [1], 2, 3)

                    def z_group(k, half):
                        zb = 4 + (k * 2 + half) % 2
                        for kc in range(8):
                            P.op(PE, lambda e, kc=kc: e.matmul(banks[zb][:, :], uT[:, kc, 2 + 128 * k:130 + 128 * k], wO[:, kc, half * 512:(half + 1) * 512], start=(kc == 0), stop=(kc == 7)),
                                 reads=[U["b_uT"], b_wres], writes=[bk[zb]])
                        P.op(ACT, lambda e: e.activation(sz[:, k, half * 512:(half + 1) * 512], banks[zb][:, :], AF.Silu), reads=[bk[zb]], writes=[b_sz])

                    pre_()
                    for fc in range(12):
                        fcs_[fc]()
                        if fc < 8:
                            z_group(fc // 2, fc % 2)
                    post_()
                    en = [0]
                    pend = {"front": None, "back": None}
                    for k in range(4):
                        jj = oi * 4 + k
                        pi = k % 2
                        gtm, b_gtm, xdt, b_xdt, bia, b_bia = gtm2[pi], b_gtm2[pi], xdt2[pi], b_xdt2[pi], bia2[pi], b_bia2[pi]
                        yt, yt2, b_yt, b_yt2, ssb, b_ssb = yt_2[pi], yt2_2[pi], b_yt_2[pi], b_yt2_2[pi], ssb2[pi], b_ssb2[pi]
                        for d_ in range(2):
                            P.dma(SP, lambda e, pi=pi, d_=d_, jj=jj: e.dma_start(out=pfb[pi][:, d_, :], in_=PFB[d_, jj, :, :]), reads=[b_PFB], writes=[b_pfb[pi]], sbuf=b_pfb[pi])
                        dt, adt, cs, tot = S["dt"], S["adt"], S["cs"], S["tot"]
                        P.op(DVE, lambda e, k=k: e.tensor_scalar(bia[:, 0:16], cs[:, k, 0:16], -1.0, None, ALU.mult), reads=[S["b_dtq"]], writes=[b_bia])
                        P.op(DVE, lambda e, k=k: e.tensor_tensor(bia[:, 16:32], cs[:, k, 16:32], adt[:, k, 16:32], ALU.subtract), reads=[S["b_dtq"]], writes=[b_bia])
                        P.op(DVE, lambda e, k=k: e.tensor_tensor(bia[:, 48:64], tot[:, k, 16:32], bia[:, 16:32], ALU.subtract), reads=[S["b_dtq"], b_bia], writes=[b_bia])
                        P.op(ACT, lambda e, k=k: e.activation(bia[:, 32:48], cs[:, k, 0:16], AF.Exp), reads=[S["b_dtq"]], writes=[b_bia])
                        P.op(ACT, lambda e: e.activation(bia[:, 48:64], bia[:, 48:64], AF.Exp), reads=[b_bia], writes=[b_bia])
                        P.op(DVE, lambda e, k=k: e.tensor_copy(ahl[:, 0, :], adt[:, k, :]), reads=[S["b_dtq"]], writes=[b_ahl])
                        P.op(DVE, lambda e: e.tensor_copy(ahf[:, :], ahl[:, 0, :]), reads=[b_ahl], writes=[b_ahf])
                        P.op(DVE, lambda e, k=k: e.tensor_tensor(ahl[:, 1, :], adt[:, k, :], ahf[:, :], ALU.subtract), reads=[S["b_dtq"], b_ahf, b_ahl], writes=[b_ahl])
                        xt3 = S["xtm"][:, k, :].rearrange("p (h d) -> p h d", h=16)
                        for d_ in range(2):
                            P.op(POOL if d_ else DVE, lambda e, d_=d_, k=k, xt3=xt3: e.tensor_tensor(xdt[:, d_, :, :], xt3, dt[:, k, 16 * d_:16 * d_ + 16].unsqueeze(2).to_broadcast([128, 16, 64]), ALU.mult),
                                 reads=[S["b_xtm"], S["b_dtq"]], writes=[b_xdt])
                        for g in range(2):
                            P.op(PE, lambda e, g=g, k=k: e.matmul(banks[4][:, g * 128:(g + 1) * 128], S["xcT"][:, 8 + g, 128 * k:128 * k + 128], S["xcT"][:, 10 + g, 128 * k:128 * k + 128], start=True, stop=True),
                                 reads=[S["b_xcT"]], writes=[bk[4]])
                        for g in range(2):
                            P.op(DVE, lambda e, g=g: e.tensor_tensor(gtm[:, 2 * g, :], banks[4][:, g * 128:(g + 1) * 128], maskf, ALU.mult), reads=[bk[4], b_cst], writes=[b_gtm])
                            P.op(DVE, lambda e, g=g: e.tensor_tensor(gtm[:, 2 * g + 1, :], banks[4][:, g * 128:(g + 1) * 128], maskb, ALU.mult), reads=[bk[4], b_cst], writes=[b_gtm])
                        for g in range(2):
                            P.op(PE, lambda e, g=g, k=k, pi=pi: e.matmul(banks[g][:, :], S["xcT"][:, 10 + g, 128 * k:128 * k + 128], pfb[pi][:, 0, g * 512:(g + 1) * 512], start=True, stop=True),
                                 reads=[S["b_xcT"], b_pfb[pi]], writes=[bk[g]])
                        for g in range(2):
                            sl = slice(g * 512, (g + 1) * 512)
                            y3 = yt[:, sl].rearrange("p (h d) -> p h d", h=8)
                            P.op(DVE, lambda e, g=g, y3=y3: e.tensor_tensor(y3, banks[g][:, :].rearrange("p (h d) -> p h d", h=8), bia[:, 32 + 8 * g:40 + 8 * g].unsqueeze(2).to_broadcast([128, 8, 64]), ALU.mult),
                                 reads=[bk[g], b_bia], writes=[b_yt])
                        for g in range(2):
                            P.op(PE, lambda e, g=g, k=k, pi=pi: e.matmul(banks[g][:, :], S["xcT"][:, 10 + g, 128 * k:128 * k + 128], pfb[pi][:, 1, g * 512:(g + 1) * 512], start=True, stop=True),
                                 reads=[S["b_xcT"], b_pfb[pi]], writes=[bk[g]])
                        for g in range(2):
                            sl = slice(g * 512, (g + 1) * 512)
                            y23 = yt2[:, sl].rearrange("p (h d) -> p h d", h=8)
                            P.op(DVE, lambda e, g=g, y23=y23: e.tensor_tensor(y23, banks[g][:, :].rearrange("p (h d) -> p h d", h=8), bia[:, 48 + 8 * g:56 + 8 * g].unsqueeze(2).to_broadcast([128, 8, 64]), ALU.mult),
                                 reads=[bk[g], b_bia], writes=[b_yt2])
                        P.op(POOL, lambda e: e.tensor_tensor(yt[:, :], yt[:, :], yt2[:, :], ALU.add), reads=[b_yt, b_yt2], writes=[b_yt])
                        items = [(h, d_) for h in range(16) for d_ in range(2)]

                        def front(ii, k=k):
                            h, d_ = items[ii]
                            g = h // 8
                            q = ii % 6
                            rhs = umb[:, 0, :] if d_ == 0 else umb[:, 1, :]
                            sq_ = ii % 5
                            for hl in range(2):
                                P.op(PE, lambda e, hl=hl: e.matmul(segq[sq_], ahl[:, hl, 16 * d_ + h:16 * d_ + h + 1].to_broadcast([128, 128]), rhs, start=(hl == 0), stop=(hl == 1), skip_group_check=True),
                                     reads=[b_ahl, b_cstb], writes=[b_segq[sq_]])
                            P.op(ACT, lambda e: e.activation(ef[q][:, :], segq[sq_], AF.Exp, bias=bia[:, 16 * d_ + h:16 * d_ + h + 1]), reads=[b_segq[sq_], b_bia], writes=[b_ef[q]])
                            P.op(DVE, lambda e: e.scalar_tensor_tensor(mt[q][:, :], ef[q][:, :], 1.0, gtm[:, 2 * g + d_, :], ALU.min, ALU.mult),
                                 reads=[b_ef[q], b_gtm], writes=[b_mt[q]])

                        def back(ii):
                            h, d_ = items[ii]
                            q = ii % 6
                            P.op(PE, lambda e: e.matmul(banks[6 + h // 8][:, (h % 8) * 64:(h % 8) * 64 + 64], mt[q][:, :], xdt[:, d_, h, :], start=(d_ == 0), stop=(d_ == 1), skip_group_check=True),
                                 reads=[b_mt[q], b_xdt], writes=[bk[6 + h // 8]])

                        LA = 4
                        for ii in range(LA):
                            front(ii)
                        for ii in range(32):
                            if ii + LA < 32:
                                front(ii + LA)
                            back(ii)
                            if ii == 8 and pend["front"] is not None:
                                pend["front"]()
                                pend["front"] = None
                        if pend["back"] is not None:
                            pend["back"]()
                            pend["back"] = None
                        for g in range(2):
                            sl = slice(g * 512, (g + 1) * 512)
                            P.op(DVE, lambda e, g=g, sl=sl: e.tensor_tensor(yt[:, sl], yt[:, sl], banks[6 + g][:, :], ALU.add), reads=[b_yt, bk[6 + g]], writes=[b_yt])

                        def tail_front(k=k, yt=yt, yt2=yt2, b_yt=b_yt, b_yt2=b_yt2, ssb=ssb, b_ssb=b_ssb, xt3=xt3):
                            P.op(POOL, lambda e: e.tensor_tensor(yt2[:, :].rearrange("p (h d) -> p h d", h=16), xt3, Dbc.unsqueeze(2).to_broadcast([128, 16, 64]), ALU.mult),
                                 reads=[S["b_xtm"], b_gt, b_yt2], writes=[b_yt2])
                            P.op(POOL, lambda e: e.tensor_tensor(yt[:, :], yt[:, :], yt2[:, :], ALU.add), reads=[b_yt, b_yt2], writes=[b_yt])
                            P.op(DVE, lambda e: e.tensor_tensor(yt[:, :], yt[:, :], sz[:, k, :], ALU.mult), reads=[b_yt, b_sz], writes=[b_yt])
                            rms_rstd(ACT, yt[:, :], 128, D, jq[:, :], b_jq, sso[:, k:k + 1], b_sso4[k], [b_yt])
                            P.op(DVE, lambda e: e.scalar_tensor_tensor(ssb[:, :], yt[:, :], sso[:, k:k + 1], ssmw, ALU.mult, ALU.mult), reads=[b_yt, b_sso4[k], b_gt], writes=[b_ssb])

                        def tail_back(k=k, ssb=ssb, b_ssb=b_ssb):
                            tpv = banks[4][:, :].bitcast(BF16)
                            for fc in range(8):
                                P.op(PE, lambda e, fc=fc: e.transpose(tpv[:, fc * 128:(fc + 1) * 128], ssb[:, fc * 128:(fc + 1) * 128], identb[:, :]), reads=[b_ssb, b_idb], writes=[bk[4]])
                            P.op(ACT, lambda e: e.copy(mixT[:, 8:16, 128 * k:128 * k + 128], tpv[:, :].rearrange("p (f t) -> p f t", f=8)), reads=[bk[4]], writes=[b_mix])

                        pend["front"], pend["back"] = tail_front, tail_back
                    pend["front"]()
                    pend["back"]()
                    P.barrier()
                    P.release(mkO)
                with ExitStack() as st:
                    mkM = P.mark()
                    NWP = 6
                    wpool = [sb(st, f"wp{i}", [128, 4096], BF16) for i in range(NWP)]
                    b_wp = [P.buf() for _ in range(NWP)]
                    h1 = sb(st, "h1", [128, 4, D])
                    b_h1 = [P.buf() for _ in range(4)]
                    u2b = [sb(st, f"u2b{i}", [128, D], BF16) for i in range(2)]
                    b_u2b = [P.buf() for _ in range(2)]
                    u2T = sb(st, "u2T", [128, 8, 512], BF16)
                    b_u2T = P.buf()
                    rT = [sb(st, f"rT{i}", [128, 512], BF16) for i in range(2)]
                    b_rT = [P.buf() for _ in range(2)]
                    aT = [sb(st, f"aT{i}", [128, 4, 512], BF16) for i in range(2)]
                    b_aT = [P.buf() for _ in range(2)]
                    ss2 = sb(st, "ss2", [128, 4])
                    b_ss2 = [P.buf() for _ in range(4)]
                    jq = sb(st, "jq2", [128, D], BF16)
                    b_jq = P.buf()
                    ot = [sb(st, f"ot{i}", [128, D]) for i in range(2)]
                    b_ot = [P.buf() for _ in range(2)]
                    for k in range(4):
                        P.dma(SP, lambda e, k=k: e.dma_start(out=h1[:, k, :], in_=xw[j][t_own, 2 + 128 * k:130 + 128 * k, :]), writes=[b_h1[k]], sbuf=b_h1[k])
                    mn = [0]
                    wvs = []
                    for cb in range(4):
                        wi = wpn[0] % NWP
                        wpn[0] += 1
                        wv = wpool[wi][:, :].rearrange("p (k c) -> p k c", k=16)
                        P.dma(SP, lambda e, wv=wv, cb=cb: e.dma_start(out=wv, in_=ws_out[:, :, cb * 256:(cb + 1) * 256]), reads=[b_ws["out"]], writes=[b_wp[wi]], sbuf=b_wp[wi])
                        wvs.append((wv, wi))
                    tpv = banks[2][:, :].bitcast(BF16).rearrange("p (k t) -> p k t", k=8)

                    def n2_front(k):
                        rms_rstd(ACT, h1[:, k, :], 128, D, jq[:, :], b_jq, ss2[:, k:k + 1], b_ss2[k], [b_h1[k]])
                        P.op(ACT, lambda e: e.activation(u2b[k % 2][:, :], h1[:, k, :], AF.Copy, scale=ss2[:, k:k + 1]), reads=[b_h1[k], b_ss2[k]], writes=[b_u2b[k % 2]])

                    def n2_back(k):
                        for kc in range(8):
                            P.op(PE, lambda e, kc=kc: e.transpose(tpv[:, kc, :], u2b[k % 2][:, kc * 128:(kc + 1) * 128], identb[:, :]), reads=[b_u2b[k % 2], b_idb], writes=[bk[2]])
                        P.op(DVE, lambda e: e.tensor_tensor(u2T[:, :, 128 * k:128 * k + 128], tpv, w2T.unsqueeze(2).to_broadcast([128, 8, 128]), ALU.mult),
                             reads=[bk[2], b_gt], writes=[b_u2T])

                    for k in range(4):
                        for cb in range(4):
                            wv, wi = wvs[cb]
                            pb = mn[0] % 2
                            mn[0] += 1
                            for kc in range(16):
                                P.op(PE, lambda e, kc=kc, wv=wv: e.matmul(banks[pb][:, 0:256], mixT[:, kc, 128 * k:128 * k + 128], wv[:, kc, :], start=(kc == 0), stop=(kc == 15)),
                                     reads=[b_mix, b_wp[wi]], writes=[bk[pb]])
                            P.op(DVE, lambda e, cb=cb: e.tensor_tensor(h1[:, k, cb * 256:(cb + 1) * 256], h1[:, k, cb * 256:(cb + 1) * 256], banks[pb][:, 0:256], ALU.add),
                                 reads=[b_h1[k], bk[pb]], writes=[b_h1[k]])
                        n2_front(k)
                        if k >= 1:
                            n2_back(k - 1)
                    n2_back(3)
                    un = [0]
                    dn_w = {}

                    def emit_up(p_):
                        wi = wpn[0] % NWP
                        wpn[0] += 1
                        wu = wpool[wi][:, :].rearrange("p (k c) -> p k c", k=8)
                        P.dma(SP, lambda e: e.dma_start(out=wu, in_=ws_up[:, :, p_ * 512:(p_ + 1) * 512]), reads=[b_ws["up"]], writes=[b_wp[wi]], sbuf=b_wp[wi])
                        wj = wpn[0] % NWP
                        wpn[0] += 1
                        wd = wpool[wj][:, :].rearrange("p (k c) -> p k c", k=4)
                        P.dma(SP, lambda e: e.dma_start(out=wd, in_=ws_dn[:, p_ * 4:(p_ + 1) * 4, :]), reads=[b_ws["dn"]], writes=[b_wp[wj]], sbuf=b_wp[wj])
                        dn_w[p_] = (wd, wj)
                        ai = p_ % 2
                        for fc in range(4):
                            pb = 3 + (un[0] % 2)
                            ri = un[0] % 2
                            un[0] += 1
                            for kc in range(8):
                                P.op(PE, lambda e, kc=kc: e.matmul(banks[pb][:, :], wu[:, kc, fc * 128:(fc + 1) * 128], u2T[:, kc, :], start=(kc == 0), stop=(kc == 7)),
                                     reads=[b_u2T, b_wp[wi]], writes=[bk[pb]])
                            P.op(ACT, lambda e: e.activation(rT[ri][:, :], banks[pb][:, :], AF.Relu), reads=[bk[pb]], writes=[b_rT[ri]])
                            P.op(DVE, lambda e: e.tensor_tensor(aT[ai][:, fc, :], rT[ri][:, :], rT[ri][:, :], ALU.mult), reads=[b_rT[ri]], writes=[b_aT[ai]])

                    def emit_down(p_):
                        wd, wj = dn_w[p_]
                        ai = p_ % 2
                        for k in range(4):
                            for ch in range(2):
                                pb = 5 + (un[0] % 2)
                                un[0] += 1
                                for fc in range(4):
                                    P.op(PE, lambda e, fc=fc: e.matmul(banks[pb][:, :], aT[ai][:, fc, 128 * k:128 * k + 128], wd[:, fc, ch * 512:(ch + 1) * 512], start=(fc == 0), stop=(fc == 3)),
                                         reads=[b_aT[ai], b_wp[wj]], writes=[bk[pb]])
                                P.op(DVE, lambda e: e.tensor_tensor(h1[:, k, ch * 512:(ch + 1) * 512], h1[:, k, ch * 512:(ch + 1) * 512], banks[pb][:, :], ALU.add),
                                     reads=[b_h1[k], bk[pb]], writes=[b_h1[k]])

                    emit_up(0)
                    for p_ in range(8):
                        if p_ + 1 < 8:
                            emit_up(p_ + 1)
                        emit_down(p_)
                    for k in range(4):
                        P.op(ACT, lambda e, k=k: e.activation(jq[:, :], h1[:, k, :], AF.Square, accum_out=ss2[:, k:k + 1]), reads=[b_h1[k]], writes=[b_jq, b_ss2[k]])
                    for k in range(4):
                        P.op(DVE, lambda e, k=k: e.tensor_scalar(ss2[:, k:k + 1], ss2[:, k:k + 1], 1.0 / D, EPS, ALU.mult, ALU.add), reads=[b_ss2[k]], writes=[b_ss2[k]])
                    for k in range(4):
                        P.op(ACT, lambda e, k=k: e.activation(ss2[:, k:k + 1], ss2[:, k:k + 1], AF.Ln), reads=[b_ss2[k]], writes=[b_ss2[k]])
                    for k in range(4):
                        P.op(ACT, lambda e, k=k: e.activation(ss2[:, k:k + 1], ss2[:, k:k + 1], AF.Exp, scale=-0.5), reads=[b_ss2[k]], writes=[b_ss2[k]])
                    for k in range(4):
                        oi2 = k % 2
                        P.op(DVE, lambda e, k=k, oi2=oi2: e.scalar_tensor_tensor(ot[oi2][:, :], h1[:, k, :], ss2[:, k:k + 1], finw, ALU.mult, ALU.mult), reads=[b_h1[k], b_ss2[k], b_gt], writes=[b_ot[oi2]])
                        P.dma(POOL, lambda e, k=k, oi2=oi2: e.dma_start(out=yout[j][oi * 512 + 128 * k:oi * 512 + 128 * k + 128, :], in_=ot[oi2][:, :]), reads=[b_ot[oi2]], sbuf=b_ot[oi2])
                    P.barrier()
                    P.release(mkM)
            P.release(mkT)
    P.finish()
    return nc, P


def _consts():
    c = np.zeros((128, 6 * 128 + 4 * 512), np.float32)
    i = np.arange(128)
    c[:, 0:128] = np.eye(128)
    c[:, 128:256] = (i[:, None] <= i[None, :])
    c[:, 256:384] = -1.0 * (i[:, None] < i[None, :])
    c[:, 384:512] = 1.0
    c[:, 512:640] = (i[None, :] >= i[:, None])
    c[:, 640:768] = (i[None, :] <= i[:, None])
    q = np.arange(512)
    for kk in range(4):
        c[:, 768 + 512 * kk:768 + 512 * (kk + 1)] = np.abs((128 * kk + i)[:, None] - q[None, :])
    return c


def _job_arrays(cfg, seq, own_start, params, is_prompt):
    meta = params["meta_tokens"]
    S = seq.shape[0]
    full = np.concatenate([meta, seq], axis=0)
    L = full.shape[0]
    NOWN = cfg.NOWN
    own_s0 = 16 + own_start
    own_s1 = own_s0 + NOWN * 512
    tiles = []
    kinds = []
    tiles.append(np.arange(-2, 18)); kinds.append("L")
    for s0 in range(16, own_s0, 512):
        tiles.append(np.arange(s0 - 2, s0 + 514)); kinds.append("L")
    for s0 in range(L - 512, own_s1 - 1, -512):
        tiles.append(np.arange(s0 + 513, s0 - 3, -1)); kinds.append("R")
    for s0 in range(own_s0, own_s1, 512):
        tiles.append(np.arange(s0 - 2, s0 + 514)); kinds.append("O")
    nt = len(tiles)
    xwin = np.zeros((nt, 516, D), np.float32)
    for t, idx in enumerate(tiles):
        ok = (idx >= 0) & (idx < L)
        xwin[t, np.nonzero(ok)[0]] = full[idx[ok]]
    pos = [tiles[0][2:18]] + [tl[2:514] for tl in tiles[1:]]
    kindtok = np.concatenate([np.full(len(p), {"L": 0, "R": 1, "O": 2}[k]) for p, k in zip(pos, kinds)])
    pos = np.concatenate(pos).astype(np.int64)
    Ls = len(pos)
    slopes = 2.0 ** (-(np.arange(NH) + 1.0))
    cpos, rpos = (pos // 128).astype(np.float32), (pos % 128).astype(np.float32)
    kaug = np.zeros((NH, 8, Ls), np.float32)
    for h in range(NH):
        sl = slopes[h]
        left = np.stack([-np.ones(Ls), -np.ones(Ls), sl * 128 * cpos, sl * rpos])
        right = -left
        ml = (kindtok != 1)[None, :]
        mr = (kindtok != 0)[None, :]
        kaug[h, 0:4] = left * ml
        kaug[h, 4:8] = right * mr
    qpos = np.arange(own_s0, own_s1)
    qc, qr = (qpos // 128).astype(np.float32), (qpos % 128).astype(np.float32)
    qaug = np.zeros((3, NH, 8, NOWN * 512), np.float32)
    for h in range(NH):
        sl = slopes[h]
        qa = np.stack([sl * 128 * qc, sl * qr, np.ones_like(qc), np.ones_like(qc)])
        qaug[0, h, 0:4] = qa; qaug[0, h, 4:8] = qa
        qaug[1, h, 0:4] = qa
        qaug[2, h, 4:8] = qa
    cw = params["conv_w"][0]
    cb = params["conv_b"][0]
    tab = np.zeros((nt, 128, TT), np.float32)
    cwT = cw.T.reshape(12, 128, 5).transpose(1, 0, 2)
    cbT = cb.reshape(12, 128).T
    last_left = max(t for t, k in enumerate(kinds) if k == "L")
    for t, k in enumerate(kinds):
        taps = cwT[:, :, ::-1] if k == "R" else cwT
        tab[t, :, 0:60] = taps.reshape(128, 60)
        tab[t, :, 60:72] = cbT
        if k == "R":
            prim_b, prim_a, sf, sb_ = params["dt_bias_b"][0], params["a_log_b"][0], 0.0, 1.0
        else:
            prim_b, prim_a, sf, sb_ = params["dt_bias_f"][0], params["a_log_f"][0], 1.0, 0.0
        tab[t, :, 72:88] = prim_b[None, :]
        tab[t, :, 88:104] = params["dt_bias_b"][0][None, :]
        tab[t, :, 104:120] = prim_a[None, :]
        tab[t, :, 120:136] = params["a_log_b"][0][None, :]
        tab[t, :, 136] = sf
        tab[t, :, 137] = sb_
        tab[t, :, 138] = 1.0 if t == last_left else 0.0
        tab[t, :, 139] = 0.0 if t == last_left else 1.0
    return dict(xw=xwin, kaug=kaug.astype(ml_dtypes.bfloat16), qaug=qaug.astype(ml_dtypes.bfloat16), ttab=tab)


_CACHE = {}


def run(cfg, inputs):
    p = {k: np.asarray(v, np.float32) for k, v in inputs.items()}
    key = (cfg.NC, cfg.SP, cfg.SS, cfg.NSEQ, cfg.NOWN)
    if key not in _CACHE:
        _CACHE[key] = build_program(cfg)
    nc, P = _CACHE[key]
    gt = np.zeros((128, 8 + 8 + 16 + 1024 + 1024 + 128 + 256), np.float32)
    gt[:, 0:8] = p["norm1_w"][0].reshape(8, 128).T
    gt[:, 8:16] = p["norm2_w"][0].reshape(8, 128).T
    gt[:, 16:32] = p["d_skip"][0][None, :]
    gt[:, 32:1056] = p["ssm_norm_w"][0][None, :]
    gt[:, 1056:2080] = p["final_norm_w"][None, :]
    gt[:, 2080:2208] = p["attn_norm_w"][0][None, :]
    gt[:, 2208:2272] = p["lambda_q1"][0][None, :]
    gt[:, 2272:2336] = p["lambda_k1"][0][None, :]
    gt[:, 2336:2400] = p["lambda_q2"][0][None, :]
    gt[:, 2400:2464] = p["lambda_k2"][0][None, :]
    cst = _consts()
    in_maps = []
    xp = p["x_prompt"][0]
    xs = p["x_sample"]
    for c in range(cfg.NC):
        m = {"w_in": p["w_in"][0], "w_out": p["w_out"][0], "w_up": p["w_up"][0], "w_dn": p["w_down"][0], "cst": cst, "gtab": gt}
        ja = [_job_arrays(cfg, xp, c * cfg.OWNT, p, True)]
        for s in range(cfg.NSEQ):
            ja.append(_job_arrays(cfg, xs[c * cfg.NSEQ + s], 0, p, False))
        for j, a in enumerate(ja):
            m[f"xw{j}"] = a["xw"]
            m[f"kaug{j}"] = a["kaug"]
            m[f"qaug{j}"] = a["qaug"]
            m[f"ttab{j}"] = a["ttab"]
        in_maps.append(m)
    res = run_bass_kernel_spmd(nc, in_maps, core_ids=list(range(cfg.NC)))
    _CACHE["last_exec_ns"] = getattr(res, "exec_time_ns", None)
    yp = np.concatenate([res.results[c]["y0"] for c in range(cfg.NC)], axis=0)[None]
    ys = np.stack([res.results[c][f"y{1 + s}"] for c in range(cfg.NC) for s in range(cfg.NSEQ)], axis=0)
    return yp.astype(np.float32), ys.astype(np.float32)


def kernel(**inputs):
    cfg = Cfg(ncores=8, sp=16384, ss=2048, nseq=4, nown=4)
    return run(cfg, inputs)
```

```python
import math
from contextlib import ExitStack
import numpy as np
import ml_dtypes
import concourse.bass as bass
import concourse.mybir as mybir
from concourse.bass_utils import run_bass_kernel_spmd

F32 = mybir.dt.float32
BF16 = mybir.dt.bfloat16
AF = mybir.ActivationFunctionType
ALU = mybir.AluOpType
PE, ACT, DVE, POOL, SP = "tensor", "scalar", "vector", "gpsimd", "sync"
COMPUTE = (PE, ACT, DVE, POOL)
ENGS = (PE, ACT, DVE, POOL, SP)

D = 1024
NH = 8
EPS = 1e-5
TT = 140
N_META = 16
LAM_INIT = 0.8 - 0.6 * math.exp(-0.3 * 0)
CQ, CK, CV, CZ, CX, CDT = 0, 1024, 2048, 3072, 4096, 5632


class Buf:
    __slots__ = ("name", "w", "rd", "dsem", "dcnt", "keep")

    def __init__(self, name=""):
        self.name = name
        self.keep = bool(name)
        self.w = None
        self.rd = []
        self.dsem = None
        self.dcnt = 0


class Op:
    __slots__ = ("eng", "fn", "deps", "is_dma", "flag", "sem", "val", "dbuf", "win", "pos")

    def __init__(self, eng, fn, is_dma):
        self.eng, self.fn, self.is_dma = eng, fn, is_dma
        self.deps = []
        self.flag = False
        self.sem = None
        self.val = 0
        self.dbuf = None
        self.win = 0
        self.pos = 0


class _Rec:
    def __init__(self):
        self.call = None

    def __getattr__(self, name):
        def f(*a, **k):
            self.call = (name, a, k)
            return None
        return f


class Prog:
    def __init__(self, nc):
        self.nc = nc
        self.pending = []
        self.bufs = []
        self.win = 0
        self.engs = {PE: nc.tensor, ACT: nc.scalar, DVE: nc.vector, POOL: nc.gpsimd, SP: nc.sync}
        self.esem = {e: nc.alloc_semaphore(f"prog_{e}") for e in ENGS}
        self.ecnt = {e: 0 for e in ENGS}
        self.seen = {e: {} for e in ENGS}
        self.winlast = {}
        self.lastop = {e: None for e in ENGS}
        self.n_ops = 0
        self.n_waits = 0
        self.sempool = []
        self.sempool_sw = []
        self.semq = {}
        self.nsem = 0

    def buf(self, name=""):
        b = Buf(name)
        self.bufs.append(b)
        return b

    def mark(self):
        return len(self.bufs)

    def release(self, mk):
        del self.bufs[mk:]

    def _add(self, op, reads, writes):
        deps = {}
        raw = set()
        for b in reads:
            if b.w is not None:
                deps[id(b.w)] = b.w
                raw.add(id(b.w))
        for b in writes:
            if b.w is not None and not (op.is_dma and b.w.is_dma and b.w.dbuf is op.dbuf):
                deps[id(b.w)] = b.w
            for r in b.rd:
                deps[id(r)] = r
        dl = []
        for d in deps.values():
            if d is op:
                continue
            if (not d.is_dma) and (not op.is_dma) and d.eng == op.eng:
                if op.eng == PE or id(d) not in raw:
                    continue
            dl.append(d)
        op.deps = dl
        for b in reads:
            b.rd.append(op)
        for b in writes:
            b.w = op
            b.rd = []
        op.win = self.win
        self.pending.append(op)
        if not op.is_dma:
            self.lastop[op.eng] = op
        self.n_ops += 1
        return op

    @staticmethod
    def _bind(fn):
        r = _Rec()
        fn(r)
        c = r.call
        return lambda e: getattr(e, c[0])(*c[1], **c[2])

    def op(self, eng, fn, reads=(), writes=()):
        return self._add(Op(eng, self._bind(fn), False), list(reads), list(writes))

    def dma(self, eng, fn, reads=(), writes=(), sbuf=None):
        o = Op(eng, self._bind(fn), True)
        o.dbuf = sbuf
        return self._add(o, list(reads), list(writes))

    def barrier(self):
        deps = {}
        for e in COMPUTE:
            if self.lastop[e] is not None:
                deps[id(self.lastop[e])] = self.lastop[e]
        for b in self.bufs:
            if b.w is not None and b.w.is_dma:
                deps[id(b.w)] = b.w
            for r in b.rd:
                if r.is_dma:
                    deps[id(r)] = r
        b0 = Op(SP, lambda e: e.nop(), False)
        b0.deps = list(deps.values())
        b0.win = self.win
        self.pending.append(b0)
        self.lastop[SP] = b0
        for e in COMPUTE + (SP,):
            o = Op(e, lambda en: en.nop(), False)
            o.deps = [b0]
            o.win = self.win
            self.pending.append(o)
            self.lastop[e] = o
        for b in self.bufs:
            b.w = None
            b.rd = []
        self.flush()
        for b in self.bufs:
            if b.dsem is not None:
                (self.sempool_sw if self.semq.get(id(b.dsem)) == POOL else self.sempool).append((b.dsem, b.dcnt))
                b.dsem = None

    def flush(self):
        nc = self.nc
        ops = self.pending
        self.pending = []
        last = {}
        for o in ops:
            for d in o.deps:
                d.flag = True
            if not o.is_dma:
                last[o.eng] = o
        for e, o in last.items():
            o.flag = True
            self.winlast[(self.win, e)] = o
        for o in ops:
            if o.is_dma:
                b = o.dbuf
                if b.dsem is None:
                    pool = self.sempool_sw if o.eng == POOL else self.sempool
                    if pool:
                        b.dsem, b.dcnt = pool.pop()
                    else:
                        self.nsem += 1
                        b.dsem = nc.alloc_semaphore(f"dma_{self.nsem}")
                        b.dcnt = 0
                    self.semq[id(b.dsem)] = o.eng
                b.dcnt += 16
                o.sem, o.val = b.dsem, b.dcnt
            elif o.flag:
                self.ecnt[o.eng] += 1
                o.sem, o.val = self.esem[o.eng], self.ecnt[o.eng]
        for o in ops:
            e = self.engs[o.eng]
            need = {}
            for d in o.deps:
                if d.sem is None:
                    d = self.winlast[(d.win, d.eng)]
                k = id(d.sem)
                if k not in need or need[k][1] < d.val:
                    need[k] = (d.sem, d.val)
            sn = self.seen[o.eng]
            for k, (s, v) in need.items():
                if sn.get(k, 0) >= v:
                    continue
                e.wait_ge(s, v)
                sn[k] = v
                self.n_waits += 1
            ins = o.fn(e)
            if o.is_dma:
                ins.then_inc(o.sem, 16)
            elif o.flag:
                ins.then_inc(o.sem, 1)
        self.win += 1

    def finish(self):
        self.flush()
        fe = self.engs[SP]
        for b in self.bufs:
            if b.dsem is not None:
                fe.wait_ge(b.dsem, b.dcnt)
        for (sm, cnt) in self.sempool + self.sempool_sw:
            if cnt > 0:
                fe.wait_ge(sm, cnt)


class Rot:
    def __init__(self, items):
        self.items = items
        self.i = 0

    def next(self):
        it = self.items[self.i % len(self.items)]
        self.i += 1
        return it


class Cfg:
    def __init__(self, ncores=8, sp=16384, ss=2048, nseq=4, nown=4):
        self.NC, self.SP, self.SS, self.NSEQ, self.NOWN = ncores, sp, ss, nseq, nown
        assert sp == ncores * nown * 512 and ss == nown * 512
        self.NCTX = sp // 512 - nown
        self.jobs = [self.NCTX] + [0] * nseq
        self.OWNT = nown * 512

    def ntiles(self, j):
        return 1 + self.jobs[j] + self.NOWN

    def L(self, j):
        return 16 + 512 * (self.ntiles(j) - 1)

    def nch(self, j):
        return 1 + 4 * (self.ntiles(j) - 1)


def build_program(cfg):
    nc = bass.Bass("TRN2", target_bir_lowering=False)
    P = Prog(nc)
    NOWN, OWNT = cfg.NOWN, cfg.OWNT
    NJ = len(cfg.jobs)
    Lmax = max(cfg.L(j) for j in range(NJ))
    NCHmax = max(cfg.nch(j) for j in range(NJ))

    def din(name, shape, dt=F32):
        return nc.dram_tensor(name, list(shape), dt, kind="ExternalInput").ap()

    def dscr(name, shape, dt=BF16):
        return nc.dram_tensor(name, list(shape), dt, kind="Internal").ap()

    xw = [din(f"xw{j}", [cfg.ntiles(j), 516, D]) for j in range(NJ)]
    kaug = [din(f"kaug{j}", [NH, 8, cfg.L(j)], BF16) for j in range(NJ)]
    qaug = [din(f"qaug{j}", [3, NH, 8, OWNT], BF16) for j in range(NJ)]
    ttab = [din(f"ttab{j}", [cfg.ntiles(j), 128, TT]) for j in range(NJ)]
    yout = [nc.dram_tensor(f"y{j}", [OWNT, D], F32, kind="ExternalOutput").ap() for j in range(NJ)]
    w_in = din("w_in", [D, 5664])
    w_out = din("w_out", [2048, D])
    w_up = din("w_up", [D, 4096])
    w_dn = din("w_dn", [4096, D])
    cst = din("cst", [128, 6 * 128 + 4 * 512])
    gtab = din("gtab", [128, 8 + 8 + 16 + 1024 + 1024 + 128 + 256])
    ws_in = dscr("ws_in", [128, 8, 5664])
    ws_out = dscr("ws_out", [128, 16, D])
    ws_up = dscr("ws_up", [128, 8, 4096])
    ws_dn = dscr("ws_dn", [128, 32, D])
    KT = dscr("KT", [NH, 2, 72, Lmax])
    QT = dscr("QT", [NH, 2, 64, OWNT])
    VA = dscr("VA", [NH, 128, NCHmax, 129])
    PFB = dscr("PFB", [2, 4 * NOWN, 128, D])
    SBS = dscr("SBS", [4 * NOWN, 128, D])
    b_ws = {k: P.buf("ws_" + k) for k in ("in", "out", "up", "dn")}
    b_KT, b_QT, b_VA, b_PFB, b_SBS = P.buf("KT"), P.buf("QT"), P.buf("VA"), P.buf("PFB"), P.buf("SBS")

    ES = ExitStack()
    G = ES

    _nm = [0]

    def sb(st, name, shape, dt=F32):
        _nm[0] += 1
        return st.enter_context(nc.sbuf_tensor(f"{name}_{_nm[0]}", list(shape), dt))

    allbanks = nc.alloc_psum_tensor("allbanks", [128, 4096], F32)
    banks = [allbanks[:, 512 * i:512 * (i + 1)] for i in range(8)]
    bk = [P.buf(f"bank{i}") for i in range(8)]

    cst_t = sb(G, "cst_t", [128, 6 * 128 + 4 * 512])
    gtab_t = sb(G, "gtab_t", [128, 8 + 8 + 16 + 1024 + 1024 + 128 + 256])
    identb = sb(G, "identb", [128, 128], BF16)
    lam_t = sb(G, "lam_t", [128, 8])
    CF = sb(G, "CF", [128, D])
    CB = sb(G, "CB", [128, D])
    decs = sb(G, "decs", [128, 4 * NOWN, 16])
    b_cst, b_gt, b_idb, b_lam, b_CF, b_CB, b_decs = (P.buf(n) for n in ("cst", "gt", "idb", "lam", "CF", "CB", "decs"))
    identf = cst_t[:, 0:128]
    uincl = cst_t[:, 128:256]
    negus = cst_t[:, 256:384]
    onesf = cst_t[:, 384:512]
    maskf = cst_t[:, 512:640]
    maskb = cst_t[:, 640:768]
    absd = [cst_t[:, 768 + 512 * i: 768 + 512 * (i + 1)] for i in range(4)]
    w1T = gtab_t[:, 0:8]
    w2T = gtab_t[:, 8:16]
    Dbc = gtab_t[:, 16:32]
    ssmw = gtab_t[:, 32:32 + 1024]
    finw = gtab_t[:, 1056:1056 + 1024]
    attw = gtab_t[:, 2080:2080 + 128]
    lamv = gtab_t[:, 2208:2208 + 256]

    P.dma(SP, lambda e: e.dma_start(out=cst_t[:], in_=cst[:, :]), writes=[b_cst], sbuf=b_cst)
    P.dma(SP, lambda e: e.dma_start(out=gtab_t[:], in_=gtab[:, :]), writes=[b_gt], sbuf=b_gt)
    P.op(DVE, lambda e: e.tensor_copy(identb[:], identf), reads=[b_cst], writes=[b_idb])
    umb = sb(G, "umb", [128, 2, 128], BF16)
    b_cstb = P.buf("cstb")
    P.op(DVE, lambda e: e.tensor_copy(umb[:, 0, :], uincl), reads=[b_cst], writes=[b_cstb])
    P.op(DVE, lambda e: e.tensor_copy(umb[:, 1, :], negus), reads=[b_cst], writes=[b_cstb])
    P.op(DVE, lambda e: e.tensor_tensor(lamv[:, 0:64], lamv[:, 0:64], lamv[:, 64:128], ALU.mult),
         reads=[b_gt], writes=[b_gt])
    P.op(DVE, lambda e: e.tensor_tensor(lamv[:, 128:192], lamv[:, 128:192], lamv[:, 192:256], ALU.mult),
         reads=[b_gt], writes=[b_gt])
    P.op(DVE, lambda e: e.reduce_sum(lam_t[:, 0:1], lamv[:, 0:64], mybir.AxisListType.X), reads=[b_gt], writes=[b_lam])
    P.op(DVE, lambda e: e.reduce_sum(lam_t[:, 1:2], lamv[:, 128:192], mybir.AxisListType.X), reads=[b_gt], writes=[b_lam])
    P.op(ACT, lambda e: e.activation(lam_t[:, 2:4], lam_t[:, 0:2], AF.Exp), reads=[b_lam], writes=[b_lam])
    P.op(DVE, lambda e: e.tensor_tensor(lam_t[:, 4:5], lam_t[:, 3:4], lam_t[:, 2:3], ALU.subtract), reads=[b_lam], writes=[b_lam])
    P.op(DVE, lambda e: e.tensor_scalar(lam_t[:, 4:5], lam_t[:, 4:5], -LAM_INIT, None, ALU.add), reads=[b_lam], writes=[b_lam])
    P.op(DVE, lambda e: e.tensor_scalar(attw, attw, 1.0 - LAM_INIT, None, ALU.mult), reads=[b_gt], writes=[b_gt])
    neglam = lam_t[:, 4:5]

    with ExitStack() as st:
        stf = [sb(st, f"stf{i}", [128, 2048]) for i in range(4)]
        stb = [sb(st, f"stb{i}", [128, 2048], BF16) for i in range(4)]
        b_stf = [P.buf(f"stf{i}") for i in range(4)]
        b_stb = [P.buf(f"stb{i}") for i in range(4)]
        cnt = [0]

        def conv_w(W, scr, bscr, K, N):
            for kc in range(K // 128):
                for c0 in range(0, N, 2048):
                    w = min(2048, N - c0)
                    i = cnt[0] % 4
                    ce = (DVE, ACT)[cnt[0] % 2]
                    cnt[0] += 1
                    P.dma(SP, lambda e, i=i, kc=kc, c0=c0, w=w: e.dma_start(out=stf[i][:, 0:w], in_=W[kc * 128:(kc + 1) * 128, c0:c0 + w]),
                          writes=[b_stf[i]], sbuf=b_stf[i])
                    if ce == ACT:
                        P.op(ACT, lambda e, i=i, w=w: e.copy(stb[i][:, 0:w], stf[i][:, 0:w]), reads=[b_stf[i]], writes=[b_stb[i]])
                    else:
                        P.op(ce, lambda e, i=i, w=w: e.tensor_copy(stb[i][:, 0:w], stf[i][:, 0:w]), reads=[b_stf[i]], writes=[b_stb[i]])
                    P.dma(POOL, lambda e, i=i, kc=kc, c0=c0, w=w: e.dma_start(out=scr[:, kc, c0:c0 + w], in_=stb[i][:, 0:w]),
                          reads=[b_stb[i]], writes=[bscr], sbuf=b_stb[i])

        conv_w(w_in, ws_in, b_ws["in"], D, 5664)
        conv_w(w_out, ws_out, b_ws["out"], 2048, D)
        conv_w(w_up, ws_up, b_ws["up"], D, 4096)
        conv_w(w_dn, ws_dn, b_ws["dn"], 4096, D)
        P.barrier()

    def make_uT(st, pfx):
        xs = [sb(st, f"{pfx}xs{i}", [128, D]) for i in range(2)]
        xb = [sb(st, f"{pfx}xb{i}", [128, D], BF16) for i in range(2)]
        sq = sb(st, f"{pfx}sq", [128, D], BF16)
        ss = sb(st, f"{pfx}ss", [128, 4])
        uT = sb(st, f"{pfx}uT", [128, 8, 516], BF16)
        return dict(xs=xs, xb=xb, sq=sq, ss=ss, uT=uT,
                    b_xs=[P.buf() for _ in range(2)], b_xb=[P.buf() for _ in range(2)],
                    b_sq=P.buf(), b_ss=[P.buf(), P.buf()], b_uT=P.buf(), cnt=[0])

    def rms_rstd(eng_sq, src_ap, n, width, sqjunk, b_junk, ssap, b_ss, src_bufs):
        P.op(ACT, lambda e: e.activation(sqjunk, src_ap, AF.Square, accum_out=ssap), reads=src_bufs, writes=[b_junk, b_ss])
        P.op(ACT, lambda e: e.activation(ssap, ssap, AF.Ln, scale=1.0 / width, bias=EPS), reads=[b_ss], writes=[b_ss])
        P.op(ACT, lambda e: e.activation(ssap, ssap, AF.Exp, scale=-0.5), reads=[b_ss], writes=[b_ss])

    def uT_thunks(U, src, W, wT, tp_bank):
        uT = U["uT"]
        tpv = banks[tp_bank][:, :].bitcast(BF16).rearrange("p (k t) -> p k t", k=8)
        out = []
        r0 = 0
        while r0 < W:
            n = min(128, W - r0)

            def mk(r0=r0, n=n):
                st_ = {}

                def stepA():
                    i = U["cnt"][0] % 2
                    U["cnt"][0] += 1
                    st_["i"] = i
                    xs, xb, bxs, bxb = U["xs"][i], U["xb"][i], U["b_xs"][i], U["b_xb"][i]
                    P.dma(SP, lambda e: e.dma_start(out=xs[0:n, :], in_=src[r0:r0 + n, :]), writes=[bxs], sbuf=bxs)
                    ssap = U["ss"][0:n, i:i + 1]
                    rms_rstd(ACT, xs[0:n, :], n, D, U["sq"][0:n, :], U["b_sq"], ssap, U["b_ss"][i], [bxs])
                    P.op(ACT, lambda e: e.activation(xb[0:n, :], xs[0:n, :], AF.Copy, scale=ssap), reads=[bxs, U["b_ss"][i]], writes=[bxb])

                def stepB():
                    i = st_["i"]
                    xb, bxb = U["xb"][i], U["b_xb"][i]
                    for kc in range(8):
                        P.op(PE, lambda e, kc=kc: e.transpose(tpv[:, kc, 0:n], xb[0:n, kc * 128:(kc + 1) * 128], identb[0:n, 0:n]),
                             reads=[bxb, b_idb], writes=[bk[tp_bank]])
                    P.op(DVE, lambda e: e.tensor_tensor(uT[:, :, r0:r0 + n], tpv[:, :, 0:n], wT.unsqueeze(2).to_broadcast([128, 8, n]), ALU.mult),
                         reads=[bk[tp_bank], b_gt], writes=[U["b_uT"]])
                return stepA, stepB
            out.append(mk())
            r0 += n
        return out

    def build_uT(U, src, W, wT, tp_bank):
        for (sa, sb_) in uT_thunks(U, src, W, wT, tp_bank):
            sa()
            sb_()

    def make_ssd(st, pfx):
        S = dict()
        S["tab"] = sb(st, f"{pfx}tab", [128, TT])
        S["raw"] = [sb(st, f"{pfx}raw{i}", [128, 516]) for i in range(2)]
        S["acc"] = [sb(st, f"{pfx}acc{i}", [128, 512]) for i in range(2)]
        S["xcT"] = sb(st, f"{pfx}xcT", [128, 12, 512], BF16)
        S["xtm"] = sb(st, f"{pfx}xtm", [128, 4, D], BF16)
        S["btm"] = sb(st, f"{pfx}btm", [128, 4, 256], BF16)
        S["dt"] = sb(st, f"{pfx}dt", [128, 4, 32])
        S["adt"] = sb(st, f"{pfx}adt", [128, 4, 32])
        S["cs"] = sb(st, f"{pfx}cs", [128, 4, 32])
        S["tot"] = sb(st, f"{pfx}tot", [128, 4, 32])
        S["A2"] = sb(st, f"{pfx}A2", [128, 32])
        S["tmp32"] = sb(st, f"{pfx}tmp32", [128, 32])
        for k in ("tab", "xcT", "xtm", "btm", "dtq", "A2", "tmp32"):
            S["b_" + k] = P.buf(pfx + k)
        S["b_raw"] = [P.buf() for _ in range(2)]
        S["b_acc"] = [P.buf() for _ in range(2)]
        S["n"] = 0
        return S

    def ssd_par(st, S, pfx):
        S2 = dict(S)
        S2["tab"] = sb(st, f"{pfx}tabB", [128, TT])
        S2["dt"] = sb(st, f"{pfx}dtB", [128, 4, 32])
        S2["adt"] = sb(st, f"{pfx}adtB", [128, 4, 32])
        S2["cs"] = sb(st, f"{pfx}csB", [128, 4, 32])
        S2["tot"] = sb(st, f"{pfx}totB", [128, 4, 32])
        S2["A2"] = sb(st, f"{pfx}A2B", [128, 32])
        for k in ("tab", "dtq", "A2"):
            S2["b_" + k] = P.buf()
        return S2

    def ssd_thunks(S, U, j, t, T, W, wv, cx, cdt, pb_main, pb_small, pb_tp):
        uT = U["uT"]
        tab = S["tab"]
        nchk = max(1, T // 128)
        cs_ = min(T, 128)
        def pre():
            P.dma(SP, lambda e: e.dma_start(out=tab[:], in_=ttab[j][t, :, :]), writes=[S["b_tab"]], sbuf=S["b_tab"])
            P.op(ACT, lambda e: e.activation(S["A2"][:], tab[:, 104:136], AF.Exp), reads=[S["b_tab"]], writes=[S["b_A2"]])
            P.op(DVE, lambda e: e.tensor_scalar(S["A2"][:], S["A2"][:], -1.0, None, ALU.mult), reads=[S["b_A2"]], writes=[S["b_A2"]])
            for k in range(nchk):
                for kc in range(8):
                    P.op(PE, lambda e, kc=kc, k=k: e.matmul(banks[pb_small][0:cs_, 64 + 32 * k:96 + 32 * k], uT[:, kc, 2 + 128 * k:2 + 128 * k + cs_], wv[:, kc, cdt:cdt + 32],
                                                            start=(kc == 0), stop=(kc == 7)),
                         reads=[U["b_uT"], b_wres], writes=[bk[pb_small]])
            dtr = banks[pb_small][0:cs_, 64:64 + 32 * nchk].rearrange("p (k c) -> p k c", c=32)
            dt, adt = S["dt"], S["adt"]
            P.op(DVE, lambda e: e.tensor_copy(dt[0:cs_, 0:nchk, 16:32], dtr[:, :, 16:32]), reads=[bk[pb_small]], writes=[S["b_dtq"]])
            P.op(DVE, lambda e: e.tensor_scalar(dt[0:cs_, 0:nchk, 0:16], dtr[:, :, 0:16], tab[0:cs_, 136:137], None, ALU.mult),
                 reads=[bk[pb_small], S["b_tab"]], writes=[S["b_dtq"]])
            P.op(DVE, lambda e: e.scalar_tensor_tensor(dt[0:cs_, 0:nchk, 0:16], dt[0:cs_, 0:nchk, 16:32], tab[0:cs_, 137:138], dt[0:cs_, 0:nchk, 0:16], ALU.mult, ALU.add),
                 reads=[S["b_dtq"], S["b_tab"]], writes=[S["b_dtq"]])
            P.op(DVE, lambda e: e.tensor_tensor(dt[0:cs_, 0:nchk, :], dt[0:cs_, 0:nchk, :], tab[0:cs_, 72:104].unsqueeze(1).to_broadcast([cs_, nchk, 32]), ALU.add),
                 reads=[S["b_dtq"], S["b_tab"]], writes=[S["b_dtq"]])
            P.op(ACT, lambda e: e.activation(dt[0:cs_, 0:nchk, :], dt[0:cs_, 0:nchk, :], AF.Exp), reads=[S["b_dtq"]], writes=[S["b_dtq"]])
            P.op(ACT, lambda e: e.activation(dt[0:cs_, 0:nchk, :], dt[0:cs_, 0:nchk, :], AF.Ln, bias=1.0), reads=[S["b_dtq"]], writes=[S["b_dtq"]])
            P.op(DVE, lambda e: e.tensor_tensor(adt[0:cs_, 0:nchk, :], dt[0:cs_, 0:nchk, :], S["A2"][0:cs_, :].unsqueeze(1).to_broadcast([cs_, nchk, 32]), ALU.mult),
                 reads=[S["b_dtq"], S["b_A2"]], writes=[S["b_dtq"]])
        def pre2():
            dt, adt = S["dt"], S["adt"]
            for k in range(nchk):
                P.op(PE, lambda e, k=k: e.matmul(banks[pb_small][0:cs_, 192 + 32 * k:224 + 32 * k], uincl[0:cs_, 0:cs_], adt[0:cs_, k, :], start=True, stop=True),
                     reads=[S["b_dtq"], b_cst], writes=[bk[pb_small]])
                P.op(PE, lambda e, k=k: e.matmul(banks[pb_small][:, 320 + 32 * k:352 + 32 * k], onesf[0:cs_, :], adt[0:cs_, k, :], start=True, stop=True),
                     reads=[S["b_dtq"], b_cst], writes=[bk[pb_small]])
            P.op(DVE, lambda e: e.tensor_copy(S["cs"][0:cs_, 0:nchk, :], banks[pb_small][0:cs_, 192:192 + 32 * nchk].rearrange("p (k c) -> p k c", c=32)),
                 reads=[bk[pb_small]], writes=[S["b_dtq"]])
            P.op(DVE, lambda e: e.tensor_copy(S["tot"][:, 0:nchk, :], banks[pb_small][:, 320:320 + 32 * nchk].rearrange("p (k c) -> p k c", c=32)),
                 reads=[bk[pb_small]], writes=[S["b_dtq"]])

        pend_silu = []

        def one_fc(fc):
            i = S["n"] % 2
            S["n"] += 1
            raw, acc, braw, bacc = S["raw"][i], S["acc"][i], S["b_raw"][i], S["b_acc"][i]
            pbm = pb_main[fc % len(pb_main)]
            wa = min(W, 512)
            for kc in range(8):
                P.op(PE, lambda e, kc=kc, fc=fc, pbm=pbm, wa=wa: e.matmul(banks[pbm][:, 0:wa], wv[:, kc, cx + fc * 128: cx + (fc + 1) * 128], uT[:, kc, 0:wa],
                                                                   start=(kc == 0), stop=(kc == 7)),
                     reads=[U["b_uT"], b_wres], writes=[bk[pbm]])
            P.op(ACT, lambda e, raw=raw, pbm=pbm, wa=wa: e.copy(raw[:, 0:wa], banks[pbm][:, 0:wa]), reads=[bk[pbm]], writes=[braw])
            if W > 512:
                for kc in range(8):
                    P.op(PE, lambda e, kc=kc, fc=fc: e.matmul(banks[pb_small][:, 0:W - 512], wv[:, kc, cx + fc * 128: cx + (fc + 1) * 128], uT[:, kc, 512:W],
                                                             start=(kc == 0), stop=(kc == 7)),
                         reads=[U["b_uT"], b_wres], writes=[bk[pb_small]])
                P.op(ACT, lambda e, raw=raw: e.copy(raw[:, 512:W], banks[pb_small][:, 0:W - 512]), reads=[bk[pb_small]], writes=[braw])
            flush_silu()
            P.op(DVE, lambda e, raw=raw, acc=acc, fc=fc: e.tensor_scalar(acc[:, 0:T], raw[:, 0:T], tab[:, fc * 5:fc * 5 + 1], tab[:, 60 + fc:61 + fc], ALU.mult, ALU.add),
                 reads=[braw, S["b_tab"]], writes=[bacc])
            for jj in range(1, 5):
                P.op(DVE, lambda e, raw=raw, acc=acc, fc=fc, jj=jj: e.scalar_tensor_tensor(acc[:, 0:T], raw[:, jj:jj + T], tab[:, fc * 5 + jj:fc * 5 + jj + 1], acc[:, 0:T], ALU.mult, ALU.add),
                     reads=[braw, S["b_tab"], bacc], writes=[bacc])
            pend_silu.append((acc, bacc, fc))

        def flush_silu():
            while pend_silu:
                acc, bacc, fc = pend_silu.pop(0)
                P.op(ACT, lambda e: e.activation(S["xcT"][:, fc, 0:T], acc[:, 0:T], AF.Silu), reads=[bacc], writes=[S["b_xcT"]])

        def post():
            flush_silu()
            tpv = banks[pb_tp][:, :].bitcast(BF16)
            for k in range(nchk):
                for fc in range(8):
                    P.op(PE, lambda e, k=k, fc=fc: e.transpose(tpv[0:cs_, fc * 128:(fc + 1) * 128], S["xcT"][:, fc, 128 * k:128 * k + cs_], identb[:, :]),
                         reads=[S["b_xcT"], b_idb], writes=[bk[pb_tp]])
                P.op(ACT, lambda e, k=k: e.copy(S["xtm"][0:cs_, k, :], tpv[0:cs_, :]), reads=[bk[pb_tp]], writes=[S["b_xtm"]])
                for g in range(2):
                    P.op(PE, lambda e, k=k, g=g: e.transpose(tpv[0:cs_, g * 128:(g + 1) * 128], S["xcT"][:, 8 + g, 128 * k:128 * k + cs_], identb[:, :]),
                         reads=[S["b_xcT"], b_idb], writes=[bk[pb_tp]])
                P.op(ACT, lambda e, k=k: e.copy(S["btm"][0:cs_, k, :], tpv[0:cs_, 0:256]), reads=[bk[pb_tp]], writes=[S["b_btm"]])
        def fc_then_pre2(fc):
            one_fc(fc)
            pre2()

        return pre, [((lambda fc=fc: fc_then_pre2(fc)) if fc == 1 else (lambda fc=fc: one_fc(fc))) for fc in range(12)], post

    def ssd_prep(S, U, j, t, T, W, wv, cx, cdt, pb_main, pb_small, pb_tp):
        pre, fcs, post = ssd_thunks(S, U, j, t, T, W, wv, cx, cdt, pb_main, pb_small, pb_tp)
        pre()
        for f in fcs:
            f()
        post()

    b_wres = P.buf("wres")

    for j in range(NJ):
        nt = cfg.ntiles(j)
        nctx = cfg.jobs[j]
        with ExitStack() as st:
            mkS = P.mark()
            wS = sb(st, "wS", [128, 8, 4640], BF16)
            b_wq = P.buf()
            P.dma(SP, lambda e: e.dma_start(out=wS[:, :, 3072:4640], in_=ws_in[:, :, 4096:5664]), reads=[b_ws["in"]], writes=[b_wres], sbuf=b_wres)
            P.dma(SP, lambda e: e.dma_start(out=wS[:, :, 0:3072], in_=ws_in[:, :, 0:3072]), reads=[b_ws["in"]], writes=[b_wq], sbuf=b_wq)
            for m in range(2):
                P.dma(POOL, lambda e, m=m: e.dma_start(out=KT[:, m, 64:72, 0:cfg.L(j)], in_=kaug[j][:, :, :]), writes=[b_KT], sbuf=b_KT)
            U0 = make_uT(st, "s")
            U1 = dict(U0)
            U1["uT"] = sb(st, "suT1", [128, 8, 516], BF16)
            U1["b_uT"] = P.buf()
            Us = [U0, U1]
            S_0 = make_ssd(st, "s")
            S_1 = ssd_par(st, S_0, "s")
            Ss = [S_0, S_1]
            kst = [sb(st, f"kst{i}", [128, 512], BF16) for i in range(3)]
            b_kst = [P.buf() for _ in range(3)]
            vst = sb(st, "vst", [128, 8, 4, 129], BF16)
            b_vst = P.buf("vst")
            wgt = sb(st, "wgt", [128, 32])
            xwp = sb(st, "xwp", [128, 16, 64], BF16)
            xws = sb(st, "xws", [128, 16, 64], BF16)
            decp = sb(st, "decp", [128, 32])
            snap = [sb(st, f"snap{i}", [128, D], BF16) for i in range(2)]
            b_wgt, b_xwp, b_xws, b_decp = P.buf(), P.buf(), P.buf(), P.buf()
            b_snap = [P.buf() for _ in range(2)]
            P.op(POOL, lambda e: e.memset(vst[:], 1.0), writes=[b_vst])
            P.op(POOL, lambda e: e.memset(CF[:], 0.0), writes=[b_CF])
            P.op(POOL, lambda e: e.memset(CB[:], 0.0), writes=[b_CB])
            kn = [0]
            sn = [0]

            def make_states(S, t, own, oi, cs_, nchk):
                def one_chunk(k):
                    jj = oi * 4 + k
                    dt, cs, tot = S["dt"], S["cs"], S["tot"]
                    nw = 32 if own else 16
                    xt3 = S["xtm"][0:cs_, k, :].rearrange("p (h d) -> p h d", h=16)
                    P.op(DVE, lambda e: e.tensor_tensor(wgt[0:cs_, 0:16], tot[0:cs_, k, 0:16], cs[0:cs_, k, 0:16], ALU.subtract), reads=[S["b_dtq"]], writes=[b_wgt])
                    if own:
                        P.op(DVE, lambda e: e.tensor_tensor(wgt[0:cs_, 16:32], cs[0:cs_, k, 16:32], S["adt"][0:cs_, k, 16:32], ALU.subtract), reads=[S["b_dtq"], b_wgt], writes=[b_wgt])
                    yield
                    P.op(ACT, lambda e: e.activation(wgt[0:cs_, 0:nw], wgt[0:cs_, 0:nw], AF.Exp), reads=[b_wgt], writes=[b_wgt])
                    P.op(ACT, lambda e: e.activation(decp[:, :], tot[:, k, :], AF.Exp), reads=[S["b_dtq"]], writes=[b_decp])
                    yield
                    P.op(DVE, lambda e: e.tensor_tensor(wgt[0:cs_, 0:nw], wgt[0:cs_, 0:nw], dt[0:cs_, k, 0:nw], ALU.mult), reads=[b_wgt, S["b_dtq"]], writes=[b_wgt])
                    P.op(DVE, lambda e: e.tensor_tensor(xwp[0:cs_, :, :], xt3, wgt[0:cs_, 0:16].unsqueeze(2).to_broadcast([cs_, 16, 64]), ALU.mult),
                         reads=[S["b_xtm"], b_wgt], writes=[b_xwp])
                    if own:
                        P.op(DVE, lambda e: e.tensor_tensor(xws[0:cs_, :, :], xt3, wgt[0:cs_, 16:32].unsqueeze(2).to_broadcast([cs_, 16, 64]), ALU.mult),
                             reads=[S["b_xtm"], b_wgt], writes=[b_xws])
                    yield
                    for g in range(2):
                        P.op(PE, lambda e, g=g: e.matmul(banks[6 + g][:, :], S["btm"][0:cs_, k, g * 128:(g + 1) * 128], xwp[0:cs_, g * 8:(g + 1) * 8, :].rearrange("p h d -> p (h d)"),
                                                         start=True, stop=True),
                             reads=[S["b_btm"], b_xwp], writes=[bk[6 + g]])
                    if not own:
                        carry, bcar = CB, b_CB
                    else:
                        carry, bcar = CF, b_CF
                        i = sn[0] % 2
                        sn[0] += 1
                        P.op(ACT, lambda e: e.copy(snap[i][:], CF[:]), reads=[b_CF], writes=[b_snap[i]])
                        P.dma(POOL, lambda e: e.dma_start(out=PFB[0, jj, :, :], in_=snap[i][:]), reads=[b_snap[i]], writes=[b_PFB], sbuf=b_snap[i])
                    yield
                    c3 = carry[:].rearrange("p (h d) -> p h d", h=16)
                    P.op(DVE, lambda e: e.tensor_tensor(c3, c3, decp[:, 0:16].unsqueeze(2).to_broadcast([128, 16, 64]), ALU.mult), reads=[bcar, b_decp], writes=[bcar])
                    for g in range(2):
                        P.op(DVE, lambda e, g=g: e.tensor_tensor(carry[:, g * 512:(g + 1) * 512], carry[:, g * 512:(g + 1) * 512], banks[6 + g][:, :], ALU.add),
                             reads=[bcar, bk[6 + g]], writes=[bcar])
                    if own:
                        P.op(POOL, lambda e: e.tensor_copy(decs[:, jj, :], decp[:, 16:32]), reads=[b_decp], writes=[b_decs])
                    yield
                    if own:
                        i2 = sn[0] % 2
                        sn[0] += 1
                        for g in range(2):
                            P.op(PE, lambda e, g=g: e.matmul(banks[6 + g][:, :], S["btm"][0:cs_, k, g * 128:(g + 1) * 128], xws[0:cs_, g * 8:(g + 1) * 8, :].rearrange("p h d -> p (h d)"),
                                                             start=True, stop=True),
                                 reads=[S["b_btm"], b_xws], writes=[bk[6 + g]])
                            P.op(ACT, lambda e, g=g: e.copy(snap[i2][:, g * 512:(g + 1) * 512], banks[6 + g][:, :]), reads=[bk[6 + g]], writes=[b_snap[i2]])
                        P.dma(POOL, lambda e: e.dma_start(out=SBS[jj, :, :], in_=snap[i2][:]), reads=[b_snap[i2]], writes=[b_SBS], sbuf=b_snap[i2])
                    yield

                def stages(k):
                    gen = one_chunk(k)
                    return [(lambda: next(gen, None)) for _ in range(6)]

                def tile_end():
                    if not own:
                        P.op(DVE, lambda e: e.scalar_tensor_tensor(CF[:], CB[:], S["tab"][:, 138:139], CF[:], ALU.mult, ALU.add), reads=[b_CB, b_CF, S["b_tab"]], writes=[b_CF])
                        P.op(DVE, lambda e: e.tensor_scalar(CB[:], CB[:], S["tab"][:, 139:140], None, ALU.mult), reads=[b_CB, S["b_tab"]], writes=[b_CB])
                    return None
                out = []
                for k in range(nchk):
                    out += stages(k)
                return out + [tile_end]

            pend_states = []
            for t in range(nt):
                T = 16 if t == 0 else 512
                W = T + 4
                own = t > nctx
                oi = t - nctx - 1
                soff = 0 if t == 0 else 16 + 512 * (t - 1)
                ch0 = 0 if t == 0 else 1 + 4 * (t - 1)
                nchk = max(1, T // 128)
                cs_ = min(T, 128)
                U = Us[t % 2]
                S = Ss[t % 2]
                if t == 0:
                    build_uT(U, xw[j][0], W, w1T, 0)
                uT = U["uT"]
                side = []
                if t + 1 < nt:
                    th = uT_thunks(Us[(t + 1) % 2], xw[j][t + 1], 516, w1T, 0)
                    side.append(th[0][0])
                    for q_ in range(1, len(th)):
                        side.append(th[q_][0])
                        side.append(th[q_ - 1][1])
                    side.append(th[-1][1])
                if pend_states:
                    merged = []
                    ps_ = list(pend_states)
                    while side or ps_:
                        if side:
                            merged.append(side.pop(0))
                        if side:
                            merged.append(side.pop(0))
                        if ps_:
                            merged.append(ps_.pop(0))
                    side = merged
                    pend_states = []
                pre, fcs, post = ssd_thunks(S, U, j, t, T, W, wS, 3072, 4608, [5], 3, 4)
                main = [pre]
                groups = []

                def kq_group(isq, h, T=T, uT=uT, U=U, soff=soff, oi=oi):
                    cbase = (0 if isq else 1024) + h * 128
                    pb = 1 + (kn[0] % 2)
                    i = kn[0] % 3
                    kn[0] += 1
                    for kc in range(8):
                        P.op(PE, lambda e, kc=kc: e.matmul(banks[pb][:, 0:T], wS[:, kc, cbase:cbase + 128], uT[:, kc, 2:2 + T], start=(kc == 0), stop=(kc == 7)),
                             reads=[U["b_uT"], b_wq], writes=[bk[pb]])
                    if isq:
                        P.op(ACT, lambda e: e.activation(kst[i][:, 0:T], banks[pb][:, 0:T], AF.Copy, scale=0.125), reads=[bk[pb]], writes=[b_kst[i]])
                        for m in range(2):
                            P.dma(POOL, lambda e, m=m: e.dma_start(out=QT[h, m, :, oi * 512:oi * 512 + T], in_=kst[i][64 * m:64 * m + 64, 0:T]),
                                  reads=[b_kst[i]], writes=[b_QT], sbuf=b_kst[i])
                    else:
                        P.op(ACT, lambda e: e.copy(kst[i][:, 0:T], banks[pb][:, 0:T]), reads=[bk[pb]], writes=[b_kst[i]])
                        for m in range(2):
                            P.dma(POOL, lambda e, m=m: e.dma_start(out=KT[h, m, 0:64, soff:soff + T], in_=kst[i][64 * m:64 * m + 64, 0:T]),
                                  reads=[b_kst[i]], writes=[b_KT], sbuf=b_kst[i])

                for isq in ([False, True] if own else [False]):
                    for h in range(NH):
                        groups.append(lambda isq=isq, h=h: kq_group(isq, h))

                def v_group(k, half, uT=uT, U=U, cs_=cs_):
                    pb = 1 + (kn[0] % 2)
                    kn[0] += 1
                    for kc in range(8):
                        P.op(PE, lambda e, kc=kc: e.matmul(banks[pb][0:cs_, :], uT[:, kc, 2 + 128 * k:2 + 128 * k + cs_], wS[:, kc, 2048 + half * 512:2048 + (half + 1) * 512],
                                                           start=(kc == 0), stop=(kc == 7)),
                             reads=[U["b_uT"], b_wq], writes=[bk[pb]])
                    P.op(ACT, lambda e: e.copy(vst[0:cs_, half * 4:(half + 1) * 4, k, 0:128], banks[pb][0:cs_, :].rearrange("p (h e) -> p h e", h=4)),
                         reads=[bk[pb]], writes=[b_vst])

                for k in range(nchk):
                    for half in range(2):
                        groups.append(lambda k=k, half=half: v_group(k, half))

                def v_store(t=t, ch0=ch0):
                    if t == 0:
                        P.dma(POOL, lambda e: e.dma_start(out=VA[:, 0:16, 0, :].rearrange("h p e -> p h e"), in_=vst[0:16, :, 0, :]),
                              reads=[b_vst], writes=[b_VA], sbuf=b_vst)
                    else:
                        P.dma(POOL, lambda e: e.dma_start(out=VA[:, :, ch0:ch0 + 4, :].rearrange("h p c e -> p h c e"), in_=vst[:, :, :, :]),
                              reads=[b_vst], writes=[b_VA], sbuf=b_vst)
                groups.append(v_store)
                ng_, done_ = len(groups), 0
                for fi_, f_ in enumerate(fcs):
                    main.append(f_)
                    upto_ = ((fi_ + 1) * ng_) // len(fcs)
                    main += groups[done_:upto_]
                    done_ = upto_
                main += groups[done_:]
                stride = max(1, len(main) // (len(side) + 1)) if side else len(main)
                si_ = 0
                for mi, f in enumerate(main):
                    f()
                    if side and (mi + 1) % stride == 0 and si_ < len(side):
                        side[si_]()
                        si_ += 1
                while si_ < len(side):
                    side[si_]()
                    si_ += 1
                post()
                pend_states = make_states(S, t, own, oi, cs_, nchk)
            for f in pend_states:
                f()
            pend_states = []
            ldb = [sb(st, f"ldb{i}", [128, D], BF16) for i in range(2)]
            b_ldb = [P.buf() for _ in range(2)]
            for jj in range(4 * NOWN - 1, -1, -1):
                i = sn[0] % 2
                sn[0] += 1
                P.op(ACT, lambda e, i=i: e.copy(snap[i][:], CB[:]), reads=[b_CB], writes=[b_snap[i]])
                P.dma(POOL, lambda e, i=i, jj=jj: e.dma_start(out=PFB[1, jj, :, :], in_=snap[i][:]), reads=[b_snap[i]], writes=[b_PFB], sbuf=b_snap[i])
                P.dma(SP, lambda e, i=i, jj=jj: e.dma_start(out=ldb[i][:], in_=SBS[jj, :, :]), reads=[b_SBS], writes=[b_ldb[i]], sbuf=b_ldb[i])
                c3 = CB[:].rearrange("p (h d) -> p h d", h=16)
                P.op(DVE, lambda e, c3=c3, jj=jj: e.tensor_tensor(c3, c3, decs[:, jj, :].unsqueeze(2).to_broadcast([128, 16, 64]), ALU.mult), reads=[b_CB, b_decs], writes=[b_CB])
                P.op(DVE, lambda e, i=i: e.tensor_tensor(CB[:], CB[:], ldb[i][:], ALU.add), reads=[b_CB, b_ldb[i]], writes=[b_CB])
            P.barrier()
            P.release(mkS)

        with ExitStack() as stT:
            mkT = P.mark()
            wO = sb(stT, "wO", [128, 8, 2592], BF16)
            P.dma(SP, lambda e: e.dma_start(out=wO[:, :, 0:1024], in_=ws_in[:, :, 3072:4096]), reads=[b_ws["in"]], writes=[b_wres], sbuf=b_wres)
            P.dma(SP, lambda e: e.dma_start(out=wO[:, :, 1024:2592], in_=ws_in[:, :, 4096:5664]), reads=[b_ws["in"]], writes=[b_wres], sbuf=b_wres)
            mixT = sb(stT, "mixT", [128, 16, 512], BF16)
            b_mix = P.buf("mixT")
            UT = make_uT(stT, "t")
            wpn = [0]
            nk_chunks = cfg.nch(j)
            for oi in range(NOWN):
                t_own = nctx + 1 + oi
                with ExitStack() as st:
                    mkA = P.mark()
                    qv = [[sb(st, f"qv{r}_{v}", [72, 2, 512], BF16) for v in range(3)] for r in range(2)]
                    b_qv = [P.buf() for _ in range(2)]
                    NKV = 4
                    kp = [sb(st, f"kp{i}", [72, 2, 2048], BF16) for i in range(NKV)]
                    vp = [sb(st, f"vp{i}", [128, 16, 129], BF16) for i in range(NKV)]
                    b_kv = [P.buf() for _ in range(NKV)]
                    tmpb = [sb(st, f"tmpb{i}", [128, 1024]) for i in range(2)]
                    b_tmpb = [P.buf() for _ in range(2)]
                    pt = [sb(st, f"pt{i}", [128, 1024], BF16) for i in range(3)]
                    b_pt = [P.buf() for _ in range(3)]
                    ob = sb(st, "ob", [128, 8, 129])
                    rl = sb(st, "rl", [128, 8])
                    o1 = sb(st, "o1", [128, 4, 128])
                    o2 = sb(st, "o2", [128, 4, 128])
                    ssq = sb(st, "ssq", [128, 4])
                    junk = sb(st, "junk", [128, 128], BF16)
                    junkf = sb(st, "junkf", [128, 128])
                    attb = sb(st, "attb", [128, 4, 128], BF16)
                    b_ob, b_rl, b_o1, b_o2, b_ssq, b_junk, b_attb = (P.buf() for _ in range(7))
                    pn = [0]
                    sbn = [0]
                    pieces = [(0, 1)] + [(c0, min(16, nk_chunks - c0)) for c0 in range(1, nk_chunks, 16)]
                    acc_v = [banks[4 + a // 3][:, (a % 3) * 129:(a % 3) * 129 + 129] for a in range(8)]
                    epi_gen = [None]
                    carry = [None]
                    th_ = uT_thunks(UT, xw[j][t_own], 516, w1T, 7)
                    ut_side = [th_[0][0]]
                    for q_ in range(1, len(th_)):
                        ut_side.append(th_[q_][0])
                        ut_side.append(th_[q_ - 1][1])
                    ut_side.append(th_[-1][1])

                    def epilogue(h):
                        P.op(DVE, lambda e: e.reciprocal(rl[:, :].unsqueeze(2), ob[:, :, 128:129]), reads=[b_ob], writes=[b_rl])
                        P.op(DVE, lambda e: e.tensor_tensor(o1[:, :, :], ob[:, 0:4, 0:128], rl[:, 0:4].unsqueeze(2).to_broadcast([128, 4, 128]), ALU.mult), reads=[b_ob, b_rl], writes=[b_o1])
                        P.op(DVE, lambda e: e.tensor_tensor(o2[:, :, :], ob[:, 4:8, 0:128], rl[:, 4:8].unsqueeze(2).to_broadcast([128, 4, 128]), ALU.mult), reads=[b_ob, b_rl], writes=[b_o2])
                        P.op(DVE, lambda e: e.scalar_tensor_tensor(o1[:, :, :], o2[:, :, :], neglam, o1[:, :, :], ALU.mult, ALU.add), reads=[b_o1, b_o2, b_lam], writes=[b_o1])
                        yield
                        for qc in range(4):
                            P.op(DVE, lambda e, qc=qc: e.scalar_tensor_tensor(junkf[:, :], o1[:, qc, :], 1.0, o1[:, qc, :], ALU.mult, ALU.mult, accum_out=ssq[:, qc:qc + 1]),
                                 reads=[b_o1], writes=[b_junk, b_ssq])
                        yield
                        P.op(ACT, lambda e: e.activation(ssq[:, :], ssq[:, :], AF.Ln, scale=1.0 / 128, bias=EPS), reads=[b_ssq], writes=[b_ssq])
                        P.op(ACT, lambda e: e.activation(ssq[:, :], ssq[:, :], AF.Exp, scale=-0.5), reads=[b_ssq], writes=[b_ssq])
                        yield
                        P.op(DVE, lambda e: e.tensor_tensor(o1[:, :, :], o1[:, :, :], ssq[:, :].unsqueeze(2).to_broadcast([128, 4, 128]), ALU.mult), reads=[b_o1, b_ssq], writes=[b_o1])
                        P.op(DVE, lambda e: e.tensor_tensor(attb[:, :, :], o1[:, :, :], attw.unsqueeze(1).to_broadcast([128, 4, 128]), ALU.mult), reads=[b_o1, b_gt], writes=[b_attb])
                        yield
                        tpv = banks[7][:, :].bitcast(BF16)
                        for qc in range(4):
                            P.op(PE, lambda e, qc=qc: e.transpose(tpv[:, qc * 128:(qc + 1) * 128], attb[:, qc, :], identb[:, :]), reads=[b_attb, b_idb], writes=[bk[7]])
                        yield
                        P.op(DVE, lambda e: e.tensor_copy(mixT[:, h, :], banks[7][:, :].bitcast(BF16)[:, 0:512]), reads=[bk[7]], writes=[b_mix])
                        yield


                    for h in range(NH):
                        slope = 2.0 ** (-(h + 1))
                        r = h % 2
                        for v in range(3):
                            P.dma(SP, lambda e, r=r, v=v, h=h: e.dma_start(out=qv[r][v][0:64, :, :], in_=QT[h, :, :, oi * 512:(oi + 1) * 512].rearrange("m d q -> d m q")),
                                  reads=[b_QT], writes=[b_qv[r]], sbuf=b_qv[r])
                            for m in range(2):
                                P.dma(SP, lambda e, r=r, v=v, h=h, m=m: e.dma_start(out=qv[r][v][64:72, m, :], in_=qaug[j][v, h, :, oi * 512:(oi + 1) * 512]),
                                      writes=[b_qv[r]], sbuf=b_qv[r])
                        first_in_bank = {4: True, 5: True, 6: True}
                        steps = []
                        for (c0, ncn) in pieces:
                            for ci in range(ncn):
                                steps.append((c0, ncn, ci))
                        stinfo = {}

                        def emit_qk(si, h=h, r=r, slope=slope):
                            c0, ncn, ci = steps[si]
                            if ci == 0:
                                pi = pn[0] % NKV
                                pn[0] += 1
                                koff0 = 0 if c0 == 0 else 16 + 128 * (c0 - 1)
                                klen = 16 if c0 == 0 else 128 * ncn
                                P.dma(SP, lambda e: e.dma_start(out=kp[pi][:, :, 0:klen], in_=KT[h, :, :, koff0:koff0 + klen].rearrange("m r l -> r m l")),
                                      reads=[b_KT], writes=[b_kv[pi]], sbuf=b_kv[pi])
                                if c0 == 0:
                                    P.dma(SP, lambda e: e.dma_start(out=vp[pi][0:16, 0, :], in_=VA[h, 0:16, 0, :]), reads=[b_VA], writes=[b_kv[pi]], sbuf=b_kv[pi])
                                else:
                                    P.dma(SP, lambda e: e.dma_start(out=vp[pi][:, 0:ncn, :], in_=VA[h, :, c0:c0 + ncn, :]), reads=[b_VA], writes=[b_kv[pi]], sbuf=b_kv[pi])
                                stinfo["pi"] = pi
                            pi = stinfo["pi"]
                            c = c0 + ci
                            ks = 16 if c == 0 else 128
                            kt = 0 if c == 0 else 1 + (c - 1) // 4
                            kk = (c - 1) % 4
                            if kt <= nctx:
                                var, KR, ovl = 0, 72, False
                            elif kt < t_own:
                                var, KR, ovl = 1, 72, False
                            elif kt > t_own:
                                var, KR, ovl = 2, 72, False
                            else:
                                var, KR, ovl = 0, 64, True
                            sbi = sbn[0] % 2
                            pti = sbn[0] % 3
                            sbn[0] += 1
                            Bs = (2 * sbi, 2 * sbi + 1)
                            for m in range(2):
                                P.op(PE, lambda e, m=m: e.matmul(banks[Bs[m]][0:ks, :], kp[pi][0:KR, m, 128 * ci:128 * ci + ks], qv[r][var][0:KR, m, :], start=True, stop=True),
                                     reads=[b_kv[pi], b_qv[r]], writes=[bk[Bs[m]]])
                            if ovl:
                                tb = tmpb[sbi]
                                for m in range(2):
                                    P.op(DVE, lambda e, m=m: e.scalar_tensor_tensor(tb[:, m * 512:(m + 1) * 512], absd[kk], -slope, banks[Bs[m]][:, :], ALU.mult, ALU.add),
                                         reads=[b_cst, bk[Bs[m]]], writes=[b_tmpb[sbi]])
                                P.op(ACT, lambda e: e.activation(pt[pti][:, :], tb[:, :], AF.Exp), reads=[b_tmpb[sbi]], writes=[b_pt[pti]])
                            else:
                                P.op(ACT, lambda e: e.activation(pt[pti][0:ks, :], allbanks[0:ks, 512 * Bs[0]:512 * Bs[0] + 1024], AF.Exp),
                                     reads=[bk[Bs[0]], bk[Bs[1]]], writes=[b_pt[pti]])
                            return (pi, ci, ks, pti)

                        def emit_pv(info, first_in_bank=first_in_bank):
                            pi, ci, ks, pti = info
                            for a in range(8):
                                m, qc = a // 4, a % 4
                                bkn = 4 + a // 3
                                stt = first_in_bank[bkn]
                                first_in_bank[bkn] = False
                                P.op(PE, lambda e, a=a, m=m, qc=qc, stt=stt: e.matmul(acc_v[a], pt[pti][0:ks, m * 512 + qc * 128:m * 512 + (qc + 1) * 128], vp[pi][0:ks, ci, :],
                                                                                 start=stt, stop=False, skip_group_check=True),
                                     reads=[b_pt[pti], b_kv[pi]], writes=[bk[bkn]])

                        prev = None
                        for si in range(len(steps)):
                            cur = emit_qk(si)
                            if si == 0 and carry[0] is not None:
                                carry[0]()
                                carry[0] = None
                            if prev is not None:
                                emit_pv(prev)
                            prev = cur
                            if epi_gen[0] is not None and si % 2 == 1:
                                if next(epi_gen[0], "done") == "done":
                                    epi_gen[0] = None
                            if ut_side and si % 2 == 0 and si >= 2:
                                ut_side.pop(0)()

                        def finish(prev=prev, emit_pv=emit_pv, h=h):
                            emit_pv(prev)
                            while epi_gen[0] is not None:
                                if next(epi_gen[0], "done") == "done":
                                    epi_gen[0] = None
                            for bi in range(3):
                                na = 3 if bi < 2 else 2
                                P.op(DVE, lambda e, bi=bi, na=na: e.tensor_copy(ob[:, bi * 3:bi * 3 + na, :], banks[4 + bi][:, 0:129 * na].rearrange("p (a c) -> p a c", c=129)),
                                     reads=[bk[4 + bi]], writes=[b_ob])
                            epi_gen[0] = epilogue(h)

                        carry[0] = finish
                    carry[0]()
                    while epi_gen[0] is not None:
                        if next(epi_gen[0], "done") == "done":
                            epi_gen[0] = None
                    while ut_side:
                        ut_side.pop(0)()
                    P.barrier()
                    P.release(mkA)
                with ExitStack() as st:
                    mkO = P.mark()
                    U = UT
                    S = make_ssd(st, "o")
                    sz = sb(st, "sz", [128, 4, D], BF16)
                    b_sz = P.buf()
                    pfb = [sb(st, f"pfb{i}", [128, 2, D], BF16) for i in range(2)]
                    b_pfb = [P.buf() for _ in range(2)]
                    gtm2 = [sb(st, f"gtm{i}", [128, 4, 128]) for i in range(2)]
                    b_gtm2 = [P.buf() for _ in range(2)]
                    ef = [sb(st, f"ef{i}", [128, 128]) for i in range(6)]
                    b_ef = [P.buf() for _ in range(6)]
                    mt = [sb(st, f"mt{i}", [128, 128], BF16) for i in range(6)]
                    b_mt = [P.buf() for _ in range(6)]
                    xdt2 = [sb(st, f"xdt{i}", [128, 2, 16, 64], BF16) for i in range(2)]
                    b_xdt2 = [P.buf() for _ in range(2)]
                    bia2 = [sb(st, f"bia{i}", [128, 64]) for i in range(2)]
                    b_bia2 = [P.buf() for _ in range(2)]
                    yt_2 = [sb(st, f"yt{i}", [128, D]) for i in range(2)]
                    yt2_2 = [sb(st, f"ytt{i}", [128, D]) for i in range(2)]
                    b_yt_2 = [P.buf() for _ in range(2)]
                    b_yt2_2 = [P.buf() for _ in range(2)]
                    ssb2 = [sb(st, f"ssb{i}", [128, D], BF16) for i in range(2)]
                    b_ssb2 = [P.buf() for _ in range(2)]
                    sso = sb(st, "sso", [128, 4])
                    b_sso4 = [P.buf() for _ in range(4)]
                    jq = sb(st, "jq", [128, D], BF16)
                    b_jq = P.buf()
                    ahl = sb(st, "ahl", [128, 2, 32], BF16)
                    ahf = sb(st, "ahf", [128, 32])
                    b_ahl, b_ahf = P.buf(), P.buf()
                    SEGB = (0, 1, 2, 3, 5)
                    segq = [banks[bq][:, 0:128] for bq in SEGB]
                    b_segq = [bk[bq] for bq in SEGB]
                    uT = U["uT"]
                    pre_, fcs_, post_ = ssd_thunks(S, U, j, t_own, 512, 516, wO, 1024, 2560, [1], 2, 3)

                    def z_group(k, half):
                        zb = 4 + (k * 2 + half) % 2
                        for kc in range(8):
                            P.op(PE, lambda e, kc=kc: e.matmul(banks[zb][:, :], uT[:, kc, 2 + 128 * k:130 + 128 * k], wO[:, kc, half * 512:(half + 1) * 512], start=(kc == 0), stop=(kc == 7)),
                                 reads=[U["b_uT"], b_wres], writes=[bk[zb]])
                        P.op(ACT, lambda e: e.activation(sz[:, k, half * 512:(half + 1) * 512], banks[zb][:, :], AF.Silu), reads=[bk[zb]], writes=[b_sz])

                    pre_()
                    for fc in range(12):
                        fcs_[fc]()
                        if fc >= 4:
                            z_group((fc - 4) // 2, (fc - 4) % 2)
                    post_()
                    en = [0]
                    pend = {"front": None, "back": None}
                    for k in range(4):
                        jj = oi * 4 + k
                        pi = k % 2
                        gtm, b_gtm, xdt, b_xdt, bia, b_bia = gtm2[pi], b_gtm2[pi], xdt2[pi], b_xdt2[pi], bia2[pi], b_bia2[pi]
                        yt, yt2, b_yt, b_yt2, ssb, b_ssb = yt_2[pi], yt2_2[pi], b_yt_2[pi], b_yt2_2[pi], ssb2[pi], b_ssb2[pi]
                        for d_ in range(2):
                            P.dma(SP, lambda e, pi=pi, d_=d_, jj=jj: e.dma_start(out=pfb[pi][:, d_, :], in_=PFB[d_, jj, :, :]), reads=[b_PFB], writes=[b_pfb[pi]], sbuf=b_pfb[pi])
                        dt, adt, cs, tot = S["dt"], S["adt"], S["cs"], S["tot"]
                        P.op(DVE, lambda e, k=k: e.tensor_scalar(bia[:, 0:16], cs[:, k, 0:16], -1.0, None, ALU.mult), reads=[S["b_dtq"]], writes=[b_bia])
                        P.op(DVE, lambda e, k=k: e.tensor_tensor(bia[:, 16:32], cs[:, k, 16:32], adt[:, k, 16:32], ALU.subtract), reads=[S["b_dtq"]], writes=[b_bia])
                        P.op(DVE, lambda e, k=k: e.tensor_tensor(bia[:, 48:64], tot[:, k, 16:32], bia[:, 16:32], ALU.subtract), reads=[S["b_dtq"], b_bia], writes=[b_bia])
                        P.op(ACT, lambda e, k=k: e.activation(bia[:, 32:48], cs[:, k, 0:16], AF.Exp), reads=[S["b_dtq"]], writes=[b_bia])
                        P.op(ACT, lambda e: e.activation(bia[:, 48:64], bia[:, 48:64], AF.Exp), reads=[b_bia], writes=[b_bia])
                        P.op(DVE, lambda e, k=k: e.tensor_copy(ahl[:, 0, :], adt[:, k, :]), reads=[S["b_dtq"]], writes=[b_ahl])
                        P.op(DVE, lambda e: e.tensor_copy(ahf[:, :], ahl[:, 0, :]), reads=[b_ahl], writes=[b_ahf])
                        P.op(DVE, lambda e, k=k: e.tensor_tensor(ahl[:, 1, :], adt[:, k, :], ahf[:, :], ALU.subtract), reads=[S["b_dtq"], b_ahf, b_ahl], writes=[b_ahl])
                        xt3 = S["xtm"][:, k, :].rearrange("p (h d) -> p h d", h=16)
                        for d_ in range(2):
                            P.op(POOL if d_ else DVE, lambda e, d_=d_, k=k, xt3=xt3: e.tensor_tensor(xdt[:, d_, :, :], xt3, dt[:, k, 16 * d_:16 * d_ + 16].unsqueeze(2).to_broadcast([128, 16, 64]), ALU.mult),
                                 reads=[S["b_xtm"], S["b_dtq"]], writes=[b_xdt])
                        for g in range(2):
                            P.op(PE, lambda e, g=g, k=k: e.matmul(banks[4][:, g * 128:(g + 1) * 128], S["xcT"][:, 8 + g, 128 * k:128 * k + 128], S["xcT"][:, 10 + g, 128 * k:128 * k + 128], start=True, stop=True),
                                 reads=[S["b_xcT"]], writes=[bk[4]])
                        for g in range(2):
                            P.op(DVE, lambda e, g=g: e.tensor_tensor(gtm[:, 2 * g, :], banks[4][:, g * 128:(g + 1) * 128], maskf, ALU.mult), reads=[bk[4], b_cst], writes=[b_gtm])
                            P.op(DVE, lambda e, g=g: e.tensor_tensor(gtm[:, 2 * g + 1, :], banks[4][:, g * 128:(g + 1) * 128], maskb, ALU.mult), reads=[bk[4], b_cst], writes=[b_gtm])
                        for g in range(2):
                            P.op(PE, lambda e, g=g, k=k, pi=pi: e.matmul(banks[g][:, :], S["xcT"][:, 10 + g, 128 * k:128 * k + 128], pfb[pi][:, 0, g * 512:(g + 1) * 512], start=True, stop=True),
                                 reads=[S["b_xcT"], b_pfb[pi]], writes=[bk[g]])
                        for g in range(2):
                            sl = slice(g * 512, (g + 1) * 512)
                            y3 = yt[:, sl].rearrange("p (h d) -> p h d", h=8)
                            P.op(DVE, lambda e, g=g, y3=y3: e.tensor_tensor(y3, banks[g][:, :].rearrange("p (h d) -> p h d", h=8), bia[:, 32 + 8 * g:40 + 8 * g].unsqueeze(2).to_broadcast([128, 8, 64]), ALU.mult),
                                 reads=[bk[g], b_bia], writes=[b_yt])
                        for g in range(2):
                            P.op(PE, lambda e, g=g, k=k, pi=pi: e.matmul(banks[g][:, :], S["xcT"][:, 10 + g, 128 * k:128 * k + 128], pfb[pi][:, 1, g * 512:(g + 1) * 512], start=True, stop=True),
                                 reads=[S["b_xcT"], b_pfb[pi]], writes=[bk[g]])
                        for g in range(2):
                            sl = slice(g * 512, (g + 1) * 512)
                            y23 = yt2[:, sl].rearrange("p (h d) -> p h d", h=8)
                            P.op(DVE, lambda e, g=g, y23=y23: e.tensor_tensor(y23, banks[g][:, :].rearrange("p (h d) -> p h d", h=8), bia[:, 48 + 8 * g:56 + 8 * g].unsqueeze(2).to_broadcast([128, 8, 64]), ALU.mult),
                                 reads=[bk[g], b_bia], writes=[b_yt2])
                        P.op(POOL, lambda e: e.tensor_tensor(yt[:, :], yt[:, :], yt2[:, :], ALU.add), reads=[b_yt, b_yt2], writes=[b_yt])
                        items = [(h, d_) for h in range(16) for d_ in range(2)]

                        def front(ii, k=k):
                            h, d_ = items[ii]
                            g = h // 8
                            q = ii % 6
                            rhs = umb[:, 0, :] if d_ == 0 else umb[:, 1, :]
                            sq_ = ii % 5
                            for hl in range(2):
                                P.op(PE, lambda e, hl=hl: e.matmul(segq[sq_], ahl[:, hl, 16 * d_ + h:16 * d_ + h + 1].to_broadcast([128, 128]), rhs, start=(hl == 0), stop=(hl == 1), skip_group_check=True),
                                     reads=[b_ahl, b_cstb], writes=[b_segq[sq_]])
                            P.op(ACT, lambda e: e.activation(ef[q][:, :], segq[sq_], AF.Exp, bias=bia[:, 16 * d_ + h:16 * d_ + h + 1]), reads=[b_segq[sq_], b_bia], writes=[b_ef[q]])
                            P.op(DVE, lambda e: e.scalar_tensor_tensor(mt[q][:, :], ef[q][:, :], 1.0, gtm[:, 2 * g + d_, :], ALU.min, ALU.mult),
                                 reads=[b_ef[q], b_gtm], writes=[b_mt[q]])

                        def back(ii):
                            h, d_ = items[ii]
                            q = ii % 6
                            P.op(PE, lambda e: e.matmul(banks[6 + h // 8][:, (h % 8) * 64:(h % 8) * 64 + 64], mt[q][:, :], xdt[:, d_, h, :], start=(d_ == 0), stop=(d_ == 1), skip_group_check=True),
                                 reads=[b_mt[q], b_xdt], writes=[bk[6 + h // 8]])

                        LA = 4
                        for ii in range(LA):
                            front(ii)
                        for ii in range(32):
                            if ii + LA < 32:
                                front(ii + LA)
                            back(ii)
                            if ii == 8 and pend["front"] is not None:
                                pend["front"]()
                                pend["front"] = None
                        if pend["back"] is not None:
                            pend["back"]()
                            pend["back"] = None
                        for g in range(2):
                            sl = slice(g * 512, (g + 1) * 512)
                            P.op(DVE, lambda e, g=g, sl=sl: e.tensor_tensor(yt[:, sl], yt[:, sl], banks[6 + g][:, :], ALU.add), reads=[b_yt, bk[6 + g]], writes=[b_yt])

                        def tail_front(k=k, yt=yt, yt2=yt2, b_yt=b_yt, b_yt2=b_yt2, ssb=ssb, b_ssb=b_ssb, xt3=xt3):
                            P.op(POOL, lambda e: e.tensor_tensor(yt2[:, :].rearrange("p (h d) -> p h d", h=16), xt3, Dbc.unsqueeze(2).to_broadcast([128, 16, 64]), ALU.mult),
                                 reads=[S["b_xtm"], b_gt, b_yt2], writes=[b_yt2])
                            P.op(POOL, lambda e: e.tensor_tensor(yt[:, :], yt[:, :], yt2[:, :], ALU.add), reads=[b_yt, b_yt2], writes=[b_yt])
                            P.op(DVE, lambda e: e.tensor_tensor(yt[:, :], yt[:, :], sz[:, k, :], ALU.mult), reads=[b_yt, b_sz], writes=[b_yt])
                            rms_rstd(ACT, yt[:, :], 128, D, jq[:, :], b_jq, sso[:, k:k + 1], b_sso4[k], [b_yt])
                            P.op(DVE, lambda e: e.scalar_tensor_tensor(ssb[:, :], yt[:, :], sso[:, k:k + 1], ssmw, ALU.mult, ALU.mult), reads=[b_yt, b_sso4[k], b_gt], writes=[b_ssb])

                        def tail_back(k=k, ssb=ssb, b_ssb=b_ssb):
                            tpv = banks[4][:, :].bitcast(BF16)
                            for fc in range(8):
                                P.op(PE, lambda e, fc=fc: e.transpose(tpv[:, fc * 128:(fc + 1) * 128], ssb[:, fc * 128:(fc + 1) * 128], identb[:, :]), reads=[b_ssb, b_idb], writes=[bk[4]])
                            P.op(ACT, lambda e: e.copy(mixT[:, 8:16, 128 * k:128 * k + 128], tpv[:, :].rearrange("p (f t) -> p f t", f=8)), reads=[bk[4]], writes=[b_mix])

                        pend["front"], pend["back"] = tail_front, tail_back
                    pend["front"]()
                    pend["back"]()
                    P.barrier()
                    P.release(mkO)
                with ExitStack() as st:
                    mkM = P.mark()
                    NWP = 6
                    wpool = [sb(st, f"wp{i}", [128, 4096], BF16) for i in range(NWP)]
                    b_wp = [P.buf() for _ in range(NWP)]
                    h1 = sb(st, "h1", [128, 4, D])
                    b_h1 = [P.buf() for _ in range(4)]
                    u2b = [sb(st, f"u2b{i}", [128, D], BF16) for i in range(2)]
                    b_u2b = [P.buf() for _ in range(2)]
                    u2T = sb(st, "u2T", [128, 8, 512], BF16)
                    b_u2T = P.buf()
                    rT = [sb(st, f"rT{i}", [128, 512], BF16) for i in range(2)]
                    b_rT = [P.buf() for _ in range(2)]
                    aT = [sb(st, f"aT{i}", [128, 4, 512], BF16) for i in range(2)]
                    b_aT = [P.buf() for _ in range(2)]
                    ss2 = sb(st, "ss2", [128, 4])
                    b_ss2 = [P.buf() for _ in range(4)]
                    jq = sb(st, "jq2", [128, D], BF16)
                    b_jq = P.buf()
                    ot = [sb(st, f"ot{i}", [128, D]) for i in range(2)]
                    b_ot = [P.buf() for _ in range(2)]
                    for k in range(4):
                        P.dma(SP, lambda e, k=k: e.dma_start(out=h1[:, k, :], in_=xw[j][t_own, 2 + 128 * k:130 + 128 * k, :]), writes=[b_h1[k]], sbuf=b_h1[k])
                    mn = [0]
                    wvs = []
                    for cb in range(4):
                        wi = wpn[0] % NWP
                        wpn[0] += 1
                        wv = wpool[wi][:, :].rearrange("p (k c) -> p k c", k=16)
                        P.dma(SP, lambda e, wv=wv, cb=cb: e.dma_start(out=wv, in_=ws_out[:, :, cb * 256:(cb + 1) * 256]), reads=[b_ws["out"]], writes=[b_wp[wi]], sbuf=b_wp[wi])
                        wvs.append((wv, wi))
                    tpv = banks[2][:, :].bitcast(BF16).rearrange("p (k t) -> p k t", k=8)

                    def n2_front(k):
                        rms_rstd(ACT, h1[:, k, :], 128, D, jq[:, :], b_jq, ss2[:, k:k + 1], b_ss2[k], [b_h1[k]])
                        P.op(ACT, lambda e: e.activation(u2b[k % 2][:, :], h1[:, k, :], AF.Copy, scale=ss2[:, k:k + 1]), reads=[b_h1[k], b_ss2[k]], writes=[b_u2b[k % 2]])

                    def n2_back(k):
                        for kc in range(8):
                            P.op(PE, lambda e, kc=kc: e.transpose(tpv[:, kc, :], u2b[k % 2][:, kc * 128:(kc + 1) * 128], identb[:, :]), reads=[b_u2b[k % 2], b_idb], writes=[bk[2]])
                        P.op(DVE, lambda e: e.tensor_tensor(u2T[:, :, 128 * k:128 * k + 128], tpv, w2T.unsqueeze(2).to_broadcast([128, 8, 128]), ALU.mult),
                             reads=[bk[2], b_gt], writes=[b_u2T])

                    for k in range(4):
                        for cb in range(4):
                            wv, wi = wvs[cb]
                            pb = mn[0] % 2
                            mn[0] += 1
                            for kc in range(16):
                                P.op(PE, lambda e, kc=kc, wv=wv: e.matmul(banks[pb][:, 0:256], mixT[:, kc, 128 * k:128 * k + 128], wv[:, kc, :], start=(kc == 0), stop=(kc == 15)),
                                     reads=[b_mix, b_wp[wi]], writes=[bk[pb]])
                            P.op(DVE, lambda e, cb=cb: e.tensor_tensor(h1[:, k, cb * 256:(cb + 1) * 256], h1[:, k, cb * 256:(cb + 1) * 256], banks[pb][:, 0:256], ALU.add),
                                 reads=[b_h1[k], bk[pb]], writes=[b_h1[k]])
                        n2_front(k)
                        if k >= 1:
                            n2_back(k - 1)
                    n2_back(3)
                    un = [0]
                    dn_w = {}

                    def emit_up(p_):
                        wi = wpn[0] % NWP
                        wpn[0] += 1
                        wu = wpool[wi][:, :].rearrange("p (k c) -> p k c", k=8)
                        P.dma(SP, lambda e: e.dma_start(out=wu, in_=ws_up[:, :, p_ * 512:(p_ + 1) * 512]), reads=[b_ws["up"]], writes=[b_wp[wi]], sbuf=b_wp[wi])
                        wj = wpn[0] % NWP
                        wpn[0] += 1
                        wd = wpool[wj][:, :].rearrange("p (k c) -> p k c", k=4)
                        P.dma(SP, lambda e: e.dma_start(out=wd, in_=ws_dn[:, p_ * 4:(p_ + 1) * 4, :]), reads=[b_ws["dn"]], writes=[b_wp[wj]], sbuf=b_wp[wj])
                        dn_w[p_] = (wd, wj)
                        ai = p_ % 2
                        for fc in range(4):
                            pb = 3 + (un[0] % 2)
                            ri = un[0] % 2
                            un[0] += 1
                            for kc in range(8):
                                P.op(PE, lambda e, kc=kc: e.matmul(banks[pb][:, :], wu[:, kc, fc * 128:(fc + 1) * 128], u2T[:, kc, :], start=(kc == 0), stop=(kc == 7)),
                                     reads=[b_u2T, b_wp[wi]], writes=[bk[pb]])
                            P.op(ACT, lambda e: e.activation(rT[ri][:, :], banks[pb][:, :], AF.Relu), reads=[bk[pb]], writes=[b_rT[ri]])
                            P.op(DVE, lambda e: e.tensor_tensor(aT[ai][:, fc, :], rT[ri][:, :], rT[ri][:, :], ALU.mult), reads=[b_rT[ri]], writes=[b_aT[ai]])

                    def emit_down(p_):
                        wd, wj = dn_w[p_]
                        ai = p_ % 2
                        for k in range(4):
                            for ch in range(2):
                                pb = 5 + (un[0] % 2)
                                un[0] += 1
                                for fc in range(4):
                                    P.op(PE, lambda e, fc=fc: e.matmul(banks[pb][:, :], aT[ai][:, fc, 128 * k:128 * k + 128], wd[:, fc, ch * 512:(ch + 1) * 512], start=(fc == 0), stop=(fc == 3)),
                                         reads=[b_aT[ai], b_wp[wj]], writes=[bk[pb]])
                                P.op(DVE, lambda e: e.tensor_tensor(h1[:, k, ch * 512:(ch + 1) * 512], h1[:, k, ch * 512:(ch + 1) * 512], banks[pb][:, :], ALU.add),
                                     reads=[b_h1[k], bk[pb]], writes=[b_h1[k]])

                    emit_up(0)
                    for p_ in range(8):
                        if p_ + 1 < 8:
                            emit_up(p_ + 1)
                        emit_down(p_)
                    for k in range(4):
                        P.op(ACT, lambda e, k=k: e.activation(jq[:, :], h1[:, k, :], AF.Square, accum_out=ss2[:, k:k + 1]), reads=[b_h1[k]], writes=[b_jq, b_ss2[k]])
                    for k in range(4):
                        P.op(DVE, lambda e, k=k: e.tensor_scalar(ss2[:, k:k + 1], ss2[:, k:k + 1], 1.0 / D, EPS, ALU.mult, ALU.add), reads=[b_ss2[k]], writes=[b_ss2[k]])
                    for k in range(4):
                        P.op(ACT, lambda e, k=k: e.activation(ss2[:, k:k + 1], ss2[:, k:k + 1], AF.Ln), reads=[b_ss2[k]], writes=[b_ss2[k]])
                    for k in range(4):
                        P.op(ACT, lambda e, k=k: e.activation(ss2[:, k:k + 1], ss2[:, k:k + 1], AF.Exp, scale=-0.5), reads=[b_ss2[k]], writes=[b_ss2[k]])
                    for k in range(4):
                        oi2 = k % 2
                        P.op(DVE, lambda e, k=k, oi2=oi2: e.scalar_tensor_tensor(ot[oi2][:, :], h1[:, k, :], ss2[:, k:k + 1], finw, ALU.mult, ALU.mult), reads=[b_h1[k], b_ss2[k], b_gt], writes=[b_ot[oi2]])
                        P.dma(POOL, lambda e, k=k, oi2=oi2: e.dma_start(out=yout[j][oi * 512 + 128 * k:oi * 512 + 128 * k + 128, :], in_=ot[oi2][:, :]), reads=[b_ot[oi2]], sbuf=b_ot[oi2])
                    P.barrier()
                    P.release(mkM)
            P.release(mkT)
    P.finish()
    return nc, P


def _consts():
    c = np.zeros((128, 6 * 128 + 4 * 512), np.float32)
    i = np.arange(128)
    c[:, 0:128] = np.eye(128)
    c[:, 128:256] = (i[:, None] <= i[None, :])
    c[:, 256:384] = -1.0 * (i[:, None] < i[None, :])
    c[:, 384:512] = 1.0
    c[:, 512:640] = (i[None, :] >= i[:, None])
    c[:, 640:768] = (i[None, :] <= i[:, None])
    q = np.arange(512)
    for kk in range(4):
        c[:, 768 + 512 * kk:768 + 512 * (kk + 1)] = np.abs((128 * kk + i)[:, None] - q[None, :])
    return c


def _job_arrays(cfg, seq, own_start, params, is_prompt):
    meta = params["meta_tokens"]
    S = seq.shape[0]
    full = np.concatenate([meta, seq], axis=0)
    L = full.shape[0]
    NOWN = cfg.NOWN
    own_s0 = 16 + own_start
    own_s1 = own_s0 + NOWN * 512
    tiles = []
    kinds = []
    tiles.append(np.arange(-2, 18)); kinds.append("L")
    for s0 in range(16, own_s0, 512):
        tiles.append(np.arange(s0 - 2, s0 + 514)); kinds.append("L")
    for s0 in range(L - 512, own_s1 - 1, -512):
        tiles.append(np.arange(s0 + 513, s0 - 3, -1)); kinds.append("R")
    for s0 in range(own_s0, own_s1, 512):
        tiles.append(np.arange(s0 - 2, s0 + 514)); kinds.append("O")
    nt = len(tiles)
    xwin = np.zeros((nt, 516, D), np.float32)
    for t, idx in enumerate(tiles):
        ok = (idx >= 0) & (idx < L)
        xwin[t, np.nonzero(ok)[0]] = full[idx[ok]]
    pos = [tiles[0][2:18]] + [tl[2:514] for tl in tiles[1:]]
    kindtok = np.concatenate([np.full(len(p), {"L": 0, "R": 1, "O": 2}[k]) for p, k in zip(pos, kinds)])
    pos = np.concatenate(pos).astype(np.int64)
    Ls = len(pos)
    slopes = 2.0 ** (-(np.arange(NH) + 1.0))
    cpos, rpos = (pos // 128).astype(np.float32), (pos % 128).astype(np.float32)
    kaug = np.zeros((NH, 8, Ls), np.float32)
    for h in range(NH):
        sl = slopes[h]
        left = np.stack([-np.ones(Ls), -np.ones(Ls), sl * 128 * cpos, sl * rpos])
        right = -left
        ml = (kindtok != 1)[None, :]
        mr = (kindtok != 0)[None, :]
        kaug[h, 0:4] = left * ml
        kaug[h, 4:8] = right * mr
    qpos = np.arange(own_s0, own_s1)
    qc, qr = (qpos // 128).astype(np.float32), (qpos % 128).astype(np.float32)
    qaug = np.zeros((3, NH, 8, NOWN * 512), np.float32)
    for h in range(NH):
        sl = slopes[h]
        qa = np.stack([sl * 128 * qc, sl * qr, np.ones_like(qc), np.ones_like(qc)])
        qaug[0, h, 0:4] = qa; qaug[0, h, 4:8] = qa
        qaug[1, h, 0:4] = qa
        qaug[2, h, 4:8] = qa
    cw = params["conv_w"][0]
    cb = params["conv_b"][0]
    tab = np.zeros((nt, 128, TT), np.float32)
    cwT = cw.T.reshape(12, 128, 5).transpose(1, 0, 2)
    cbT = cb.reshape(12, 128).T
    last_left = max(t for t, k in enumerate(kinds) if k == "L")
    for t, k in enumerate(kinds):
        taps = cwT[:, :, ::-1] if k == "R" else cwT
        tab[t, :, 0:60] = taps.reshape(128, 60)
        tab[t, :, 60:72] = cbT
        if k == "R":
            prim_b, prim_a, sf, sb_ = params["dt_bias_b"][0], params["a_log_b"][0], 0.0, 1.0
        else:
            prim_b, prim_a, sf, sb_ = params["dt_bias_f"][0], params["a_log_f"][0], 1.0, 0.0
        tab[t, :, 72:88] = prim_b[None, :]
        tab[t, :, 88:104] = params["dt_bias_b"][0][None, :]
        tab[t, :, 104:120] = prim_a[None, :]
        tab[t, :, 120:136] = params["a_log_b"][0][None, :]
        tab[t, :, 136] = sf
        tab[t, :, 137] = sb_
        tab[t, :, 138] = 1.0 if t == last_left else 0.0
        tab[t, :, 139] = 0.0 if t == last_left else 1.0
    return dict(xw=xwin, kaug=kaug.astype(ml_dtypes.bfloat16), qaug=qaug.astype(ml_dtypes.bfloat16), ttab=tab)


_CACHE = {}


def run(cfg, inputs):
    p = {k: np.asarray(v, np.float32) for k, v in inputs.items()}
    key = (cfg.NC, cfg.SP, cfg.SS, cfg.NSEQ, cfg.NOWN)
    if key not in _CACHE:
        _CACHE[key] = build_program(cfg)
    nc, P = _CACHE[key]
    gt = np.zeros((128, 8 + 8 + 16 + 1024 + 1024 + 128 + 256), np.float32)
    gt[:, 0:8] = p["norm1_w"][0].reshape(8, 128).T
    gt[:, 8:16] = p["norm2_w"][0].reshape(8, 128).T
    gt[:, 16:32] = p["d_skip"][0][None, :]
    gt[:, 32:1056] = p["ssm_norm_w"][0][None, :]
    gt[:, 1056:2080] = p["final_norm_w"][None, :]
    gt[:, 2080:2208] = p["attn_norm_w"][0][None, :]
    gt[:, 2208:2272] = p["lambda_q1"][0][None, :]
    gt[:, 2272:2336] = p["lambda_k1"][0][None, :]
    gt[:, 2336:2400] = p["lambda_q2"][0][None, :]
    gt[:, 2400:2464] = p["lambda_k2"][0][None, :]
    cst = _consts()
    in_maps = []
    xp = p["x_prompt"][0]
    xs = p["x_sample"]
    for c in range(cfg.NC):
        m = {"w_in": p["w_in"][0], "w_out": p["w_out"][0], "w_up": p["w_up"][0], "w_dn": p["w_down"][0], "cst": cst, "gtab": gt}
        ja = [_job_arrays(cfg, xp, c * cfg.OWNT, p, True)]
        for s in range(cfg.NSEQ):
            ja.append(_job_arrays(cfg, xs[c * cfg.NSEQ + s], 0, p, False))
        for j, a in enumerate(ja):
            m[f"xw{j}"] = a["xw"]
            m[f"kaug{j}"] = a["kaug"]
            m[f"qaug{j}"] = a["qaug"]
            m[f"ttab{j}"] = a["ttab"]
        in_maps.append(m)
    res = run_bass_kernel_spmd(nc, in_maps, core_ids=list(range(cfg.NC)))
    _CACHE["last_exec_ns"] = getattr(res, "exec_time_ns", None)
    yp = np.concatenate([res.results[c]["y0"] for c in range(cfg.NC)], axis=0)[None]
    ys = np.stack([res.results[c][f"y{1 + s}"] for c in range(cfg.NC) for s in range(cfg.NSEQ)], axis=0)
    return yp.astype(np.float32), ys.astype(np.float32)


def kernel(**inputs):
    cfg = Cfg(ncores=8, sp=16384, ss=2048, nseq=4, nown=4)
    return run(cfg, inputs)
```
